# Optimizing a Trainium2 kernel written in Bass

```python
import math
import jax, jax.numpy as jnp
from jax import lax
import numpy as np


D_MODEL = 1024
BATCH = 4
SEQ = 4096
DEPTH = 1

PLE_DIM = 256
D_MIX = D_MODEL
M_HEADS = 4
M_HEAD_DIM = D_MIX // 2 // M_HEADS
M_WIDTH = M_HEADS * M_HEAD_DIM
CHUNK = 64
CONV_WIDTH = 4
F_BIAS_LO = 3.0
F_BIAS_HI = 6.0
A_HEADS = 4
A_HEAD_DIM = (D_MIX - M_WIDTH) // A_HEADS
A_QK_DIM = A_HEAD_DIM // 2
A_WIDTH = A_HEADS * A_HEAD_DIM
ROPE_THETA = 500000.0
ROPE_DIM = A_QK_DIM // 4
Q_BLOCK = 128
D_FF = 4 * D_MODEL
EPS = 1e-6
SPLIT_SIZES = (M_WIDTH, M_WIDTH, M_WIDTH, M_WIDTH, M_HEADS, M_HEADS, A_WIDTH, A_WIDTH, A_WIDTH)
D_IN = 4 * M_WIDTH + 2 * M_HEADS + 3 * A_WIDTH

kernel_name = 'hybrid_mlstm_diffattn_block'


def rmsnorm(x, g):
    xf = x.astype(jnp.float32)
    y = xf * lax.rsqrt(jnp.mean(xf * xf, axis=-1, keepdims=True) + EPS)
    return y * g.astype(jnp.float32)


def split_columns(z):
    parts, off = [], 0
    for s in SPLIT_SIZES:
        parts.append(z[..., off:off + s])
        off += s
    return parts


def lambda_init_fn(layer):
    return 0.8 - 0.6 * math.exp(-0.3 * layer)


def rope_tables(seq):
    pos = jnp.arange(seq, dtype=jnp.float32)
    inv_freq = ROPE_THETA ** (-jnp.arange(0, ROPE_DIM, 2, dtype=jnp.float32) / ROPE_DIM)
    ang = pos[:, None] * inv_freq[None, :]
    return jnp.cos(ang), jnp.sin(ang)


def partial_rope(x, cos, sin):
    half = ROPE_DIM // 2
    c = cos[:, None, None, :]
    s = sin[:, None, None, :]
    x1 = x[..., :half]
    x2 = x[..., half:ROPE_DIM]
    return jnp.concatenate([x1 * c - x2 * s, x2 * c + x1 * s, x[..., ROPE_DIM:]], axis=-1)


def mlstm_mixer(mq, mk, mv, mo, mi, mf, conv_w, conv_b, igate_b, fgate_b, norm_g):
    B, S, _ = mq.shape
    nc = S // CHUNK
    qk = jnp.concatenate([mq, mk], axis=-1)
    C2 = qk.shape[-1]
    qk = lax.conv_general_dilated(qk, conv_w.astype(jnp.float32).reshape(CONV_WIDTH, 1, C2),
                                  window_strides=(1,), padding=[(CONV_WIDTH - 1, 0)],
                                  dimension_numbers=('NWC', 'WIO', 'NWC'),
                                  feature_group_count=C2) + conv_b
    qk = jax.nn.silu(qk)
    q = qk[..., :M_WIDTH] * (M_HEAD_DIM ** -0.5)
    k = qk[..., M_WIDTH:]

    def heads_to_chunks(t):
        return t.reshape(B, nc, CHUNK, M_HEADS, M_HEAD_DIM).transpose(1, 0, 3, 2, 4)

    def gates_to_chunks(t):
        return t.reshape(B, nc, CHUNK, M_HEADS).transpose(1, 0, 3, 2)

    qc, kc, vc = heads_to_chunks(q), heads_to_chunks(k), heads_to_chunks(mv)
    li = gates_to_chunks(mi + igate_b)
    lf = jax.nn.log_sigmoid(gates_to_chunks(mf + fgate_b))
    bcum = jnp.cumsum(lf, axis=-1)
    causal = jnp.tril(jnp.ones((CHUNK, CHUNK), dtype=bool))

    def step(carry, xs):
        Cm, n, m = carry
        qt, kt, vt, lit, bt = xs
        Dlog = bt[..., :, None] - bt[..., None, :] + lit[..., None, :]
        Dlog = jnp.where(causal, Dlog, -jnp.inf)
        inter = bt + m[..., None]
        m_t = jnp.maximum(inter, jnp.max(Dlog, axis=-1))
        w_inter = jnp.exp(inter - m_t)
        Smat = jnp.einsum('bhtd,bhsd->bhts', qt, kt) * jnp.exp(Dlog - m_t[..., None])
        num = w_inter[..., None] * jnp.einsum('bhvd,bhtd->bhtv', Cm, qt) \
            + jnp.einsum('bhts,bhsv->bhtv', Smat, vt)
        den = w_inter * jnp.einsum('bhd,bhtd->bht', n, qt) + jnp.sum(Smat, axis=-1)
        h = num / jnp.maximum(jnp.abs(den), jnp.exp(-m_t))[..., None]
        bL = bt[..., -1]
        g = bL[..., None] - bt + lit
        m_new = jnp.maximum(bL + m, jnp.max(g, axis=-1))
        a_s = jnp.exp(g - m_new[..., None])
        decay = jnp.exp(bL + m - m_new)
        C_new = decay[..., None, None] * Cm + jnp.einsum('bhs,bhsv,bhsd->bhvd', a_s, vt, kt)
        n_new = decay[..., None] * n + jnp.einsum('bhs,bhsd->bhd', a_s, kt)
        return (C_new, n_new, m_new), h

    init = (jnp.zeros((B, M_HEADS, M_HEAD_DIM, M_HEAD_DIM), jnp.float32),
            jnp.zeros((B, M_HEADS, M_HEAD_DIM), jnp.float32),
            jnp.zeros((B, M_HEADS), jnp.float32))
    _, hs = lax.scan(step, init, (qc, kc, vc, li, bcum))
    hs = hs.transpose(1, 0, 3, 2, 4).reshape(B, S, M_HEADS, M_HEAD_DIM)
    hs = rmsnorm(hs, norm_g.reshape(M_HEADS, M_HEAD_DIM)).reshape(B, S, M_WIDTH)
    return jax.nn.sigmoid(mo) * hs


def diff_attn_mixer(aq, ak, av, q_norm_g, k_norm_g, lq1, lk1, lq2, lk2, sub_g, lam_init, cos, sin):
    B, S, _ = aq.shape
    nb = S // Q_BLOCK
    q = rmsnorm(aq.reshape(B, S, A_HEADS, 2, A_QK_DIM), q_norm_g)
    k = rmsnorm(ak.reshape(B, S, A_HEADS, 2, A_QK_DIM), k_norm_g)
    q = partial_rope(q, cos, sin).transpose(0, 2, 3, 1, 4)
    k = partial_rope(k, cos, sin).transpose(0, 2, 3, 1, 4)
    v = av.reshape(B, S, A_HEADS, A_HEAD_DIM).transpose(0, 2, 1, 3)
    lam = jnp.exp(jnp.sum(lq1 * lk1)) - jnp.exp(jnp.sum(lq2 * lk2)) + lam_init
    scale = A_QK_DIM ** -0.5
    qb = q.reshape(B, A_HEADS, 2, nb, Q_BLOCK, A_QK_DIM).transpose(3, 0, 1, 2, 4, 5)
    kpos = jnp.arange(S)

    def block(args):
        qblk, bi = args
        s = jnp.einsum('bhmqd,bhmkd->bhmqk', qblk, k) * scale
        qpos = bi * Q_BLOCK + jnp.arange(Q_BLOCK)
        mask = kpos[None, :] <= qpos[:, None]
        a = jax.nn.softmax(jnp.where(mask, s, -jnp.inf), axis=-1)
        a = a[:, :, 0] - lam * a[:, :, 1]
        return jnp.einsum('bhqk,bhkd->bhqd', a, v)

    o = lax.map(block, (qb, jnp.arange(nb)))
    o = o.transpose(1, 0, 3, 2, 4).reshape(B, S, A_HEADS, A_HEAD_DIM)
    o = rmsnorm(o, sub_g.reshape(A_HEADS, A_HEAD_DIM)) * (1.0 - lam_init)
    return o.reshape(B, S, A_WIDTH)


def setup_inputs(seed: int = 0) -> dict:
    key = jax.random.key(seed)
    ks = jax.random.split(key, 24)
    f32 = jnp.float32

    def nrm(k, shape, scale):
        return jax.random.normal(k, shape, f32) * scale

    def gain(k, shape):
        return 1.0 + 0.02 * jax.random.normal(k, shape, f32)

    fgate_b = jnp.linspace(F_BIAS_LO, F_BIAS_HI, M_HEADS, dtype=f32)[None, :] \
        + 0.1 * jax.random.normal(ks[5], (DEPTH, M_HEADS), f32)
    return {
        'x': nrm(ks[0], (BATCH, SEQ, D_MODEL), 1.0),
        'p': nrm(ks[1], (DEPTH, BATCH, SEQ, PLE_DIM), 1.0),
        'attn_norm_g': gain(ks[2], (DEPTH, D_MODEL)),
        'w_in': nrm(ks[3], (DEPTH, D_MODEL, D_IN), D_MODEL ** -0.5),
        'conv_w': nrm(ks[4], (DEPTH, CONV_WIDTH, 2 * M_WIDTH), CONV_WIDTH ** -0.5),
        'conv_b': nrm(ks[6], (DEPTH, 2 * M_WIDTH), 0.02),
        'igate_b': nrm(ks[7], (DEPTH, M_HEADS), 0.1),
        'fgate_b': fgate_b,
        'mlstm_norm_g': gain(ks[8], (DEPTH, M_WIDTH)),
        'q_norm_g': gain(ks[9], (DEPTH, A_QK_DIM)),
        'k_norm_g': gain(ks[10], (DEPTH, A_QK_DIM)),
        'lambda_q1': nrm(ks[11], (DEPTH, A_QK_DIM), 0.1),
        'lambda_k1': nrm(ks[12], (DEPTH, A_QK_DIM), 0.1),
        'lambda_q2': nrm(ks[13], (DEPTH, A_QK_DIM), 0.1),
        'lambda_k2': nrm(ks[14], (DEPTH, A_QK_DIM), 0.1),
        'attn_sub_norm_g': gain(ks[15], (DEPTH, A_WIDTH)),
        'w_out': nrm(ks[16], (DEPTH, D_MIX, D_MODEL), D_MIX ** -0.5),
        'mlp_norm_g': gain(ks[17], (DEPTH, D_MODEL)),
        'w_up': nrm(ks[18], (DEPTH, D_MODEL, D_FF), D_MODEL ** -0.5),
        'w_down': nrm(ks[19], (DEPTH, D_FF, D_MODEL), D_FF ** -0.5),
        'ple_norm_g': gain(ks[20], (DEPTH, D_MODEL)),
        'w_ple_gate': nrm(ks[21], (DEPTH, D_MODEL, D_MODEL), D_MODEL ** -0.5),
        'w_ple_proj': nrm(ks[22], (DEPTH, PLE_DIM, D_MODEL), PLE_DIM ** -0.5),
    }


def reference(x, p, attn_norm_g, w_in, conv_w, conv_b, igate_b, fgate_b, mlstm_norm_g,
              q_norm_g, k_norm_g, lambda_q1, lambda_k1, lambda_q2, lambda_k2,
              attn_sub_norm_g, w_out, mlp_norm_g, w_up, w_down, ple_norm_g,
              w_ple_gate, w_ple_proj):
    out_dtype = x.dtype
    B, S, _ = x.shape
    h = x.astype(jnp.float32)
    cos, sin = rope_tables(S)
    for l in range(DEPTH):
        u = rmsnorm(h, attn_norm_g[l])
        z = jnp.matmul(u, w_in[l].astype(jnp.float32))
        mq, mk, mv, mo, mi, mf, aq, ak, av = split_columns(z)
        y_m = mlstm_mixer(mq, mk, mv, mo, mi, mf, conv_w[l], conv_b[l],
                          igate_b[l], fgate_b[l], mlstm_norm_g[l])
        y_a = diff_attn_mixer(aq, ak, av, q_norm_g[l], k_norm_g[l],
                              lambda_q1[l].astype(jnp.float32), lambda_k1[l].astype(jnp.float32),
                              lambda_q2[l].astype(jnp.float32), lambda_k2[l].astype(jnp.float32),
                              attn_sub_norm_g[l], lambda_init_fn(l), cos, sin)
        y = jnp.concatenate([y_m, y_a], axis=-1)
        h = h + jnp.matmul(y, w_out[l].astype(jnp.float32))
        u2 = rmsnorm(h, mlp_norm_g[l])
        hid = jnp.square(jax.nn.relu(jnp.matmul(u2, w_up[l].astype(jnp.float32))))
        h = h + jnp.matmul(hid, w_down[l].astype(jnp.float32))
        gate = jax.nn.sigmoid(jnp.matmul(rmsnorm(h, ple_norm_g[l]), w_ple_gate[l].astype(jnp.float32)))
        e = jnp.matmul(p[l].astype(jnp.float32), w_ple_proj[l].astype(jnp.float32))
        h = h + gate * e
    return h.astype(out_dtype)
```

```python
import math
from contextlib import ExitStack

import numpy as np
import concourse.bass as bass
import concourse.mybir as mybir
from concourse.bass_utils import run_bass_kernel_spmd

F32 = mybir.dt.float32
BF16 = mybir.dt.bfloat16
AF = mybir.ActivationFunctionType
ALU = mybir.AluOpType
AX = mybir.AxisListType

SAME_ENGINE_SYNC = True
A_LAG = 1
A2_LAG = 6
PE_WARM_REPS = 1
EPS = 1e-6
NT = 16
NEG = -30000.0
LAM_INIT = 0.8 - 0.6 * math.exp(0.0)
LNC = math.log(128 ** -0.5)

O_MQ, O_MK, O_MV, O_MO, O_MI, O_MF, O_AQ, O_AK, O_AV = 0, 512, 1024, 1536, 2048, 2052, 2056, 2568, 3080

C_G1, C_G2, C_G3, C_GY, C_CW, C_CB, C_QG, C_KG, C_LAM, C_FL, C_CSO, C_CSP, C_ID, C_TRI, C_GB = (
    0, 8, 16, 24, 32, 64, 72, 136, 200, 456, 460, 716, 972, 1100, 1228)
C_TOT = 1230


class _Eng:
    def __init__(self, name, h, sem):
        self.name, self.h, self.sem = name, h, sem
        self.n = 0
        self.count = 0
        self.incs = []
        self.last = None
        self.last_seq = 0
        self.waited = {}
        self.dsems = []
        self.dcnt = []
        self.dnext = 0


class Sched:
    def __init__(self, nc, stack, ndma):
        self.nc = nc
        hs = {'pe': nc.tensor, 'act': nc.scalar, 'dve': nc.vector, 'pool': nc.gpsimd, 'sp': nc.sync}
        self.e = {}
        for k, h in hs.items():
            sem = stack.enter_context(nc.semaphore("s_" + k))
            self.e[k] = _Eng(k, h, sem)
            for i in range(ndma.get(k, 0)):
                self.e[k].dsems.append(stack.enter_context(nc.semaphore("d_%s%d" % (k, i))))
                self.e[k].dcnt.append(0)
        self.st = {}

    def _target(self, dep):
        if dep[0] == 'd':
            return dep[1], dep[2]
        p = self.e[dep[1]]
        seq = dep[2]
        found = None
        for (s, c) in reversed(p.incs):
            if s >= seq:
                found = c
            else:
                break
        if found is None:
            p.count += 1
            p.last.then_inc(p.sem, 1)
            p.incs.append((p.last_seq, p.count))
            found = p.count
        return p.sem, found

    def _wait(self, eng, deps):
        E = self.e[eng]
        for dep in deps:
            if dep is None:
                continue
            if dep[0] == 'c' and dep[1] == eng and (eng == 'pe' or not SAME_ENGINE_SYNC):
                continue
            sem, val = self._target(dep)
            key = id(sem)
            if E.waited.get(key, 0) >= val:
                continue
            E.h.wait_ge(sem, val)
            E.waited[key] = val

    def _deps(self, r, w, eng=None):
        deps = []
        for k in r:
            s = self.st.get(k)
            if s is not None:
                deps.append(s[0])
                if isinstance(k, tuple) and k[0] == 'bank':
                    deps.extend(d for d in s[1].values() if not (d[0] == 'c' and d[1] == eng))
        for k in w:
            s = self.st.get(k)
            if s is not None:
                deps.append(s[0])
                deps.extend(s[1].values())
        return deps

    def _record(self, dep, r, w):
        rk = (dep[0], dep[1] if dep[0] == 'c' else id(dep[1]))
        for k in r:
            s = self.st.setdefault(k, [None, {}])
            s[1][rk] = dep
        for k in w:
            self.st[k] = [dep, {}]

    def op(self, eng, fn, r=(), w=(), sig=None):
        E = self.e[eng]
        self._wait(eng, self._deps(r, w, eng))
        inst = fn(E.h)
        E.n += 1
        E.last = inst
        E.last_seq = E.n
        if sig or (sig is None and eng != 'pe'):
            E.count += 1
            inst.then_inc(E.sem, 1)
            E.incs.append((E.n, E.count))
        self._record(('c', eng, E.n), r, w)
        return inst

    def dma(self, q, out, in_, r=(), w=(), **kw):
        E = self.e[q]
        self._wait(q, self._deps(r, w))
        i = E.dnext
        E.dnext = (i + 1) % len(E.dsems)
        sem = E.dsems[i]
        if E.dcnt[i] > 0:
            key = id(sem)
            if E.waited.get(key, 0) < E.dcnt[i]:
                E.h.wait_ge(sem, E.dcnt[i])
                E.waited[key] = E.dcnt[i]
        E.dcnt[i] += 16
        E.h.dma_start(out=out, in_=in_, **kw).then_inc(sem, 16)
        dep = ('d', sem, E.dcnt[i])
        self._record(dep, r, w)
        return dep

    def barrier(self):
        deps = []
        for s in self.st.values():
            if s[0] is not None:
                deps.append(s[0])
            deps.extend(s[1].values())
        for eng in self.e:
            self._wait(eng, deps)

    def finish(self, eng, keys):
        self._wait(eng, [self.st[k][0] for k in keys if k in self.st])


def build(dbg=(), n_pre=NT, n_own=NT, phases='CDE'):
    nc = bass.Bass("TRN2", target_bir_lowering=False)

    def din(name, shape):
        return nc.dram_tensor(name, list(shape), F32, kind="ExternalInput").ap()

    xo = din("xo", [2048, 1024])
    xp = din("xp", [2048, 1024])
    po = din("po", [2048, 256])
    w_in = din("w_in", [1024, 3592])
    w_out = din("w_out", [1024, 1024])
    w_up = din("w_up", [1024, 4096])
    w_down = din("w_down", [4096, 1024])
    w_gate = din("w_gate", [1024, 1024])
    w_ple = din("w_ple", [256, 1024])
    cst_d = din("cst", [128, C_TOT])
    cstb_d = din("cstb", [128, 128 + 1024])
    out_d = nc.dram_tensor("out", [2048, 1024], F32, kind="ExternalOutput").ap()
    dbg_d = {}
    for name, shape, dt in dbg:
        dbg_d[name] = nc.dram_tensor("dbg_" + name, list(shape), dt, kind="ExternalOutput").ap()

    with ExitStack() as st:
        S = Sched(nc, st, {'sp': 12, 'pool': 8})

        def sb(name, shape, dt=F32):
            return st.enter_context(nc.sbuf_tensor("t_" + name, list(shape), dt))

        def freed(name, shape, dt=F32):
            return nc.sbuf_tensor(name, list(shape), dt)

        cst = sb("cst", [128, C_TOT])
        cstb = sb("cstb", [128, 128 + 1024], BF16)
        identb = cstb[:, 0:128]
        identf = cst[:, C_ID:C_ID + 128]
        tri = cst[:, C_TRI:C_TRI + 128]
        banks = [st.enter_context(nc.psum_tensor("bank%d" % i, [128, 512], F32)) for i in range(8)]
        small = sb("small", [128, 64])
        block = st.enter_context(nc.Block())

        S.dma('sp', cst[:], cst_d, w=['cst'])
        S.dma('pool', cstb[:], cstb_d, w=['cstb'])

        def bk(i):
            return ('bank', i)

        def rstd_from_ss(ss_ap, out_ap, n, key_ss, key_out, mul=None):
            S.op('act', lambda e: e.activation(out_ap, ss_ap, AF.Ln, bias=epsc[:ss_ap.shape[0], 0:1], scale=1.0 / n),
                 r=[key_ss, 'consts'], w=[key_out])
            S.op('act', lambda e: e.activation(out_ap, out_ap, AF.Exp, scale=-0.5), r=[key_out], w=[key_out])
            if mul is not None:
                S.op('dve', lambda e: e.tensor_scalar(out_ap, out_ap, float(mul), None, ALU.mult), r=[key_out], w=[key_out])

        epsc = sb("epsc", [128, 4])
        ones4 = sb("ones4", [4, 128])
        S.op('pool', lambda e: e.memset(epsc[:, 0:1], EPS), w=['consts'])
        S.op('pool', lambda e: e.memset(epsc[:, 1:2], 1.0), w=['consts'])
        S.op('pool', lambda e: e.memset(ones4[:], 1.0), w=['consts4'])

        lamv = cst[:, C_LAM:C_LAM + 256]
        lj = sb("lj", [128, 64])
        S.op('dve', lambda e: e.scalar_tensor_tensor(out=lj[:], in0=lamv[:, 0:64], scalar=1.0, in1=lamv[:, 64:128],
                                                     op0=ALU.mult, op1=ALU.mult, accum_out=small[:, 0:1]),
             r=['cst'], w=['lj', 'small'])
        S.op('dve', lambda e: e.scalar_tensor_tensor(out=lj[:], in0=lamv[:, 128:192], scalar=1.0, in1=lamv[:, 192:256],
                                                     op0=ALU.mult, op1=ALU.mult, accum_out=small[:, 1:2]),
             r=['cst', 'lj'], w=['lj', 'small'])
        S.op('act', lambda e: e.activation(small[:, 2:4], small[:, 0:2], AF.Exp), r=['small'], w=['small'])
        S.op('dve', lambda e: e.tensor_tensor(small[:, 4:5], small[:, 2:3], small[:, 3:4], ALU.subtract), r=['small'], w=['small'])
        S.op('dve', lambda e: e.tensor_scalar(small[:, 5:6], small[:, 4:5], float(LAM_INIT), None, ALU.add), r=['small'], w=['small'])
        lam_ap = small[:, 5:6]
        gpar = sb("gpar", [4, 8])
        gb = cst[0:4, C_GB:C_GB + 2]
        fl = cst[0:4, C_FL:C_FL + 4]
        S.op('dve', lambda e: e.tensor_copy(gpar[:, 0:1], fl[:, 0:1]), r=['cst'], w=['gpar'])
        S.op('dve', lambda e: e.scalar_tensor_tensor(out=gpar[:, 1:2], in0=gb[:, 0:1], scalar=fl[:, 0:1], in1=fl[:, 1:2],
                                                     op0=ALU.mult, op1=ALU.add), r=['cst', 'gpar'], w=['gpar'])
        S.op('pool', lambda e: e.memset(gpar[:, 2:3], 1.0), r=['gpar'], w=['gpar'])
        S.op('dve', lambda e: e.tensor_copy(gpar[:, 3:4], gb[:, 0:1]), r=['cst', 'gpar'], w=['gpar'])
        S.op('dve', lambda e: e.tensor_scalar(gpar[:, 4:5], gb[:, 1:2], -1.0, None, ALU.mult), r=['cst', 'gpar'], w=['gpar'])
        S.op('dve', lambda e: e.tensor_copy(gpar[:, 5:6], fl[:, 0:1]), r=['cst', 'gpar'], w=['gpar'])
        pbias = cst[:, C_FL + 2:C_FL + 3]

        yT = sb("yT", [128, 8, 2048], BF16)
        ab = ExitStack()

        def sa(name, shape, dt=F32):
            return ab.enter_context(nc.sbuf_tensor("t_" + name, list(shape), dt))

        win = sa("win", [128, 8, 3592], BF16)
        kT = sa("kT", [128, 4, 4096], BF16)
        vx = sa("vx", [128, 32, 4, 130], BF16)
        xt = sa("xt", [128, 1024])
        u = sa("u", [128, 1024], BF16)
        uT = [sa("uT%d" % i, [128, 8, 128], BF16) for i in range(2)]
        ssb = sa("ssb", [128, 8])
        raw = sa("raw", [128, 8, 131])
        acc = sa("acc", [128, 4, 128])
        qkT2 = [sa("qkT%d" % i, [128, 8, 128], BF16) for i in range(2)]
        gf = sa("gf", [4, 8, 128])
        carry = sa("carry", [4, 8])
        Gt = [small[:, 16 + 12 * i:28 + 12 * i] for i in range(3)]
        mvx = [sa("mvx%d" % i, [128, 4, 130], BF16) for i in range(2)]
        sigo2 = [sa("sigo%d" % i, [128, 512], BF16) for i in range(2)]
        mls = sa("mls", [128, 512], BF16)
        ktl = [mls[:, i * 128:(i + 1) * 128] for i in range(2)]
        stm = [mls[:, 256 + i * 128:256 + (i + 1) * 128] for i in range(2)]
        Pst = sa("Pst", [128, 4, 130])
        Cbf = sa("Cbf", [128, 4, 130], BF16)
        ml = sa("ml", [128, 32])
        qf = sa("qf", [128, 512])
        qb16 = sa("qb16", [128, 512], BF16)
        qtok = [sa("qtok%d" % i, [128, 512], BF16) for i in range(2)]
        qTg = [sa("qTg%d" % i, [128, 4, 256], BF16) for i in range(2)]
        Et = [sa("Et%d" % i, [128, 512], BF16) for i in range(2)]
        ofin = [sa("ofin%d" % i, [128, 128]) for i in range(2)]
        yat = [sa("yat%d" % i, [128, 512], BF16) for i in range(2)]
        al = sa("al", [128, 16])
        ae = small[:, 8:16]

        w_in_v = w_in.rearrange("(k p) n -> p k n", p=128)
        for k in range(8):
            S.dma('pool', win[:, k, :], w_in_v[:, k, :], w=[('win', k)])
        WIN_KEYS = [('win', k) for k in range(8)]

        S.op('pool', lambda e: e.memset(vx[:, :, :, 128:130], 1.0), w=[('vx', j) for j in range(32)])
        S.op('pool', lambda e: e.tensor_scalar(vx[:, 0:16, :, 128:130], vx[:, 0:16, :, 128:130], cst[:, C_FL:C_FL + 1], 1.0,
                                               ALU.mult, ALU.mult),
             r=['cst'] + [('vx', j) for j in range(16)], w=[('vx', j) for j in range(16)])
        for i in range(2):
            S.op('pool', lambda e: e.memset(mvx[i][:, :, 128:130], 1.0), w=[('mvx', i)])
        for i in range(3):
            S.op('pool', lambda e: e.memset(Gt[i], 1.0), w=[('Gt', i)])
        S.op('pool', lambda e: e.memset(qTg[0][:], 0.0), w=[('qTg', 0, 0), ('qTg', 0, 1)])
        S.op('pool', lambda e: e.memset(qTg[1][:], 0.0), w=[('qTg', 1, 0), ('qTg', 1, 1)])
        S.op('pool', lambda e: e.memset(Pst[:], 0.0), w=[('Pst', h) for h in range(4)])
        S.op('pool', lambda e: e.memset(Cbf[:], 0.0), w=['Cbf'])
        S.op('pool', lambda e: e.memset(raw[:], 0.0), w=[('raw', c) for c in range(8)])
        S.op('pool', lambda e: e.memset(carry[:], 0.0), w=[('carry', j) for j in range(8)])

        cw = cst[:, C_CW:C_CW + 32].rearrange("p (c j) -> p c j", j=4)
        cb = cst[:, C_CB:C_CB + 8]
        cnt = {'tt': 0}

        def norm_to_uT(x_tile, kx, gcol, uT_t, kuT, scr=None):
            if scr is None:
                u_, ku, ssb_, kss, tb = u, 'u', ssb, 'ssb', 0
            else:
                u_, ku, ssb_, kss, tb = scr
            return _norm_to_uT(x_tile, kx, gcol, uT_t, kuT, u_, ku, ssb_, kss, tb)

        def _norm_to_uT(x_tile, kx, gcol, uT_t, kuT, u, ku, ssb, kss, tb):
            _norm_a1(x_tile, kx, u, ku, ssb, kss)
            _norm_a2(gcol, uT_t, kuT, u, ku, tb)

        def _norm_a1(x_tile, kx, u, ku, ssb, kss):
            S.op('act', lambda e: e.activation(u[:], x_tile, AF.Square, accum_out=ssb[:, 0:1]), r=[kx], w=[ku, kss])
            rstd_from_ss(ssb[:, 0:1], ssb[:, 1:2], 1024.0, kss, kss)
            S.op('dve', lambda e: e.tensor_scalar(u[:], x_tile, ssb[:, 1:2], None, ALU.mult), r=[kx, kss], w=[ku])

        def _norm_a2(gcol, uT_t, kuT, u, ku, tb):
            psb = banks[tb][:].bitcast(BF16)
            for k in range(8):
                S.op('pe', lambda e: e.transpose(psb[:, k * 128:(k + 1) * 128], u[:, k * 128:(k + 1) * 128], identb),
                     r=[ku, 'cstb'], w=[bk(tb)], sig=k == 7)
            g_bc = cst[:, gcol:gcol + 8].unsqueeze(2).to_broadcast([128, 8, 128])
            S.op('dve', lambda e: e.tensor_tensor(uT_t[:], psb[:, 0:1024].rearrange("p (k t) -> p k t", k=8), g_bc, ALU.mult),
                 r=[bk(tb), 'cst'], w=[kuT])

        def transpose_to(src_tok, ksrc, dst_ap, kdst, scale_cols=None, bank=0):
            psb = banks[bank][:].bitcast(BF16)
            for k in range(4):
                S.op('pe', lambda e: e.transpose(psb[:, k * 128:(k + 1) * 128], src_tok[:, k * 128:(k + 1) * 128], identb),
                     r=[ksrc, 'cstb'], w=[bk(bank)], sig=k == 3)
            src = psb[:, 0:512].rearrange("p (k t) -> p k t", k=4)
            if scale_cols is None:
                S.op('dve', lambda e: e.tensor_copy(dst_ap, src), r=[bk(bank)], w=[kdst])
            else:
                g_bc = scale_cols.unsqueeze(2).to_broadcast([128, 4, 128])
                S.op('dve', lambda e: e.tensor_tensor(dst_ap, src, g_bc, ALU.mult), r=[bk(bank), 'cst'], w=[kdst])

        def proj_tm(uT_t, kuT, col, bank):
            for k in range(8):
                S.op('pe', lambda e: e.matmul(banks[bank][:, :], uT_t[:, k, :], win[:, k, col:col + 512],
                                              start=(k == 0), stop=(k == 7)),
                     r=[kuT, ('win', k)], w=[bk(bank)], sig=k == 7)

        def rr(*gens):
            gens = [g for g in gens if g is not None]
            while gens:
                for g in list(gens):
                    try:
                        next(g)
                    except StopIteration:
                        gens.remove(g)
                yield

        def run(gen):
            for _ in gen:
                pass


        def qk_prep(bank, gcol, cs, qb16, k16):
            q3 = qf[:].rearrange("p (g d) -> p g d", d=64)
            S.op('act', lambda e: e.activation(qf[:], banks[bank][:, :], AF.Square), r=[bk(bank)], w=['qf'])
            S.op('dve', lambda e: e.tensor_reduce(out=al[:, 0:8], in_=q3, axis=AX.X, op=ALU.add), r=['qf'], w=['al'])
            S.op('act', lambda e: e.activation(al[:, 8:16], al[:, 0:8], AF.Ln, bias=epsc[:, 0:1], scale=1.0 / 64),
                 r=['al', 'consts'], w=['al'])
            S.op('act', lambda e: e.activation(al[:, 8:16], al[:, 8:16], AF.Exp, scale=-0.5), r=['al'], w=['al'])
            S.op('dve', lambda e: e.tensor_tensor(q3, banks[bank][:, :].rearrange("p (g d) -> p g d", d=64),
                                                  al[:, 8:16].unsqueeze(2).to_broadcast([128, 8, 64]), ALU.mult),
                 r=[bk(bank), 'al'], w=['qf'])
            gg = cst[:, gcol:gcol + 64].unsqueeze(1).to_broadcast([128, 8, 64])
            o3 = qb16[:].rearrange("p (g d) -> p g d", d=64)
            yield S.op('dve', lambda e: e.tensor_tensor(o3, q3, gg, ALU.mult), r=['qf', 'cst'], w=[k16])
            gg16 = cst[:, gcol:gcol + 16].unsqueeze(1).to_broadcast([128, 8, 16])
            yield S.op('dve', lambda e: e.tensor_tensor(q3[:, :, 0:16], q3[:, :, 0:16], gg16, ALU.mult), r=['qf', 'cst'], w=['qf'])
            x1, x2 = q3[:, :, 0:8], q3[:, :, 8:16]
            cc = cs[:, 0:8].unsqueeze(1).to_broadcast([128, 8, 8])
            sn = cs[:, 8:16].unsqueeze(1).to_broadcast([128, 8, 8])
            r4 = [q3[:, :, 16 + 8 * a_:24 + 8 * a_] for a_ in range(4)]
            S.op('dve', lambda e: e.tensor_tensor(r4[0], x1, cc, ALU.mult), r=['qf', 'cst'], w=['qf'])
            S.op('dve', lambda e: e.tensor_tensor(r4[1], x2, sn, ALU.mult), r=['qf', 'cst'], w=['qf'])
            S.op('dve', lambda e: e.tensor_tensor(r4[2], x2, cc, ALU.mult), r=['qf', 'cst'], w=['qf'])
            yield S.op('dve', lambda e: e.tensor_tensor(r4[3], x1, sn, ALU.mult), r=['qf', 'cst'], w=['qf'])
            S.op('dve', lambda e: e.tensor_tensor(o3[:, :, 0:8], r4[0], r4[1], ALU.subtract), r=['qf', k16], w=[k16])
            yield S.op('dve', lambda e: e.tensor_tensor(o3[:, :, 8:16], r4[2], r4[3], ALU.add), r=['qf', k16], w=[k16])

        def attention_group(g):
            kbs = list(range(16)) + [16 + j for j in range(2 * g + 2)]
            units = [(h, idx, kb) for h in range(4) for idx, kb in enumerate(kbs)]
            PSB = (4, 5)
            for tt in range(2):
                psb = banks[4 + tt][:].bitcast(BF16)
                for k in range(4):
                    S.op('pe', lambda e: e.transpose(psb[:, k * 128:(k + 1) * 128], qtok[tt][:, k * 128:(k + 1) * 128], identb),
                         r=[('qtok', tt), 'cstb'], w=[bk(4 + tt)], sig=(k == 3))
                yield
                c_ = tt * 128
                S.op('act', lambda e: e.activation(qTg[0][0:64, :, c_:c_ + 128], psb[0:64, 0:512].rearrange("p (k t) -> p k t", k=4),
                                                   AF.Copy), r=[bk(4 + tt)], w=[('qTg', 0, tt)])
                yield S.op('act', lambda e: e.activation(qTg[1][64:128, :, c_:c_ + 128],
                                                         psb[64:128, 0:512].rearrange("p (k t) -> p k t", k=4), AF.Copy),
                           r=[bk(4 + tt)], w=[('qTg', 1, tt)])
            QK = [('qTg', m, tt) for m in range(2) for tt in range(2)]

            def st_mm(n):
                h, idx, kb = units[n]
                pb = PSB[n % 2]
                for rep_ in range(PE_WARM_REPS):
                    S.op('pe', lambda e: e.matmul(banks[pb][:, 0:256], kT[:, h, kb * 128:(kb + 1) * 128],
                                                  qTg[0][:, h, :], start=True, stop=True),
                         r=[('kT', kb)] + QK, w=[bk(pb)], sig=False)
                    S.op('pe', lambda e: e.matmul(banks[pb][:, 256:512], kT[:, h, kb * 128:(kb + 1) * 128],
                                                  qTg[1][:, h, :], start=True, stop=True),
                         r=[('kT', kb)] + QK, w=[bk(pb)], sig=(rep_ == PE_WARM_REPS - 1))

            st_mm(0)
            for n, (h, idx, kb) in enumerate(units):
                if n + 1 < len(units):
                    st_mm(n + 1)
                pb = PSB[n % 2]
                E = Et[n % 2]
                kE = ('Et', n % 2)
                S.op('act', lambda e: e.activation(E[:], banks[pb][:, :], AF.Exp, scale=0.125), r=[bk(pb)], w=[kE])
                own = kb - 16
                if own == 2 * g:
                    S.op('dve', lambda e: e.tensor_tensor(E[:], E[:], cstb[:, 128:640], ALU.mult), r=[kE, 'cstb'], w=[kE])
                elif own == 2 * g + 1:
                    S.op('dve', lambda e: e.tensor_tensor(E[:], E[:], cstb[:, 640:1152], ALU.mult), r=[kE, 'cstb'], w=[kE])
                for m in range(2):
                    for qb in range(2):
                        if qb == 0 and own == 2 * g + 1:
                            continue
                        last = (own == 2 * g) if qb == 0 else (own == 2 * g + 1)
                        a = 6 + m
                        o0 = qb * 130
                        S.op('pe', lambda e: e.matmul(banks[a][:, o0:o0 + 129], E[:, m * 256 + qb * 128: m * 256 + qb * 128 + 128],
                                                      vx[:, kb, h, 0:129], start=(idx == 0 and qb == 0), stop=last,
                                                      skip_group_check=True),
                             r=[kE, ('vx', kb)], w=[bk(a)], sig=(m == 1 and qb == 1))
                yield
                if idx == len(kbs) - 1:
                    den6 = banks[6][:, 0:260].rearrange("p (a c) -> p a c", c=130)[:, :, 128]
                    den7 = banks[7][:, 0:260].rearrange("p (a c) -> p a c", c=130)[:, :, 128]
                    S.op('dve', lambda e: e.reciprocal(ae[:, 2:4], den7), r=[bk(7)], w=[('ae', 1)])
                    S.op('dve', lambda e: e.tensor_scalar(ae[:, 2:4], ae[:, 2:4], lam_ap, None, ALU.mult), r=[('ae', 1), 'small'], w=[('ae', 1)])
                    S.op('dve', lambda e: e.reciprocal(ae[:, 0:2], den6), r=[bk(6)], w=[('ae', 0)])
                    for qb in range(2):
                        o0 = qb * 130
                        S.op('act', lambda e: e.activation(ofin[qb][:], banks[7][:, o0:o0 + 128], AF.Copy, scale=ae[:, 2 + qb:3 + qb]),
                             r=[bk(7), ('ae', 1)], w=[('ofin', qb)])
                        S.op('dve', lambda e: e.scalar_tensor_tensor(out=ofin[qb][:], in0=banks[6][:, o0:o0 + 128], scalar=ae[:, qb:qb + 1],
                                                                     in1=ofin[qb][:], op0=ALU.mult, op1=ALU.subtract),
                             r=[bk(6), ('ae', 0), ('ofin', qb)], w=[('ofin', qb)])
                    yield
                    for qb in range(2):
                        S.op('act', lambda e: e.activation(yat[qb][:, h * 128:(h + 1) * 128], ofin[qb][:], AF.Square,
                                                           accum_out=ae[:, 4 + qb:5 + qb]),
                             r=[('ofin', qb)], w=[('yat', qb), ('ae', 2 + qb)])
                    S.op('act', lambda e: e.activation(ae[:, 6:8], ae[:, 4:6], AF.Ln, bias=epsc[:, 0:1], scale=1.0 / 128),
                         r=[('ae', 2), ('ae', 3), 'consts'], w=[('ae', 4)])
                    S.op('act', lambda e: e.activation(ae[:, 6:8], ae[:, 6:8], AF.Exp, scale=-0.5), r=[('ae', 4)], w=[('ae', 4)])
                    for qb in range(2):
                        S.op('dve', lambda e: e.tensor_scalar(yat[qb][:, h * 128:(h + 1) * 128], ofin[qb][:], ae[:, 6 + qb:7 + qb],
                                                               float(1.0 - LAM_INIT), ALU.mult, ALU.mult),
                             r=[('ofin', qb), ('ae', 4)], w=[('yat', qb)])
                    yield
            for qb in range(2):
                t0 = (2 * g + qb) * 128
                transpose_to(yat[qb], ('yat', qb), yT[:, 4:8, t0:t0 + 128], ('yT', 2 * g + qb, 1),
                             scale_cols=cst[:, C_GY + 4:C_GY + 8], bank=4 + qb)
                yield

        NTT = n_pre + n_own
        def tt_info(n):
            own = n >= n_pre
            i = n - n_pre if own else n
            return own, i

        def stageA(n):
            own, i = tt_info(n)
            xsrc = xo if own else xp
            S.dma('sp', xt[:], xsrc[i * 128:(i + 1) * 128, :], w=['xt'])
            _norm_a1(xt[:], 'xt', u, 'u', ssb, 'ssb')
            for _ in range(A2_LAG):
                yield
            _norm_a2(C_G1, uT[n % 2], ('uT', n % 2), u, 'u', 0)
            yield

        def front(n):
            own, i = tt_info(n)
            blk = (16 + i) if own else i
            uT_t, kuT = uT[n % 2], ('uT', n % 2)
            Gc, kG = Gt[n % 3], ('Gt', n % 3)
            mv_t, kmv = mvx[n % 2], ('mvx', n % 2)
            qkT, kq = qkT2[n % 2], n % 2
            sigo = sigo2[n % 2]
            if own:
                GB, FMB, TMB = 0, (1, 1), (1, 1)
            else:
                GB, FMB, TMB = 1, (2, 3), (6, 7)
            need_q = own or i == NT - 1

            def T2():
                for half in ((0, 1) if need_q else (1,)):
                    pbk = FMB[half]
                    c0 = half * 4
                    for c in range(c0, c0 + 4):
                        col = (O_MQ + c * 128) if c < 4 else (O_MK + (c - 4) * 128)
                        for k in range(8):
                            S.op('pe', lambda e: e.matmul(banks[pbk][:, (c % 4) * 128:(c % 4 + 1) * 128], win[:, k, col:col + 128],
                                                          uT_t[:, k, :], start=(k == 0), stop=(k == 7)),
                                 r=[kuT, ('win', k)], w=[bk(pbk)], sig=(k == 7))
                    RK = [('raw', c) for c in range(c0, c0 + 4)]
                    if own:
                        S.op('dve', lambda e: e.tensor_copy(raw[:, c0:c0 + 4, 3:131], banks[pbk][:, :].rearrange("p (c t) -> p c t", c=4)),
                             r=[bk(pbk)], w=RK)
                    else:
                        S.op('act', lambda e: e.activation(raw[:, c0:c0 + 4, 3:131], banks[pbk][:, :].rearrange("p (c t) -> p c t", c=4),
                                                           AF.Copy), r=[bk(pbk)], w=RK)
                    conv = own or half == 1
                    if conv:
                        for c in range(c0, c0 + 4):
                            if own:
                                S.op('dve', lambda e: e.tensor_scalar(acc[:, c % 4, :], banks[pbk][:, (c % 4) * 128:(c % 4 + 1) * 128],
                                                                      cw[:, c, 3:4], cb[:, c:c + 1], ALU.mult, ALU.add),
                                     r=[bk(pbk), 'cst'], w=[('acc', c % 4)])
                            else:
                                S.op('act', lambda e: e.activation(acc[:, c % 4, :], banks[pbk][:, (c % 4) * 128:(c % 4 + 1) * 128],
                                                                   AF.Identity, bias=cb[:, c:c + 1], scale=cw[:, c, 3:4]),
                                     r=[bk(pbk), 'cst'], w=[('acc', c % 4)])
                    yield
                    if conv:
                        for c in range(c0, c0 + 4):
                            for j in range(3):
                                yield S.op('dve', lambda e: e.scalar_tensor_tensor(out=acc[:, c % 4, :], in0=raw[:, c, j:j + 128],
                                                                                   scalar=cw[:, c, j:j + 1], in1=acc[:, c % 4, :],
                                                                                   op0=ALU.mult, op1=ALU.add),
                                           r=[('raw', c), 'cst', ('acc', c % 4)], w=[('acc', c % 4)])
                    if own:
                        yield S.op('dve', lambda e: e.tensor_copy(raw[:, c0:c0 + 4, 0:3], raw[:, c0:c0 + 4, 128:131]), r=RK, w=RK)
                    else:
                        yield S.op('act', lambda e: e.activation(raw[:, c0:c0 + 4, 0:3], raw[:, c0:c0 + 4, 128:131], AF.Copy), r=RK, w=RK)
                    if conv:
                        tmp = raw[:, c0:c0 + 4, 3:131]
                        AK = [('acc', c) for c in range(4)]
                        S.op('act', lambda e: e.activation(tmp, acc[:, :, :], AF.Exp, scale=-1.0), r=AK + RK, w=RK)
                        S.op('act', lambda e: e.activation(tmp, tmp, AF.Ln, bias=epsc[:, 1:2]), r=RK + ['consts'], w=RK)
                        yield S.op('act', lambda e: e.activation(tmp, tmp, AF.Exp, scale=-1.0), r=RK, w=RK)
                        yield S.op('dve', lambda e: e.tensor_tensor(qkT[:, c0:c0 + 4, :], acc[:, :, :], tmp, ALU.mult),
                                   r=AK + RK, w=[('qkT', kq, half)])

            def T3():
                for gi, col in enumerate((O_MI, O_MF)):
                    for k in range(8):
                        S.op('pe', lambda e: e.matmul(banks[GB][0:4, gi * 128:(gi + 1) * 128], win[:, k, col:col + 4],
                                                      uT_t[:, k, :], start=(k == 0), stop=(k == 7)),
                             r=[kuT, ('win', k)], w=[bk(GB)], sig=(k == 7))
                po_ = 2 if own else 0
                G = lambda j: ('gf', j)
                C = lambda j: ('carry', j)
                S.op('act', lambda e: e.activation(gf[:, 0, :], banks[GB][0:4, 0:128], AF.Identity,
                                                   bias=gpar[:, po_ + 1:po_ + 2], scale=gpar[:, po_:po_ + 1]),
                     r=[bk(GB), 'gpar'], w=[G(0)])
                yield S.op('act', lambda e: e.activation(gf[:, 1, :], banks[GB][0:4, 128:256], AF.Exp, bias=gpar[:, 4:5], scale=-1.0),
                           r=[bk(GB), 'gpar'], w=[G(1)])
                yield S.op('act', lambda e: e.activation(gf[:, 1, :], gf[:, 1, :], AF.Ln, bias=epsc[0:4, 1:2]),
                           r=[G(1), 'consts'], w=[G(1)])
                if not own:
                    yield S.op('dve', lambda e: e.tensor_scalar(gf[:, 1, :], gf[:, 1, :], gpar[:, 5:6], None, ALU.mult),
                               r=[G(1), 'gpar'], w=[G(1)])
                yield S.op('dve', lambda e: e.tensor_tensor_scan(gf[:, 2, :], ones4[:], gf[:, 1, :], carry[:, 0:1], ALU.mult, ALU.add),
                           r=[G(1), 'consts4', C(0)], w=[G(2)])
                yield S.op('dve', lambda e: e.tensor_tensor(gf[:, 3, :], gf[:, 0, :], gf[:, 2, :], ALU.add), r=[G(0), G(2)], w=[G(3)])
                yield S.op('dve', lambda e: e.tensor_tensor_scan(gf[:, 4, :], ones4[:], gf[:, 3, :], carry[:, 1:2], ALU.mult, ALU.max),
                           r=[G(3), 'consts4', C(1)], w=[G(4)])
                S.op('dve', lambda e: e.tensor_scalar(carry[:, 2:3], carry[:, 1:2], -1.0, float(LNC), ALU.mult, ALU.add),
                     r=[C(1)], w=[C(2)])
                S.op('dve', lambda e: e.tensor_scalar(carry[:, 3:4], carry[:, 1:2], -1.0, None, ALU.mult), r=[C(1)], w=[C(3)])
                yield S.op('dve', lambda e: e.tensor_tensor(carry[:, 4:5], carry[:, 1:2], gf[:, 4, 127:128], ALU.subtract),
                           r=[C(1), G(4)], w=[C(4)])
                yield S.op('act', lambda e: e.activation(gf[:, 5, :], gf[:, 3, :], AF.Exp, bias=carry[:, 2:3]), r=[G(3), C(2)], w=[G(5)])
                yield S.op('act', lambda e: e.activation(gf[:, 6, :], gf[:, 2, :], AF.Exp, bias=carry[:, 3:4]), r=[G(2), C(3)], w=[G(6)])
                yield S.op('act', lambda e: e.activation(gf[:, 7, :], ones4[:], AF.Exp, bias=carry[:, 4:5], scale=0.0),
                           r=[C(4), 'consts4'], w=[G(7)])
                S.op('dve', lambda e: e.tensor_copy(carry[:, 0:1], gf[:, 2, 127:128]), r=[G(2), C(0)], w=[C(0)])
                yield S.op('dve', lambda e: e.tensor_copy(carry[:, 1:2], gf[:, 4, 127:128]), r=[G(4), C(1)], w=[C(1)])
                for a in range(3):
                    S.op('pe', lambda e: e.transpose(banks[GB][:, 256 + a * 4:256 + a * 4 + 4], gf[:, 5 + a, :], identf[0:4, 0:4]),
                         r=[G(5 + a), 'cst'], w=[bk(GB)], sig=(a == 2))
                yield S.op('dve', lambda e: e.tensor_copy(Gc, banks[GB][:, 256:268]), r=[bk(GB)], w=[kG])

            def T4():
                tb = [0]

                def nxt():
                    tb[0] += 1
                    return TMB[tb[0] % 2]
                b_ = nxt()
                proj_tm(uT_t, kuT, O_MV, b_)
                if own:
                    yield S.op('dve', lambda e: e.tensor_copy(mv_t[:, :, 0:128], banks[b_][:, :].rearrange("p (h d) -> p h d", h=4)),
                               r=[bk(b_)], w=[kmv])
                else:
                    yield S.op('act', lambda e: e.activation(mv_t[:, :, 0:128], banks[b_][:, :].rearrange("p (h d) -> p h d", h=4), AF.Copy),
                               r=[bk(b_)], w=[kmv])
                if own:
                    b_ = nxt()
                    proj_tm(uT_t, kuT, O_MO, b_)
                    S.op('act', lambda e: e.activation(qf[:], banks[b_][:, :], AF.Exp, scale=-1.0), r=[bk(b_)], w=['qf'])
                    S.op('act', lambda e: e.activation(qf[:], qf[:], AF.Ln, bias=epsc[:, 1:2]), r=['qf', 'consts'], w=['qf'])
                    yield S.op('act', lambda e: e.activation(sigo[:], qf[:], AF.Exp, scale=-1.0), r=['qf'], w=[('sigo', n % 2)])
                cs_k = cst[:, (C_CSO if own else C_CSP) + i * 16:(C_CSO if own else C_CSP) + i * 16 + 16]
                b_ = nxt()
                proj_tm(uT_t, kuT, O_AK, b_)
                yield from qk_prep(b_, C_KG, cs_k, qb16, 'qb16')
                transpose_to(qb16, 'qb16', kT[:, :, blk * 128:(blk + 1) * 128], ('kT', blk), bank=0)
                yield
                b_ = nxt()
                proj_tm(uT_t, kuT, O_AV, b_)
                if own:
                    yield S.op('dve', lambda e: e.tensor_copy(vx[:, blk, :, 0:128], banks[b_][:, :].rearrange("p (h d) -> p h d", h=4)),
                               r=[bk(b_)], w=[('vx', blk)])
                else:
                    yield S.op('act', lambda e: e.activation(vx[:, blk, :, 0:128], banks[b_][:, :].rearrange("p (h d) -> p h d", h=4), AF.Copy),
                               r=[bk(b_)], w=[('vx', blk)])
                if own:
                    b_ = nxt()
                    proj_tm(uT_t, kuT, O_AQ, b_)
                    yield from qk_prep(b_, C_QG, cs_k, qtok[i % 2], ('qtok', i % 2))

            yield from rr(T2(), T3(), T4())
            if own and i == 0 and 'qkT0' in dbg_d:
                S.dma('sp', dbg_d['qkT0'], qkT[:], r=[('qkT', kq, 0), ('qkT', kq, 1)], w=['dbg_qkT0'])
                S.finish('sp', ['dbg_qkT0'])

        def back(n):
            own, i = tt_info(n)
            Gc, kG = Gt[n % 3], ('Gt', n % 3)
            Gp, kGp = Gt[(n + 2) % 3], ('Gt', (n + 2) % 3)
            mv_t, kmv = mvx[n % 2], ('mvx', n % 2)
            qkT, kq = qkT2[n % 2], n % 2
            sigo = sigo2[n % 2]
            HB = (2, 3) if own else (4, 5)
            MK = [('ktl', 0), ('ktl', 1), ('stm', 0), ('stm', 1)]

            def head(h):
                e_ = h % 2
                kt_, kkt = ktl[e_], ('ktl', e_)
                st_, kst = stm[e_], ('stm', e_)
                hb = HB[e_]
                B = banks[hb]
                psb = B[:].bitcast(BF16)
                S.op('pe', lambda e: e.transpose(psb[:, 0:128], qkT[:, 4 + h, :], identb), r=[('qkT', kq, 1), 'cstb'], w=[bk(hb)], sig=True)
                yield
                yield S.op('dve', lambda e: e.tensor_scalar(kt_, psb[:, 0:128], Gc[:, h:h + 1], None, ALU.mult),
                           r=[bk(hb), kG], w=[kkt])
                if own:
                    S.op('pe', lambda e: e.matmul(B[:, 64:192], qkT[:, 4 + h, :], qkT[:, h, :], start=True, stop=True),
                         r=[('qkT', kq, 0), ('qkT', kq, 1)], w=[bk(hb)], sig=True)
                    yield
                    yield S.op('dve', lambda e: e.scalar_tensor_tensor(out=st_, in0=B[:, 64:192], scalar=Gc[:, h:h + 1],
                                                                       in1=tri, op0=ALU.mult, op1=ALU.mult),
                               r=[bk(hb), kG, 'cst'], w=[kst])
                    S.op('pe', lambda e: e.matmul(B[:, 192:321], qkT[:, h, :], Cbf[:, h, 0:129], start=True, stop=False),
                         r=[('qkT', kq, 0), 'Cbf'], w=[bk(hb)], sig=False)
                    S.op('pe', lambda e: e.matmul(B[:, 192:321], st_, mv_t[:, h, 0:129], start=False, stop=True),
                         r=[kst, kmv], w=[bk(hb)], sig=True)
                    yield
                S.op('pe', lambda e: e.matmul(B[:, 336:465], kt_, mv_t[:, h, 0:129], start=True, stop=True),
                     r=[kkt, kmv], w=[bk(hb)], sig=True)
                yield
                yield S.op('dve', lambda e: e.scalar_tensor_tensor(out=Pst[:, h, 0:129], in0=Pst[:, h, 0:129], scalar=Gp[:, 8 + h:9 + h],
                                                                   in1=B[:, 336:465], op0=ALU.mult, op1=ALU.add),
                           r=[bk(hb), kGp, ('Pst', h)], w=[('Pst', h)])

            def pair_epilogue(p):
                c = 8 * p
                m = ml[:, 16 * p:16 * p + 16]
                for e_ in range(2):
                    S.op('dve', lambda e: e.tensor_copy(m[:, e_:e_ + 1], banks[HB[e_]][:, 320:321]), r=[bk(HB[e_])], w=[('ml', p, 0)])
                S.op('dve', lambda e: e.scalar_tensor_tensor(out=m[:, 2:4], in0=m[:, 0:2], scalar=-1.0, in1=m[:, 0:2],
                                                             op0=ALU.mult, op1=ALU.max), r=[('ml', p, 0)], w=[('ml', p, 1)])
                S.op('dve', lambda e: e.tensor_tensor(m[:, 2:4], m[:, 2:4], Gc[:, 4 + 2 * p:6 + 2 * p], ALU.max),
                     r=[('ml', p, 1), kG], w=[('ml', p, 1)])
                yield S.op('dve', lambda e: e.reciprocal(m[:, 4:6], m[:, 2:4]), r=[('ml', p, 1)], w=[('ml', p, 2)])
                for e_ in range(2):
                    B = banks[HB[e_]]
                    h_ = 2 * p + e_
                    yield S.op('act', lambda e: e.activation(mls[:, h_ * 128:(h_ + 1) * 128], B[:, 192:320], AF.Square,
                                                             accum_out=m[:, 6 + e_:7 + e_]),
                               r=[bk(HB[e_])], w=[MK[h_], ('ml', p, 3 + e_)])
                S.op('dve', lambda e: e.tensor_tensor(m[:, 8:10], m[:, 4:6], m[:, 4:6], ALU.mult), r=[('ml', p, 2)], w=[('ml', p, 5)])
                yield S.op('dve', lambda e: e.tensor_tensor(m[:, 8:10], m[:, 8:10], m[:, 6:8], ALU.mult),
                           r=[('ml', p, 5), ('ml', p, 3), ('ml', p, 4)], w=[('ml', p, 5)])
                S.op('act', lambda e: e.activation(m[:, 10:12], m[:, 8:10], AF.Ln, bias=epsc[:, 0:1], scale=1.0 / 128),
                     r=[('ml', p, 5), 'consts'], w=[('ml', p, 6)])
                yield S.op('act', lambda e: e.activation(m[:, 10:12], m[:, 10:12], AF.Exp, scale=-0.5), r=[('ml', p, 6)], w=[('ml', p, 6)])
                yield S.op('dve', lambda e: e.tensor_tensor(m[:, 12:14], m[:, 10:12], m[:, 4:6], ALU.mult),
                           r=[('ml', p, 6), ('ml', p, 2)], w=[('ml', p, 7)])
                for e_ in range(2):
                    h = 2 * p + e_
                    B = banks[HB[e_]]
                    yield S.op('dve', lambda e: e.scalar_tensor_tensor(out=mls[:, h * 128:(h + 1) * 128], in0=B[:, 192:320],
                                                                       scalar=m[:, 12 + e_:13 + e_], in1=sigo[:, h * 128:(h + 1) * 128],
                                                                       op0=ALU.mult, op1=ALU.mult),
                               r=[bk(HB[e_]), ('ml', p, 7), ('sigo', n % 2)], w=[MK[h]])
                for e_ in range(2):
                    h = 2 * p + e_
                    B = banks[HB[e_]]
                    psb = B[:].bitcast(BF16)
                    S.op('pe', lambda e: e.transpose(psb[:, 0:128], mls[:, h * 128:(h + 1) * 128], identb),
                         r=[MK[h], 'cstb'], w=[bk(HB[e_])], sig=True)
                    yield
                    yield S.op('dve', lambda e: e.tensor_scalar(yT[:, h, i * 128:(i + 1) * 128], psb[:, 0:128],
                                                                cst[:, C_GY + h:C_GY + h + 1], None, ALU.mult),
                               r=[bk(HB[e_]), 'cst'], w=[('yT', i, 0)])

            for p in range(2):
                yield from rr(head(2 * p), head(2 * p + 1))
                if own:
                    yield from pair_epilogue(p)
            if own or i == NT - 1:
                yield S.op('dve', lambda e: e.tensor_tensor(Cbf[:, :, 0:129], Pst[:, :, 0:129],
                                                            Gc[:, 8:12].unsqueeze(2).to_broadcast([128, 4, 129]), ALU.mult),
                           r=[('Pst', h) for h in range(4)] + [kG], w=['Cbf'])

        STEPS = {'main': 0, 'side': 0}

        def wrr(main, side, per):
            credit = 0.0
            for _ in main:
                STEPS['main'] += 1
                credit += per
                while side is not None and credit >= 1.0:
                    credit -= 1.0
                    STEPS['side'] += 1
                    try:
                        next(side)
                    except StopIteration:
                        side = None
            return side

        def dump(name, ap, keys):
            if name in dbg_d:
                S.dma('sp', dbg_d[name], ap, r=keys, w=['dbg_' + name])
                S.finish('sp', ['dbg_' + name])

        side = None
        side_rate = [1.0]
        main_per_tt = [60]
        if NTT > 0:
            run(stageA(0))
            run(rr(stageA(1) if NTT > 1 else None, front(0)))
        for n in range(NTT):
            own, i = tt_info(n)
            def _late(gen, rounds):
                for _ in range(rounds):
                    yield
                yield from gen
            nxt_front = rr(_late(stageA(n + 2), A_LAG) if n + 2 < NTT else None, front(n + 1) if n + 1 < NTT else None)
            if own and i % 2 == 1:
                if side is not None:
                    run(side)
                side = attention_group(i // 2)
                for _ in range(4):
                    next(side)
                g_ = i // 2
                side_total = (18 + 2 * g_) * 4 + 4 * 4 + 2
                side_rate[0] = 1.15 * side_total / (2.0 * max(main_per_tt[0], 1))
            m0 = STEPS['main']
            side = wrr(rr(nxt_front, back(n)), side, side_rate[0])
            if own:
                main_per_tt[0] = STEPS['main'] - m0
            if own and i == 3 and 'yT01' in dbg_d:
                if side is not None:
                    run(side)
                    side = None
                dump('yT01', yT[:, :, 0:256], [('yT', a, b) for a in range(2) for b in range(2)])
        if side is not None:
            run(side)

        if 'yT' in dbg_d:
            S.dma('sp', dbg_d['yT'], yT[:], r=[('yT', i, j) for i in range(16) for j in range(2)], w=['dbg_yT'])
            S.finish('sp', ['dbg_yT'])

        ab.close()
        if 'C' not in phases:
            S.barrier()
            return nc

        hres = sb("hres", [128, 16, 1024])
        cd = ExitStack()

        def sc(name, shape, dt=F32):
            return cd.enter_context(nc.sbuf_tensor("t_" + name, list(shape), dt))

        u2T = sc("u2T", [128, 8, 2048], BF16)
        wout = sc("wout", [128, 8, 1024], BF16)
        u = sc("u_c", [128, 1024], BF16)
        ssb = sc("ssb_c", [128, 8])
        wupb = [sc("wup%d" % i, [128, 8, 512], BF16) for i in range(2)]
        wdnb = [sc("wdn%d" % i, [128, 4, 1024], BF16) for i in range(2)]
        hidT = [sc("hidT%d" % i, [128, 4, 512], BF16) for i in range(2)]
        relu_t = sc("relu_t", [128, 512])
        w_out_v = w_out.rearrange("(k p) n -> p k n", p=128)
        S.barrier()
        for k in range(8):
            S.dma('pool', wout[:, k, :], w_out_v[:, k, :], w=[('wout', k)])
        w_up_v = w_up.rearrange("(k p) n -> p k n", p=128)
        w_dn_v = w_down.rearrange("(k p) n -> p k n", p=128)

        def load_ffn_w(j):
            b = j % 2
            S.dma('pool', wupb[b][:], w_up_v[:, :, j * 512:(j + 1) * 512], w=[('wup', b)])
            S.dma('pool', wdnb[b][:], w_dn_v[:, j * 4:(j + 1) * 4, :], w=[('wdn', b)])

        u_c2 = sc("u_c2", [128, 1024], BF16)
        ssb_c2 = sc("ssb_c2", [128, 8])
        CSCR = [(u, 'u_c', ssb, 'ssb_c', 0), (u_c2, 'u_c2', ssb_c2, 'ssb_c2', 1)]

        def c_stream(i):
            S.dma('sp', hres[:, i, :], xo[i * 128:(i + 1) * 128, :], w=[('h', i)])
            for half in range(2):
                pbk = 2 + (i * 2 + half) % 6
                for k in range(8):
                    S.op('pe', lambda e: e.matmul(banks[pbk][:, :], yT[:, k, i * 128:(i + 1) * 128],
                                                  wout[:, k, half * 512:(half + 1) * 512], start=(k == 0), stop=(k == 7)),
                         r=[('yT', i, 0), ('yT', i, 1), ('wout', k)], w=[bk(pbk)], sig=k == 7)
                yield S.op('dve', lambda e: e.tensor_tensor(hres[:, i, half * 512:(half + 1) * 512], banks[pbk][:, :],
                                                            hres[:, i, half * 512:(half + 1) * 512], ALU.add),
                           r=[bk(pbk), ('h', i)], w=[('h', i)])
            if i == 0:
                load_ffn_w(0)
                load_ffn_w(1)
            sc_ = CSCR[i % 2]
            _norm_a1(hres[:, i, :], ('h', i), sc_[0], sc_[1], sc_[2], sc_[3])
            yield
            yield
            _norm_a2(C_G2, u2T[:, :, i * 128:(i + 1) * 128], ('u2T', i), sc_[0], sc_[1], sc_[4])
            yield

        def lagged0(gens, lag):
            live = []
            pending = list(gens)
            rnd = 0
            while live or pending:
                if pending and rnd % lag == 0:
                    live.append(pending.pop(0))
                for g in list(live):
                    try:
                        next(g)
                    except StopIteration:
                        live.remove(g)
                rnd += 1

        lagged0([c_stream(i) for i in range(NT)], 2)
        if 'h1' in dbg_d:
            S.dma('sp', dbg_d['h1'], hres[:], r=[('h', i) for i in range(16)], w=['dbg_h1'])
            S.finish('sp', ['dbg_h1'])

        nacc = 0
        for j in range(8):
            b = j % 2
            for tg in range(4):
                hb = (j * 4 + tg) % 2
                for m in range(4):
                    pbk = 2 + nacc % 6
                    nacc += 1
                    for k in range(8):
                        S.op('pe', lambda e: e.matmul(banks[pbk][:, :], wupb[b][:, k, m * 128:(m + 1) * 128],
                                                      u2T[:, k, tg * 512:(tg + 1) * 512], start=(k == 0), stop=(k == 7)),
                             r=[('wup', b)] + [('u2T', tg * 4 + q) for q in range(4)], w=[bk(pbk)], sig=k == 7)
                    S.op('act', lambda e: e.activation(relu_t[:], banks[pbk][:, :], AF.Relu), r=[bk(pbk)], w=['relu_t'])
                    S.op('act', lambda e: e.activation(hidT[hb][:, m, :], relu_t[:], AF.Square),
                         r=['relu_t'], w=[('hidT', hb)])
                for q in range(4):
                    i = tg * 4 + q
                    for half in range(2):
                        pbk = 2 + nacc % 6
                        nacc += 1
                        for m in range(4):
                            S.op('pe', lambda e: e.matmul(banks[pbk][:, :], hidT[hb][:, m, q * 128:(q + 1) * 128],
                                                          wdnb[b][:, m, half * 512:(half + 1) * 512], start=(m == 0), stop=(m == 3)),
                                 r=[('hidT', hb), ('wdn', b)], w=[bk(pbk)], sig=m == 3)
                        S.op('dve', lambda e: e.tensor_tensor(hres[:, i, half * 512:(half + 1) * 512], banks[pbk][:, :],
                                                              hres[:, i, half * 512:(half + 1) * 512], ALU.add),
                             r=[bk(pbk), ('h', i)], w=[('h', i)])
            if j + 2 < 8:
                load_ffn_w(j + 2)
        if 'h2' in dbg_d:
            S.dma('sp', dbg_d['h2'], hres[:], r=[('h', i) for i in range(16)], w=['dbg_h2'])
            S.finish('sp', ['dbg_h2'])
        S.barrier()
        cd.close()

        wg = sb("wg", [128, 8, 1024], BF16)
        wp = sb("wp", [128, 2, 1024], BF16)
        NS = 4
        u_e = [sb("u_e%d" % i, [128, 1024], BF16) for i in range(NS)]
        ssb_e = [sb("ssb_e%d" % i, [128, 8]) for i in range(NS)]
        u3T = [sb("u3T%d" % i, [128, 8, 128], BF16) for i in range(NS)]
        pt = [sb("pt%d" % i, [128, 256], BF16) for i in range(NS)]
        pT = [sb("pT%d" % i, [128, 2, 128], BF16) for i in range(NS)]
        gsb = [sb("gsb%d" % i, [128, 1024]) for i in range(NS)]
        w_g_v = w_gate.rearrange("(k p) n -> p k n", p=128)
        w_p_v = w_ple.rearrange("(k p) n -> p k n", p=128)
        for k in range(8):
            S.dma('pool', wg[:, k, :], w_g_v[:, k, :], w=[('wg', k)])
        S.dma('pool', wp[:], w_p_v, w=['wp'])

        def pe_stream(i):
            b = i % NS
            tb = b % 2
            S.dma('pool', pt[b][:], po[i * 128:(i + 1) * 128, :], w=[('pt', b)])
            norm_to_uT(hres[:, i, :], ('h', i), C_G3, u3T[b], ('u3T', b), scr=(u_e[b], ('u_e', b), ssb_e[b], ('ssb_e', b), tb))
            yield
            psb = banks[tb][:].bitcast(BF16)
            for k in range(2):
                S.op('pe', lambda e: e.transpose(psb[:, k * 128:(k + 1) * 128], pt[b][:, k * 128:(k + 1) * 128], identb),
                     r=[('pt', b), 'cstb'], w=[bk(tb)], sig=k == 1)
            yield S.op('dve', lambda e: e.tensor_copy(pT[b][:], psb[:, 0:256].rearrange("p (k t) -> p k t", k=2)),
                       r=[bk(tb)], w=[('pT', b)])
            for half in range(2):
                pg = 2 + b
                pe_ = 6 + (b % 2)
                for k in range(8):
                    S.op('pe', lambda e: e.matmul(banks[pg][:, :], u3T[b][:, k, :], wg[:, k, half * 512:(half + 1) * 512],
                                                  start=(k == 0), stop=(k == 7)), r=[('u3T', b), ('wg', k)], w=[bk(pg)], sig=k == 7)
                sl = slice(half * 512, (half + 1) * 512)
                yield S.op('act', lambda e: e.activation(gsb[b][:, sl], banks[pg][:, :], AF.Sigmoid), r=[bk(pg)], w=[('gsb', b, half)])
                for k in range(2):
                    S.op('pe', lambda e: e.matmul(banks[pe_][:, :], pT[b][:, k, :], wp[:, k, half * 512:(half + 1) * 512],
                                                  start=(k == 0), stop=(k == 1)), r=[('pT', b), 'wp'], w=[bk(pe_)], sig=k == 1)
                yield S.op('dve', lambda e: e.tensor_tensor(gsb[b][:, sl], banks[pe_][:, :], gsb[b][:, sl], ALU.mult),
                           r=[bk(pe_), ('gsb', b, half)], w=[('gsb', b, half)])
                yield S.op('dve', lambda e: e.tensor_tensor(gsb[b][:, sl], gsb[b][:, sl], hres[:, i, sl], ALU.add),
                           r=[('gsb', b, half), ('h', i)], w=[('gsb', b, half)])
            S.dma('sp', out_d[i * 128:(i + 1) * 128, :], gsb[b][:], r=[('gsb', b, 0), ('gsb', b, 1)],
                  w=[('out', i)])
            yield

        def lagged(gens, lag):
            live = []
            pending = list(gens)
            rnd = 0
            while live or pending:
                if pending and rnd % lag == 0:
                    live.append(pending.pop(0))
                for g in list(live):
                    try:
                        next(g)
                    except StopIteration:
                        live.remove(g)
                rnd += 1

        lagged([pe_stream(i) for i in range(NT)], 2)
        S.finish('sp', [('out', i) for i in range(NT)])
    return nc


def _consts(core, inp):
    c = np.zeros((128, C_TOT), np.float32)

    def pk(v):
        return np.asarray(v, np.float32).reshape(-1, 128).T

    c[:, C_G1:C_G1 + 8] = pk(inp['attn_norm_g'][0])
    c[:, C_G2:C_G2 + 8] = pk(inp['mlp_norm_g'][0])
    c[:, C_G3:C_G3 + 8] = pk(inp['ple_norm_g'][0])
    c[:, C_GY:C_GY + 4] = pk(inp['mlstm_norm_g'][0])
    c[:, C_GY + 4:C_GY + 8] = pk(inp['attn_sub_norm_g'][0])
    cw = np.asarray(inp['conv_w'][0], np.float32)
    c[:, C_CW:C_CW + 32] = cw.T.reshape(8, 128, 4).transpose(1, 0, 2).reshape(128, 32)
    c[:, C_CB:C_CB + 8] = pk(inp['conv_b'][0])
    c[:, C_QG:C_QG + 64] = np.asarray(inp['q_norm_g'][0], np.float32)[None, :]
    c[:, C_KG:C_KG + 64] = np.asarray(inp['k_norm_g'][0], np.float32)[None, :]
    for j, nm in enumerate(('lambda_q1', 'lambda_k1', 'lambda_q2', 'lambda_k2')):
        c[:, C_LAM + j * 64:C_LAM + (j + 1) * 64] = np.asarray(inp[nm][0], np.float32)[None, :]
    odd = core % 2
    c[:, C_FL + 0] = 1.0 if odd else 0.0
    c[:, C_FL + 1] = 0.0 if odd else NEG
    c[:, C_FL + 2] = 0.0 if odd else NEG
    inv = (np.float32(500000.0) ** (-np.arange(0, 16, 2, dtype=np.float32) / np.float32(16))).astype(np.float32)
    for off, base in ((C_CSO, odd * 2048), (C_CSP, 0)):
        pos = (base + np.arange(2048, dtype=np.float32)).astype(np.float32)
        ang = (pos[:, None] * inv[None, :]).astype(np.float32)
        cs = np.concatenate([np.cos(ang), np.sin(ang)], axis=1).astype(np.float32)
        c[:, off:off + 256] = cs.reshape(16, 128, 16).transpose(1, 0, 2).reshape(128, 256)
    c[:, C_ID:C_ID + 128] = np.eye(128, dtype=np.float32)
    c[:, C_TRI:C_TRI + 128] = np.triu(np.ones((128, 128), np.float32))
    c[0:4, C_GB] = np.asarray(inp['igate_b'][0], np.float32)
    c[0:4, C_GB + 1] = np.asarray(inp['fgate_b'][0], np.float32)
    return c


def _constb():
    b = np.zeros((128, 128 + 1024), np.float32)
    b[:, 0:128] = np.eye(128, dtype=np.float32)
    tri = np.triu(np.ones((128, 128), np.float32))
    for m in range(2):
        b[:, 128 + m * 256:128 + m * 256 + 128] = tri
        b[:, 128 + m * 256 + 128:128 + m * 256 + 256] = 1.0
        b[:, 640 + m * 256:640 + m * 256 + 128] = 0.0
        b[:, 640 + m * 256 + 128:640 + m * 256 + 256] = tri
    return b


def make_in_maps(inp):
    x = np.asarray(inp['x'], np.float32)
    p = np.asarray(inp['p'], np.float32)
    shared = {
        'w_in': np.ascontiguousarray(inp['w_in'][0], dtype=np.float32),
        'w_out': np.ascontiguousarray(inp['w_out'][0], dtype=np.float32),
        'w_up': np.ascontiguousarray(inp['w_up'][0], dtype=np.float32),
        'w_down': np.ascontiguousarray(inp['w_down'][0], dtype=np.float32),
        'w_gate': np.ascontiguousarray(inp['w_ple_gate'][0], dtype=np.float32),
        'w_ple': np.ascontiguousarray(inp['w_ple_proj'][0], dtype=np.float32),
        'cstb': _constb(),
    }
    zeros = np.zeros((2048, 1024), np.float32)
    maps = []
    for c in range(8):
        b, hf = c // 2, c % 2
        m = dict(shared)
        m['xo'] = np.ascontiguousarray(x[b, hf * 2048:(hf + 1) * 2048])
        m['xp'] = np.ascontiguousarray(x[b, 0:2048]) if hf else zeros
        m['po'] = np.ascontiguousarray(p[0, b, hf * 2048:(hf + 1) * 2048])
        m['cst'] = _consts(c, inp)
        maps.append(m)
    return maps


def kernel(**inputs):
    nc = build()
    maps = make_in_maps(inputs)
    res = run_bass_kernel_spmd(nc, maps, core_ids=list(range(8)))
    out = np.zeros((4, 4096, 1024), np.float32)
    for c in range(8):
        out[c // 2, (c % 2) * 2048:(c % 2 + 1) * 2048] = res.results[c]['out']
    return out
```

```python
import math
from contextlib import ExitStack

import numpy as np
import concourse.bass as bass
import concourse.mybir as mybir
from concourse.bass_utils import run_bass_kernel_spmd

F32 = mybir.dt.float32
BF16 = mybir.dt.bfloat16
AF = mybir.ActivationFunctionType
ALU = mybir.AluOpType
AX = mybir.AxisListType

SAME_ENGINE_SYNC = True
A_LAG = 1
A2_LAG = 6
PE_WARM_REPS = 1
EPS = 1e-6
NT = 16
NEG = -30000.0
LAM_INIT = 0.8 - 0.6 * math.exp(0.0)
LNC = math.log(128 ** -0.5)

O_MQ, O_MK, O_MV, O_MO, O_MI, O_MF, O_AQ, O_AK, O_AV = 0, 512, 1024, 1536, 2048, 2052, 2056, 2568, 3080

C_G1, C_G2, C_G3, C_GY, C_CW, C_CB, C_QG, C_KG, C_LAM, C_FL, C_CSO, C_CSP, C_ID, C_TRI, C_GB = (
    0, 8, 16, 24, 32, 64, 72, 136, 200, 456, 460, 716, 972, 1100, 1228)
C_TOT = 1230


class _Eng:
    def __init__(self, name, h, sem):
        self.name, self.h, self.sem = name, h, sem
        self.n = 0
        self.count = 0
        self.incs = []
        self.last = None
        self.last_seq = 0
        self.waited = {}
        self.dsems = []
        self.dcnt = []
        self.dnext = 0


class Sched:
    def __init__(self, nc, stack, ndma):
        self.nc = nc
        hs = {'pe': nc.tensor, 'act': nc.scalar, 'dve': nc.vector, 'pool': nc.gpsimd, 'sp': nc.sync}
        self.e = {}
        for k, h in hs.items():
            sem = stack.enter_context(nc.semaphore("s_" + k))
            self.e[k] = _Eng(k, h, sem)
            for i in range(ndma.get(k, 0)):
                self.e[k].dsems.append(stack.enter_context(nc.semaphore("d_%s%d" % (k, i))))
                self.e[k].dcnt.append(0)
        self.st = {}

    def _target(self, dep):
        if dep[0] == 'd':
            return dep[1], dep[2]
        p = self.e[dep[1]]
        seq = dep[2]
        found = None
        for (s, c) in reversed(p.incs):
            if s >= seq:
                found = c
            else:
                break
        if found is None:
            p.count += 1
            p.last.then_inc(p.sem, 1)
            p.incs.append((p.last_seq, p.count))
            found = p.count
        return p.sem, found

    def _wait(self, eng, deps):
        E = self.e[eng]
        for dep in deps:
            if dep is None:
                continue
            if dep[0] == 'c' and dep[1] == eng and (eng == 'pe' or not SAME_ENGINE_SYNC):
                continue
            sem, val = self._target(dep)
            key = id(sem)
            if E.waited.get(key, 0) >= val:
                continue
            E.h.wait_ge(sem, val)
            E.waited[key] = val

    def _deps(self, r, w, eng=None):
        deps = []
        for k in r:
            s = self.st.get(k)
            if s is not None:
                deps.append(s[0])
                if isinstance(k, tuple) and k[0] == 'bank':
                    deps.extend(d for d in s[1].values() if not (d[0] == 'c' and d[1] == eng))
        for k in w:
            s = self.st.get(k)
            if s is not None:
                deps.append(s[0])
                deps.extend(s[1].values())
        return deps

    def _record(self, dep, r, w):
        rk = (dep[0], dep[1] if dep[0] == 'c' else id(dep[1]))
        for k in r:
            s = self.st.setdefault(k, [None, {}])
            s[1][rk] = dep
        for k in w:
            self.st[k] = [dep, {}]

    def op(self, eng, fn, r=(), w=(), sig=None):
        E = self.e[eng]
        self._wait(eng, self._deps(r, w, eng))
        inst = fn(E.h)
        E.n += 1
        E.last = inst
        E.last_seq = E.n
        if sig or (sig is None and eng != 'pe'):
            E.count += 1
            inst.then_inc(E.sem, 1)
            E.incs.append((E.n, E.count))
        self._record(('c', eng, E.n), r, w)
        return inst

    def dma(self, q, out, in_, r=(), w=(), **kw):
        E = self.e[q]
        self._wait(q, self._deps(r, w))
        i = E.dnext
        E.dnext = (i + 1) % len(E.dsems)
        sem = E.dsems[i]
        if E.dcnt[i] > 0:
            key = id(sem)
            if E.waited.get(key, 0) < E.dcnt[i]:
                E.h.wait_ge(sem, E.dcnt[i])
                E.waited[key] = E.dcnt[i]
        E.dcnt[i] += 16
        E.h.dma_start(out=out, in_=in_, **kw).then_inc(sem, 16)
        dep = ('d', sem, E.dcnt[i])
        self._record(dep, r, w)
        return dep

    def barrier(self):
        deps = []
        for s in self.st.values():
            if s[0] is not None:
                deps.append(s[0])
            deps.extend(s[1].values())
        for eng in self.e:
            self._wait(eng, deps)

    def finish(self, eng, keys):
        self._wait(eng, [self.st[k][0] for k in keys if k in self.st])


def build(dbg=(), n_pre=NT, n_own=NT, phases='CDE'):
    nc = bass.Bass("TRN2", target_bir_lowering=False)

    def din(name, shape):
        return nc.dram_tensor(name, list(shape), F32, kind="ExternalInput").ap()

    xo = din("xo", [2048, 1024])
    xp = din("xp", [2048, 1024])
    po = din("po", [2048, 256])
    w_in = din("w_in", [1024, 3592])
    w_out = din("w_out", [1024, 1024])
    w_up = din("w_up", [1024, 4096])
    w_down = din("w_down", [4096, 1024])
    w_gate = din("w_gate", [1024, 1024])
    w_ple = din("w_ple", [256, 1024])
    cst_d = din("cst", [128, C_TOT])
    cstb_d = din("cstb", [128, 128 + 1024])
    out_d = nc.dram_tensor("out", [2048, 1024], F32, kind="ExternalOutput").ap()
    dbg_d = {}
    for name, shape, dt in dbg:
        dbg_d[name] = nc.dram_tensor("dbg_" + name, list(shape), dt, kind="ExternalOutput").ap()

    with ExitStack() as st:
        S = Sched(nc, st, {'sp': 12, 'pool': 8})

        def sb(name, shape, dt=F32):
            return st.enter_context(nc.sbuf_tensor("t_" + name, list(shape), dt))

        def freed(name, shape, dt=F32):
            return nc.sbuf_tensor(name, list(shape), dt)

        cst = sb("cst", [128, C_TOT])
        cstb = sb("cstb", [128, 128 + 1024], BF16)
        identb = cstb[:, 0:128]
        identf = cst[:, C_ID:C_ID + 128]
        tri = cst[:, C_TRI:C_TRI + 128]
        banks = [st.enter_context(nc.psum_tensor("bank%d" % i, [128, 512], F32)) for i in range(8)]
        small = sb("small", [128, 64])
        block = st.enter_context(nc.Block())

        S.dma('sp', cst[:], cst_d, w=['cst'])
        S.dma('pool', cstb[:], cstb_d, w=['cstb'])

        def bk(i):
            return ('bank', i)

        def rstd_from_ss(ss_ap, out_ap, n, key_ss, key_out, mul=None):
            S.op('act', lambda e: e.activation(out_ap, ss_ap, AF.Ln, bias=epsc[:ss_ap.shape[0], 0:1], scale=1.0 / n),
                 r=[key_ss, 'consts'], w=[key_out])
            S.op('act', lambda e: e.activation(out_ap, out_ap, AF.Exp, scale=-0.5), r=[key_out], w=[key_out])
            if mul is not None:
                S.op('dve', lambda e: e.tensor_scalar(out_ap, out_ap, float(mul), None, ALU.mult), r=[key_out], w=[key_out])

        epsc = sb("epsc", [128, 4])
        ones4 = sb("ones4", [4, 128])
        S.op('pool', lambda e: e.memset(epsc[:, 0:1], EPS), w=['consts'])
        S.op('pool', lambda e: e.memset(epsc[:, 1:2], 1.0), w=['consts'])
        S.op('pool', lambda e: e.memset(ones4[:], 1.0), w=['consts4'])

        lamv = cst[:, C_LAM:C_LAM + 256]
        lj = sb("lj", [128, 64])
        S.op('dve', lambda e: e.scalar_tensor_tensor(out=lj[:], in0=lamv[:, 0:64], scalar=1.0, in1=lamv[:, 64:128],
                                                     op0=ALU.mult, op1=ALU.mult, accum_out=small[:, 0:1]),
             r=['cst'], w=['lj', 'small'])
        S.op('dve', lambda e: e.scalar_tensor_tensor(out=lj[:], in0=lamv[:, 128:192], scalar=1.0, in1=lamv[:, 192:256],
                                                     op0=ALU.mult, op1=ALU.mult, accum_out=small[:, 1:2]),
             r=['cst', 'lj'], w=['lj', 'small'])
        S.op('act', lambda e: e.activation(small[:, 2:4], small[:, 0:2], AF.Exp), r=['small'], w=['small'])
        S.op('dve', lambda e: e.tensor_tensor(small[:, 4:5], small[:, 2:3], small[:, 3:4], ALU.subtract), r=['small'], w=['small'])
        S.op('dve', lambda e: e.tensor_scalar(small[:, 5:6], small[:, 4:5], float(LAM_INIT), None, ALU.add), r=['small'], w=['small'])
        lam_ap = small[:, 5:6]
        gpar = sb("gpar", [4, 8])
        gb = cst[0:4, C_GB:C_GB + 2]
        fl = cst[0:4, C_FL:C_FL + 4]
        S.op('dve', lambda e: e.tensor_copy(gpar[:, 0:1], fl[:, 0:1]), r=['cst'], w=['gpar'])
        S.op('dve', lambda e: e.scalar_tensor_tensor(out=gpar[:, 1:2], in0=gb[:, 0:1], scalar=fl[:, 0:1], in1=fl[:, 1:2],
                                                     op0=ALU.mult, op1=ALU.add), r=['cst', 'gpar'], w=['gpar'])
        S.op('pool', lambda e: e.memset(gpar[:, 2:3], 1.0), r=['gpar'], w=['gpar'])
        S.op('dve', lambda e: e.tensor_copy(gpar[:, 3:4], gb[:, 0:1]), r=['cst', 'gpar'], w=['gpar'])
        S.op('dve', lambda e: e.tensor_scalar(gpar[:, 4:5], gb[:, 1:2], -1.0, None, ALU.mult), r=['cst', 'gpar'], w=['gpar'])
        S.op('dve', lambda e: e.tensor_copy(gpar[:, 5:6], fl[:, 0:1]), r=['cst', 'gpar'], w=['gpar'])
        pbias = cst[:, C_FL + 2:C_FL + 3]

        yT = sb("yT", [128, 8, 2048], BF16)
        ab = ExitStack()

        def sa(name, shape, dt=F32):
            return ab.enter_context(nc.sbuf_tensor("t_" + name, list(shape), dt))

        win = sa("win", [128, 8, 3592], BF16)
        kT = sa("kT", [128, 4, 4096], BF16)
        vx = sa("vx", [128, 32, 4, 130], BF16)
        xt = sa("xt", [128, 1024])
        u = sa("u", [128, 1024], BF16)
        uT = [sa("uT%d" % i, [128, 8, 128], BF16) for i in range(2)]
        ssb = sa("ssb", [128, 8])
        raw = sa("raw", [128, 8, 131])
        acc = sa("acc", [128, 4, 128])
        qkT2 = [sa("qkT%d" % i, [128, 8, 128], BF16) for i in range(2)]
        gf = sa("gf", [4, 8, 128])
        carry = sa("carry", [4, 8])
        Gt = [small[:, 16 + 12 * i:28 + 12 * i] for i in range(3)]
        mvx = [sa("mvx%d" % i, [128, 4, 130], BF16) for i in range(2)]
        sigo2 = [sa("sigo%d" % i, [128, 512], BF16) for i in range(2)]
        mls = sa("mls", [128, 512], BF16)
        ktl = [mls[:, i * 128:(i + 1) * 128] for i in range(2)]
        stm = [mls[:, 256 + i * 128:256 + (i + 1) * 128] for i in range(2)]
        Pst = sa("Pst", [128, 4, 130])
        Cbf = sa("Cbf", [128, 4, 130], BF16)
        ml = sa("ml", [128, 32])
        qf = sa("qf", [128, 512])
        qb16 = sa("qb16", [128, 512], BF16)
        qtok = [sa("qtok%d" % i, [128, 512], BF16) for i in range(2)]
        qTg = [sa("qTg%d" % i, [128, 4, 256], BF16) for i in range(2)]
        Et = [sa("Et%d" % i, [128, 512], BF16) for i in range(2)]
        ofin = [sa("ofin%d" % i, [128, 128]) for i in range(2)]
        yat = [sa("yat%d" % i, [128, 512], BF16) for i in range(2)]
        al = sa("al", [128, 16])
        ae = small[:, 8:16]

        w_in_v = w_in.rearrange("(k p) n -> p k n", p=128)
        WGRP = [(O_MK, O_MV), (O_MV, O_MO), (O_MI, O_AQ), (O_AK, O_AV), (O_AV, 3592), (O_MQ, O_MK), (O_MO, O_MI), (O_AQ, O_AK)]
        for gi_, (c0_, c1_) in enumerate(WGRP):
            S.dma('pool', win[:, :, c0_:c1_], w_in_v[:, :, c0_:c1_], w=[('win', gi_)])

        def wkey(col):
            for gi_, (c0_, c1_) in enumerate(WGRP):
                if c0_ <= col < c1_:
                    return ('win', gi_)
            raise ValueError(col)

        S.op('pool', lambda e: e.memset(vx[:, :, :, 128:130], 1.0), w=[('vx', j) for j in range(32)])
        S.op('pool', lambda e: e.tensor_scalar(vx[:, 0:16, :, 128:130], vx[:, 0:16, :, 128:130], cst[:, C_FL:C_FL + 1], 1.0,
                                               ALU.mult, ALU.mult),
             r=['cst'] + [('vx', j) for j in range(16)], w=[('vx', j) for j in range(16)])
        for i in range(2):
            S.op('pool', lambda e: e.memset(mvx[i][:, :, 128:130], 1.0), w=[('mvx', i)])
        for i in range(3):
            S.op('pool', lambda e: e.memset(Gt[i], 1.0), w=[('Gt', i)])
        S.op('pool', lambda e: e.memset(qTg[0][:], 0.0), w=[('qTg', 0, 0), ('qTg', 0, 1)])
        S.op('pool', lambda e: e.memset(qTg[1][:], 0.0), w=[('qTg', 1, 0), ('qTg', 1, 1)])
        S.op('pool', lambda e: e.memset(Pst[:], 0.0), w=[('Pst', h) for h in range(4)])
        S.op('pool', lambda e: e.memset(Cbf[:], 0.0), w=['Cbf'])
        S.op('pool', lambda e: e.memset(raw[:], 0.0), w=[('raw', c) for c in range(8)])
        S.op('pool', lambda e: e.memset(carry[:], 0.0), w=[('carry', j) for j in range(8)])

        cw = cst[:, C_CW:C_CW + 32].rearrange("p (c j) -> p c j", j=4)
        cb = cst[:, C_CB:C_CB + 8]
        cnt = {'tt': 0}

        def norm_to_uT(x_tile, kx, gcol, uT_t, kuT, scr=None):
            if scr is None:
                u_, ku, ssb_, kss, tb = u, 'u', ssb, 'ssb', 0
            else:
                u_, ku, ssb_, kss, tb = scr
            return _norm_to_uT(x_tile, kx, gcol, uT_t, kuT, u_, ku, ssb_, kss, tb)

        def _norm_to_uT(x_tile, kx, gcol, uT_t, kuT, u, ku, ssb, kss, tb):
            _norm_a1(x_tile, kx, u, ku, ssb, kss)
            _norm_a2(gcol, uT_t, kuT, u, ku, tb)

        def _norm_a1(x_tile, kx, u, ku, ssb, kss):
            S.op('act', lambda e: e.activation(u[:], x_tile, AF.Square, accum_out=ssb[:, 0:1]), r=[kx], w=[ku, kss])
            rstd_from_ss(ssb[:, 0:1], ssb[:, 1:2], 1024.0, kss, kss)
            S.op('dve', lambda e: e.tensor_scalar(u[:], x_tile, ssb[:, 1:2], None, ALU.mult), r=[kx, kss], w=[ku])

        def _norm_a2(gcol, uT_t, kuT, u, ku, tb):
            psb = banks[tb][:].bitcast(BF16)
            for k in range(8):
                S.op('pe', lambda e: e.transpose(psb[:, k * 128:(k + 1) * 128], u[:, k * 128:(k + 1) * 128], identb),
                     r=[ku, 'cstb'], w=[bk(tb)], sig=k == 7)
            g_bc = cst[:, gcol:gcol + 8].unsqueeze(2).to_broadcast([128, 8, 128])
            S.op('dve', lambda e: e.tensor_tensor(uT_t[:], psb[:, 0:1024].rearrange("p (k t) -> p k t", k=8), g_bc, ALU.mult),
                 r=[bk(tb), 'cst'], w=[kuT])

        def transpose_to(src_tok, ksrc, dst_ap, kdst, scale_cols=None, bank=0):
            psb = banks[bank][:].bitcast(BF16)
            for k in range(4):
                S.op('pe', lambda e: e.transpose(psb[:, k * 128:(k + 1) * 128], src_tok[:, k * 128:(k + 1) * 128], identb),
                     r=[ksrc, 'cstb'], w=[bk(bank)], sig=k == 3)
            src = psb[:, 0:512].rearrange("p (k t) -> p k t", k=4)
            if scale_cols is None:
                S.op('dve', lambda e: e.tensor_copy(dst_ap, src), r=[bk(bank)], w=[kdst])
            else:
                g_bc = scale_cols.unsqueeze(2).to_broadcast([128, 4, 128])
                S.op('dve', lambda e: e.tensor_tensor(dst_ap, src, g_bc, ALU.mult), r=[bk(bank), 'cst'], w=[kdst])

        def proj_tm(uT_t, kuT, col, bank):
            for k in range(8):
                S.op('pe', lambda e: e.matmul(banks[bank][:, :], uT_t[:, k, :], win[:, k, col:col + 512],
                                              start=(k == 0), stop=(k == 7)),
                     r=[kuT, wkey(col)], w=[bk(bank)], sig=k == 7)

        def rr(*gens):
            gens = [g for g in gens if g is not None]
            while gens:
                for g in list(gens):
                    try:
                        next(g)
                    except StopIteration:
                        gens.remove(g)
                yield

        def run(gen):
            for _ in gen:
                pass


        def qk_prep(bank, gcol, cs, qb16, k16):
            q3 = qf[:].rearrange("p (g d) -> p g d", d=64)
            S.op('act', lambda e: e.activation(qf[:], banks[bank][:, :], AF.Square), r=[bk(bank)], w=['qf'])
            S.op('dve', lambda e: e.tensor_reduce(out=al[:, 0:8], in_=q3, axis=AX.X, op=ALU.add), r=['qf'], w=['al'])
            S.op('act', lambda e: e.activation(al[:, 8:16], al[:, 0:8], AF.Ln, bias=epsc[:, 0:1], scale=1.0 / 64),
                 r=['al', 'consts'], w=['al'])
            S.op('act', lambda e: e.activation(al[:, 8:16], al[:, 8:16], AF.Exp, scale=-0.5), r=['al'], w=['al'])
            S.op('dve', lambda e: e.tensor_tensor(q3, banks[bank][:, :].rearrange("p (g d) -> p g d", d=64),
                                                  al[:, 8:16].unsqueeze(2).to_broadcast([128, 8, 64]), ALU.mult),
                 r=[bk(bank), 'al'], w=['qf'])
            gg = cst[:, gcol:gcol + 64].unsqueeze(1).to_broadcast([128, 8, 64])
            o3 = qb16[:].rearrange("p (g d) -> p g d", d=64)
            yield S.op('dve', lambda e: e.tensor_tensor(o3, q3, gg, ALU.mult), r=['qf', 'cst'], w=[k16])
            gg16 = cst[:, gcol:gcol + 16].unsqueeze(1).to_broadcast([128, 8, 16])
            yield S.op('dve', lambda e: e.tensor_tensor(q3[:, :, 0:16], q3[:, :, 0:16], gg16, ALU.mult), r=['qf', 'cst'], w=['qf'])
            x1, x2 = q3[:, :, 0:8], q3[:, :, 8:16]
            cc = cs[:, 0:8].unsqueeze(1).to_broadcast([128, 8, 8])
            sn = cs[:, 8:16].unsqueeze(1).to_broadcast([128, 8, 8])
            r4 = [q3[:, :, 16 + 8 * a_:24 + 8 * a_] for a_ in range(4)]
            S.op('dve', lambda e: e.tensor_tensor(r4[0], x1, cc, ALU.mult), r=['qf', 'cst'], w=['qf'])
            S.op('dve', lambda e: e.tensor_tensor(r4[1], x2, sn, ALU.mult), r=['qf', 'cst'], w=['qf'])
            S.op('dve', lambda e: e.tensor_tensor(r4[2], x2, cc, ALU.mult), r=['qf', 'cst'], w=['qf'])
            yield S.op('dve', lambda e: e.tensor_tensor(r4[3], x1, sn, ALU.mult), r=['qf', 'cst'], w=['qf'])
            S.op('dve', lambda e: e.tensor_tensor(o3[:, :, 0:8], r4[0], r4[1], ALU.subtract), r=['qf', k16], w=[k16])
            yield S.op('dve', lambda e: e.tensor_tensor(o3[:, :, 8:16], r4[2], r4[3], ALU.add), r=['qf', k16], w=[k16])

        def attention_group(g):
            kbs = list(range(16)) + [16 + j for j in range(2 * g + 2)]
            units = [(h, idx, kb) for h in range(4) for idx, kb in enumerate(kbs)]
            PSB = (4, 5)
            for tt in range(2):
                psb = banks[4 + tt][:].bitcast(BF16)
                for k in range(4):
                    S.op('pe', lambda e: e.transpose(psb[:, k * 128:(k + 1) * 128], qtok[tt][:, k * 128:(k + 1) * 128], identb),
                         r=[('qtok', tt), 'cstb'], w=[bk(4 + tt)], sig=(k == 3))
                yield
                c_ = tt * 128
                S.op('act', lambda e: e.activation(qTg[0][0:64, :, c_:c_ + 128], psb[0:64, 0:512].rearrange("p (k t) -> p k t", k=4),
                                                   AF.Copy), r=[bk(4 + tt)], w=[('qTg', 0, tt)])
                yield S.op('act', lambda e: e.activation(qTg[1][64:128, :, c_:c_ + 128],
                                                         psb[64:128, 0:512].rearrange("p (k t) -> p k t", k=4), AF.Copy),
                           r=[bk(4 + tt)], w=[('qTg', 1, tt)])
            QK = [('qTg', m, tt) for m in range(2) for tt in range(2)]

            def st_mm(n):
                h, idx, kb = units[n]
                pb = PSB[n % 2]
                for rep_ in range(PE_WARM_REPS):
                    S.op('pe', lambda e: e.matmul(banks[pb][:, 0:256], kT[:, h, kb * 128:(kb + 1) * 128],
                                                  qTg[0][:, h, :], start=True, stop=True),
                         r=[('kT', kb)] + QK, w=[bk(pb)], sig=False)
                    S.op('pe', lambda e: e.matmul(banks[pb][:, 256:512], kT[:, h, kb * 128:(kb + 1) * 128],
                                                  qTg[1][:, h, :], start=True, stop=True),
                         r=[('kT', kb)] + QK, w=[bk(pb)], sig=(rep_ == PE_WARM_REPS - 1))

            st_mm(0)
            for n, (h, idx, kb) in enumerate(units):
                if n + 1 < len(units):
                    st_mm(n + 1)
                pb = PSB[n % 2]
                E = Et[n % 2]
                kE = ('Et', n % 2)
                S.op('act', lambda e: e.activation(E[:], banks[pb][:, :], AF.Exp, scale=0.125), r=[bk(pb)], w=[kE])
                own = kb - 16
                if own == 2 * g:
                    S.op('dve', lambda e: e.tensor_tensor(E[:], E[:], cstb[:, 128:640], ALU.mult), r=[kE, 'cstb'], w=[kE])
                elif own == 2 * g + 1:
                    S.op('dve', lambda e: e.tensor_tensor(E[:], E[:], cstb[:, 640:1152], ALU.mult), r=[kE, 'cstb'], w=[kE])
                for m in range(2):
                    for qb in range(2):
                        if qb == 0 and own == 2 * g + 1:
                            continue
                        last = (own == 2 * g) if qb == 0 else (own == 2 * g + 1)
                        a = 6 + m
                        o0 = qb * 130
                        S.op('pe', lambda e: e.matmul(banks[a][:, o0:o0 + 129], E[:, m * 256 + qb * 128: m * 256 + qb * 128 + 128],
                                                      vx[:, kb, h, 0:129], start=(idx == 0 and qb == 0), stop=last,
                                                      skip_group_check=True),
                             r=[kE, ('vx', kb)], w=[bk(a)], sig=(m == 1 and qb == 1))
                yield
                if idx == len(kbs) - 1:
                    den6 = banks[6][:, 0:260].rearrange("p (a c) -> p a c", c=130)[:, :, 128]
                    den7 = banks[7][:, 0:260].rearrange("p (a c) -> p a c", c=130)[:, :, 128]
                    S.op('dve', lambda e: e.reciprocal(ae[:, 2:4], den7), r=[bk(7)], w=[('ae', 1)])
                    S.op('dve', lambda e: e.tensor_scalar(ae[:, 2:4], ae[:, 2:4], lam_ap, None, ALU.mult), r=[('ae', 1), 'small'], w=[('ae', 1)])
                    S.op('dve', lambda e: e.reciprocal(ae[:, 0:2], den6), r=[bk(6)], w=[('ae', 0)])
                    for qb in range(2):
                        o0 = qb * 130
                        S.op('act', lambda e: e.activation(ofin[qb][:], banks[7][:, o0:o0 + 128], AF.Copy, scale=ae[:, 2 + qb:3 + qb]),
                             r=[bk(7), ('ae', 1)], w=[('ofin', qb)])
                        S.op('dve', lambda e: e.scalar_tensor_tensor(out=ofin[qb][:], in0=banks[6][:, o0:o0 + 128], scalar=ae[:, qb:qb + 1],
                                                                     in1=ofin[qb][:], op0=ALU.mult, op1=ALU.subtract),
                             r=[bk(6), ('ae', 0), ('ofin', qb)], w=[('ofin', qb)])
                    yield
                    for qb in range(2):
                        S.op('act', lambda e: e.activation(yat[qb][:, h * 128:(h + 1) * 128], ofin[qb][:], AF.Square,
                                                           accum_out=ae[:, 4 + qb:5 + qb]),
                             r=[('ofin', qb)], w=[('yat', qb), ('ae', 2 + qb)])
                    S.op('act', lambda e: e.activation(ae[:, 6:8], ae[:, 4:6], AF.Ln, bias=epsc[:, 0:1], scale=1.0 / 128),
                         r=[('ae', 2), ('ae', 3), 'consts'], w=[('ae', 4)])
                    S.op('act', lambda e: e.activation(ae[:, 6:8], ae[:, 6:8], AF.Exp, scale=-0.5), r=[('ae', 4)], w=[('ae', 4)])
                    for qb in range(2):
                        S.op('dve', lambda e: e.tensor_scalar(yat[qb][:, h * 128:(h + 1) * 128], ofin[qb][:], ae[:, 6 + qb:7 + qb],
                                                               float(1.0 - LAM_INIT), ALU.mult, ALU.mult),
                             r=[('ofin', qb), ('ae', 4)], w=[('yat', qb)])
                    yield
            for qb in range(2):
                t0 = (2 * g + qb) * 128
                transpose_to(yat[qb], ('yat', qb), yT[:, 4:8, t0:t0 + 128], ('yT', 2 * g + qb, 1),
                             scale_cols=cst[:, C_GY + 4:C_GY + 8], bank=4 + qb)
                yield

        NTT = n_pre + n_own
        def tt_info(n):
            own = n >= n_pre
            i = n - n_pre if own else n
            return own, i

        def stageA(n):
            own, i = tt_info(n)
            xsrc = xo if own else xp
            S.dma('sp', xt[:], xsrc[i * 128:(i + 1) * 128, :], w=['xt'])
            _norm_a1(xt[:], 'xt', u, 'u', ssb, 'ssb')
            for _ in range(A2_LAG):
                yield
            _norm_a2(C_G1, uT[n % 2], ('uT', n % 2), u, 'u', 0)
            yield

        def front(n):
            own, i = tt_info(n)
            blk = (16 + i) if own else i
            uT_t, kuT = uT[n % 2], ('uT', n % 2)
            Gc, kG = Gt[n % 3], ('Gt', n % 3)
            mv_t, kmv = mvx[n % 2], ('mvx', n % 2)
            qkT, kq = qkT2[n % 2], n % 2
            sigo = sigo2[n % 2]
            if own:
                GB, FMB, TMB = 0, (1, 1), (1, 1)
            else:
                GB, FMB, TMB = 1, (2, 3), (6, 7)
            need_q = own or i == NT - 1

            def T2():
                for half in ((0, 1) if need_q else (1,)):
                    pbk = FMB[half]
                    c0 = half * 4
                    for c in range(c0, c0 + 4):
                        col = (O_MQ + c * 128) if c < 4 else (O_MK + (c - 4) * 128)
                        for k in range(8):
                            S.op('pe', lambda e: e.matmul(banks[pbk][:, (c % 4) * 128:(c % 4 + 1) * 128], win[:, k, col:col + 128],
                                                          uT_t[:, k, :], start=(k == 0), stop=(k == 7)),
                                 r=[kuT, wkey(col)], w=[bk(pbk)], sig=(k == 7))
                    RK = [('raw', c) for c in range(c0, c0 + 4)]
                    if own:
                        S.op('dve', lambda e: e.tensor_copy(raw[:, c0:c0 + 4, 3:131], banks[pbk][:, :].rearrange("p (c t) -> p c t", c=4)),
                             r=[bk(pbk)], w=RK)
                    else:
                        S.op('act', lambda e: e.activation(raw[:, c0:c0 + 4, 3:131], banks[pbk][:, :].rearrange("p (c t) -> p c t", c=4),
                                                           AF.Copy), r=[bk(pbk)], w=RK)
                    conv = own or half == 1
                    if conv:
                        for c in range(c0, c0 + 4):
                            if own:
                                S.op('dve', lambda e: e.tensor_scalar(acc[:, c % 4, :], banks[pbk][:, (c % 4) * 128:(c % 4 + 1) * 128],
                                                                      cw[:, c, 3:4], cb[:, c:c + 1], ALU.mult, ALU.add),
                                     r=[bk(pbk), 'cst'], w=[('acc', c % 4)])
                            else:
                                S.op('act', lambda e: e.activation(acc[:, c % 4, :], banks[pbk][:, (c % 4) * 128:(c % 4 + 1) * 128],
                                                                   AF.Identity, bias=cb[:, c:c + 1], scale=cw[:, c, 3:4]),
                                     r=[bk(pbk), 'cst'], w=[('acc', c % 4)])
                    yield
                    if conv:
                        for c in range(c0, c0 + 4):
                            for j in range(3):
                                yield S.op('dve', lambda e: e.scalar_tensor_tensor(out=acc[:, c % 4, :], in0=raw[:, c, j:j + 128],
                                                                                   scalar=cw[:, c, j:j + 1], in1=acc[:, c % 4, :],
                                                                                   op0=ALU.mult, op1=ALU.add),
                                           r=[('raw', c), 'cst', ('acc', c % 4)], w=[('acc', c % 4)])
                    if own:
                        yield S.op('dve', lambda e: e.tensor_copy(raw[:, c0:c0 + 4, 0:3], raw[:, c0:c0 + 4, 128:131]), r=RK, w=RK)
                    else:
                        yield S.op('act', lambda e: e.activation(raw[:, c0:c0 + 4, 0:3], raw[:, c0:c0 + 4, 128:131], AF.Copy), r=RK, w=RK)
                    if conv:
                        tmp = raw[:, c0:c0 + 4, 3:131]
                        AK = [('acc', c) for c in range(4)]
                        S.op('act', lambda e: e.activation(tmp, acc[:, :, :], AF.Exp, scale=-1.0), r=AK + RK, w=RK)
                        S.op('act', lambda e: e.activation(tmp, tmp, AF.Ln, bias=epsc[:, 1:2]), r=RK + ['consts'], w=RK)
                        yield S.op('act', lambda e: e.activation(tmp, tmp, AF.Exp, scale=-1.0), r=RK, w=RK)
                        yield S.op('dve', lambda e: e.tensor_tensor(qkT[:, c0:c0 + 4, :], acc[:, :, :], tmp, ALU.mult),
                                   r=AK + RK, w=[('qkT', kq, half)])

            def T3():
                for gi, col in enumerate((O_MI, O_MF)):
                    for k in range(8):
                        S.op('pe', lambda e: e.matmul(banks[GB][0:4, gi * 128:(gi + 1) * 128], win[:, k, col:col + 4],
                                                      uT_t[:, k, :], start=(k == 0), stop=(k == 7)),
                             r=[kuT, wkey(col)], w=[bk(GB)], sig=(k == 7))
                po_ = 2 if own else 0
                G = lambda j: ('gf', j)
                C = lambda j: ('carry', j)
                S.op('act', lambda e: e.activation(gf[:, 0, :], banks[GB][0:4, 0:128], AF.Identity,
                                                   bias=gpar[:, po_ + 1:po_ + 2], scale=gpar[:, po_:po_ + 1]),
                     r=[bk(GB), 'gpar'], w=[G(0)])
                yield S.op('act', lambda e: e.activation(gf[:, 1, :], banks[GB][0:4, 128:256], AF.Exp, bias=gpar[:, 4:5], scale=-1.0),
                           r=[bk(GB), 'gpar'], w=[G(1)])
                yield S.op('act', lambda e: e.activation(gf[:, 1, :], gf[:, 1, :], AF.Ln, bias=epsc[0:4, 1:2]),
                           r=[G(1), 'consts'], w=[G(1)])
                if not own:
                    yield S.op('dve', lambda e: e.tensor_scalar(gf[:, 1, :], gf[:, 1, :], gpar[:, 5:6], None, ALU.mult),
                               r=[G(1), 'gpar'], w=[G(1)])
                yield S.op('dve', lambda e: e.tensor_tensor_scan(gf[:, 2, :], ones4[:], gf[:, 1, :], carry[:, 0:1], ALU.mult, ALU.add),
                           r=[G(1), 'consts4', C(0)], w=[G(2)])
                yield S.op('dve', lambda e: e.tensor_tensor(gf[:, 3, :], gf[:, 0, :], gf[:, 2, :], ALU.add), r=[G(0), G(2)], w=[G(3)])
                yield S.op('dve', lambda e: e.tensor_tensor_scan(gf[:, 4, :], ones4[:], gf[:, 3, :], carry[:, 1:2], ALU.mult, ALU.max),
                           r=[G(3), 'consts4', C(1)], w=[G(4)])
                S.op('dve', lambda e: e.tensor_scalar(carry[:, 2:3], carry[:, 1:2], -1.0, float(LNC), ALU.mult, ALU.add),
                     r=[C(1)], w=[C(2)])
                S.op('dve', lambda e: e.tensor_scalar(carry[:, 3:4], carry[:, 1:2], -1.0, None, ALU.mult), r=[C(1)], w=[C(3)])
                yield S.op('dve', lambda e: e.tensor_tensor(carry[:, 4:5], carry[:, 1:2], gf[:, 4, 127:128], ALU.subtract),
                           r=[C(1), G(4)], w=[C(4)])
                yield S.op('act', lambda e: e.activation(gf[:, 5, :], gf[:, 3, :], AF.Exp, bias=carry[:, 2:3]), r=[G(3), C(2)], w=[G(5)])
                yield S.op('act', lambda e: e.activation(gf[:, 6, :], gf[:, 2, :], AF.Exp, bias=carry[:, 3:4]), r=[G(2), C(3)], w=[G(6)])
                yield S.op('act', lambda e: e.activation(gf[:, 7, :], ones4[:], AF.Exp, bias=carry[:, 4:5], scale=0.0),
                           r=[C(4), 'consts4'], w=[G(7)])
                S.op('dve', lambda e: e.tensor_copy(carry[:, 0:1], gf[:, 2, 127:128]), r=[G(2), C(0)], w=[C(0)])
                yield S.op('dve', lambda e: e.tensor_copy(carry[:, 1:2], gf[:, 4, 127:128]), r=[G(4), C(1)], w=[C(1)])
                for a in range(3):
                    S.op('pe', lambda e: e.transpose(banks[GB][:, 256 + a * 4:256 + a * 4 + 4], gf[:, 5 + a, :], identf[0:4, 0:4]),
                         r=[G(5 + a), 'cst'], w=[bk(GB)], sig=(a == 2))
                yield S.op('dve', lambda e: e.tensor_copy(Gc, banks[GB][:, 256:268]), r=[bk(GB)], w=[kG])

            def T4():
                tb = [0]

                def nxt():
                    tb[0] += 1
                    return TMB[tb[0] % 2]
                b_ = nxt()
                proj_tm(uT_t, kuT, O_MV, b_)
                if own:
                    yield S.op('dve', lambda e: e.tensor_copy(mv_t[:, :, 0:128], banks[b_][:, :].rearrange("p (h d) -> p h d", h=4)),
                               r=[bk(b_)], w=[kmv])
                else:
                    yield S.op('act', lambda e: e.activation(mv_t[:, :, 0:128], banks[b_][:, :].rearrange("p (h d) -> p h d", h=4), AF.Copy),
                               r=[bk(b_)], w=[kmv])
                if own:
                    b_ = nxt()
                    proj_tm(uT_t, kuT, O_MO, b_)
                    S.op('act', lambda e: e.activation(qf[:], banks[b_][:, :], AF.Exp, scale=-1.0), r=[bk(b_)], w=['qf'])
                    S.op('act', lambda e: e.activation(qf[:], qf[:], AF.Ln, bias=epsc[:, 1:2]), r=['qf', 'consts'], w=['qf'])
                    yield S.op('act', lambda e: e.activation(sigo[:], qf[:], AF.Exp, scale=-1.0), r=['qf'], w=[('sigo', n % 2)])
                cs_k = cst[:, (C_CSO if own else C_CSP) + i * 16:(C_CSO if own else C_CSP) + i * 16 + 16]
                b_ = nxt()
                proj_tm(uT_t, kuT, O_AK, b_)
                yield from qk_prep(b_, C_KG, cs_k, qb16, 'qb16')
                transpose_to(qb16, 'qb16', kT[:, :, blk * 128:(blk + 1) * 128], ('kT', blk), bank=0)
                yield
                b_ = nxt()
                proj_tm(uT_t, kuT, O_AV, b_)
                if own:
                    yield S.op('dve', lambda e: e.tensor_copy(vx[:, blk, :, 0:128], banks[b_][:, :].rearrange("p (h d) -> p h d", h=4)),
                               r=[bk(b_)], w=[('vx', blk)])
                else:
                    yield S.op('act', lambda e: e.activation(vx[:, blk, :, 0:128], banks[b_][:, :].rearrange("p (h d) -> p h d", h=4), AF.Copy),
                               r=[bk(b_)], w=[('vx', blk)])
                if own:
                    b_ = nxt()
                    proj_tm(uT_t, kuT, O_AQ, b_)
                    yield from qk_prep(b_, C_QG, cs_k, qtok[i % 2], ('qtok', i % 2))

            yield from rr(T2(), T3(), T4())
            if own and i == 0 and 'qkT0' in dbg_d:
                S.dma('sp', dbg_d['qkT0'], qkT[:], r=[('qkT', kq, 0), ('qkT', kq, 1)], w=['dbg_qkT0'])
                S.finish('sp', ['dbg_qkT0'])

        def back(n):
            own, i = tt_info(n)
            Gc, kG = Gt[n % 3], ('Gt', n % 3)
            Gp, kGp = Gt[(n + 2) % 3], ('Gt', (n + 2) % 3)
            mv_t, kmv = mvx[n % 2], ('mvx', n % 2)
            qkT, kq = qkT2[n % 2], n % 2
            sigo = sigo2[n % 2]
            HB = (2, 3) if own else (4, 5)
            MK = [('ktl', 0), ('ktl', 1), ('stm', 0), ('stm', 1)]

            def head(h):
                e_ = h % 2
                kt_, kkt = ktl[e_], ('ktl', e_)
                st_, kst = stm[e_], ('stm', e_)
                hb = HB[e_]
                B = banks[hb]
                psb = B[:].bitcast(BF16)
                S.op('pe', lambda e: e.transpose(psb[:, 0:128], qkT[:, 4 + h, :], identb), r=[('qkT', kq, 1), 'cstb'], w=[bk(hb)], sig=True)
                yield
                yield S.op('dve', lambda e: e.tensor_scalar(kt_, psb[:, 0:128], Gc[:, h:h + 1], None, ALU.mult),
                           r=[bk(hb), kG], w=[kkt])
                if own:
                    S.op('pe', lambda e: e.matmul(B[:, 64:192], qkT[:, 4 + h, :], qkT[:, h, :], start=True, stop=True),
                         r=[('qkT', kq, 0), ('qkT', kq, 1)], w=[bk(hb)], sig=True)
                    yield
                    yield S.op('dve', lambda e: e.scalar_tensor_tensor(out=st_, in0=B[:, 64:192], scalar=Gc[:, h:h + 1],
                                                                       in1=tri, op0=ALU.mult, op1=ALU.mult),
                               r=[bk(hb), kG, 'cst'], w=[kst])
                    S.op('pe', lambda e: e.matmul(B[:, 192:321], qkT[:, h, :], Cbf[:, h, 0:129], start=True, stop=False),
                         r=[('qkT', kq, 0), 'Cbf'], w=[bk(hb)], sig=False)
                    S.op('pe', lambda e: e.matmul(B[:, 192:321], st_, mv_t[:, h, 0:129], start=False, stop=True),
                         r=[kst, kmv], w=[bk(hb)], sig=True)
                    yield
                S.op('pe', lambda e: e.matmul(B[:, 336:465], kt_, mv_t[:, h, 0:129], start=True, stop=True),
                     r=[kkt, kmv], w=[bk(hb)], sig=True)
                yield
                yield S.op('dve', lambda e: e.scalar_tensor_tensor(out=Pst[:, h, 0:129], in0=Pst[:, h, 0:129], scalar=Gp[:, 8 + h:9 + h],
                                                                   in1=B[:, 336:465], op0=ALU.mult, op1=ALU.add),
                           r=[bk(hb), kGp, ('Pst', h)], w=[('Pst', h)])

            def pair_epilogue(p):
                c = 8 * p
                m = ml[:, 16 * p:16 * p + 16]
                for e_ in range(2):
                    S.op('dve', lambda e: e.tensor_copy(m[:, e_:e_ + 1], banks[HB[e_]][:, 320:321]), r=[bk(HB[e_])], w=[('ml', p, 0)])
                S.op('dve', lambda e: e.scalar_tensor_tensor(out=m[:, 2:4], in0=m[:, 0:2], scalar=-1.0, in1=m[:, 0:2],
                                                             op0=ALU.mult, op1=ALU.max), r=[('ml', p, 0)], w=[('ml', p, 1)])
                S.op('dve', lambda e: e.tensor_tensor(m[:, 2:4], m[:, 2:4], Gc[:, 4 + 2 * p:6 + 2 * p], ALU.max),
                     r=[('ml', p, 1), kG], w=[('ml', p, 1)])
                yield S.op('dve', lambda e: e.reciprocal(m[:, 4:6], m[:, 2:4]), r=[('ml', p, 1)], w=[('ml', p, 2)])
                for e_ in range(2):
                    B = banks[HB[e_]]
                    h_ = 2 * p + e_
                    yield S.op('act', lambda e: e.activation(mls[:, h_ * 128:(h_ + 1) * 128], B[:, 192:320], AF.Square,
                                                             accum_out=m[:, 6 + e_:7 + e_]),
                               r=[bk(HB[e_])], w=[MK[h_], ('ml', p, 3 + e_)])
                S.op('dve', lambda e: e.tensor_tensor(m[:, 8:10], m[:, 4:6], m[:, 4:6], ALU.mult), r=[('ml', p, 2)], w=[('ml', p, 5)])
                yield S.op('dve', lambda e: e.tensor_tensor(m[:, 8:10], m[:, 8:10], m[:, 6:8], ALU.mult),
                           r=[('ml', p, 5), ('ml', p, 3), ('ml', p, 4)], w=[('ml', p, 5)])
                S.op('act', lambda e: e.activation(m[:, 10:12], m[:, 8:10], AF.Ln, bias=epsc[:, 0:1], scale=1.0 / 128),
                     r=[('ml', p, 5), 'consts'], w=[('ml', p, 6)])
                yield S.op('act', lambda e: e.activation(m[:, 10:12], m[:, 10:12], AF.Exp, scale=-0.5), r=[('ml', p, 6)], w=[('ml', p, 6)])
                yield S.op('dve', lambda e: e.tensor_tensor(m[:, 12:14], m[:, 10:12], m[:, 4:6], ALU.mult),
                           r=[('ml', p, 6), ('ml', p, 2)], w=[('ml', p, 7)])
                for e_ in range(2):
                    h = 2 * p + e_
                    B = banks[HB[e_]]
                    yield S.op('dve', lambda e: e.scalar_tensor_tensor(out=mls[:, h * 128:(h + 1) * 128], in0=B[:, 192:320],
                                                                       scalar=m[:, 12 + e_:13 + e_], in1=sigo[:, h * 128:(h + 1) * 128],
                                                                       op0=ALU.mult, op1=ALU.mult),
                               r=[bk(HB[e_]), ('ml', p, 7), ('sigo', n % 2)], w=[MK[h]])
                for e_ in range(2):
                    h = 2 * p + e_
                    B = banks[HB[e_]]
                    psb = B[:].bitcast(BF16)
                    S.op('pe', lambda e: e.transpose(psb[:, 0:128], mls[:, h * 128:(h + 1) * 128], identb),
                         r=[MK[h], 'cstb'], w=[bk(HB[e_])], sig=True)
                    yield
                    yield S.op('dve', lambda e: e.tensor_scalar(yT[:, h, i * 128:(i + 1) * 128], psb[:, 0:128],
                                                                cst[:, C_GY + h:C_GY + h + 1], None, ALU.mult),
                               r=[bk(HB[e_]), 'cst'], w=[('yT', i, 0)])

            for p in range(2):
                yield from rr(head(2 * p), head(2 * p + 1))
                if own:
                    yield from pair_epilogue(p)
            if own or i == NT - 1:
                yield S.op('dve', lambda e: e.tensor_tensor(Cbf[:, :, 0:129], Pst[:, :, 0:129],
                                                            Gc[:, 8:12].unsqueeze(2).to_broadcast([128, 4, 129]), ALU.mult),
                           r=[('Pst', h) for h in range(4)] + [kG], w=['Cbf'])

        STEPS = {'main': 0, 'side': 0}

        def wrr(main, side, per):
            credit = 0.0
            for _ in main:
                STEPS['main'] += 1
                credit += per
                while side is not None and credit >= 1.0:
                    credit -= 1.0
                    STEPS['side'] += 1
                    try:
                        next(side)
                    except StopIteration:
                        side = None
            return side

        def dump(name, ap, keys):
            if name in dbg_d:
                S.dma('sp', dbg_d[name], ap, r=keys, w=['dbg_' + name])
                S.finish('sp', ['dbg_' + name])

        side = None
        side_rate = [1.0]
        main_per_tt = [60]
        if NTT > 0:
            run(stageA(0))
            run(rr(stageA(1) if NTT > 1 else None, front(0)))
        for n in range(NTT):
            own, i = tt_info(n)
            def _late(gen, rounds):
                for _ in range(rounds):
                    yield
                yield from gen
            nxt_front = rr(_late(stageA(n + 2), A_LAG) if n + 2 < NTT else None, front(n + 1) if n + 1 < NTT else None)
            if own and i % 2 == 1:
                if side is not None:
                    run(side)
                side = attention_group(i // 2)
                for _ in range(4):
                    next(side)
                g_ = i // 2
                side_total = (18 + 2 * g_) * 4 + 4 * 4 + 2
                side_rate[0] = 1.15 * side_total / (2.0 * max(main_per_tt[0], 1))
            m0 = STEPS['main']
            side = wrr(rr(nxt_front, back(n)), side, side_rate[0])
            if own:
                main_per_tt[0] = STEPS['main'] - m0
            if own and i == 3 and 'yT01' in dbg_d:
                if side is not None:
                    run(side)
                    side = None
                dump('yT01', yT[:, :, 0:256], [('yT', a, b) for a in range(2) for b in range(2)])
        if side is not None:
            run(side)

        if 'yT' in dbg_d:
            S.dma('sp', dbg_d['yT'], yT[:], r=[('yT', i, j) for i in range(16) for j in range(2)], w=['dbg_yT'])
            S.finish('sp', ['dbg_yT'])

        ab.close()
        if 'C' not in phases:
            S.barrier()
            return nc

        hres = sb("hres", [128, 16, 1024])
        cd = ExitStack()

        def sc(name, shape, dt=F32):
            return cd.enter_context(nc.sbuf_tensor("t_" + name, list(shape), dt))

        u2T = sc("u2T", [128, 8, 2048], BF16)
        wout = sc("wout", [128, 8, 1024], BF16)
        u = sc("u_c", [128, 1024], BF16)
        ssb = sc("ssb_c", [128, 8])
        wupb = [sc("wup%d" % i, [128, 8, 512], BF16) for i in range(2)]
        wdnb = [sc("wdn%d" % i, [128, 4, 1024], BF16) for i in range(2)]
        hidT = [sc("hidT%d" % i, [128, 4, 512], BF16) for i in range(2)]
        relu_t = sc("relu_t", [128, 512])
        w_out_v = w_out.rearrange("(k p) n -> p k n", p=128)
        S.barrier()
        for k in range(8):
            S.dma('pool', wout[:, k, :], w_out_v[:, k, :], w=[('wout', k)])
        w_up_v = w_up.rearrange("(k p) n -> p k n", p=128)
        w_dn_v = w_down.rearrange("(k p) n -> p k n", p=128)

        def load_ffn_w(j):
            b = j % 2
            S.dma('pool', wupb[b][:], w_up_v[:, :, j * 512:(j + 1) * 512], w=[('wup', b)])
            S.dma('pool', wdnb[b][:], w_dn_v[:, j * 4:(j + 1) * 4, :], w=[('wdn', b)])

        u_c2 = sc("u_c2", [128, 1024], BF16)
        ssb_c2 = sc("ssb_c2", [128, 8])
        CSCR = [(u, 'u_c', ssb, 'ssb_c', 0), (u_c2, 'u_c2', ssb_c2, 'ssb_c2', 1)]

        def c_stream(i):
            S.dma('sp', hres[:, i, :], xo[i * 128:(i + 1) * 128, :], w=[('h', i)])
            for half in range(2):
                pbk = 2 + (i * 2 + half) % 6
                for k in range(8):
                    S.op('pe', lambda e: e.matmul(banks[pbk][:, :], yT[:, k, i * 128:(i + 1) * 128],
                                                  wout[:, k, half * 512:(half + 1) * 512], start=(k == 0), stop=(k == 7)),
                         r=[('yT', i, 0), ('yT', i, 1), ('wout', k)], w=[bk(pbk)], sig=k == 7)
                yield S.op('dve', lambda e: e.tensor_tensor(hres[:, i, half * 512:(half + 1) * 512], banks[pbk][:, :],
                                                            hres[:, i, half * 512:(half + 1) * 512], ALU.add),
                           r=[bk(pbk), ('h', i)], w=[('h', i)])
            if i == 0:
                load_ffn_w(0)
                load_ffn_w(1)
            sc_ = CSCR[i % 2]
            _norm_a1(hres[:, i, :], ('h', i), sc_[0], sc_[1], sc_[2], sc_[3])
            yield
            yield
            _norm_a2(C_G2, u2T[:, :, i * 128:(i + 1) * 128], ('u2T', i), sc_[0], sc_[1], sc_[4])
            yield

        def lagged0(gens, lag):
            live = []
            pending = list(gens)
            rnd = 0
            while live or pending:
                if pending and rnd % lag == 0:
                    live.append(pending.pop(0))
                for g in list(live):
                    try:
                        next(g)
                    except StopIteration:
                        live.remove(g)
                rnd += 1

        lagged0([c_stream(i) for i in range(NT)], 2)
        if 'h1' in dbg_d:
            S.dma('sp', dbg_d['h1'], hres[:], r=[('h', i) for i in range(16)], w=['dbg_h1'])
            S.finish('sp', ['dbg_h1'])

        nacc = 0
        for j in range(8):
            b = j % 2
            for tg in range(4):
                hb = (j * 4 + tg) % 2
                for m in range(4):
                    pbk = 2 + nacc % 6
                    nacc += 1
                    for k in range(8):
                        S.op('pe', lambda e: e.matmul(banks[pbk][:, :], wupb[b][:, k, m * 128:(m + 1) * 128],
                                                      u2T[:, k, tg * 512:(tg + 1) * 512], start=(k == 0), stop=(k == 7)),
                             r=[('wup', b)] + [('u2T', tg * 4 + q) for q in range(4)], w=[bk(pbk)], sig=k == 7)
                    S.op('act', lambda e: e.activation(relu_t[:], banks[pbk][:, :], AF.Relu), r=[bk(pbk)], w=['relu_t'])
                    S.op('act', lambda e: e.activation(hidT[hb][:, m, :], relu_t[:], AF.Square),
                         r=['relu_t'], w=[('hidT', hb)])
                for q in range(4):
                    i = tg * 4 + q
                    for half in range(2):
                        pbk = 2 + nacc % 6
                        nacc += 1
                        for m in range(4):
                            S.op('pe', lambda e: e.matmul(banks[pbk][:, :], hidT[hb][:, m, q * 128:(q + 1) * 128],
                                                          wdnb[b][:, m, half * 512:(half + 1) * 512], start=(m == 0), stop=(m == 3)),
                                 r=[('hidT', hb), ('wdn', b)], w=[bk(pbk)], sig=m == 3)
                        S.op('dve', lambda e: e.tensor_tensor(hres[:, i, half * 512:(half + 1) * 512], banks[pbk][:, :],
                                                              hres[:, i, half * 512:(half + 1) * 512], ALU.add),
                             r=[bk(pbk), ('h', i)], w=[('h', i)])
            if j + 2 < 8:
                load_ffn_w(j + 2)
        if 'h2' in dbg_d:
            S.dma('sp', dbg_d['h2'], hres[:], r=[('h', i) for i in range(16)], w=['dbg_h2'])
            S.finish('sp', ['dbg_h2'])
        S.barrier()
        cd.close()

        wg = sb("wg", [128, 8, 1024], BF16)
        wp = sb("wp", [128, 2, 1024], BF16)
        NS = 4
        u_e = [sb("u_e%d" % i, [128, 1024], BF16) for i in range(NS)]
        ssb_e = [sb("ssb_e%d" % i, [128, 8]) for i in range(NS)]
        u3T = [sb("u3T%d" % i, [128, 8, 128], BF16) for i in range(NS)]
        pt = [sb("pt%d" % i, [128, 256], BF16) for i in range(NS)]
        pT = [sb("pT%d" % i, [128, 2, 128], BF16) for i in range(NS)]
        gsb = [sb("gsb%d" % i, [128, 1024]) for i in range(NS)]
        w_g_v = w_gate.rearrange("(k p) n -> p k n", p=128)
        w_p_v = w_ple.rearrange("(k p) n -> p k n", p=128)
        for k in range(8):
            S.dma('pool', wg[:, k, :], w_g_v[:, k, :], w=[('wg', k)])
        S.dma('pool', wp[:], w_p_v, w=['wp'])

        def pe_stream(i):
            b = i % NS
            tb = b % 2
            S.dma('pool', pt[b][:], po[i * 128:(i + 1) * 128, :], w=[('pt', b)])
            norm_to_uT(hres[:, i, :], ('h', i), C_G3, u3T[b], ('u3T', b), scr=(u_e[b], ('u_e', b), ssb_e[b], ('ssb_e', b), tb))
            yield
            psb = banks[tb][:].bitcast(BF16)
            for k in range(2):
                S.op('pe', lambda e: e.transpose(psb[:, k * 128:(k + 1) * 128], pt[b][:, k * 128:(k + 1) * 128], identb),
                     r=[('pt', b), 'cstb'], w=[bk(tb)], sig=k == 1)
            yield S.op('dve', lambda e: e.tensor_copy(pT[b][:], psb[:, 0:256].rearrange("p (k t) -> p k t", k=2)),
                       r=[bk(tb)], w=[('pT', b)])
            for half in range(2):
                pg = 2 + b
                pe_ = 6 + (b % 2)
                for k in range(8):
                    S.op('pe', lambda e: e.matmul(banks[pg][:, :], u3T[b][:, k, :], wg[:, k, half * 512:(half + 1) * 512],
                                                  start=(k == 0), stop=(k == 7)), r=[('u3T', b), ('wg', k)], w=[bk(pg)], sig=k == 7)
                sl = slice(half * 512, (half + 1) * 512)
                yield S.op('act', lambda e: e.activation(gsb[b][:, sl], banks[pg][:, :], AF.Sigmoid), r=[bk(pg)], w=[('gsb', b, half)])
                for k in range(2):
                    S.op('pe', lambda e: e.matmul(banks[pe_][:, :], pT[b][:, k, :], wp[:, k, half * 512:(half + 1) * 512],
                                                  start=(k == 0), stop=(k == 1)), r=[('pT', b), 'wp'], w=[bk(pe_)], sig=k == 1)
                yield S.op('dve', lambda e: e.tensor_tensor(gsb[b][:, sl], banks[pe_][:, :], gsb[b][:, sl], ALU.mult),
                           r=[bk(pe_), ('gsb', b, half)], w=[('gsb', b, half)])
                yield S.op('dve', lambda e: e.tensor_tensor(gsb[b][:, sl], gsb[b][:, sl], hres[:, i, sl], ALU.add),
                           r=[('gsb', b, half), ('h', i)], w=[('gsb', b, half)])
            S.dma('sp', out_d[i * 128:(i + 1) * 128, :], gsb[b][:], r=[('gsb', b, 0), ('gsb', b, 1)],
                  w=[('out', i)])
            yield

        def lagged(gens, lag):
            live = []
            pending = list(gens)
            rnd = 0
            while live or pending:
                if pending and rnd % lag == 0:
                    live.append(pending.pop(0))
                for g in list(live):
                    try:
                        next(g)
                    except StopIteration:
                        live.remove(g)
                rnd += 1

        lagged([pe_stream(i) for i in range(NT)], 2)
        S.finish('sp', [('out', i) for i in range(NT)])
    return nc


def _consts(core, inp):
    c = np.zeros((128, C_TOT), np.float32)

    def pk(v):
        return np.asarray(v, np.float32).reshape(-1, 128).T

    c[:, C_G1:C_G1 + 8] = pk(inp['attn_norm_g'][0])
    c[:, C_G2:C_G2 + 8] = pk(inp['mlp_norm_g'][0])
    c[:, C_G3:C_G3 + 8] = pk(inp['ple_norm_g'][0])
    c[:, C_GY:C_GY + 4] = pk(inp['mlstm_norm_g'][0])
    c[:, C_GY + 4:C_GY + 8] = pk(inp['attn_sub_norm_g'][0])
    cw = np.asarray(inp['conv_w'][0], np.float32)
    c[:, C_CW:C_CW + 32] = cw.T.reshape(8, 128, 4).transpose(1, 0, 2).reshape(128, 32)
    c[:, C_CB:C_CB + 8] = pk(inp['conv_b'][0])
    c[:, C_QG:C_QG + 64] = np.asarray(inp['q_norm_g'][0], np.float32)[None, :]
    c[:, C_KG:C_KG + 64] = np.asarray(inp['k_norm_g'][0], np.float32)[None, :]
    for j, nm in enumerate(('lambda_q1', 'lambda_k1', 'lambda_q2', 'lambda_k2')):
        c[:, C_LAM + j * 64:C_LAM + (j + 1) * 64] = np.asarray(inp[nm][0], np.float32)[None, :]
    odd = core % 2
    c[:, C_FL + 0] = 1.0 if odd else 0.0
    c[:, C_FL + 1] = 0.0 if odd else NEG
    c[:, C_FL + 2] = 0.0 if odd else NEG
    inv = (np.float32(500000.0) ** (-np.arange(0, 16, 2, dtype=np.float32) / np.float32(16))).astype(np.float32)
    for off, base in ((C_CSO, odd * 2048), (C_CSP, 0)):
        pos = (base + np.arange(2048, dtype=np.float32)).astype(np.float32)
        ang = (pos[:, None] * inv[None, :]).astype(np.float32)
        cs = np.concatenate([np.cos(ang), np.sin(ang)], axis=1).astype(np.float32)
        c[:, off:off + 256] = cs.reshape(16, 128, 16).transpose(1, 0, 2).reshape(128, 256)
    c[:, C_ID:C_ID + 128] = np.eye(128, dtype=np.float32)
    c[:, C_TRI:C_TRI + 128] = np.triu(np.ones((128, 128), np.float32))
    c[0:4, C_GB] = np.asarray(inp['igate_b'][0], np.float32)
    c[0:4, C_GB + 1] = np.asarray(inp['fgate_b'][0], np.float32)
    return c


def _constb():
    b = np.zeros((128, 128 + 1024), np.float32)
    b[:, 0:128] = np.eye(128, dtype=np.float32)
    tri = np.triu(np.ones((128, 128), np.float32))
    for m in range(2):
        b[:, 128 + m * 256:128 + m * 256 + 128] = tri
        b[:, 128 + m * 256 + 128:128 + m * 256 + 256] = 1.0
        b[:, 640 + m * 256:640 + m * 256 + 128] = 0.0
        b[:, 640 + m * 256 + 128:640 + m * 256 + 256] = tri
    return b


def make_in_maps(inp):
    x = np.asarray(inp['x'], np.float32)
    p = np.asarray(inp['p'], np.float32)
    shared = {
        'w_in': np.ascontiguousarray(inp['w_in'][0], dtype=np.float32),
        'w_out': np.ascontiguousarray(inp['w_out'][0], dtype=np.float32),
        'w_up': np.ascontiguousarray(inp['w_up'][0], dtype=np.float32),
        'w_down': np.ascontiguousarray(inp['w_down'][0], dtype=np.float32),
        'w_gate': np.ascontiguousarray(inp['w_ple_gate'][0], dtype=np.float32),
        'w_ple': np.ascontiguousarray(inp['w_ple_proj'][0], dtype=np.float32),
        'cstb': _constb(),
    }
    zeros = np.zeros((2048, 1024), np.float32)
    maps = []
    for c in range(8):
        b, hf = c // 2, c % 2
        m = dict(shared)
        m['xo'] = np.ascontiguousarray(x[b, hf * 2048:(hf + 1) * 2048])
        m['xp'] = np.ascontiguousarray(x[b, 0:2048]) if hf else zeros
        m['po'] = np.ascontiguousarray(p[0, b, hf * 2048:(hf + 1) * 2048])
        m['cst'] = _consts(c, inp)
        maps.append(m)
    return maps


def kernel(**inputs):
    nc = build()
    maps = make_in_maps(inputs)
    res = run_bass_kernel_spmd(nc, maps, core_ids=list(range(8)))
    out = np.zeros((4, 4096, 1024), np.float32)
    for c in range(8):
        out[c // 2, (c % 2) * 2048:(c % 2 + 1) * 2048] = res.results[c]['out']
    return out
```

```python
import math
from contextlib import ExitStack

import numpy as np
import concourse.bass as bass
import concourse.mybir as mybir
from concourse.bass_utils import run_bass_kernel_spmd

F32 = mybir.dt.float32
BF16 = mybir.dt.bfloat16
AF = mybir.ActivationFunctionType
ALU = mybir.AluOpType
AX = mybir.AxisListType

SAME_ENGINE_SYNC = True
A_LAG = 1
A2_LAG = 6
PE_WARM_REPS = 1
EPS = 1e-6
NT = 16
NEG = -30000.0
LAM_INIT = 0.8 - 0.6 * math.exp(0.0)
LNC = math.log(128 ** -0.5)

O_MQ, O_MK, O_MV, O_MO, O_MI, O_MF, O_AQ, O_AK, O_AV = 0, 512, 1024, 1536, 2048, 2052, 2056, 2568, 3080

C_G1, C_G2, C_G3, C_GY, C_CW, C_CB, C_QG, C_KG, C_LAM, C_FL, C_CSO, C_CSP, C_ID, C_TRI, C_GB = (
    0, 8, 16, 24, 32, 64, 72, 136, 200, 456, 460, 716, 972, 1100, 1228)
C_TOT = 1230


class _Eng:
    def __init__(self, name, h, sem):
        self.name, self.h, self.sem = name, h, sem
        self.n = 0
        self.count = 0
        self.incs = []
        self.last = None
        self.last_seq = 0
        self.waited = {}
        self.dsems = []
        self.dcnt = []
        self.dnext = 0


class Sched:
    def __init__(self, nc, stack, ndma):
        self.nc = nc
        hs = {'pe': nc.tensor, 'act': nc.scalar, 'dve': nc.vector, 'pool': nc.gpsimd, 'sp': nc.sync}
        self.e = {}
        for k, h in hs.items():
            sem = stack.enter_context(nc.semaphore("s_" + k))
            self.e[k] = _Eng(k, h, sem)
            for i in range(ndma.get(k, 0)):
                self.e[k].dsems.append(stack.enter_context(nc.semaphore("d_%s%d" % (k, i))))
                self.e[k].dcnt.append(0)
        self.st = {}

    def _target(self, dep):
        if dep[0] == 'd':
            return dep[1], dep[2]
        p = self.e[dep[1]]
        seq = dep[2]
        found = None
        for (s, c) in reversed(p.incs):
            if s >= seq:
                found = c
            else:
                break
        if found is None:
            p.count += 1
            p.last.then_inc(p.sem, 1)
            p.incs.append((p.last_seq, p.count))
            found = p.count
        return p.sem, found

    def _wait(self, eng, deps):
        E = self.e[eng]
        for dep in deps:
            if dep is None:
                continue
            if dep[0] == 'c' and dep[1] == eng and (eng == 'pe' or not SAME_ENGINE_SYNC):
                continue
            sem, val = self._target(dep)
            key = id(sem)
            if E.waited.get(key, 0) >= val:
                continue
            E.h.wait_ge(sem, val)
            E.waited[key] = val

    def _deps(self, r, w, eng=None):
        deps = []
        for k in r:
            s = self.st.get(k)
            if s is not None:
                deps.append(s[0])
                if isinstance(k, tuple) and k[0] == 'bank':
                    deps.extend(d for d in s[1].values() if not (d[0] == 'c' and d[1] == eng))
        for k in w:
            s = self.st.get(k)
            if s is not None:
                deps.append(s[0])
                deps.extend(s[1].values())
        return deps

    def _record(self, dep, r, w):
        rk = (dep[0], dep[1] if dep[0] == 'c' else id(dep[1]))
        for k in r:
            s = self.st.setdefault(k, [None, {}])
            s[1][rk] = dep
        for k in w:
            self.st[k] = [dep, {}]

    def op(self, eng, fn, r=(), w=(), sig=None):
        E = self.e[eng]
        self._wait(eng, self._deps(r, w, eng))
        inst = fn(E.h)
        E.n += 1
        E.last = inst
        E.last_seq = E.n
        if sig or (sig is None and eng != 'pe'):
            E.count += 1
            inst.then_inc(E.sem, 1)
            E.incs.append((E.n, E.count))
        self._record(('c', eng, E.n), r, w)
        return inst

    def dma(self, q, out, in_, r=(), w=(), **kw):
        E = self.e[q]
        self._wait(q, self._deps(r, w))
        i = E.dnext
        E.dnext = (i + 1) % len(E.dsems)
        sem = E.dsems[i]
        if E.dcnt[i] > 0:
            key = id(sem)
            if E.waited.get(key, 0) < E.dcnt[i]:
                E.h.wait_ge(sem, E.dcnt[i])
                E.waited[key] = E.dcnt[i]
        E.dcnt[i] += 16
        E.h.dma_start(out=out, in_=in_, **kw).then_inc(sem, 16)
        dep = ('d', sem, E.dcnt[i])
        self._record(dep, r, w)
        return dep

    def barrier(self):
        deps = []
        for s in self.st.values():
            if s[0] is not None:
                deps.append(s[0])
            deps.extend(s[1].values())
        for eng in self.e:
            self._wait(eng, deps)

    def finish(self, eng, keys):
        self._wait(eng, [self.st[k][0] for k in keys if k in self.st])


def build(dbg=(), n_pre=NT, n_own=NT, phases='CDE'):
    nc = bass.Bass("TRN2", target_bir_lowering=False)

    def din(name, shape):
        return nc.dram_tensor(name, list(shape), F32, kind="ExternalInput").ap()

    xo = din("xo", [2048, 1024])
    xp = din("xp", [2048, 1024])
    po = din("po", [2048, 256])
    w_in = din("w_in", [1024, 3592])
    w_out = din("w_out", [1024, 1024])
    w_up = din("w_up", [1024, 4096])
    w_down = din("w_down", [4096, 1024])
    w_gate = din("w_gate", [1024, 1024])
    w_ple = din("w_ple", [256, 1024])
    cst_d = din("cst", [128, C_TOT])
    cstb_d = din("cstb", [128, 128 + 1024])
    out_d = nc.dram_tensor("out", [2048, 1024], F32, kind="ExternalOutput").ap()
    dbg_d = {}
    for name, shape, dt in dbg:
        dbg_d[name] = nc.dram_tensor("dbg_" + name, list(shape), dt, kind="ExternalOutput").ap()

    with ExitStack() as st:
        S = Sched(nc, st, {'sp': 12, 'pool': 8})

        def sb(name, shape, dt=F32):
            return st.enter_context(nc.sbuf_tensor("t_" + name, list(shape), dt))

        def freed(name, shape, dt=F32):
            return nc.sbuf_tensor(name, list(shape), dt)

        cst = sb("cst", [128, C_TOT])
        cstb = sb("cstb", [128, 128 + 1024], BF16)
        identb = cstb[:, 0:128]
        identf = cst[:, C_ID:C_ID + 128]
        tri = cst[:, C_TRI:C_TRI + 128]
        banks = [st.enter_context(nc.psum_tensor("bank%d" % i, [128, 512], F32)) for i in range(8)]
        small = sb("small", [128, 64])
        block = st.enter_context(nc.Block())

        S.dma('sp', cst[:], cst_d, w=['cst'])
        S.dma('pool', cstb[:], cstb_d, w=['cstb'])

        def bk(i):
            return ('bank', i)

        def rstd_from_ss(ss_ap, out_ap, n, key_ss, key_out, mul=None):
            S.op('act', lambda e: e.activation(out_ap, ss_ap, AF.Ln, bias=epsc[:ss_ap.shape[0], 0:1], scale=1.0 / n),
                 r=[key_ss, 'consts'], w=[key_out])
            S.op('act', lambda e: e.activation(out_ap, out_ap, AF.Exp, scale=-0.5), r=[key_out], w=[key_out])
            if mul is not None:
                S.op('dve', lambda e: e.tensor_scalar(out_ap, out_ap, float(mul), None, ALU.mult), r=[key_out], w=[key_out])

        epsc = sb("epsc", [128, 4])
        ones4 = sb("ones4", [4, 128])
        S.op('pool', lambda e: e.memset(epsc[:, 0:1], EPS), w=['consts'])
        S.op('pool', lambda e: e.memset(epsc[:, 1:2], 1.0), w=['consts'])
        S.op('pool', lambda e: e.memset(ones4[:], 1.0), w=['consts4'])

        lamv = cst[:, C_LAM:C_LAM + 256]
        lj = sb("lj", [128, 64])
        S.op('dve', lambda e: e.scalar_tensor_tensor(out=lj[:], in0=lamv[:, 0:64], scalar=1.0, in1=lamv[:, 64:128],
                                                     op0=ALU.mult, op1=ALU.mult, accum_out=small[:, 0:1]),
             r=['cst'], w=['lj', 'small'])
        S.op('dve', lambda e: e.scalar_tensor_tensor(out=lj[:], in0=lamv[:, 128:192], scalar=1.0, in1=lamv[:, 192:256],
                                                     op0=ALU.mult, op1=ALU.mult, accum_out=small[:, 1:2]),
             r=['cst', 'lj'], w=['lj', 'small'])
        S.op('act', lambda e: e.activation(small[:, 2:4], small[:, 0:2], AF.Exp), r=['small'], w=['small'])
        S.op('dve', lambda e: e.tensor_tensor(small[:, 4:5], small[:, 2:3], small[:, 3:4], ALU.subtract), r=['small'], w=['small'])
        S.op('dve', lambda e: e.tensor_scalar(small[:, 5:6], small[:, 4:5], float(LAM_INIT), None, ALU.add), r=['small'], w=['small'])
        lam_ap = small[:, 5:6]
        gpar = sb("gpar", [4, 8])
        gb = cst[0:4, C_GB:C_GB + 2]
        fl = cst[0:4, C_FL:C_FL + 4]
        S.op('dve', lambda e: e.tensor_copy(gpar[:, 0:1], fl[:, 0:1]), r=['cst'], w=['gpar'])
        S.op('dve', lambda e: e.scalar_tensor_tensor(out=gpar[:, 1:2], in0=gb[:, 0:1], scalar=fl[:, 0:1], in1=fl[:, 1:2],
                                                     op0=ALU.mult, op1=ALU.add), r=['cst', 'gpar'], w=['gpar'])
        S.op('pool', lambda e: e.memset(gpar[:, 2:3], 1.0), r=['gpar'], w=['gpar'])
        S.op('dve', lambda e: e.tensor_copy(gpar[:, 3:4], gb[:, 0:1]), r=['cst', 'gpar'], w=['gpar'])
        S.op('dve', lambda e: e.tensor_scalar(gpar[:, 4:5], gb[:, 1:2], -1.0, None, ALU.mult), r=['cst', 'gpar'], w=['gpar'])
        S.op('dve', lambda e: e.tensor_copy(gpar[:, 5:6], fl[:, 0:1]), r=['cst', 'gpar'], w=['gpar'])
        pbias = cst[:, C_FL + 2:C_FL + 3]

        yT = sb("yT", [128, 8, 2048], BF16)
        ab = ExitStack()

        def sa(name, shape, dt=F32):
            return ab.enter_context(nc.sbuf_tensor("t_" + name, list(shape), dt))

        win = sa("win", [128, 8, 3592], BF16)
        kT = sa("kT", [128, 4, 4096], BF16)
        vx = sa("vx", [128, 32, 4, 130], BF16)
        xt = sa("xt", [128, 1024])
        u = sa("u", [128, 1024], BF16)
        uT = [sa("uT%d" % i, [128, 8, 128], BF16) for i in range(2)]
        ssb = sa("ssb", [128, 8])
        raw = sa("raw", [128, 8, 131])
        acc = sa("acc", [128, 4, 128])
        qkT2 = [sa("qkT%d" % i, [128, 8, 128], BF16) for i in range(2)]
        gf = sa("gf", [4, 8, 128])
        carry = sa("carry", [4, 8])
        Gt = [small[:, 16 + 12 * i:28 + 12 * i] for i in range(3)]
        mvx = [sa("mvx%d" % i, [128, 4, 130], BF16) for i in range(2)]
        sigo2 = [sa("sigo%d" % i, [128, 512], BF16) for i in range(2)]
        mls = sa("mls", [128, 512], BF16)
        ktl = [mls[:, i * 128:(i + 1) * 128] for i in range(2)]
        stm = [mls[:, 256 + i * 128:256 + (i + 1) * 128] for i in range(2)]
        Pst = sa("Pst", [128, 4, 130])
        Cbf = sa("Cbf", [128, 4, 130], BF16)
        ml = sa("ml", [128, 32])
        qf = sa("qf", [128, 512])
        qb16 = sa("qb16", [128, 512], BF16)
        qtok = [sa("qtok%d" % i, [128, 512], BF16) for i in range(2)]
        qTg = [sa("qTg%d" % i, [128, 4, 256], BF16) for i in range(2)]
        Et = [sa("Et%d" % i, [128, 512], BF16) for i in range(2)]
        ofin = [sa("ofin%d" % i, [128, 128]) for i in range(2)]
        yat = [sa("yat%d" % i, [128, 512], BF16) for i in range(2)]
        al = sa("al", [128, 16])
        ae = small[:, 8:16]

        w_in_v = w_in.rearrange("(k p) n -> p k n", p=128)
        WGRP = [(O_MK, O_MV), (O_MV, O_MO), (O_MI, O_AQ), (O_AK, O_AV), (O_AV, 3592), (O_MQ, O_MK), (O_MO, O_MI), (O_AQ, O_AK)]
        for gi_, (c0_, c1_) in enumerate(WGRP):
            S.dma('pool', win[:, :, c0_:c1_], w_in_v[:, :, c0_:c1_], w=[('win', gi_)])

        def wkey(col):
            for gi_, (c0_, c1_) in enumerate(WGRP):
                if c0_ <= col < c1_:
                    return ('win', gi_)
            raise ValueError(col)

        S.op('pool', lambda e: e.memset(vx[:, :, :, 128:130], 1.0), w=[('vx', j) for j in range(32)])
        S.op('pool', lambda e: e.tensor_scalar(vx[:, 0:16, :, 128:130], vx[:, 0:16, :, 128:130], cst[:, C_FL:C_FL + 1], 1.0,
                                               ALU.mult, ALU.mult),
             r=['cst'] + [('vx', j) for j in range(16)], w=[('vx', j) for j in range(16)])
        for i in range(2):
            S.op('pool', lambda e: e.memset(mvx[i][:, :, 128:130], 1.0), w=[('mvx', i)])
        for i in range(3):
            S.op('pool', lambda e: e.memset(Gt[i], 1.0), w=[('Gt', i)])
        S.op('pool', lambda e: e.memset(qTg[0][:], 0.0), w=[('qTg', 0, 0), ('qTg', 0, 1)])
        S.op('pool', lambda e: e.memset(qTg[1][:], 0.0), w=[('qTg', 1, 0), ('qTg', 1, 1)])
        S.op('pool', lambda e: e.memset(Pst[:], 0.0), w=[('Pst', h) for h in range(4)])
        S.op('pool', lambda e: e.memset(Cbf[:], 0.0), w=['Cbf'])
        S.op('pool', lambda e: e.memset(raw[:], 0.0), w=[('raw', c) for c in range(8)])
        S.op('pool', lambda e: e.memset(carry[:], 0.0), w=[('carry', j) for j in range(8)])

        cw = cst[:, C_CW:C_CW + 32].rearrange("p (c j) -> p c j", j=4)
        cb = cst[:, C_CB:C_CB + 8]
        cnt = {'tt': 0}

        def norm_to_uT(x_tile, kx, gcol, uT_t, kuT, scr=None):
            if scr is None:
                u_, ku, ssb_, kss, tb = u, 'u', ssb, 'ssb', 0
            else:
                u_, ku, ssb_, kss, tb = scr
            return _norm_to_uT(x_tile, kx, gcol, uT_t, kuT, u_, ku, ssb_, kss, tb)

        def _norm_to_uT(x_tile, kx, gcol, uT_t, kuT, u, ku, ssb, kss, tb):
            _norm_a1(x_tile, kx, u, ku, ssb, kss)
            _norm_a2(gcol, uT_t, kuT, u, ku, tb)

        def _norm_a1(x_tile, kx, u, ku, ssb, kss):
            S.op('act', lambda e: e.activation(u[:], x_tile, AF.Square, accum_out=ssb[:, 0:1]), r=[kx], w=[ku, kss])
            rstd_from_ss(ssb[:, 0:1], ssb[:, 1:2], 1024.0, kss, kss)
            S.op('dve', lambda e: e.tensor_scalar(u[:], x_tile, ssb[:, 1:2], None, ALU.mult), r=[kx, kss], w=[ku])

        def _norm_a2(gcol, uT_t, kuT, u, ku, tb):
            psb = banks[tb][:].bitcast(BF16)
            for k in range(8):
                S.op('pe', lambda e: e.transpose(psb[:, k * 128:(k + 1) * 128], u[:, k * 128:(k + 1) * 128], identb),
                     r=[ku, 'cstb'], w=[bk(tb)], sig=k == 7)
            g_bc = cst[:, gcol:gcol + 8].unsqueeze(2).to_broadcast([128, 8, 128])
            S.op('dve', lambda e: e.tensor_tensor(uT_t[:], psb[:, 0:1024].rearrange("p (k t) -> p k t", k=8), g_bc, ALU.mult),
                 r=[bk(tb), 'cst'], w=[kuT])

        def transpose_to(src_tok, ksrc, dst_ap, kdst, scale_cols=None, bank=0):
            psb = banks[bank][:].bitcast(BF16)
            for k in range(4):
                S.op('pe', lambda e: e.transpose(psb[:, k * 128:(k + 1) * 128], src_tok[:, k * 128:(k + 1) * 128], identb),
                     r=[ksrc, 'cstb'], w=[bk(bank)], sig=k == 3)
            src = psb[:, 0:512].rearrange("p (k t) -> p k t", k=4)
            if scale_cols is None:
                S.op('dve', lambda e: e.tensor_copy(dst_ap, src), r=[bk(bank)], w=[kdst])
            else:
                g_bc = scale_cols.unsqueeze(2).to_broadcast([128, 4, 128])
                S.op('dve', lambda e: e.tensor_tensor(dst_ap, src, g_bc, ALU.mult), r=[bk(bank), 'cst'], w=[kdst])

        def proj_tm(uT_t, kuT, col, bank):
            for k in range(8):
                S.op('pe', lambda e: e.matmul(banks[bank][:, :], uT_t[:, k, :], win[:, k, col:col + 512],
                                              start=(k == 0), stop=(k == 7)),
                     r=[kuT, wkey(col)], w=[bk(bank)], sig=k == 7)

        def rr(*gens):
            gens = [g for g in gens if g is not None]
            while gens:
                for g in list(gens):
                    try:
                        next(g)
                    except StopIteration:
                        gens.remove(g)
                yield

        def run(gen):
            for _ in gen:
                pass


        def qk_prep(bank, gcol, cs, qb16, k16):
            q3 = qf[:].rearrange("p (g d) -> p g d", d=64)
            S.op('act', lambda e: e.activation(qf[:], banks[bank][:, :], AF.Square), r=[bk(bank)], w=['qf'])
            S.op('dve', lambda e: e.tensor_reduce(out=al[:, 0:8], in_=q3, axis=AX.X, op=ALU.add), r=['qf'], w=['al'])
            S.op('act', lambda e: e.activation(al[:, 8:16], al[:, 0:8], AF.Ln, bias=epsc[:, 0:1], scale=1.0 / 64),
                 r=['al', 'consts'], w=['al'])
            S.op('act', lambda e: e.activation(al[:, 8:16], al[:, 8:16], AF.Exp, scale=-0.5), r=['al'], w=['al'])
            S.op('dve', lambda e: e.tensor_tensor(q3, banks[bank][:, :].rearrange("p (g d) -> p g d", d=64),
                                                  al[:, 8:16].unsqueeze(2).to_broadcast([128, 8, 64]), ALU.mult),
                 r=[bk(bank), 'al'], w=['qf'])
            gg = cst[:, gcol:gcol + 64].unsqueeze(1).to_broadcast([128, 8, 64])
            o3 = qb16[:].rearrange("p (g d) -> p g d", d=64)
            yield S.op('dve', lambda e: e.tensor_tensor(o3, q3, gg, ALU.mult), r=['qf', 'cst'], w=[k16])
            gg16 = cst[:, gcol:gcol + 16].unsqueeze(1).to_broadcast([128, 8, 16])
            yield S.op('dve', lambda e: e.tensor_tensor(q3[:, :, 0:16], q3[:, :, 0:16], gg16, ALU.mult), r=['qf', 'cst'], w=['qf'])
            x1, x2 = q3[:, :, 0:8], q3[:, :, 8:16]
            cc = cs[:, 0:8].unsqueeze(1).to_broadcast([128, 8, 8])
            sn = cs[:, 8:16].unsqueeze(1).to_broadcast([128, 8, 8])
            r4 = [q3[:, :, 16 + 8 * a_:24 + 8 * a_] for a_ in range(4)]
            S.op('dve', lambda e: e.tensor_tensor(r4[0], x1, cc, ALU.mult), r=['qf', 'cst'], w=['qf'])
            S.op('dve', lambda e: e.tensor_tensor(r4[1], x2, sn, ALU.mult), r=['qf', 'cst'], w=['qf'])
            S.op('dve', lambda e: e.tensor_tensor(r4[2], x2, cc, ALU.mult), r=['qf', 'cst'], w=['qf'])
            yield S.op('dve', lambda e: e.tensor_tensor(r4[3], x1, sn, ALU.mult), r=['qf', 'cst'], w=['qf'])
            S.op('dve', lambda e: e.tensor_tensor(o3[:, :, 0:8], r4[0], r4[1], ALU.subtract), r=['qf', k16], w=[k16])
            yield S.op('dve', lambda e: e.tensor_tensor(o3[:, :, 8:16], r4[2], r4[3], ALU.add), r=['qf', k16], w=[k16])

        def attention_group(g):
            kbs = list(range(16)) + [16 + j for j in range(2 * g + 2)]
            units = [(h, idx, kb) for h in range(4) for idx, kb in enumerate(kbs)]
            PSB = (4, 5)
            for tt in range(2):
                psb = banks[4 + tt][:].bitcast(BF16)
                for k in range(4):
                    S.op('pe', lambda e: e.transpose(psb[:, k * 128:(k + 1) * 128], qtok[tt][:, k * 128:(k + 1) * 128], identb),
                         r=[('qtok', tt), 'cstb'], w=[bk(4 + tt)], sig=(k == 3))
                yield
                c_ = tt * 128
                S.op('act', lambda e: e.activation(qTg[0][0:64, :, c_:c_ + 128], psb[0:64, 0:512].rearrange("p (k t) -> p k t", k=4),
                                                   AF.Copy), r=[bk(4 + tt)], w=[('qTg', 0, tt)])
                yield S.op('act', lambda e: e.activation(qTg[1][64:128, :, c_:c_ + 128],
                                                         psb[64:128, 0:512].rearrange("p (k t) -> p k t", k=4), AF.Copy),
                           r=[bk(4 + tt)], w=[('qTg', 1, tt)])
            QK = [('qTg', m, tt) for m in range(2) for tt in range(2)]

            def st_mm(n):
                h, idx, kb = units[n]
                pb = PSB[n % 2]
                for rep_ in range(PE_WARM_REPS):
                    S.op('pe', lambda e: e.matmul(banks[pb][:, 0:256], kT[:, h, kb * 128:(kb + 1) * 128],
                                                  qTg[0][:, h, :], start=True, stop=True),
                         r=[('kT', kb)] + QK, w=[bk(pb)], sig=False)
                    S.op('pe', lambda e: e.matmul(banks[pb][:, 256:512], kT[:, h, kb * 128:(kb + 1) * 128],
                                                  qTg[1][:, h, :], start=True, stop=True),
                         r=[('kT', kb)] + QK, w=[bk(pb)], sig=(rep_ == PE_WARM_REPS - 1))

            st_mm(0)
            for n, (h, idx, kb) in enumerate(units):
                if n + 1 < len(units):
                    st_mm(n + 1)
                pb = PSB[n % 2]
                E = Et[n % 2]
                kE = ('Et', n % 2)
                S.op('act', lambda e: e.activation(E[:], banks[pb][:, :], AF.Exp, scale=0.125), r=[bk(pb)], w=[kE])
                own = kb - 16
                if own == 2 * g:
                    S.op('dve', lambda e: e.tensor_tensor(E[:], E[:], cstb[:, 128:640], ALU.mult), r=[kE, 'cstb'], w=[kE])
                elif own == 2 * g + 1:
                    S.op('dve', lambda e: e.tensor_tensor(E[:], E[:], cstb[:, 640:1152], ALU.mult), r=[kE, 'cstb'], w=[kE])
                for m in range(2):
                    for qb in range(2):
                        if qb == 0 and own == 2 * g + 1:
                            continue
                        last = (own == 2 * g) if qb == 0 else (own == 2 * g + 1)
                        a = 6 + m
                        o0 = qb * 130
                        S.op('pe', lambda e: e.matmul(banks[a][:, o0:o0 + 129], E[:, m * 256 + qb * 128: m * 256 + qb * 128 + 128],
                                                      vx[:, kb, h, 0:129], start=(idx == 0 and qb == 0), stop=last,
                                                      skip_group_check=True),
                             r=[kE, ('vx', kb)], w=[bk(a)], sig=(m == 1 and qb == 1))
                yield
                if idx == len(kbs) - 1:
                    den6 = banks[6][:, 0:260].rearrange("p (a c) -> p a c", c=130)[:, :, 128]
                    den7 = banks[7][:, 0:260].rearrange("p (a c) -> p a c", c=130)[:, :, 128]
                    S.op('dve', lambda e: e.reciprocal(ae[:, 2:4], den7), r=[bk(7)], w=[('ae', 1)])
                    S.op('dve', lambda e: e.tensor_scalar(ae[:, 2:4], ae[:, 2:4], lam_ap, None, ALU.mult), r=[('ae', 1), 'small'], w=[('ae', 1)])
                    S.op('dve', lambda e: e.reciprocal(ae[:, 0:2], den6), r=[bk(6)], w=[('ae', 0)])
                    for qb in range(2):
                        o0 = qb * 130
                        S.op('act', lambda e: e.activation(ofin[qb][:], banks[7][:, o0:o0 + 128], AF.Copy, scale=ae[:, 2 + qb:3 + qb]),
                             r=[bk(7), ('ae', 1)], w=[('ofin', qb)])
                        S.op('dve', lambda e: e.scalar_tensor_tensor(out=ofin[qb][:], in0=banks[6][:, o0:o0 + 128], scalar=ae[:, qb:qb + 1],
                                                                     in1=ofin[qb][:], op0=ALU.mult, op1=ALU.subtract),
                             r=[bk(6), ('ae', 0), ('ofin', qb)], w=[('ofin', qb)])
                    yield
                    for qb in range(2):
                        S.op('act', lambda e: e.activation(yat[qb][:, h * 128:(h + 1) * 128], ofin[qb][:], AF.Square,
                                                           accum_out=ae[:, 4 + qb:5 + qb]),
                             r=[('ofin', qb)], w=[('yat', qb), ('ae', 2 + qb)])
                    S.op('act', lambda e: e.activation(ae[:, 6:8], ae[:, 4:6], AF.Ln, bias=epsc[:, 0:1], scale=1.0 / 128),
                         r=[('ae', 2), ('ae', 3), 'consts'], w=[('ae', 4)])
                    S.op('act', lambda e: e.activation(ae[:, 6:8], ae[:, 6:8], AF.Exp, scale=-0.5), r=[('ae', 4)], w=[('ae', 4)])
                    for qb in range(2):
                        S.op('dve', lambda e: e.tensor_scalar(yat[qb][:, h * 128:(h + 1) * 128], ofin[qb][:], ae[:, 6 + qb:7 + qb],
                                                               float(1.0 - LAM_INIT), ALU.mult, ALU.mult),
                             r=[('ofin', qb), ('ae', 4)], w=[('yat', qb)])
                    yield
            for qb in range(2):
                t0 = (2 * g + qb) * 128
                transpose_to(yat[qb], ('yat', qb), yT[:, 4:8, t0:t0 + 128], ('yT', 2 * g + qb, 1),
                             scale_cols=cst[:, C_GY + 4:C_GY + 8], bank=4 + qb)
                yield

        NTT = n_pre + n_own
        def tt_info(n):
            own = n >= n_pre
            i = n - n_pre if own else n
            return own, i

        def stageA(n):
            own, i = tt_info(n)
            xsrc = xo if own else xp
            S.dma('sp', xt[:], xsrc[i * 128:(i + 1) * 128, :], w=['xt'])
            _norm_a1(xt[:], 'xt', u, 'u', ssb, 'ssb')
            for _ in range(A2_LAG):
                yield
            _norm_a2(C_G1, uT[n % 2], ('uT', n % 2), u, 'u', 0)
            yield

        def front(n):
            own, i = tt_info(n)
            blk = (16 + i) if own else i
            uT_t, kuT = uT[n % 2], ('uT', n % 2)
            Gc, kG = Gt[n % 3], ('Gt', n % 3)
            mv_t, kmv = mvx[n % 2], ('mvx', n % 2)
            qkT, kq = qkT2[n % 2], n % 2
            sigo = sigo2[n % 2]
            if own:
                GB, FMB, TMB = 0, (1, 1), (1, 1)
            else:
                GB, FMB, TMB = 1, (2, 3), (6, 7)
            need_q = own or i == NT - 1

            def T2():
                for half in ((0, 1) if need_q else (1,)):
                    pbk = FMB[half]
                    c0 = half * 4
                    for c in range(c0, c0 + 4):
                        col = (O_MQ + c * 128) if c < 4 else (O_MK + (c - 4) * 128)
                        for k in range(8):
                            S.op('pe', lambda e: e.matmul(banks[pbk][:, (c % 4) * 128:(c % 4 + 1) * 128], win[:, k, col:col + 128],
                                                          uT_t[:, k, :], start=(k == 0), stop=(k == 7)),
                                 r=[kuT, wkey(col)], w=[bk(pbk)], sig=(k == 7))
                    RK = [('raw', c) for c in range(c0, c0 + 4)]
                    if own:
                        S.op('dve', lambda e: e.tensor_copy(raw[:, c0:c0 + 4, 3:131], banks[pbk][:, :].rearrange("p (c t) -> p c t", c=4)),
                             r=[bk(pbk)], w=RK)
                    else:
                        S.op('act', lambda e: e.activation(raw[:, c0:c0 + 4, 3:131], banks[pbk][:, :].rearrange("p (c t) -> p c t", c=4),
                                                           AF.Copy), r=[bk(pbk)], w=RK)
                    conv = own or half == 1
                    if conv:
                        for c in range(c0, c0 + 4):
                            if own:
                                S.op('dve', lambda e: e.tensor_scalar(acc[:, c % 4, :], banks[pbk][:, (c % 4) * 128:(c % 4 + 1) * 128],
                                                                      cw[:, c, 3:4], cb[:, c:c + 1], ALU.mult, ALU.add),
                                     r=[bk(pbk), 'cst'], w=[('acc', c % 4)])
                            else:
                                S.op('act', lambda e: e.activation(acc[:, c % 4, :], banks[pbk][:, (c % 4) * 128:(c % 4 + 1) * 128],
                                                                   AF.Identity, bias=cb[:, c:c + 1], scale=cw[:, c, 3:4]),
                                     r=[bk(pbk), 'cst'], w=[('acc', c % 4)])
                    yield
                    if conv:
                        for c in range(c0, c0 + 4):
                            for j in range(3):
                                yield S.op('dve', lambda e: e.scalar_tensor_tensor(out=acc[:, c % 4, :], in0=raw[:, c, j:j + 128],
                                                                                   scalar=cw[:, c, j:j + 1], in1=acc[:, c % 4, :],
                                                                                   op0=ALU.mult, op1=ALU.add),
                                           r=[('raw', c), 'cst', ('acc', c % 4)], w=[('acc', c % 4)])
                    if own:
                        yield S.op('dve', lambda e: e.tensor_copy(raw[:, c0:c0 + 4, 0:3], raw[:, c0:c0 + 4, 128:131]), r=RK, w=RK)
                    else:
                        yield S.op('act', lambda e: e.activation(raw[:, c0:c0 + 4, 0:3], raw[:, c0:c0 + 4, 128:131], AF.Copy), r=RK, w=RK)
                    if conv:
                        tmp = raw[:, c0:c0 + 4, 3:131]
                        AK = [('acc', c) for c in range(4)]
                        S.op('act', lambda e: e.activation(tmp, acc[:, :, :], AF.Exp, scale=-1.0), r=AK + RK, w=RK)
                        S.op('act', lambda e: e.activation(tmp, tmp, AF.Ln, bias=epsc[:, 1:2]), r=RK + ['consts'], w=RK)
                        yield S.op('act', lambda e: e.activation(tmp, tmp, AF.Exp, scale=-1.0), r=RK, w=RK)
                        yield S.op('dve', lambda e: e.tensor_tensor(qkT[:, c0:c0 + 4, :], acc[:, :, :], tmp, ALU.mult),
                                   r=AK + RK, w=[('qkT', kq, half)])

            def T3():
                for gi, col in enumerate((O_MI, O_MF)):
                    for k in range(8):
                        S.op('pe', lambda e: e.matmul(banks[GB][0:4, gi * 128:(gi + 1) * 128], win[:, k, col:col + 4],
                                                      uT_t[:, k, :], start=(k == 0), stop=(k == 7)),
                             r=[kuT, wkey(col)], w=[bk(GB)], sig=(k == 7))
                po_ = 2 if own else 0
                G = lambda j: ('gf', j)
                C = lambda j: ('carry', j)
                S.op('act', lambda e: e.activation(gf[:, 0, :], banks[GB][0:4, 0:128], AF.Identity,
                                                   bias=gpar[:, po_ + 1:po_ + 2], scale=gpar[:, po_:po_ + 1]),
                     r=[bk(GB), 'gpar'], w=[G(0)])
                yield S.op('act', lambda e: e.activation(gf[:, 1, :], banks[GB][0:4, 128:256], AF.Exp, bias=gpar[:, 4:5], scale=-1.0),
                           r=[bk(GB), 'gpar'], w=[G(1)])
                yield S.op('act', lambda e: e.activation(gf[:, 1, :], gf[:, 1, :], AF.Ln, bias=epsc[0:4, 1:2]),
                           r=[G(1), 'consts'], w=[G(1)])
                if not own:
                    yield S.op('dve', lambda e: e.tensor_scalar(gf[:, 1, :], gf[:, 1, :], gpar[:, 5:6], None, ALU.mult),
                               r=[G(1), 'gpar'], w=[G(1)])
                yield S.op('dve', lambda e: e.tensor_tensor_scan(gf[:, 2, :], ones4[:], gf[:, 1, :], carry[:, 0:1], ALU.mult, ALU.add),
                           r=[G(1), 'consts4', C(0)], w=[G(2)])
                yield S.op('dve', lambda e: e.tensor_tensor(gf[:, 3, :], gf[:, 0, :], gf[:, 2, :], ALU.add), r=[G(0), G(2)], w=[G(3)])
                yield S.op('dve', lambda e: e.tensor_tensor_scan(gf[:, 4, :], ones4[:], gf[:, 3, :], carry[:, 1:2], ALU.mult, ALU.max),
                           r=[G(3), 'consts4', C(1)], w=[G(4)])
                S.op('dve', lambda e: e.tensor_scalar(carry[:, 2:3], carry[:, 1:2], -1.0, float(LNC), ALU.mult, ALU.add),
                     r=[C(1)], w=[C(2)])
                S.op('dve', lambda e: e.tensor_scalar(carry[:, 3:4], carry[:, 1:2], -1.0, None, ALU.mult), r=[C(1)], w=[C(3)])
                yield S.op('dve', lambda e: e.tensor_tensor(carry[:, 4:5], carry[:, 1:2], gf[:, 4, 127:128], ALU.subtract),
                           r=[C(1), G(4)], w=[C(4)])
                yield S.op('act', lambda e: e.activation(gf[:, 5, :], gf[:, 3, :], AF.Exp, bias=carry[:, 2:3]), r=[G(3), C(2)], w=[G(5)])
                yield S.op('act', lambda e: e.activation(gf[:, 6, :], gf[:, 2, :], AF.Exp, bias=carry[:, 3:4]), r=[G(2), C(3)], w=[G(6)])
                yield S.op('act', lambda e: e.activation(gf[:, 7, :], ones4[:], AF.Exp, bias=carry[:, 4:5], scale=0.0),
                           r=[C(4), 'consts4'], w=[G(7)])
                S.op('dve', lambda e: e.tensor_copy(carry[:, 0:1], gf[:, 2, 127:128]), r=[G(2), C(0)], w=[C(0)])
                yield S.op('dve', lambda e: e.tensor_copy(carry[:, 1:2], gf[:, 4, 127:128]), r=[G(4), C(1)], w=[C(1)])
                for a in range(3):
                    S.op('pe', lambda e: e.transpose(banks[GB][:, 256 + a * 4:256 + a * 4 + 4], gf[:, 5 + a, :], identf[0:4, 0:4]),
                         r=[G(5 + a), 'cst'], w=[bk(GB)], sig=(a == 2))
                yield S.op('dve', lambda e: e.tensor_copy(Gc, banks[GB][:, 256:268]), r=[bk(GB)], w=[kG])

            def T4():
                tb = [0]

                def nxt():
                    tb[0] += 1
                    return TMB[tb[0] % 2]
                b_ = nxt()
                proj_tm(uT_t, kuT, O_MV, b_)
                if own:
                    yield S.op('dve', lambda e: e.tensor_copy(mv_t[:, :, 0:128], banks[b_][:, :].rearrange("p (h d) -> p h d", h=4)),
                               r=[bk(b_)], w=[kmv])
                else:
                    yield S.op('act', lambda e: e.activation(mv_t[:, :, 0:128], banks[b_][:, :].rearrange("p (h d) -> p h d", h=4), AF.Copy),
                               r=[bk(b_)], w=[kmv])
                if own:
                    b_ = nxt()
                    proj_tm(uT_t, kuT, O_MO, b_)
                    S.op('act', lambda e: e.activation(qf[:], banks[b_][:, :], AF.Exp, scale=-1.0), r=[bk(b_)], w=['qf'])
                    S.op('act', lambda e: e.activation(qf[:], qf[:], AF.Ln, bias=epsc[:, 1:2]), r=['qf', 'consts'], w=['qf'])
                    yield S.op('act', lambda e: e.activation(sigo[:], qf[:], AF.Exp, scale=-1.0), r=['qf'], w=[('sigo', n % 2)])
                cs_k = cst[:, (C_CSO if own else C_CSP) + i * 16:(C_CSO if own else C_CSP) + i * 16 + 16]
                b_ = nxt()
                proj_tm(uT_t, kuT, O_AK, b_)
                yield from qk_prep(b_, C_KG, cs_k, qb16, 'qb16')
                transpose_to(qb16, 'qb16', kT[:, :, blk * 128:(blk + 1) * 128], ('kT', blk), bank=0)
                yield
                b_ = nxt()
                proj_tm(uT_t, kuT, O_AV, b_)
                if own:
                    yield S.op('dve', lambda e: e.tensor_copy(vx[:, blk, :, 0:128], banks[b_][:, :].rearrange("p (h d) -> p h d", h=4)),
                               r=[bk(b_)], w=[('vx', blk)])
                else:
                    yield S.op('act', lambda e: e.activation(vx[:, blk, :, 0:128], banks[b_][:, :].rearrange("p (h d) -> p h d", h=4), AF.Copy),
                               r=[bk(b_)], w=[('vx', blk)])
                if own:
                    b_ = nxt()
                    proj_tm(uT_t, kuT, O_AQ, b_)
                    yield from qk_prep(b_, C_QG, cs_k, qtok[i % 2], ('qtok', i % 2))

            yield from rr(T2(), T3(), T4())
            if own and i == 0 and 'qkT0' in dbg_d:
                S.dma('sp', dbg_d['qkT0'], qkT[:], r=[('qkT', kq, 0), ('qkT', kq, 1)], w=['dbg_qkT0'])
                S.finish('sp', ['dbg_qkT0'])

        def back(n):
            own, i = tt_info(n)
            Gc, kG = Gt[n % 3], ('Gt', n % 3)
            Gp, kGp = Gt[(n + 2) % 3], ('Gt', (n + 2) % 3)
            mv_t, kmv = mvx[n % 2], ('mvx', n % 2)
            qkT, kq = qkT2[n % 2], n % 2
            sigo = sigo2[n % 2]
            HB = (2, 3) if own else (4, 5)
            MK = [('ktl', 0), ('ktl', 1), ('stm', 0), ('stm', 1)]

            def head(h):
                e_ = h % 2
                kt_, kkt = ktl[e_], ('ktl', e_)
                st_, kst = stm[e_], ('stm', e_)
                hb = HB[e_]
                B = banks[hb]
                psb = B[:].bitcast(BF16)
                S.op('pe', lambda e: e.transpose(psb[:, 0:128], qkT[:, 4 + h, :], identb), r=[('qkT', kq, 1), 'cstb'], w=[bk(hb)], sig=True)
                yield
                yield S.op('dve', lambda e: e.tensor_scalar(kt_, psb[:, 0:128], Gc[:, h:h + 1], None, ALU.mult),
                           r=[bk(hb), kG], w=[kkt])
                if own:
                    S.op('pe', lambda e: e.matmul(B[:, 64:192], qkT[:, 4 + h, :], qkT[:, h, :], start=True, stop=True),
                         r=[('qkT', kq, 0), ('qkT', kq, 1)], w=[bk(hb)], sig=True)
                    yield
                    yield S.op('dve', lambda e: e.scalar_tensor_tensor(out=st_, in0=B[:, 64:192], scalar=Gc[:, h:h + 1],
                                                                       in1=tri, op0=ALU.mult, op1=ALU.mult),
                               r=[bk(hb), kG, 'cst'], w=[kst])
                    S.op('pe', lambda e: e.matmul(B[:, 192:321], qkT[:, h, :], Cbf[:, h, 0:129], start=True, stop=False),
                         r=[('qkT', kq, 0), 'Cbf'], w=[bk(hb)], sig=False)
                    S.op('pe', lambda e: e.matmul(B[:, 192:321], st_, mv_t[:, h, 0:129], start=False, stop=True),
                         r=[kst, kmv], w=[bk(hb)], sig=True)
                    yield
                S.op('pe', lambda e: e.matmul(B[:, 336:465], kt_, mv_t[:, h, 0:129], start=True, stop=True),
                     r=[kkt, kmv], w=[bk(hb)], sig=True)
                yield
                yield S.op('dve', lambda e: e.scalar_tensor_tensor(out=Pst[:, h, 0:129], in0=Pst[:, h, 0:129], scalar=Gp[:, 8 + h:9 + h],
                                                                   in1=B[:, 336:465], op0=ALU.mult, op1=ALU.add),
                           r=[bk(hb), kGp, ('Pst', h)], w=[('Pst', h)])

            def pair_epilogue(p):
                c = 8 * p
                m = ml[:, 16 * p:16 * p + 16]
                for e_ in range(2):
                    S.op('dve', lambda e: e.tensor_copy(m[:, e_:e_ + 1], banks[HB[e_]][:, 320:321]), r=[bk(HB[e_])], w=[('ml', p, 0)])
                S.op('dve', lambda e: e.scalar_tensor_tensor(out=m[:, 2:4], in0=m[:, 0:2], scalar=-1.0, in1=m[:, 0:2],
                                                             op0=ALU.mult, op1=ALU.max), r=[('ml', p, 0)], w=[('ml', p, 1)])
                S.op('dve', lambda e: e.tensor_tensor(m[:, 2:4], m[:, 2:4], Gc[:, 4 + 2 * p:6 + 2 * p], ALU.max),
                     r=[('ml', p, 1), kG], w=[('ml', p, 1)])
                yield S.op('dve', lambda e: e.reciprocal(m[:, 4:6], m[:, 2:4]), r=[('ml', p, 1)], w=[('ml', p, 2)])
                for e_ in range(2):
                    B = banks[HB[e_]]
                    h_ = 2 * p + e_
                    yield S.op('act', lambda e: e.activation(mls[:, h_ * 128:(h_ + 1) * 128], B[:, 192:320], AF.Square,
                                                             accum_out=m[:, 6 + e_:7 + e_]),
                               r=[bk(HB[e_])], w=[MK[h_], ('ml', p, 3 + e_)])
                S.op('dve', lambda e: e.tensor_tensor(m[:, 8:10], m[:, 4:6], m[:, 4:6], ALU.mult), r=[('ml', p, 2)], w=[('ml', p, 5)])
                yield S.op('dve', lambda e: e.tensor_tensor(m[:, 8:10], m[:, 8:10], m[:, 6:8], ALU.mult),
                           r=[('ml', p, 5), ('ml', p, 3), ('ml', p, 4)], w=[('ml', p, 5)])
                S.op('act', lambda e: e.activation(m[:, 10:12], m[:, 8:10], AF.Ln, bias=epsc[:, 0:1], scale=1.0 / 128),
                     r=[('ml', p, 5), 'consts'], w=[('ml', p, 6)])
                yield S.op('act', lambda e: e.activation(m[:, 10:12], m[:, 10:12], AF.Exp, scale=-0.5), r=[('ml', p, 6)], w=[('ml', p, 6)])
                yield S.op('dve', lambda e: e.tensor_tensor(m[:, 12:14], m[:, 10:12], m[:, 4:6], ALU.mult),
                           r=[('ml', p, 6), ('ml', p, 2)], w=[('ml', p, 7)])
                for e_ in range(2):
                    h = 2 * p + e_
                    B = banks[HB[e_]]
                    yield S.op('dve', lambda e: e.scalar_tensor_tensor(out=mls[:, h * 128:(h + 1) * 128], in0=B[:, 192:320],
                                                                       scalar=m[:, 12 + e_:13 + e_], in1=sigo[:, h * 128:(h + 1) * 128],
                                                                       op0=ALU.mult, op1=ALU.mult),
                               r=[bk(HB[e_]), ('ml', p, 7), ('sigo', n % 2)], w=[MK[h]])
                for e_ in range(2):
                    h = 2 * p + e_
                    B = banks[HB[e_]]
                    psb = B[:].bitcast(BF16)
                    S.op('pe', lambda e: e.transpose(psb[:, 0:128], mls[:, h * 128:(h + 1) * 128], identb),
                         r=[MK[h], 'cstb'], w=[bk(HB[e_])], sig=True)
                    yield
                    yield S.op('dve', lambda e: e.tensor_scalar(yT[:, h, i * 128:(i + 1) * 128], psb[:, 0:128],
                                                                cst[:, C_GY + h:C_GY + h + 1], None, ALU.mult),
                               r=[bk(HB[e_]), 'cst'], w=[('yT', i, 0)])

            for p in range(2):
                yield from rr(head(2 * p), head(2 * p + 1))
                if own:
                    yield from pair_epilogue(p)
            if own or i == NT - 1:
                yield S.op('dve', lambda e: e.tensor_tensor(Cbf[:, :, 0:129], Pst[:, :, 0:129],
                                                            Gc[:, 8:12].unsqueeze(2).to_broadcast([128, 4, 129]), ALU.mult),
                           r=[('Pst', h) for h in range(4)] + [kG], w=['Cbf'])

        STEPS = {'main': 0, 'side': 0}

        def wrr(main, side, per):
            credit = 0.0
            for _ in main:
                STEPS['main'] += 1
                credit += per
                while side is not None and credit >= 1.0:
                    credit -= 1.0
                    STEPS['side'] += 1
                    try:
                        next(side)
                    except StopIteration:
                        side = None
            return side

        def dump(name, ap, keys):
            if name in dbg_d:
                S.dma('sp', dbg_d[name], ap, r=keys, w=['dbg_' + name])
                S.finish('sp', ['dbg_' + name])

        side = None
        side_rate = [1.0]
        main_per_tt = [60]
        if NTT > 0:
            run(stageA(0))
            run(rr(stageA(1) if NTT > 1 else None, front(0)))
        for n in range(NTT):
            own, i = tt_info(n)
            def _late(gen, rounds):
                for _ in range(rounds):
                    yield
                yield from gen
            nxt_front = rr(_late(stageA(n + 2), A_LAG) if n + 2 < NTT else None, front(n + 1) if n + 1 < NTT else None)
            if own and i % 2 == 1:
                if side is not None:
                    run(side)
                side = attention_group(i // 2)
                for _ in range(4):
                    next(side)
                g_ = i // 2
                side_total = (18 + 2 * g_) * 4 + 4 * 4 + 2
                side_rate[0] = 1.15 * side_total / (2.0 * max(main_per_tt[0], 1))
            m0 = STEPS['main']
            side = wrr(rr(nxt_front, back(n)), side, side_rate[0])
            if own:
                main_per_tt[0] = STEPS['main'] - m0
            if own and i == 3 and 'yT01' in dbg_d:
                if side is not None:
                    run(side)
                    side = None
                dump('yT01', yT[:, :, 0:256], [('yT', a, b) for a in range(2) for b in range(2)])
        if side is not None:
            run(side)

        if 'yT' in dbg_d:
            S.dma('sp', dbg_d['yT'], yT[:], r=[('yT', i, j) for i in range(16) for j in range(2)], w=['dbg_yT'])
            S.finish('sp', ['dbg_yT'])

        ab.close()
        if 'C' not in phases:
            S.barrier()
            return nc

        hres = sb("hres", [128, 16, 1024])
        wout = sb("wout", [128, 8, 1024], BF16)
        wp = sb("wp", [128, 2, 1024], BF16)
        cd = ExitStack()

        def sc(name, shape, dt=F32):
            return cd.enter_context(nc.sbuf_tensor("t_" + name, list(shape), dt))

        u2T = sc("u2T", [128, 8, 2048], BF16)
        u = sc("u_c", [128, 1024], BF16)
        ssb = sc("ssb_c", [128, 8])
        wupb = [sc("wup%d" % i, [128, 8, 512], BF16) for i in range(2)]
        wdnb = [sc("wdn%d" % i, [128, 4, 1024], BF16) for i in range(2)]
        hidT = [sc("hidT%d" % i, [128, 4, 512], BF16) for i in range(2)]
        relu_t = sc("relu_t", [128, 512])
        w_out_v = w_out.rearrange("(k p) n -> p k n", p=128)
        S.barrier()
        for k in range(8):
            S.dma('pool', wout[:, k, :], w_out_v[:, k, :], w=[('wout', k)])
        w_up_v = w_up.rearrange("(k p) n -> p k n", p=128)
        w_dn_v = w_down.rearrange("(k p) n -> p k n", p=128)

        def load_ffn_w(j):
            b = j % 2
            S.dma('pool', wupb[b][:], w_up_v[:, :, j * 512:(j + 1) * 512], w=[('wup', b)])
            S.dma('pool', wdnb[b][:], w_dn_v[:, j * 4:(j + 1) * 4, :], w=[('wdn', b)])

        u_c2 = sc("u_c2", [128, 1024], BF16)
        ssb_c2 = sc("ssb_c2", [128, 8])
        CSCR = [(u, 'u_c', ssb, 'ssb_c', 0), (u_c2, 'u_c2', ssb_c2, 'ssb_c2', 1)]

        def c_stream(i):
            S.dma('sp', hres[:, i, :], xo[i * 128:(i + 1) * 128, :], w=[('h', i)])
            for half in range(2):
                pbk = 2 + (i * 2 + half) % 6
                for k in range(8):
                    S.op('pe', lambda e: e.matmul(banks[pbk][:, :], yT[:, k, i * 128:(i + 1) * 128],
                                                  wout[:, k, half * 512:(half + 1) * 512], start=(k == 0), stop=(k == 7)),
                         r=[('yT', i, 0), ('yT', i, 1), ('wout', k)], w=[bk(pbk)], sig=k == 7)
                yield S.op('dve', lambda e: e.tensor_tensor(hres[:, i, half * 512:(half + 1) * 512], banks[pbk][:, :],
                                                            hres[:, i, half * 512:(half + 1) * 512], ALU.add),
                           r=[bk(pbk), ('h', i)], w=[('h', i)])
            if i == 0:
                load_ffn_w(0)
                load_ffn_w(1)
            sc_ = CSCR[i % 2]
            _norm_a1(hres[:, i, :], ('h', i), sc_[0], sc_[1], sc_[2], sc_[3])
            yield
            yield
            _norm_a2(C_G2, u2T[:, :, i * 128:(i + 1) * 128], ('u2T', i), sc_[0], sc_[1], sc_[4])
            yield

        def lagged0(gens, lag):
            live = []
            pending = list(gens)
            rnd = 0
            while live or pending:
                if pending and rnd % lag == 0:
                    live.append(pending.pop(0))
                for g in list(live):
                    try:
                        next(g)
                    except StopIteration:
                        live.remove(g)
                rnd += 1

        lagged0([c_stream(i) for i in range(NT)], 2)
        if 'h1' in dbg_d:
            S.dma('sp', dbg_d['h1'], hres[:], r=[('h', i) for i in range(16)], w=['dbg_h1'])
            S.finish('sp', ['dbg_h1'])

        nacc = 0
        for j in range(8):
            b = j % 2
            for tg in range(4):
                hb = (j * 4 + tg) % 2
                for m in range(4):
                    pbk = 2 + nacc % 6
                    nacc += 1
                    for k in range(8):
                        S.op('pe', lambda e: e.matmul(banks[pbk][:, :], wupb[b][:, k, m * 128:(m + 1) * 128],
                                                      u2T[:, k, tg * 512:(tg + 1) * 512], start=(k == 0), stop=(k == 7)),
                             r=[('wup', b)] + [('u2T', tg * 4 + q) for q in range(4)], w=[bk(pbk)], sig=k == 7)
                    S.op('act', lambda e: e.activation(relu_t[:], banks[pbk][:, :], AF.Relu), r=[bk(pbk)], w=['relu_t'])
                    S.op('act', lambda e: e.activation(hidT[hb][:, m, :], relu_t[:], AF.Square),
                         r=['relu_t'], w=[('hidT', hb)])
                for q in range(4):
                    i = tg * 4 + q
                    for half in range(2):
                        pbk = 2 + nacc % 6
                        nacc += 1
                        for m in range(4):
                            S.op('pe', lambda e: e.matmul(banks[pbk][:, :], hidT[hb][:, m, q * 128:(q + 1) * 128],
                                                          wdnb[b][:, m, half * 512:(half + 1) * 512], start=(m == 0), stop=(m == 3)),
                                 r=[('hidT', hb), ('wdn', b)], w=[bk(pbk)], sig=m == 3)
                        S.op('dve', lambda e: e.tensor_tensor(hres[:, i, half * 512:(half + 1) * 512], banks[pbk][:, :],
                                                              hres[:, i, half * 512:(half + 1) * 512], ALU.add),
                             r=[bk(pbk), ('h', i)], w=[('h', i)])
            if j + 2 < 8:
                load_ffn_w(j + 2)
            if j == 5:
                w_g_v = w_gate.rearrange("(k p) n -> p k n", p=128)
                w_p_v = w_ple.rearrange("(k p) n -> p k n", p=128)
                for k in range(8):
                    S.dma('pool', wout[:, k, :], w_g_v[:, k, :], w=[('wout', k)])
                S.dma('pool', wp[:], w_p_v, w=['wp'])
        if 'h2' in dbg_d:
            S.dma('sp', dbg_d['h2'], hres[:], r=[('h', i) for i in range(16)], w=['dbg_h2'])
            S.finish('sp', ['dbg_h2'])
        S.barrier()
        cd.close()

        wg = wout
        NS = 4
        u_e = [sb("u_e%d" % i, [128, 1024], BF16) for i in range(NS)]
        ssb_e = [sb("ssb_e%d" % i, [128, 8]) for i in range(NS)]
        u3T = [sb("u3T%d" % i, [128, 8, 128], BF16) for i in range(NS)]
        pt = [sb("pt%d" % i, [128, 256], BF16) for i in range(NS)]
        pT = [sb("pT%d" % i, [128, 2, 128], BF16) for i in range(NS)]
        gsb = [sb("gsb%d" % i, [128, 1024]) for i in range(NS)]

        def pe_stream(i):
            b = i % NS
            tb = b % 2
            S.dma('pool', pt[b][:], po[i * 128:(i + 1) * 128, :], w=[('pt', b)])
            norm_to_uT(hres[:, i, :], ('h', i), C_G3, u3T[b], ('u3T', b), scr=(u_e[b], ('u_e', b), ssb_e[b], ('ssb_e', b), tb))
            yield
            psb = banks[tb][:].bitcast(BF16)
            for k in range(2):
                S.op('pe', lambda e: e.transpose(psb[:, k * 128:(k + 1) * 128], pt[b][:, k * 128:(k + 1) * 128], identb),
                     r=[('pt', b), 'cstb'], w=[bk(tb)], sig=k == 1)
            yield S.op('dve', lambda e: e.tensor_copy(pT[b][:], psb[:, 0:256].rearrange("p (k t) -> p k t", k=2)),
                       r=[bk(tb)], w=[('pT', b)])
            for half in range(2):
                pg = 2 + b
                pe_ = 6 + (b % 2)
                for k in range(8):
                    S.op('pe', lambda e: e.matmul(banks[pg][:, :], u3T[b][:, k, :], wg[:, k, half * 512:(half + 1) * 512],
                                                  start=(k == 0), stop=(k == 7)), r=[('u3T', b), ('wout', k)], w=[bk(pg)], sig=k == 7)
                sl = slice(half * 512, (half + 1) * 512)
                yield S.op('act', lambda e: e.activation(gsb[b][:, sl], banks[pg][:, :], AF.Sigmoid), r=[bk(pg)], w=[('gsb', b, half)])
                for k in range(2):
                    S.op('pe', lambda e: e.matmul(banks[pe_][:, :], pT[b][:, k, :], wp[:, k, half * 512:(half + 1) * 512],
                                                  start=(k == 0), stop=(k == 1)), r=[('pT', b), 'wp'], w=[bk(pe_)], sig=k == 1)
                yield S.op('dve', lambda e: e.tensor_tensor(gsb[b][:, sl], banks[pe_][:, :], gsb[b][:, sl], ALU.mult),
                           r=[bk(pe_), ('gsb', b, half)], w=[('gsb', b, half)])
                yield S.op('dve', lambda e: e.tensor_tensor(gsb[b][:, sl], gsb[b][:, sl], hres[:, i, sl], ALU.add),
                           r=[('gsb', b, half), ('h', i)], w=[('gsb', b, half)])
            S.dma('sp', out_d[i * 128:(i + 1) * 128, :], gsb[b][:], r=[('gsb', b, 0), ('gsb', b, 1)],
                  w=[('out', i)])
            yield

        def lagged(gens, lag):
            live = []
            pending = list(gens)
            rnd = 0
            while live or pending:
                if pending and rnd % lag == 0:
                    live.append(pending.pop(0))
                for g in list(live):
                    try:
                        next(g)
                    except StopIteration:
                        live.remove(g)
                rnd += 1

        lagged([pe_stream(i) for i in range(NT)], 2)
        S.finish('sp', [('out', i) for i in range(NT)])
    return nc


def _consts(core, inp):
    c = np.zeros((128, C_TOT), np.float32)

    def pk(v):
        return np.asarray(v, np.float32).reshape(-1, 128).T

    c[:, C_G1:C_G1 + 8] = pk(inp['attn_norm_g'][0])
    c[:, C_G2:C_G2 + 8] = pk(inp['mlp_norm_g'][0])
    c[:, C_G3:C_G3 + 8] = pk(inp['ple_norm_g'][0])
    c[:, C_GY:C_GY + 4] = pk(inp['mlstm_norm_g'][0])
    c[:, C_GY + 4:C_GY + 8] = pk(inp['attn_sub_norm_g'][0])
    cw = np.asarray(inp['conv_w'][0], np.float32)
    c[:, C_CW:C_CW + 32] = cw.T.reshape(8, 128, 4).transpose(1, 0, 2).reshape(128, 32)
    c[:, C_CB:C_CB + 8] = pk(inp['conv_b'][0])
    c[:, C_QG:C_QG + 64] = np.asarray(inp['q_norm_g'][0], np.float32)[None, :]
    c[:, C_KG:C_KG + 64] = np.asarray(inp['k_norm_g'][0], np.float32)[None, :]
    for j, nm in enumerate(('lambda_q1', 'lambda_k1', 'lambda_q2', 'lambda_k2')):
        c[:, C_LAM + j * 64:C_LAM + (j + 1) * 64] = np.asarray(inp[nm][0], np.float32)[None, :]
    odd = core % 2
    c[:, C_FL + 0] = 1.0 if odd else 0.0
    c[:, C_FL + 1] = 0.0 if odd else NEG
    c[:, C_FL + 2] = 0.0 if odd else NEG
    inv = (np.float32(500000.0) ** (-np.arange(0, 16, 2, dtype=np.float32) / np.float32(16))).astype(np.float32)
    for off, base in ((C_CSO, odd * 2048), (C_CSP, 0)):
        pos = (base + np.arange(2048, dtype=np.float32)).astype(np.float32)
        ang = (pos[:, None] * inv[None, :]).astype(np.float32)
        cs = np.concatenate([np.cos(ang), np.sin(ang)], axis=1).astype(np.float32)
        c[:, off:off + 256] = cs.reshape(16, 128, 16).transpose(1, 0, 2).reshape(128, 256)
    c[:, C_ID:C_ID + 128] = np.eye(128, dtype=np.float32)
    c[:, C_TRI:C_TRI + 128] = np.triu(np.ones((128, 128), np.float32))
    c[0:4, C_GB] = np.asarray(inp['igate_b'][0], np.float32)
    c[0:4, C_GB + 1] = np.asarray(inp['fgate_b'][0], np.float32)
    return c


def _constb():
    b = np.zeros((128, 128 + 1024), np.float32)
    b[:, 0:128] = np.eye(128, dtype=np.float32)
    tri = np.triu(np.ones((128, 128), np.float32))
    for m in range(2):
        b[:, 128 + m * 256:128 + m * 256 + 128] = tri
        b[:, 128 + m * 256 + 128:128 + m * 256 + 256] = 1.0
        b[:, 640 + m * 256:640 + m * 256 + 128] = 0.0
        b[:, 640 + m * 256 + 128:640 + m * 256 + 256] = tri
    return b


def make_in_maps(inp):
    x = np.asarray(inp['x'], np.float32)
    p = np.asarray(inp['p'], np.float32)
    shared = {
        'w_in': np.ascontiguousarray(inp['w_in'][0], dtype=np.float32),
        'w_out': np.ascontiguousarray(inp['w_out'][0], dtype=np.float32),
        'w_up': np.ascontiguousarray(inp['w_up'][0], dtype=np.float32),
        'w_down': np.ascontiguousarray(inp['w_down'][0], dtype=np.float32),
        'w_gate': np.ascontiguousarray(inp['w_ple_gate'][0], dtype=np.float32),
        'w_ple': np.ascontiguousarray(inp['w_ple_proj'][0], dtype=np.float32),
        'cstb': _constb(),
    }
    zeros = np.zeros((2048, 1024), np.float32)
    maps = []
    for c in range(8):
        b, hf = c // 2, c % 2
        m = dict(shared)
        m['xo'] = np.ascontiguousarray(x[b, hf * 2048:(hf + 1) * 2048])
        m['xp'] = np.ascontiguousarray(x[b, 0:2048]) if hf else zeros
        m['po'] = np.ascontiguousarray(p[0, b, hf * 2048:(hf + 1) * 2048])
        m['cst'] = _consts(c, inp)
        maps.append(m)
    return maps


def kernel(**inputs):
    nc = build()
    maps = make_in_maps(inputs)
    res = run_bass_kernel_spmd(nc, maps, core_ids=list(range(8)))
    out = np.zeros((4, 4096, 1024), np.float32)
    for c in range(8):
        out[c // 2, (c % 2) * 2048:(c % 2 + 1) * 2048] = res.results[c]['out']
    return out
```

```python
import math
from contextlib import ExitStack

import numpy as np
import concourse.bass as bass
import concourse.mybir as mybir
from concourse.bass_utils import run_bass_kernel_spmd

F32 = mybir.dt.float32
BF16 = mybir.dt.bfloat16
AF = mybir.ActivationFunctionType
ALU = mybir.AluOpType
AX = mybir.AxisListType

SAME_ENGINE_SYNC = True
A_LAG = 1
A2_LAG = 6
PE_WARM_REPS = 1
EPS = 1e-6
NT = 16
NEG = -30000.0
LAM_INIT = 0.8 - 0.6 * math.exp(0.0)
LNC = math.log(128 ** -0.5)

O_MQ, O_MK, O_MV, O_MO, O_MI, O_MF, O_AQ, O_AK, O_AV = 0, 512, 1024, 1536, 2048, 2052, 2056, 2568, 3080

C_G1, C_G2, C_G3, C_GY, C_CW, C_CB, C_QG, C_KG, C_LAM, C_FL, C_CSO, C_CSP, C_ID, C_TRI, C_GB = (
    0, 8, 16, 24, 32, 64, 72, 136, 200, 456, 460, 716, 972, 1100, 1228)
C_TOT = 1230


class _Eng:
    def __init__(self, name, h, sem):
        self.name, self.h, self.sem = name, h, sem
        self.n = 0
        self.count = 0
        self.incs = []
        self.last = None
        self.last_seq = 0
        self.waited = {}
        self.dsems = []
        self.dcnt = []
        self.dnext = 0


class Sched:
    def __init__(self, nc, stack, ndma):
        self.nc = nc
        hs = {'pe': nc.tensor, 'act': nc.scalar, 'dve': nc.vector, 'pool': nc.gpsimd, 'sp': nc.sync}
        self.e = {}
        for k, h in hs.items():
            sem = stack.enter_context(nc.semaphore("s_" + k))
            self.e[k] = _Eng(k, h, sem)
            for i in range(ndma.get(k, 0)):
                self.e[k].dsems.append(stack.enter_context(nc.semaphore("d_%s%d" % (k, i))))
                self.e[k].dcnt.append(0)
        self.st = {}

    def _target(self, dep):
        if dep[0] == 'd':
            return dep[1], dep[2]
        p = self.e[dep[1]]
        seq = dep[2]
        found = None
        for (s, c) in reversed(p.incs):
            if s >= seq:
                found = c
            else:
                break
        if found is None:
            p.count += 1
            p.last.then_inc(p.sem, 1)
            p.incs.append((p.last_seq, p.count))
            found = p.count
        return p.sem, found

    def _wait(self, eng, deps):
        E = self.e[eng]
        for dep in deps:
            if dep is None:
                continue
            if dep[0] == 'c' and dep[1] == eng and (eng == 'pe' or not SAME_ENGINE_SYNC):
                continue
            sem, val = self._target(dep)
            key = id(sem)
            if E.waited.get(key, 0) >= val:
                continue
            E.h.wait_ge(sem, val)
            E.waited[key] = val

    def _deps(self, r, w, eng=None):
        deps = []
        for k in r:
            s = self.st.get(k)
            if s is not None:
                deps.append(s[0])
                if isinstance(k, tuple) and k[0] == 'bank':
                    deps.extend(d for d in s[1].values() if not (d[0] == 'c' and d[1] == eng))
        for k in w:
            s = self.st.get(k)
            if s is not None:
                deps.append(s[0])
                deps.extend(s[1].values())
        return deps

    def _record(self, dep, r, w):
        rk = (dep[0], dep[1] if dep[0] == 'c' else id(dep[1]))
        for k in r:
            s = self.st.setdefault(k, [None, {}])
            s[1][rk] = dep
        for k in w:
            self.st[k] = [dep, {}]

    def op(self, eng, fn, r=(), w=(), sig=None):
        E = self.e[eng]
        self._wait(eng, self._deps(r, w, eng))
        inst = fn(E.h)
        E.n += 1
        E.last = inst
        E.last_seq = E.n
        if sig or (sig is None and eng != 'pe'):
            E.count += 1
            inst.then_inc(E.sem, 1)
            E.incs.append((E.n, E.count))
        self._record(('c', eng, E.n), r, w)
        return inst

    def dma(self, q, out, in_, r=(), w=(), **kw):
        E = self.e[q]
        self._wait(q, self._deps(r, w))
        i = E.dnext
        E.dnext = (i + 1) % len(E.dsems)
        sem = E.dsems[i]
        if E.dcnt[i] > 0:
            key = id(sem)
            if E.waited.get(key, 0) < E.dcnt[i]:
                E.h.wait_ge(sem, E.dcnt[i])
                E.waited[key] = E.dcnt[i]
        E.dcnt[i] += 16
        E.h.dma_start(out=out, in_=in_, **kw).then_inc(sem, 16)
        dep = ('d', sem, E.dcnt[i])
        self._record(dep, r, w)
        return dep

    def barrier(self):
        deps = []
        for s in self.st.values():
            if s[0] is not None:
                deps.append(s[0])
            deps.extend(s[1].values())
        for eng in self.e:
            self._wait(eng, deps)

    def finish(self, eng, keys):
        self._wait(eng, [self.st[k][0] for k in keys if k in self.st])


def build(dbg=(), n_pre=NT, n_own=NT, phases='CDE'):
    nc = bass.Bass("TRN2", target_bir_lowering=False)

    def din(name, shape):
        return nc.dram_tensor(name, list(shape), F32, kind="ExternalInput").ap()

    xo = din("xo", [2048, 1024])
    xp = din("xp", [2048, 1024])
    po = din("po", [2048, 256])
    w_in = din("w_in", [1024, 3592])
    w_out = din("w_out", [1024, 1024])
    w_up = din("w_up", [1024, 4096])
    w_down = din("w_down", [4096, 1024])
    w_gate = din("w_gate", [1024, 1024])
    w_ple = din("w_ple", [256, 1024])
    cst_d = din("cst", [128, C_TOT])
    cstb_d = din("cstb", [128, 128 + 1024])
    out_d = nc.dram_tensor("out", [2048, 1024], F32, kind="ExternalOutput").ap()
    dbg_d = {}
    for name, shape, dt in dbg:
        dbg_d[name] = nc.dram_tensor("dbg_" + name, list(shape), dt, kind="ExternalOutput").ap()

    with ExitStack() as st:
        S = Sched(nc, st, {'sp': 12, 'pool': 8})

        def sb(name, shape, dt=F32):
            return st.enter_context(nc.sbuf_tensor("t_" + name, list(shape), dt))

        def freed(name, shape, dt=F32):
            return nc.sbuf_tensor(name, list(shape), dt)

        cst = sb("cst", [128, C_TOT])
        cstb = sb("cstb", [128, 128 + 1024], BF16)
        identb = cstb[:, 0:128]
        identf = cst[:, C_ID:C_ID + 128]
        tri = cst[:, C_TRI:C_TRI + 128]
        banks = [st.enter_context(nc.psum_tensor("bank%d" % i, [128, 512], F32)) for i in range(8)]
        small = sb("small", [128, 64])
        block = st.enter_context(nc.Block())

        S.dma('sp', cst[:], cst_d, w=['cst'])
        S.dma('pool', cstb[:], cstb_d, w=['cstb'])

        def bk(i):
            return ('bank', i)

        def rstd_from_ss(ss_ap, out_ap, n, key_ss, key_out, mul=None):
            S.op('act', lambda e: e.activation(out_ap, ss_ap, AF.Ln, bias=epsc[:ss_ap.shape[0], 0:1], scale=1.0 / n),
                 r=[key_ss, 'consts'], w=[key_out])
            S.op('act', lambda e: e.activation(out_ap, out_ap, AF.Exp, scale=-0.5), r=[key_out], w=[key_out])
            if mul is not None:
                S.op('dve', lambda e: e.tensor_scalar(out_ap, out_ap, float(mul), None, ALU.mult), r=[key_out], w=[key_out])

        epsc = sb("epsc", [128, 4])
        ones4 = sb("ones4", [4, 128])
        S.op('pool', lambda e: e.memset(epsc[:, 0:1], EPS), w=['consts'])
        S.op('pool', lambda e: e.memset(epsc[:, 1:2], 1.0), w=['consts'])
        S.op('pool', lambda e: e.memset(ones4[:], 1.0), w=['consts4'])

        lamv = cst[:, C_LAM:C_LAM + 256]
        lj = sb("lj", [128, 64])
        S.op('dve', lambda e: e.scalar_tensor_tensor(out=lj[:], in0=lamv[:, 0:64], scalar=1.0, in1=lamv[:, 64:128],
                                                     op0=ALU.mult, op1=ALU.mult, accum_out=small[:, 0:1]),
             r=['cst'], w=['lj', 'small'])
        S.op('dve', lambda e: e.scalar_tensor_tensor(out=lj[:], in0=lamv[:, 128:192], scalar=1.0, in1=lamv[:, 192:256],
                                                     op0=ALU.mult, op1=ALU.mult, accum_out=small[:, 1:2]),
             r=['cst', 'lj'], w=['lj', 'small'])
        S.op('act', lambda e: e.activation(small[:, 2:4], small[:, 0:2], AF.Exp), r=['small'], w=['small'])
        S.op('dve', lambda e: e.tensor_tensor(small[:, 4:5], small[:, 2:3], small[:, 3:4], ALU.subtract), r=['small'], w=['small'])
        S.op('dve', lambda e: e.tensor_scalar(small[:, 5:6], small[:, 4:5], float(LAM_INIT), None, ALU.add), r=['small'], w=['small'])
        lam_ap = small[:, 5:6]
        gpar = sb("gpar", [4, 8])
        gb = cst[0:4, C_GB:C_GB + 2]
        fl = cst[0:4, C_FL:C_FL + 4]
        S.op('dve', lambda e: e.tensor_copy(gpar[:, 0:1], fl[:, 0:1]), r=['cst'], w=['gpar'])
        S.op('dve', lambda e: e.scalar_tensor_tensor(out=gpar[:, 1:2], in0=gb[:, 0:1], scalar=fl[:, 0:1], in1=fl[:, 1:2],
                                                     op0=ALU.mult, op1=ALU.add), r=['cst', 'gpar'], w=['gpar'])
        S.op('pool', lambda e: e.memset(gpar[:, 2:3], 1.0), r=['gpar'], w=['gpar'])
        S.op('dve', lambda e: e.tensor_copy(gpar[:, 3:4], gb[:, 0:1]), r=['cst', 'gpar'], w=['gpar'])
        S.op('dve', lambda e: e.tensor_scalar(gpar[:, 4:5], gb[:, 1:2], -1.0, None, ALU.mult), r=['cst', 'gpar'], w=['gpar'])
        S.op('dve', lambda e: e.tensor_copy(gpar[:, 5:6], fl[:, 0:1]), r=['cst', 'gpar'], w=['gpar'])
        pbias = cst[:, C_FL + 2:C_FL + 3]

        yT = sb("yT", [128, 8, 2048], BF16)
        ab = ExitStack()

        def sa(name, shape, dt=F32):
            return ab.enter_context(nc.sbuf_tensor("t_" + name, list(shape), dt))

        win = sa("win", [128, 8, 3592], BF16)
        kT = sa("kT", [128, 4, 4096], BF16)
        vx = sa("vx", [128, 32, 4, 130], BF16)
        xt = sa("xt", [128, 1024])
        u = sa("u", [128, 1024], BF16)
        uT = [sa("uT%d" % i, [128, 8, 128], BF16) for i in range(2)]
        ssb = sa("ssb", [128, 8])
        raw = sa("raw", [128, 8, 131])
        acc = sa("acc", [128, 4, 128])
        qkT2 = [sa("qkT%d" % i, [128, 8, 128], BF16) for i in range(2)]
        gf = sa("gf", [4, 8, 128])
        carry = sa("carry", [4, 8])
        Gt = [small[:, 16 + 12 * i:28 + 12 * i] for i in range(3)]
        mvx = [sa("mvx%d" % i, [128, 4, 130], BF16) for i in range(2)]
        sigo2 = [sa("sigo%d" % i, [128, 512], BF16) for i in range(2)]
        mls = sa("mls", [128, 512], BF16)
        ktl = [mls[:, i * 128:(i + 1) * 128] for i in range(2)]
        stm = [mls[:, 256 + i * 128:256 + (i + 1) * 128] for i in range(2)]
        Pst = sa("Pst", [128, 4, 130])
        Cbf = sa("Cbf", [128, 4, 130], BF16)
        ml = sa("ml", [128, 32])
        qf = sa("qf", [128, 512])
        qb16 = sa("qb16", [128, 512], BF16)
        qtok = [sa("qtok%d" % i, [128, 512], BF16) for i in range(2)]
        qTg = [sa("qTg%d" % i, [128, 4, 256], BF16) for i in range(2)]
        Et = [sa("Et%d" % i, [128, 512], BF16) for i in range(2)]
        ofin = [sa("ofin%d" % i, [128, 128]) for i in range(2)]
        yat = [sa("yat%d" % i, [128, 512], BF16) for i in range(2)]
        al = sa("al", [128, 16])
        ae = small[:, 8:16]

        w_in_v = w_in.rearrange("(k p) n -> p k n", p=128)
        WGRP = [(O_MK, O_MV), (O_MV, O_MO), (O_MI, O_AQ), (O_AK, O_AV), (O_AV, 3592), (O_MQ, O_MK), (O_MO, O_MI), (O_AQ, O_AK)]
        for gi_, (c0_, c1_) in enumerate(WGRP):
            S.dma('pool', win[:, :, c0_:c1_], w_in_v[:, :, c0_:c1_], w=[('win', gi_)])

        def wkey(col):
            for gi_, (c0_, c1_) in enumerate(WGRP):
                if c0_ <= col < c1_:
                    return ('win', gi_)
            raise ValueError(col)

        S.op('pool', lambda e: e.memset(vx[:, :, :, 128:130], 1.0), w=[('vx', j) for j in range(32)])
        S.op('pool', lambda e: e.tensor_scalar(vx[:, 0:16, :, 128:130], vx[:, 0:16, :, 128:130], cst[:, C_FL:C_FL + 1], 1.0,
                                               ALU.mult, ALU.mult),
             r=['cst'] + [('vx', j) for j in range(16)], w=[('vx', j) for j in range(16)])
        for i in range(2):
            S.op('pool', lambda e: e.memset(mvx[i][:, :, 128:130], 1.0), w=[('mvx', i)])
        for i in range(3):
            S.op('pool', lambda e: e.memset(Gt[i], 1.0), w=[('Gt', i)])
        S.op('pool', lambda e: e.memset(qTg[0][:], 0.0), w=[('qTg', 0, 0), ('qTg', 0, 1)])
        S.op('pool', lambda e: e.memset(qTg[1][:], 0.0), w=[('qTg', 1, 0), ('qTg', 1, 1)])
        S.op('pool', lambda e: e.memset(Pst[:], 0.0), w=[('Pst', h) for h in range(4)])
        S.op('pool', lambda e: e.memset(Cbf[:], 0.0), w=['Cbf'])
        S.op('pool', lambda e: e.memset(raw[:], 0.0), w=[('raw', c) for c in range(8)])
        S.op('pool', lambda e: e.memset(carry[:], 0.0), w=[('carry', j) for j in range(8)])

        cw = cst[:, C_CW:C_CW + 32].rearrange("p (c j) -> p c j", j=4)
        cb = cst[:, C_CB:C_CB + 8]
        cnt = {'tt': 0}

        def norm_to_uT(x_tile, kx, gcol, uT_t, kuT, scr=None):
            if scr is None:
                u_, ku, ssb_, kss, tb = u, 'u', ssb, 'ssb', 0
            else:
                u_, ku, ssb_, kss, tb = scr
            return _norm_to_uT(x_tile, kx, gcol, uT_t, kuT, u_, ku, ssb_, kss, tb)

        def _norm_to_uT(x_tile, kx, gcol, uT_t, kuT, u, ku, ssb, kss, tb):
            _norm_a1(x_tile, kx, u, ku, ssb, kss)
            _norm_a2(gcol, uT_t, kuT, u, ku, tb)

        def _norm_a1(x_tile, kx, u, ku, ssb, kss):
            S.op('act', lambda e: e.activation(u[:], x_tile, AF.Square, accum_out=ssb[:, 0:1]), r=[kx], w=[ku, kss])
            rstd_from_ss(ssb[:, 0:1], ssb[:, 1:2], 1024.0, kss, kss)
            S.op('dve', lambda e: e.tensor_scalar(u[:], x_tile, ssb[:, 1:2], None, ALU.mult), r=[kx, kss], w=[ku])

        def _norm_a2(gcol, uT_t, kuT, u, ku, tb):
            psb = banks[tb][:].bitcast(BF16)
            for k in range(8):
                S.op('pe', lambda e: e.transpose(psb[:, k * 128:(k + 1) * 128], u[:, k * 128:(k + 1) * 128], identb),
                     r=[ku, 'cstb'], w=[bk(tb)], sig=k == 7)
            g_bc = cst[:, gcol:gcol + 8].unsqueeze(2).to_broadcast([128, 8, 128])
            S.op('dve', lambda e: e.tensor_tensor(uT_t[:], psb[:, 0:1024].rearrange("p (k t) -> p k t", k=8), g_bc, ALU.mult),
                 r=[bk(tb), 'cst'], w=[kuT])

        def transpose_to(src_tok, ksrc, dst_ap, kdst, scale_cols=None, bank=0):
            psb = banks[bank][:].bitcast(BF16)
            for k in range(4):
                S.op('pe', lambda e: e.transpose(psb[:, k * 128:(k + 1) * 128], src_tok[:, k * 128:(k + 1) * 128], identb),
                     r=[ksrc, 'cstb'], w=[bk(bank)], sig=k == 3)
            src = psb[:, 0:512].rearrange("p (k t) -> p k t", k=4)
            if scale_cols is None:
                S.op('dve', lambda e: e.tensor_copy(dst_ap, src), r=[bk(bank)], w=[kdst])
            else:
                g_bc = scale_cols.unsqueeze(2).to_broadcast([128, 4, 128])
                S.op('dve', lambda e: e.tensor_tensor(dst_ap, src, g_bc, ALU.mult), r=[bk(bank), 'cst'], w=[kdst])

        def proj_tm(uT_t, kuT, col, bank):
            for k in range(8):
                S.op('pe', lambda e: e.matmul(banks[bank][:, :], uT_t[:, k, :], win[:, k, col:col + 512],
                                              start=(k == 0), stop=(k == 7)),
                     r=[kuT, wkey(col)], w=[bk(bank)], sig=k == 7)

        def rr(*gens):
            gens = [g for g in gens if g is not None]
            while gens:
                for g in list(gens):
                    try:
                        next(g)
                    except StopIteration:
                        gens.remove(g)
                yield

        def run(gen):
            for _ in gen:
                pass


        def qk_prep(bank, gcol, cs, qb16, k16):
            q3 = qf[:].rearrange("p (g d) -> p g d", d=64)
            S.op('act', lambda e: e.activation(qf[:], banks[bank][:, :], AF.Square), r=[bk(bank)], w=['qf'])
            S.op('dve', lambda e: e.tensor_reduce(out=al[:, 0:8], in_=q3, axis=AX.X, op=ALU.add), r=['qf'], w=['al'])
            S.op('act', lambda e: e.activation(al[:, 8:16], al[:, 0:8], AF.Ln, bias=epsc[:, 0:1], scale=1.0 / 64),
                 r=['al', 'consts'], w=['al'])
            S.op('act', lambda e: e.activation(al[:, 8:16], al[:, 8:16], AF.Exp, scale=-0.5), r=['al'], w=['al'])
            S.op('dve', lambda e: e.tensor_tensor(q3, banks[bank][:, :].rearrange("p (g d) -> p g d", d=64),
                                                  al[:, 8:16].unsqueeze(2).to_broadcast([128, 8, 64]), ALU.mult),
                 r=[bk(bank), 'al'], w=['qf'])
            gg = cst[:, gcol:gcol + 64].unsqueeze(1).to_broadcast([128, 8, 64])
            o3 = qb16[:].rearrange("p (g d) -> p g d", d=64)
            yield S.op('dve', lambda e: e.tensor_tensor(o3, q3, gg, ALU.mult), r=['qf', 'cst'], w=[k16])
            gg16 = cst[:, gcol:gcol + 16].unsqueeze(1).to_broadcast([128, 8, 16])
            yield S.op('dve', lambda e: e.tensor_tensor(q3[:, :, 0:16], q3[:, :, 0:16], gg16, ALU.mult), r=['qf', 'cst'], w=['qf'])
            x1, x2 = q3[:, :, 0:8], q3[:, :, 8:16]
            cc = cs[:, 0:8].unsqueeze(1).to_broadcast([128, 8, 8])
            sn = cs[:, 8:16].unsqueeze(1).to_broadcast([128, 8, 8])
            r4 = [q3[:, :, 16 + 8 * a_:24 + 8 * a_] for a_ in range(4)]
            S.op('dve', lambda e: e.tensor_tensor(r4[0], x1, cc, ALU.mult), r=['qf', 'cst'], w=['qf'])
            S.op('dve', lambda e: e.tensor_tensor(r4[1], x2, sn, ALU.mult), r=['qf', 'cst'], w=['qf'])
            S.op('dve', lambda e: e.tensor_tensor(r4[2], x2, cc, ALU.mult), r=['qf', 'cst'], w=['qf'])
            yield S.op('dve', lambda e: e.tensor_tensor(r4[3], x1, sn, ALU.mult), r=['qf', 'cst'], w=['qf'])
            S.op('dve', lambda e: e.tensor_tensor(o3[:, :, 0:8], r4[0], r4[1], ALU.subtract), r=['qf', k16], w=[k16])
            yield S.op('dve', lambda e: e.tensor_tensor(o3[:, :, 8:16], r4[2], r4[3], ALU.add), r=['qf', k16], w=[k16])

        def attention_group(g):
            kbs = list(range(16)) + [16 + j for j in range(2 * g + 2)]
            units = [(h, idx, kb) for h in range(4) for idx, kb in enumerate(kbs)]
            PSB = (4, 5)
            for tt in range(2):
                psb = banks[4 + tt][:].bitcast(BF16)
                for k in range(4):
                    S.op('pe', lambda e: e.transpose(psb[:, k * 128:(k + 1) * 128], qtok[tt][:, k * 128:(k + 1) * 128], identb),
                         r=[('qtok', tt), 'cstb'], w=[bk(4 + tt)], sig=(k == 3))
                yield
                c_ = tt * 128
                S.op('act', lambda e: e.activation(qTg[0][0:64, :, c_:c_ + 128], psb[0:64, 0:512].rearrange("p (k t) -> p k t", k=4),
                                                   AF.Copy), r=[bk(4 + tt)], w=[('qTg', 0, tt)])
                yield S.op('act', lambda e: e.activation(qTg[1][64:128, :, c_:c_ + 128],
                                                         psb[64:128, 0:512].rearrange("p (k t) -> p k t", k=4), AF.Copy),
                           r=[bk(4 + tt)], w=[('qTg', 1, tt)])
            QK = [('qTg', m, tt) for m in range(2) for tt in range(2)]

            def st_mm(n):
                h, idx, kb = units[n]
                pb = PSB[n % 2]
                for rep_ in range(PE_WARM_REPS):
                    S.op('pe', lambda e: e.matmul(banks[pb][:, 0:256], kT[:, h, kb * 128:(kb + 1) * 128],
                                                  qTg[0][:, h, :], start=True, stop=True),
                         r=[('kT', kb)] + QK, w=[bk(pb)], sig=False)
                    S.op('pe', lambda e: e.matmul(banks[pb][:, 256:512], kT[:, h, kb * 128:(kb + 1) * 128],
                                                  qTg[1][:, h, :], start=True, stop=True),
                         r=[('kT', kb)] + QK, w=[bk(pb)], sig=(rep_ == PE_WARM_REPS - 1))

            st_mm(0)
            for n, (h, idx, kb) in enumerate(units):
                if n + 1 < len(units):
                    st_mm(n + 1)
                pb = PSB[n % 2]
                E = Et[n % 2]
                kE = ('Et', n % 2)
                S.op('act', lambda e: e.activation(E[:], banks[pb][:, :], AF.Exp, scale=0.125), r=[bk(pb)], w=[kE])
                own = kb - 16
                if own == 2 * g:
                    S.op('dve', lambda e: e.tensor_tensor(E[:], E[:], cstb[:, 128:640], ALU.mult), r=[kE, 'cstb'], w=[kE])
                elif own == 2 * g + 1:
                    S.op('dve', lambda e: e.tensor_tensor(E[:], E[:], cstb[:, 640:1152], ALU.mult), r=[kE, 'cstb'], w=[kE])
                for m in range(2):
                    for qb in range(2):
                        if qb == 0 and own == 2 * g + 1:
                            continue
                        last = (own == 2 * g) if qb == 0 else (own == 2 * g + 1)
                        a = 6 + m
                        o0 = qb * 130
                        S.op('pe', lambda e: e.matmul(banks[a][:, o0:o0 + 129], E[:, m * 256 + qb * 128: m * 256 + qb * 128 + 128],
                                                      vx[:, kb, h, 0:129], start=(idx == 0 and qb == 0), stop=last,
                                                      skip_group_check=True),
                             r=[kE, ('vx', kb)], w=[bk(a)], sig=(m == 1 and qb == 1))
                yield
                if idx == len(kbs) - 1:
                    den6 = banks[6][:, 0:260].rearrange("p (a c) -> p a c", c=130)[:, :, 128]
                    den7 = banks[7][:, 0:260].rearrange("p (a c) -> p a c", c=130)[:, :, 128]
                    S.op('dve', lambda e: e.reciprocal(ae[:, 2:4], den7), r=[bk(7)], w=[('ae', 1)])
                    S.op('dve', lambda e: e.tensor_scalar(ae[:, 2:4], ae[:, 2:4], lam_ap, None, ALU.mult), r=[('ae', 1), 'small'], w=[('ae', 1)])
                    S.op('dve', lambda e: e.reciprocal(ae[:, 0:2], den6), r=[bk(6)], w=[('ae', 0)])
                    for qb in range(2):
                        o0 = qb * 130
                        S.op('act', lambda e: e.activation(ofin[qb][:], banks[7][:, o0:o0 + 128], AF.Copy, scale=ae[:, 2 + qb:3 + qb]),
                             r=[bk(7), ('ae', 1)], w=[('ofin', qb)])
                        S.op('dve', lambda e: e.scalar_tensor_tensor(out=ofin[qb][:], in0=banks[6][:, o0:o0 + 128], scalar=ae[:, qb:qb + 1],
                                                                     in1=ofin[qb][:], op0=ALU.mult, op1=ALU.subtract),
                             r=[bk(6), ('ae', 0), ('ofin', qb)], w=[('ofin', qb)])
                    yield
                    for qb in range(2):
                        S.op('act', lambda e: e.activation(yat[qb][:, h * 128:(h + 1) * 128], ofin[qb][:], AF.Square,
                                                           accum_out=ae[:, 4 + qb:5 + qb]),
                             r=[('ofin', qb)], w=[('yat', qb), ('ae', 2 + qb)])
                    S.op('act', lambda e: e.activation(ae[:, 6:8], ae[:, 4:6], AF.Ln, bias=epsc[:, 0:1], scale=1.0 / 128),
                         r=[('ae', 2), ('ae', 3), 'consts'], w=[('ae', 4)])
                    S.op('act', lambda e: e.activation(ae[:, 6:8], ae[:, 6:8], AF.Exp, scale=-0.5), r=[('ae', 4)], w=[('ae', 4)])
                    for qb in range(2):
                        S.op('dve', lambda e: e.tensor_scalar(yat[qb][:, h * 128:(h + 1) * 128], ofin[qb][:], ae[:, 6 + qb:7 + qb],
                                                               float(1.0 - LAM_INIT), ALU.mult, ALU.mult),
                             r=[('ofin', qb), ('ae', 4)], w=[('yat', qb)])
                    yield
            for qb in range(2):
                t0 = (2 * g + qb) * 128
                transpose_to(yat[qb], ('yat', qb), yT[:, 4:8, t0:t0 + 128], ('yT', 2 * g + qb, 1),
                             scale_cols=cst[:, C_GY + 4:C_GY + 8], bank=4 + qb)
                yield

        NTT = n_pre + n_own
        def tt_info(n):
            own = n >= n_pre
            i = n - n_pre if own else n
            return own, i

        def stageA(n):
            own, i = tt_info(n)
            xsrc = xo if own else xp
            S.dma('sp', xt[:], xsrc[i * 128:(i + 1) * 128, :], w=['xt'])
            _norm_a1(xt[:], 'xt', u, 'u', ssb, 'ssb')
            for _ in range(A2_LAG):
                yield
            _norm_a2(C_G1, uT[n % 2], ('uT', n % 2), u, 'u', 0)
            yield

        def front(n):
            own, i = tt_info(n)
            blk = (16 + i) if own else i
            uT_t, kuT = uT[n % 2], ('uT', n % 2)
            Gc, kG = Gt[n % 3], ('Gt', n % 3)
            mv_t, kmv = mvx[n % 2], ('mvx', n % 2)
            qkT, kq = qkT2[n % 2], n % 2
            sigo = sigo2[n % 2]
            if own:
                GB, FMB, TMB = 0, (1, 1), (1, 1)
            else:
                GB, FMB, TMB = 1, (2, 3), (6, 7)
            need_q = own or i == NT - 1

            def T2():
                for half in ((0, 1) if need_q else (1,)):
                    pbk = FMB[half]
                    c0 = half * 4
                    for c in range(c0, c0 + 4):
                        col = (O_MQ + c * 128) if c < 4 else (O_MK + (c - 4) * 128)
                        for k in range(8):
                            S.op('pe', lambda e: e.matmul(banks[pbk][:, (c % 4) * 128:(c % 4 + 1) * 128], win[:, k, col:col + 128],
                                                          uT_t[:, k, :], start=(k == 0), stop=(k == 7)),
                                 r=[kuT, wkey(col)], w=[bk(pbk)], sig=(k == 7))
                    RK = [('raw', c) for c in range(c0, c0 + 4)]
                    if own:
                        S.op('dve', lambda e: e.tensor_copy(raw[:, c0:c0 + 4, 3:131], banks[pbk][:, :].rearrange("p (c t) -> p c t", c=4)),
                             r=[bk(pbk)], w=RK)
                    else:
                        S.op('act', lambda e: e.activation(raw[:, c0:c0 + 4, 3:131], banks[pbk][:, :].rearrange("p (c t) -> p c t", c=4),
                                                           AF.Copy), r=[bk(pbk)], w=RK)
                    conv = own or half == 1
                    if conv:
                        for c in range(c0, c0 + 4):
                            if own:
                                S.op('dve', lambda e: e.tensor_scalar(acc[:, c % 4, :], banks[pbk][:, (c % 4) * 128:(c % 4 + 1) * 128],
                                                                      cw[:, c, 3:4], cb[:, c:c + 1], ALU.mult, ALU.add),
                                     r=[bk(pbk), 'cst'], w=[('acc', c % 4)])
                            else:
                                S.op('act', lambda e: e.activation(acc[:, c % 4, :], banks[pbk][:, (c % 4) * 128:(c % 4 + 1) * 128],
                                                                   AF.Identity, bias=cb[:, c:c + 1], scale=cw[:, c, 3:4]),
                                     r=[bk(pbk), 'cst'], w=[('acc', c % 4)])
                    yield
                    if conv:
                        for c in range(c0, c0 + 4):
                            for j in range(3):
                                yield S.op('dve', lambda e: e.scalar_tensor_tensor(out=acc[:, c % 4, :], in0=raw[:, c, j:j + 128],
                                                                                   scalar=cw[:, c, j:j + 1], in1=acc[:, c % 4, :],
                                                                                   op0=ALU.mult, op1=ALU.add),
                                           r=[('raw', c), 'cst', ('acc', c % 4)], w=[('acc', c % 4)])
                    if own:
                        yield S.op('dve', lambda e: e.tensor_copy(raw[:, c0:c0 + 4, 0:3], raw[:, c0:c0 + 4, 128:131]), r=RK, w=RK)
                    else:
                        yield S.op('act', lambda e: e.activation(raw[:, c0:c0 + 4, 0:3], raw[:, c0:c0 + 4, 128:131], AF.Copy), r=RK, w=RK)
                    if conv:
                        tmp = raw[:, c0:c0 + 4, 3:131]
                        AK = [('acc', c) for c in range(4)]
                        S.op('act', lambda e: e.activation(tmp, acc[:, :, :], AF.Exp, scale=-1.0), r=AK + RK, w=RK)
                        S.op('act', lambda e: e.activation(tmp, tmp, AF.Ln, bias=epsc[:, 1:2]), r=RK + ['consts'], w=RK)
                        yield S.op('act', lambda e: e.activation(tmp, tmp, AF.Exp, scale=-1.0), r=RK, w=RK)
                        yield S.op('dve', lambda e: e.tensor_tensor(qkT[:, c0:c0 + 4, :], acc[:, :, :], tmp, ALU.mult),
                                   r=AK + RK, w=[('qkT', kq, half)])

            def T3():
                for gi, col in enumerate((O_MI, O_MF)):
                    for k in range(8):
                        S.op('pe', lambda e: e.matmul(banks[GB][0:4, gi * 128:(gi + 1) * 128], win[:, k, col:col + 4],
                                                      uT_t[:, k, :], start=(k == 0), stop=(k == 7)),
                             r=[kuT, wkey(col)], w=[bk(GB)], sig=(k == 7))
                po_ = 2 if own else 0
                G = lambda j: ('gf', j)
                C = lambda j: ('carry', j)
                S.op('act', lambda e: e.activation(gf[:, 0, :], banks[GB][0:4, 0:128], AF.Identity,
                                                   bias=gpar[:, po_ + 1:po_ + 2], scale=gpar[:, po_:po_ + 1]),
                     r=[bk(GB), 'gpar'], w=[G(0)])
                yield S.op('act', lambda e: e.activation(gf[:, 1, :], banks[GB][0:4, 128:256], AF.Exp, bias=gpar[:, 4:5], scale=-1.0),
                           r=[bk(GB), 'gpar'], w=[G(1)])
                yield S.op('act', lambda e: e.activation(gf[:, 1, :], gf[:, 1, :], AF.Ln, bias=epsc[0:4, 1:2]),
                           r=[G(1), 'consts'], w=[G(1)])
                if not own:
                    yield S.op('dve', lambda e: e.tensor_scalar(gf[:, 1, :], gf[:, 1, :], gpar[:, 5:6], None, ALU.mult),
                               r=[G(1), 'gpar'], w=[G(1)])
                yield S.op('dve', lambda e: e.tensor_tensor_scan(gf[:, 2, :], ones4[:], gf[:, 1, :], carry[:, 0:1], ALU.mult, ALU.add),
                           r=[G(1), 'consts4', C(0)], w=[G(2)])
                yield S.op('dve', lambda e: e.tensor_tensor(gf[:, 3, :], gf[:, 0, :], gf[:, 2, :], ALU.add), r=[G(0), G(2)], w=[G(3)])
                yield S.op('dve', lambda e: e.tensor_tensor_scan(gf[:, 4, :], ones4[:], gf[:, 3, :], carry[:, 1:2], ALU.mult, ALU.max),
                           r=[G(3), 'consts4', C(1)], w=[G(4)])
                S.op('dve', lambda e: e.tensor_scalar(carry[:, 2:3], carry[:, 1:2], -1.0, float(LNC), ALU.mult, ALU.add),
                     r=[C(1)], w=[C(2)])
                S.op('dve', lambda e: e.tensor_scalar(carry[:, 3:4], carry[:, 1:2], -1.0, None, ALU.mult), r=[C(1)], w=[C(3)])
                yield S.op('dve', lambda e: e.tensor_tensor(carry[:, 4:5], carry[:, 1:2], gf[:, 4, 127:128], ALU.subtract),
                           r=[C(1), G(4)], w=[C(4)])
                yield S.op('act', lambda e: e.activation(gf[:, 5, :], gf[:, 3, :], AF.Exp, bias=carry[:, 2:3]), r=[G(3), C(2)], w=[G(5)])
                yield S.op('act', lambda e: e.activation(gf[:, 6, :], gf[:, 2, :], AF.Exp, bias=carry[:, 3:4]), r=[G(2), C(3)], w=[G(6)])
                yield S.op('act', lambda e: e.activation(gf[:, 7, :], ones4[:], AF.Exp, bias=carry[:, 4:5], scale=0.0),
                           r=[C(4), 'consts4'], w=[G(7)])
                S.op('dve', lambda e: e.tensor_copy(carry[:, 0:1], gf[:, 2, 127:128]), r=[G(2), C(0)], w=[C(0)])
                yield S.op('dve', lambda e: e.tensor_copy(carry[:, 1:2], gf[:, 4, 127:128]), r=[G(4), C(1)], w=[C(1)])
                for a in range(3):
                    S.op('pe', lambda e: e.transpose(banks[GB][:, 256 + a * 4:256 + a * 4 + 4], gf[:, 5 + a, :], identf[0:4, 0:4]),
                         r=[G(5 + a), 'cst'], w=[bk(GB)], sig=(a == 2))
                yield S.op('dve', lambda e: e.tensor_copy(Gc, banks[GB][:, 256:268]), r=[bk(GB)], w=[kG])

            def T4():
                tb = [0]

                def nxt():
                    tb[0] += 1
                    return TMB[tb[0] % 2]
                b_ = nxt()
                proj_tm(uT_t, kuT, O_MV, b_)
                if own:
                    yield S.op('dve', lambda e: e.tensor_copy(mv_t[:, :, 0:128], banks[b_][:, :].rearrange("p (h d) -> p h d", h=4)),
                               r=[bk(b_)], w=[kmv])
                else:
                    yield S.op('act', lambda e: e.activation(mv_t[:, :, 0:128], banks[b_][:, :].rearrange("p (h d) -> p h d", h=4), AF.Copy),
                               r=[bk(b_)], w=[kmv])
                if own:
                    b_ = nxt()
                    proj_tm(uT_t, kuT, O_MO, b_)
                    S.op('act', lambda e: e.activation(qf[:], banks[b_][:, :], AF.Exp, scale=-1.0), r=[bk(b_)], w=['qf'])
                    S.op('act', lambda e: e.activation(qf[:], qf[:], AF.Ln, bias=epsc[:, 1:2]), r=['qf', 'consts'], w=['qf'])
                    yield S.op('act', lambda e: e.activation(sigo[:], qf[:], AF.Exp, scale=-1.0), r=['qf'], w=[('sigo', n % 2)])
                cs_k = cst[:, (C_CSO if own else C_CSP) + i * 16:(C_CSO if own else C_CSP) + i * 16 + 16]
                b_ = nxt()
                proj_tm(uT_t, kuT, O_AK, b_)
                yield from qk_prep(b_, C_KG, cs_k, qb16, 'qb16')
                transpose_to(qb16, 'qb16', kT[:, :, blk * 128:(blk + 1) * 128], ('kT', blk), bank=0)
                yield
                b_ = nxt()
                proj_tm(uT_t, kuT, O_AV, b_)
                if own:
                    yield S.op('dve', lambda e: e.tensor_copy(vx[:, blk, :, 0:128], banks[b_][:, :].rearrange("p (h d) -> p h d", h=4)),
                               r=[bk(b_)], w=[('vx', blk)])
                else:
                    yield S.op('act', lambda e: e.activation(vx[:, blk, :, 0:128], banks[b_][:, :].rearrange("p (h d) -> p h d", h=4), AF.Copy),
                               r=[bk(b_)], w=[('vx', blk)])
                if own:
                    b_ = nxt()
                    proj_tm(uT_t, kuT, O_AQ, b_)
                    yield from qk_prep(b_, C_QG, cs_k, qtok[i % 2], ('qtok', i % 2))

            yield from rr(T2(), T3(), T4())
            if own and i == 0 and 'qkT0' in dbg_d:
                S.dma('sp', dbg_d['qkT0'], qkT[:], r=[('qkT', kq, 0), ('qkT', kq, 1)], w=['dbg_qkT0'])
                S.finish('sp', ['dbg_qkT0'])

        def back(n):
            own, i = tt_info(n)
            Gc, kG = Gt[n % 3], ('Gt', n % 3)
            Gp, kGp = Gt[(n + 2) % 3], ('Gt', (n + 2) % 3)
            mv_t, kmv = mvx[n % 2], ('mvx', n % 2)
            qkT, kq = qkT2[n % 2], n % 2
            sigo = sigo2[n % 2]
            HB = (2, 3) if own else (4, 5)
            MK = [('ktl', 0), ('ktl', 1), ('stm', 0), ('stm', 1)]

            def head(h):
                e_ = h % 2
                kt_, kkt = ktl[e_], ('ktl', e_)
                st_, kst = stm[e_], ('stm', e_)
                hb = HB[e_]
                B = banks[hb]
                psb = B[:].bitcast(BF16)
                S.op('pe', lambda e: e.transpose(psb[:, 0:128], qkT[:, 4 + h, :], identb), r=[('qkT', kq, 1), 'cstb'], w=[bk(hb)], sig=True)
                yield
                yield S.op('dve', lambda e: e.tensor_scalar(kt_, psb[:, 0:128], Gc[:, h:h + 1], None, ALU.mult),
                           r=[bk(hb), kG], w=[kkt])
                if own:
                    S.op('pe', lambda e: e.matmul(B[:, 64:192], qkT[:, 4 + h, :], qkT[:, h, :], start=True, stop=True),
                         r=[('qkT', kq, 0), ('qkT', kq, 1)], w=[bk(hb)], sig=True)
                    yield
                    yield S.op('dve', lambda e: e.scalar_tensor_tensor(out=st_, in0=B[:, 64:192], scalar=Gc[:, h:h + 1],
                                                                       in1=tri, op0=ALU.mult, op1=ALU.mult),
                               r=[bk(hb), kG, 'cst'], w=[kst])
                    S.op('pe', lambda e: e.matmul(B[:, 192:321], qkT[:, h, :], Cbf[:, h, 0:129], start=True, stop=False),
                         r=[('qkT', kq, 0), 'Cbf'], w=[bk(hb)], sig=False)
                    S.op('pe', lambda e: e.matmul(B[:, 192:321], st_, mv_t[:, h, 0:129], start=False, stop=True),
                         r=[kst, kmv], w=[bk(hb)], sig=True)
                    yield
                S.op('pe', lambda e: e.matmul(B[:, 336:465], kt_, mv_t[:, h, 0:129], start=True, stop=True),
                     r=[kkt, kmv], w=[bk(hb)], sig=True)
                yield
                yield S.op('dve', lambda e: e.scalar_tensor_tensor(out=Pst[:, h, 0:129], in0=Pst[:, h, 0:129], scalar=Gp[:, 8 + h:9 + h],
                                                                   in1=B[:, 336:465], op0=ALU.mult, op1=ALU.add),
                           r=[bk(hb), kGp, ('Pst', h)], w=[('Pst', h)])

            def pair_epilogue(p):
                c = 8 * p
                m = ml[:, 16 * p:16 * p + 16]
                for e_ in range(2):
                    S.op('dve', lambda e: e.tensor_copy(m[:, e_:e_ + 1], banks[HB[e_]][:, 320:321]), r=[bk(HB[e_])], w=[('ml', p, 0)])
                S.op('dve', lambda e: e.scalar_tensor_tensor(out=m[:, 2:4], in0=m[:, 0:2], scalar=-1.0, in1=m[:, 0:2],
                                                             op0=ALU.mult, op1=ALU.max), r=[('ml', p, 0)], w=[('ml', p, 1)])
                S.op('dve', lambda e: e.tensor_tensor(m[:, 2:4], m[:, 2:4], Gc[:, 4 + 2 * p:6 + 2 * p], ALU.max),
                     r=[('ml', p, 1), kG], w=[('ml', p, 1)])
                yield S.op('dve', lambda e: e.reciprocal(m[:, 4:6], m[:, 2:4]), r=[('ml', p, 1)], w=[('ml', p, 2)])
                for e_ in range(2):
                    B = banks[HB[e_]]
                    h_ = 2 * p + e_
                    yield S.op('act', lambda e: e.activation(mls[:, h_ * 128:(h_ + 1) * 128], B[:, 192:320], AF.Square,
                                                             accum_out=m[:, 6 + e_:7 + e_]),
                               r=[bk(HB[e_])], w=[MK[h_], ('ml', p, 3 + e_)])
                S.op('dve', lambda e: e.tensor_tensor(m[:, 8:10], m[:, 4:6], m[:, 4:6], ALU.mult), r=[('ml', p, 2)], w=[('ml', p, 5)])
                yield S.op('dve', lambda e: e.tensor_tensor(m[:, 8:10], m[:, 8:10], m[:, 6:8], ALU.mult),
                           r=[('ml', p, 5), ('ml', p, 3), ('ml', p, 4)], w=[('ml', p, 5)])
                S.op('act', lambda e: e.activation(m[:, 10:12], m[:, 8:10], AF.Ln, bias=epsc[:, 0:1], scale=1.0 / 128),
                     r=[('ml', p, 5), 'consts'], w=[('ml', p, 6)])
                yield S.op('act', lambda e: e.activation(m[:, 10:12], m[:, 10:12], AF.Exp, scale=-0.5), r=[('ml', p, 6)], w=[('ml', p, 6)])
                yield S.op('dve', lambda e: e.tensor_tensor(m[:, 12:14], m[:, 10:12], m[:, 4:6], ALU.mult),
                           r=[('ml', p, 6), ('ml', p, 2)], w=[('ml', p, 7)])
                for e_ in range(2):
                    h = 2 * p + e_
                    B = banks[HB[e_]]
                    yield S.op('dve', lambda e: e.scalar_tensor_tensor(out=mls[:, h * 128:(h + 1) * 128], in0=B[:, 192:320],
                                                                       scalar=m[:, 12 + e_:13 + e_], in1=sigo[:, h * 128:(h + 1) * 128],
                                                                       op0=ALU.mult, op1=ALU.mult),
                               r=[bk(HB[e_]), ('ml', p, 7), ('sigo', n % 2)], w=[MK[h]])
                for e_ in range(2):
                    h = 2 * p + e_
                    B = banks[HB[e_]]
                    psb = B[:].bitcast(BF16)
                    S.op('pe', lambda e: e.transpose(psb[:, 0:128], mls[:, h * 128:(h + 1) * 128], identb),
                         r=[MK[h], 'cstb'], w=[bk(HB[e_])], sig=True)
                    yield
                    yield S.op('dve', lambda e: e.tensor_scalar(yT[:, h, i * 128:(i + 1) * 128], psb[:, 0:128],
                                                                cst[:, C_GY + h:C_GY + h + 1], None, ALU.mult),
                               r=[bk(HB[e_]), 'cst'], w=[('yT', i, 0)])

            for p in range(2):
                yield from rr(head(2 * p), head(2 * p + 1))
                if own:
                    yield from pair_epilogue(p)
            if own or i == NT - 1:
                yield S.op('dve', lambda e: e.tensor_tensor(Cbf[:, :, 0:129], Pst[:, :, 0:129],
                                                            Gc[:, 8:12].unsqueeze(2).to_broadcast([128, 4, 129]), ALU.mult),
                           r=[('Pst', h) for h in range(4)] + [kG], w=['Cbf'])

        STEPS = {'main': 0, 'side': 0}

        def wrr(main, side, per):
            credit = 0.0
            for _ in main:
                STEPS['main'] += 1
                credit += per
                while side is not None and credit >= 1.0:
                    credit -= 1.0
                    STEPS['side'] += 1
                    try:
                        next(side)
                    except StopIteration:
                        side = None
            return side

        def dump(name, ap, keys):
            if name in dbg_d:
                S.dma('sp', dbg_d[name], ap, r=keys, w=['dbg_' + name])
                S.finish('sp', ['dbg_' + name])

        side = None
        side_rate = [1.0]
        main_per_tt = [60]
        if NTT > 0:
            run(stageA(0))
            run(rr(stageA(1) if NTT > 1 else None, front(0)))
        for n in range(NTT):
            own, i = tt_info(n)
            def _late(gen, rounds):
                for _ in range(rounds):
                    yield
                yield from gen
            nxt_front = rr(_late(stageA(n + 2), A_LAG) if n + 2 < NTT else None, front(n + 1) if n + 1 < NTT else None)
            if own and i % 2 == 1:
                if side is not None:
                    run(side)
                side = attention_group(i // 2)
                for _ in range(4):
                    next(side)
                g_ = i // 2
                side_total = (18 + 2 * g_) * 4 + 4 * 4 + 2
                side_rate[0] = 1.15 * side_total / (2.0 * max(main_per_tt[0], 1))
            m0 = STEPS['main']
            side = wrr(rr(nxt_front, back(n)), side, side_rate[0])
            if own:
                main_per_tt[0] = STEPS['main'] - m0
            if own and i == 3 and 'yT01' in dbg_d:
                if side is not None:
                    run(side)
                    side = None
                dump('yT01', yT[:, :, 0:256], [('yT', a, b) for a in range(2) for b in range(2)])
        if side is not None:
            run(side)

        if 'yT' in dbg_d:
            S.dma('sp', dbg_d['yT'], yT[:], r=[('yT', i, j) for i in range(16) for j in range(2)], w=['dbg_yT'])
            S.finish('sp', ['dbg_yT'])

        ab.close()
        if 'C' not in phases:
            S.barrier()
            return nc

        hres = sb("hres", [128, 16, 1024])
        wout = sb("wout", [128, 8, 1024], BF16)
        wp = sb("wp", [128, 2, 1024], BF16)
        cd = ExitStack()

        def sc(name, shape, dt=F32):
            return cd.enter_context(nc.sbuf_tensor("t_" + name, list(shape), dt))

        u2T = sc("u2T", [128, 8, 2048], BF16)
        u = sc("u_c", [128, 1024], BF16)
        ssb = sc("ssb_c", [128, 8])
        wupb = [sc("wup%d" % i, [128, 8, 512], BF16) for i in range(2)]
        wdnb = [sc("wdn%d" % i, [128, 4, 1024], BF16) for i in range(2)]
        hidT = [sc("hidT%d" % i, [128, 4, 512], BF16) for i in range(2)]
        relu_t = sc("relu_t", [128, 512])
        w_out_v = w_out.rearrange("(k p) n -> p k n", p=128)
        S.barrier()
        for k in range(8):
            S.dma('pool', wout[:, k, :], w_out_v[:, k, :], w=[('wout', k)])
        w_up_v = w_up.rearrange("(k p) n -> p k n", p=128)
        w_dn_v = w_down.rearrange("(k p) n -> p k n", p=128)

        def load_ffn_w(j):
            b = j % 2
            S.dma('pool', wupb[b][:], w_up_v[:, :, j * 512:(j + 1) * 512], w=[('wup', b)])
            S.dma('pool', wdnb[b][:], w_dn_v[:, j * 4:(j + 1) * 4, :], w=[('wdn', b)])

        u_c2 = sc("u_c2", [128, 1024], BF16)
        ssb_c2 = sc("ssb_c2", [128, 8])
        CSCR = [(u, 'u_c', ssb, 'ssb_c', 0), (u_c2, 'u_c2', ssb_c2, 'ssb_c2', 1)]

        def c_stream(i):
            S.dma('sp', hres[:, i, :], xo[i * 128:(i + 1) * 128, :], w=[('h', i)])
            for half in range(2):
                pbk = 2 + (i * 2 + half) % 6
                for k in range(8):
                    S.op('pe', lambda e: e.matmul(banks[pbk][:, :], yT[:, k, i * 128:(i + 1) * 128],
                                                  wout[:, k, half * 512:(half + 1) * 512], start=(k == 0), stop=(k == 7)),
                         r=[('yT', i, 0), ('yT', i, 1), ('wout', k)], w=[bk(pbk)], sig=k == 7)
                yield S.op('dve', lambda e: e.tensor_tensor(hres[:, i, half * 512:(half + 1) * 512], banks[pbk][:, :],
                                                            hres[:, i, half * 512:(half + 1) * 512], ALU.add),
                           r=[bk(pbk), ('h', i)], w=[('h', i)])
            if i == 0:
                load_ffn_w(0)
                load_ffn_w(1)
            sc_ = CSCR[i % 2]
            _norm_a1(hres[:, i, :], ('h', i), sc_[0], sc_[1], sc_[2], sc_[3])
            yield
            yield
            _norm_a2(C_G2, u2T[:, :, i * 128:(i + 1) * 128], ('u2T', i), sc_[0], sc_[1], sc_[4])
            yield

        def lagged0(gens, lag):
            live = []
            pending = list(gens)
            rnd = 0
            while live or pending:
                if pending and rnd % lag == 0:
                    live.append(pending.pop(0))
                for g in list(live):
                    try:
                        next(g)
                    except StopIteration:
                        live.remove(g)
                rnd += 1

        lagged0([c_stream(i) for i in range(NT)], 1)
        if 'h1' in dbg_d:
            S.dma('sp', dbg_d['h1'], hres[:], r=[('h', i) for i in range(16)], w=['dbg_h1'])
            S.finish('sp', ['dbg_h1'])

        nacc = 0
        for j in range(8):
            b = j % 2
            for tg in range(4):
                hb = (j * 4 + tg) % 2
                for m in range(4):
                    pbk = 2 + nacc % 6
                    nacc += 1
                    for k in range(8):
                        S.op('pe', lambda e: e.matmul(banks[pbk][:, :], wupb[b][:, k, m * 128:(m + 1) * 128],
                                                      u2T[:, k, tg * 512:(tg + 1) * 512], start=(k == 0), stop=(k == 7)),
                             r=[('wup', b)] + [('u2T', tg * 4 + q) for q in range(4)], w=[bk(pbk)], sig=k == 7)
                    S.op('act', lambda e: e.activation(relu_t[:], banks[pbk][:, :], AF.Relu), r=[bk(pbk)], w=['relu_t'])
                    S.op('act', lambda e: e.activation(hidT[hb][:, m, :], relu_t[:], AF.Square),
                         r=['relu_t'], w=[('hidT', hb)])
                for q in range(4):
                    i = tg * 4 + q
                    for half in range(2):
                        pbk = 2 + nacc % 6
                        nacc += 1
                        for m in range(4):
                            S.op('pe', lambda e: e.matmul(banks[pbk][:, :], hidT[hb][:, m, q * 128:(q + 1) * 128],
                                                          wdnb[b][:, m, half * 512:(half + 1) * 512], start=(m == 0), stop=(m == 3)),
                                 r=[('hidT', hb), ('wdn', b)], w=[bk(pbk)], sig=m == 3)
                        S.op('dve', lambda e: e.tensor_tensor(hres[:, i, half * 512:(half + 1) * 512], banks[pbk][:, :],
                                                              hres[:, i, half * 512:(half + 1) * 512], ALU.add),
                             r=[bk(pbk), ('h', i)], w=[('h', i)])
            if j + 2 < 8:
                load_ffn_w(j + 2)
            if j == 5:
                w_g_v = w_gate.rearrange("(k p) n -> p k n", p=128)
                w_p_v = w_ple.rearrange("(k p) n -> p k n", p=128)
                for k in range(8):
                    S.dma('pool', wout[:, k, :], w_g_v[:, k, :], w=[('wout', k)])
                S.dma('pool', wp[:], w_p_v, w=['wp'])
        if 'h2' in dbg_d:
            S.dma('sp', dbg_d['h2'], hres[:], r=[('h', i) for i in range(16)], w=['dbg_h2'])
            S.finish('sp', ['dbg_h2'])
        S.barrier()
        cd.close()

        wg = wout
        NS = 4
        u_e = [sb("u_e%d" % i, [128, 1024], BF16) for i in range(NS)]
        ssb_e = [sb("ssb_e%d" % i, [128, 8]) for i in range(NS)]
        u3T = [sb("u3T%d" % i, [128, 8, 128], BF16) for i in range(NS)]
        pt = [sb("pt%d" % i, [128, 256], BF16) for i in range(NS)]
        pT = [sb("pT%d" % i, [128, 2, 128], BF16) for i in range(NS)]
        gsb = [sb("gsb%d" % i, [128, 1024]) for i in range(NS)]

        def pe_stream(i):
            b = i % NS
            tb = b % 2
            S.dma('pool', pt[b][:], po[i * 128:(i + 1) * 128, :], w=[('pt', b)])
            norm_to_uT(hres[:, i, :], ('h', i), C_G3, u3T[b], ('u3T', b), scr=(u_e[b], ('u_e', b), ssb_e[b], ('ssb_e', b), tb))
            yield
            psb = banks[tb][:].bitcast(BF16)
            for k in range(2):
                S.op('pe', lambda e: e.transpose(psb[:, k * 128:(k + 1) * 128], pt[b][:, k * 128:(k + 1) * 128], identb),
                     r=[('pt', b), 'cstb'], w=[bk(tb)], sig=k == 1)
            yield S.op('dve', lambda e: e.tensor_copy(pT[b][:], psb[:, 0:256].rearrange("p (k t) -> p k t", k=2)),
                       r=[bk(tb)], w=[('pT', b)])
            for half in range(2):
                pg = 2 + b
                pe_ = 6 + (b % 2)
                for k in range(8):
                    S.op('pe', lambda e: e.matmul(banks[pg][:, :], u3T[b][:, k, :], wg[:, k, half * 512:(half + 1) * 512],
                                                  start=(k == 0), stop=(k == 7)), r=[('u3T', b), ('wout', k)], w=[bk(pg)], sig=k == 7)
                sl = slice(half * 512, (half + 1) * 512)
                yield S.op('act', lambda e: e.activation(gsb[b][:, sl], banks[pg][:, :], AF.Sigmoid), r=[bk(pg)], w=[('gsb', b, half)])
                for k in range(2):
                    S.op('pe', lambda e: e.matmul(banks[pe_][:, :], pT[b][:, k, :], wp[:, k, half * 512:(half + 1) * 512],
                                                  start=(k == 0), stop=(k == 1)), r=[('pT', b), 'wp'], w=[bk(pe_)], sig=k == 1)
                yield S.op('dve', lambda e: e.tensor_tensor(gsb[b][:, sl], banks[pe_][:, :], gsb[b][:, sl], ALU.mult),
                           r=[bk(pe_), ('gsb', b, half)], w=[('gsb', b, half)])
                yield S.op('dve', lambda e: e.tensor_tensor(gsb[b][:, sl], gsb[b][:, sl], hres[:, i, sl], ALU.add),
                           r=[('gsb', b, half), ('h', i)], w=[('gsb', b, half)])
            S.dma('sp', out_d[i * 128:(i + 1) * 128, :], gsb[b][:], r=[('gsb', b, 0), ('gsb', b, 1)],
                  w=[('out', i)])
            yield

        def lagged(gens, lag):
            live = []
            pending = list(gens)
            rnd = 0
            while live or pending:
                if pending and rnd % lag == 0:
                    live.append(pending.pop(0))
                for g in list(live):
                    try:
                        next(g)
                    except StopIteration:
                        live.remove(g)
                rnd += 1

        lagged([pe_stream(i) for i in range(NT)], 2)
        S.finish('sp', [('out', i) for i in range(NT)])
    return nc


def _consts(core, inp):
    c = np.zeros((128, C_TOT), np.float32)

    def pk(v):
        return np.asarray(v, np.float32).reshape(-1, 128).T

    c[:, C_G1:C_G1 + 8] = pk(inp['attn_norm_g'][0])
    c[:, C_G2:C_G2 + 8] = pk(inp['mlp_norm_g'][0])
    c[:, C_G3:C_G3 + 8] = pk(inp['ple_norm_g'][0])
    c[:, C_GY:C_GY + 4] = pk(inp['mlstm_norm_g'][0])
    c[:, C_GY + 4:C_GY + 8] = pk(inp['attn_sub_norm_g'][0])
    cw = np.asarray(inp['conv_w'][0], np.float32)
    c[:, C_CW:C_CW + 32] = cw.T.reshape(8, 128, 4).transpose(1, 0, 2).reshape(128, 32)
    c[:, C_CB:C_CB + 8] = pk(inp['conv_b'][0])
    c[:, C_QG:C_QG + 64] = np.asarray(inp['q_norm_g'][0], np.float32)[None, :]
    c[:, C_KG:C_KG + 64] = np.asarray(inp['k_norm_g'][0], np.float32)[None, :]
    for j, nm in enumerate(('lambda_q1', 'lambda_k1', 'lambda_q2', 'lambda_k2')):
        c[:, C_LAM + j * 64:C_LAM + (j + 1) * 64] = np.asarray(inp[nm][0], np.float32)[None, :]
    odd = core % 2
    c[:, C_FL + 0] = 1.0 if odd else 0.0
    c[:, C_FL + 1] = 0.0 if odd else NEG
    c[:, C_FL + 2] = 0.0 if odd else NEG
    inv = (np.float32(500000.0) ** (-np.arange(0, 16, 2, dtype=np.float32) / np.float32(16))).astype(np.float32)
    for off, base in ((C_CSO, odd * 2048), (C_CSP, 0)):
        pos = (base + np.arange(2048, dtype=np.float32)).astype(np.float32)
        ang = (pos[:, None] * inv[None, :]).astype(np.float32)
        cs = np.concatenate([np.cos(ang), np.sin(ang)], axis=1).astype(np.float32)
        c[:, off:off + 256] = cs.reshape(16, 128, 16).transpose(1, 0, 2).reshape(128, 256)
    c[:, C_ID:C_ID + 128] = np.eye(128, dtype=np.float32)
    c[:, C_TRI:C_TRI + 128] = np.triu(np.ones((128, 128), np.float32))
    c[0:4, C_GB] = np.asarray(inp['igate_b'][0], np.float32)
    c[0:4, C_GB + 1] = np.asarray(inp['fgate_b'][0], np.float32)
    return c


def _constb():
    b = np.zeros((128, 128 + 1024), np.float32)
    b[:, 0:128] = np.eye(128, dtype=np.float32)
    tri = np.triu(np.ones((128, 128), np.float32))
    for m in range(2):
        b[:, 128 + m * 256:128 + m * 256 + 128] = tri
        b[:, 128 + m * 256 + 128:128 + m * 256 + 256] = 1.0
        b[:, 640 + m * 256:640 + m * 256 + 128] = 0.0
        b[:, 640 + m * 256 + 128:640 + m * 256 + 256] = tri
    return b


def make_in_maps(inp):
    x = np.asarray(inp['x'], np.float32)
    p = np.asarray(inp['p'], np.float32)
    shared = {
        'w_in': np.ascontiguousarray(inp['w_in'][0], dtype=np.float32),
        'w_out': np.ascontiguousarray(inp['w_out'][0], dtype=np.float32),
        'w_up': np.ascontiguousarray(inp['w_up'][0], dtype=np.float32),
        'w_down': np.ascontiguousarray(inp['w_down'][0], dtype=np.float32),
        'w_gate': np.ascontiguousarray(inp['w_ple_gate'][0], dtype=np.float32),
        'w_ple': np.ascontiguousarray(inp['w_ple_proj'][0], dtype=np.float32),
        'cstb': _constb(),
    }
    zeros = np.zeros((2048, 1024), np.float32)
    maps = []
    for c in range(8):
        b, hf = c // 2, c % 2
        m = dict(shared)
        m['xo'] = np.ascontiguousarray(x[b, hf * 2048:(hf + 1) * 2048])
        m['xp'] = np.ascontiguousarray(x[b, 0:2048]) if hf else zeros
        m['po'] = np.ascontiguousarray(p[0, b, hf * 2048:(hf + 1) * 2048])
        m['cst'] = _consts(c, inp)
        maps.append(m)
    return maps


def kernel(**inputs):
    nc = build()
    maps = make_in_maps(inputs)
    res = run_bass_kernel_spmd(nc, maps, core_ids=list(range(8)))
    out = np.zeros((4, 4096, 1024), np.float32)
    for c in range(8):
        out[c // 2, (c % 2) * 2048:(c % 2 + 1) * 2048] = res.results[c]['out']
    return out
```

```python
import math
from contextlib import ExitStack

import numpy as np
import concourse.bass as bass
import concourse.mybir as mybir
from concourse.bass_utils import run_bass_kernel_spmd

F32 = mybir.dt.float32
BF16 = mybir.dt.bfloat16
AF = mybir.ActivationFunctionType
ALU = mybir.AluOpType
AX = mybir.AxisListType

SAME_ENGINE_SYNC = True
A_LAG = 1
A2_LAG = 6
PE_WARM_REPS = 1
EPS = 1e-6
NT = 16
NEG = -30000.0
LAM_INIT = 0.8 - 0.6 * math.exp(0.0)
LNC = math.log(128 ** -0.5)

O_MQ, O_MK, O_MV, O_MO, O_MI, O_MF, O_AQ, O_AK, O_AV = 0, 512, 1024, 1536, 2048, 2052, 2056, 2568, 3080

C_G1, C_G2, C_G3, C_GY, C_CW, C_CB, C_QG, C_KG, C_LAM, C_FL, C_CSO, C_CSP, C_ID, C_TRI, C_GB = (
    0, 8, 16, 24, 32, 64, 72, 136, 200, 456, 460, 716, 972, 1100, 1228)
C_TOT = 1230


class _Eng:
    def __init__(self, name, h, sem):
        self.name, self.h, self.sem = name, h, sem
        self.n = 0
        self.count = 0
        self.incs = []
        self.last = None
        self.last_seq = 0
        self.waited = {}
        self.dsems = []
        self.dcnt = []
        self.dnext = 0


class Sched:
    def __init__(self, nc, stack, ndma):
        self.nc = nc
        hs = {'pe': nc.tensor, 'act': nc.scalar, 'dve': nc.vector, 'pool': nc.gpsimd, 'sp': nc.sync}
        self.e = {}
        for k, h in hs.items():
            sem = stack.enter_context(nc.semaphore("s_" + k))
            self.e[k] = _Eng(k, h, sem)
            for i in range(ndma.get(k, 0)):
                self.e[k].dsems.append(stack.enter_context(nc.semaphore("d_%s%d" % (k, i))))
                self.e[k].dcnt.append(0)
        self.st = {}

    def _target(self, dep):
        if dep[0] == 'd':
            return dep[1], dep[2]
        p = self.e[dep[1]]
        seq = dep[2]
        found = None
        for (s, c) in reversed(p.incs):
            if s >= seq:
                found = c
            else:
                break
        if found is None:
            p.count += 1
            p.last.then_inc(p.sem, 1)
            p.incs.append((p.last_seq, p.count))
            found = p.count
        return p.sem, found

    def _wait(self, eng, deps):
        E = self.e[eng]
        for dep in deps:
            if dep is None:
                continue
            if dep[0] == 'c' and dep[1] == eng and (eng == 'pe' or not SAME_ENGINE_SYNC):
                continue
            sem, val = self._target(dep)
            key = id(sem)
            if E.waited.get(key, 0) >= val:
                continue
            E.h.wait_ge(sem, val)
            E.waited[key] = val

    def _deps(self, r, w, eng=None):
        deps = []
        for k in r:
            s = self.st.get(k)
            if s is not None:
                deps.append(s[0])
                if isinstance(k, tuple) and k[0] == 'bank':
                    deps.extend(d for d in s[1].values() if not (d[0] == 'c' and d[1] == eng))
        for k in w:
            s = self.st.get(k)
            if s is not None:
                deps.append(s[0])
                deps.extend(s[1].values())
        return deps

    def _record(self, dep, r, w):
        rk = (dep[0], dep[1] if dep[0] == 'c' else id(dep[1]))
        for k in r:
            s = self.st.setdefault(k, [None, {}])
            s[1][rk] = dep
        for k in w:
            self.st[k] = [dep, {}]

    def op(self, eng, fn, r=(), w=(), sig=None):
        E = self.e[eng]
        self._wait(eng, self._deps(r, w, eng))
        inst = fn(E.h)
        E.n += 1
        E.last = inst
        E.last_seq = E.n
        if sig or (sig is None and eng != 'pe'):
            E.count += 1
            inst.then_inc(E.sem, 1)
            E.incs.append((E.n, E.count))
        self._record(('c', eng, E.n), r, w)
        return inst

    def dma(self, q, out, in_, r=(), w=(), **kw):
        E = self.e[q]
        self._wait(q, self._deps(r, w))
        i = E.dnext
        E.dnext = (i + 1) % len(E.dsems)
        sem = E.dsems[i]
        if E.dcnt[i] > 0:
            key = id(sem)
            if E.waited.get(key, 0) < E.dcnt[i]:
                E.h.wait_ge(sem, E.dcnt[i])
                E.waited[key] = E.dcnt[i]
        E.dcnt[i] += 16
        E.h.dma_start(out=out, in_=in_, **kw).then_inc(sem, 16)
        dep = ('d', sem, E.dcnt[i])
        self._record(dep, r, w)
        return dep

    def barrier(self):
        deps = []
        for s in self.st.values():
            if s[0] is not None:
                deps.append(s[0])
            deps.extend(s[1].values())
        for eng in self.e:
            self._wait(eng, deps)

    def finish(self, eng, keys):
        self._wait(eng, [self.st[k][0] for k in keys if k in self.st])


def build(dbg=(), n_pre=NT, n_own=NT, phases='CDE'):
    nc = bass.Bass("TRN2", target_bir_lowering=False)

    def din(name, shape):
        return nc.dram_tensor(name, list(shape), F32, kind="ExternalInput").ap()

    xo = din("xo", [2048, 1024])
    xp = din("xp", [2048, 1024])
    po = din("po", [2048, 256])
    w_in = din("w_in", [1024, 3592])
    w_out = din("w_out", [1024, 1024])
    w_up = din("w_up", [1024, 4096])
    w_down = din("w_down", [4096, 1024])
    w_gate = din("w_gate", [1024, 1024])
    w_ple = din("w_ple", [256, 1024])
    cst_d = din("cst", [128, C_TOT])
    cstb_d = din("cstb", [128, 128 + 1024])
    out_d = nc.dram_tensor("out", [2048, 1024], F32, kind="ExternalOutput").ap()
    dbg_d = {}
    for name, shape, dt in dbg:
        dbg_d[name] = nc.dram_tensor("dbg_" + name, list(shape), dt, kind="ExternalOutput").ap()

    with ExitStack() as st:
        S = Sched(nc, st, {'sp': 12, 'pool': 8})

        def sb(name, shape, dt=F32):
            return st.enter_context(nc.sbuf_tensor("t_" + name, list(shape), dt))

        def freed(name, shape, dt=F32):
            return nc.sbuf_tensor(name, list(shape), dt)

        cst = sb("cst", [128, C_TOT])
        cstb = sb("cstb", [128, 128 + 1024], BF16)
        identb = cstb[:, 0:128]
        identf = cst[:, C_ID:C_ID + 128]
        tri = cst[:, C_TRI:C_TRI + 128]
        banks = [st.enter_context(nc.psum_tensor("bank%d" % i, [128, 512], F32)) for i in range(8)]
        small = sb("small", [128, 64])
        block = st.enter_context(nc.Block())

        S.dma('sp', cst[:], cst_d, w=['cst'])
        S.dma('pool', cstb[:], cstb_d, w=['cstb'])

        def bk(i):
            return ('bank', i)

        def rstd_from_ss(ss_ap, out_ap, n, key_ss, key_out, mul=None):
            S.op('act', lambda e: e.activation(out_ap, ss_ap, AF.Ln, bias=epsc[:ss_ap.shape[0], 0:1], scale=1.0 / n),
                 r=[key_ss, 'consts'], w=[key_out])
            S.op('act', lambda e: e.activation(out_ap, out_ap, AF.Exp, scale=-0.5), r=[key_out], w=[key_out])
            if mul is not None:
                S.op('dve', lambda e: e.tensor_scalar(out_ap, out_ap, float(mul), None, ALU.mult), r=[key_out], w=[key_out])

        epsc = sb("epsc", [128, 4])
        ones4 = sb("ones4", [4, 128])
        S.op('pool', lambda e: e.memset(epsc[:, 0:1], EPS), w=['consts'])
        S.op('pool', lambda e: e.memset(epsc[:, 1:2], 1.0), w=['consts'])
        S.op('pool', lambda e: e.memset(ones4[:], 1.0), w=['consts4'])

        lamv = cst[:, C_LAM:C_LAM + 256]
        lj = sb("lj", [128, 64])
        S.op('dve', lambda e: e.scalar_tensor_tensor(out=lj[:], in0=lamv[:, 0:64], scalar=1.0, in1=lamv[:, 64:128],
                                                     op0=ALU.mult, op1=ALU.mult, accum_out=small[:, 0:1]),
             r=['cst'], w=['lj', 'small'])
        S.op('dve', lambda e: e.scalar_tensor_tensor(out=lj[:], in0=lamv[:, 128:192], scalar=1.0, in1=lamv[:, 192:256],
                                                     op0=ALU.mult, op1=ALU.mult, accum_out=small[:, 1:2]),
             r=['cst', 'lj'], w=['lj', 'small'])
        S.op('act', lambda e: e.activation(small[:, 2:4], small[:, 0:2], AF.Exp), r=['small'], w=['small'])
        S.op('dve', lambda e: e.tensor_tensor(small[:, 4:5], small[:, 2:3], small[:, 3:4], ALU.subtract), r=['small'], w=['small'])
        S.op('dve', lambda e: e.tensor_scalar(small[:, 5:6], small[:, 4:5], float(LAM_INIT), None, ALU.add), r=['small'], w=['small'])
        lam_ap = small[:, 5:6]
        gpar = sb("gpar", [4, 8])
        gb = cst[0:4, C_GB:C_GB + 2]
        fl = cst[0:4, C_FL:C_FL + 4]
        S.op('dve', lambda e: e.tensor_copy(gpar[:, 0:1], fl[:, 0:1]), r=['cst'], w=['gpar'])
        S.op('dve', lambda e: e.scalar_tensor_tensor(out=gpar[:, 1:2], in0=gb[:, 0:1], scalar=fl[:, 0:1], in1=fl[:, 1:2],
                                                     op0=ALU.mult, op1=ALU.add), r=['cst', 'gpar'], w=['gpar'])
        S.op('pool', lambda e: e.memset(gpar[:, 2:3], 1.0), r=['gpar'], w=['gpar'])
        S.op('dve', lambda e: e.tensor_copy(gpar[:, 3:4], gb[:, 0:1]), r=['cst', 'gpar'], w=['gpar'])
        S.op('dve', lambda e: e.tensor_scalar(gpar[:, 4:5], gb[:, 1:2], -1.0, None, ALU.mult), r=['cst', 'gpar'], w=['gpar'])
        S.op('dve', lambda e: e.tensor_copy(gpar[:, 5:6], fl[:, 0:1]), r=['cst', 'gpar'], w=['gpar'])
        pbias = cst[:, C_FL + 2:C_FL + 3]

        yT = sb("yT", [128, 8, 2048], BF16)
        ab = ExitStack()

        def sa(name, shape, dt=F32):
            return ab.enter_context(nc.sbuf_tensor("t_" + name, list(shape), dt))

        win = sa("win", [128, 8, 3592], BF16)
        kT = sa("kT", [128, 4, 4096], BF16)
        vx = sa("vx", [128, 32, 4, 130], BF16)
        xt = sa("xt", [128, 1024])
        u = sa("u", [128, 1024], BF16)
        uT = [sa("uT%d" % i, [128, 8, 128], BF16) for i in range(2)]
        ssb = sa("ssb", [128, 8])
        raw = sa("raw", [128, 8, 131])
        acc = sa("acc", [128, 4, 128])
        qkT2 = [sa("qkT%d" % i, [128, 8, 128], BF16) for i in range(2)]
        gf = sa("gf", [4, 8, 128])
        carry = sa("carry", [4, 8])
        Gt = [small[:, 16 + 12 * i:28 + 12 * i] for i in range(3)]
        mvx = [sa("mvx%d" % i, [128, 4, 130], BF16) for i in range(2)]
        sigo2 = [sa("sigo%d" % i, [128, 512], BF16) for i in range(2)]
        mls = sa("mls", [128, 512], BF16)
        ktl = [mls[:, i * 128:(i + 1) * 128] for i in range(2)]
        stm = [mls[:, 256 + i * 128:256 + (i + 1) * 128] for i in range(2)]
        Pst = sa("Pst", [128, 4, 130])
        Cbf = sa("Cbf", [128, 4, 130], BF16)
        ml = sa("ml", [128, 32])
        qf = sa("qf", [128, 512])
        qb16 = sa("qb16", [128, 512], BF16)
        qtok = [sa("qtok%d" % i, [128, 512], BF16) for i in range(2)]
        qTg = [sa("qTg%d" % i, [128, 4, 256], BF16) for i in range(2)]
        Et = [sa("Et%d" % i, [128, 512], BF16) for i in range(2)]
        ofin = [sa("ofin%d" % i, [128, 128]) for i in range(2)]
        yat = [sa("yat%d" % i, [128, 512], BF16) for i in range(2)]
        al = sa("al", [128, 16])
        ae = small[:, 8:16]

        w_in_v = w_in.rearrange("(k p) n -> p k n", p=128)
        WGRP = [(O_MK, O_MV), (O_MV, O_MO), (O_MI, O_AQ), (O_AK, O_AV), (O_AV, 3592), (O_MQ, O_MK), (O_MO, O_MI), (O_AQ, O_AK)]
        for gi_, (c0_, c1_) in enumerate(WGRP):
            S.dma('pool', win[:, :, c0_:c1_], w_in_v[:, :, c0_:c1_], w=[('win', gi_)])

        def wkey(col):
            for gi_, (c0_, c1_) in enumerate(WGRP):
                if c0_ <= col < c1_:
                    return ('win', gi_)
            raise ValueError(col)

        S.op('pool', lambda e: e.memset(vx[:, :, :, 128:130], 1.0), w=[('vx', j) for j in range(32)])
        S.op('pool', lambda e: e.tensor_scalar(vx[:, 0:16, :, 128:130], vx[:, 0:16, :, 128:130], cst[:, C_FL:C_FL + 1], 1.0,
                                               ALU.mult, ALU.mult),
             r=['cst'] + [('vx', j) for j in range(16)], w=[('vx', j) for j in range(16)])
        for i in range(2):
            S.op('pool', lambda e: e.memset(mvx[i][:, :, 128:130], 1.0), w=[('mvx', i)])
        for i in range(3):
            S.op('pool', lambda e: e.memset(Gt[i], 1.0), w=[('Gt', i)])
        S.op('pool', lambda e: e.memset(qTg[0][:], 0.0), w=[('qTg', 0, 0), ('qTg', 0, 1)])
        S.op('pool', lambda e: e.memset(qTg[1][:], 0.0), w=[('qTg', 1, 0), ('qTg', 1, 1)])
        S.op('pool', lambda e: e.memset(Pst[:], 0.0), w=[('Pst', h) for h in range(4)])
        S.op('pool', lambda e: e.memset(Cbf[:], 0.0), w=['Cbf'])
        S.op('pool', lambda e: e.memset(raw[:], 0.0), w=[('raw', c) for c in range(8)])
        S.op('pool', lambda e: e.memset(carry[:], 0.0), w=[('carry', j) for j in range(8)])

        cw = cst[:, C_CW:C_CW + 32].rearrange("p (c j) -> p c j", j=4)
        cb = cst[:, C_CB:C_CB + 8]
        cnt = {'tt': 0}

        def norm_to_uT(x_tile, kx, gcol, uT_t, kuT, scr=None):
            if scr is None:
                u_, ku, ssb_, kss, tb = u, 'u', ssb, 'ssb', 0
            else:
                u_, ku, ssb_, kss, tb = scr
            return _norm_to_uT(x_tile, kx, gcol, uT_t, kuT, u_, ku, ssb_, kss, tb)

        def _norm_to_uT(x_tile, kx, gcol, uT_t, kuT, u, ku, ssb, kss, tb):
            _norm_a1(x_tile, kx, u, ku, ssb, kss)
            _norm_a2(gcol, uT_t, kuT, u, ku, tb)

        def _norm_a1(x_tile, kx, u, ku, ssb, kss):
            S.op('act', lambda e: e.activation(u[:], x_tile, AF.Square, accum_out=ssb[:, 0:1]), r=[kx], w=[ku, kss])
            rstd_from_ss(ssb[:, 0:1], ssb[:, 1:2], 1024.0, kss, kss)
            S.op('dve', lambda e: e.tensor_scalar(u[:], x_tile, ssb[:, 1:2], None, ALU.mult), r=[kx, kss], w=[ku])

        def _norm_a2(gcol, uT_t, kuT, u, ku, tb):
            psb = banks[tb][:].bitcast(BF16)
            for k in range(8):
                S.op('pe', lambda e: e.transpose(psb[:, k * 128:(k + 1) * 128], u[:, k * 128:(k + 1) * 128], identb),
                     r=[ku, 'cstb'], w=[bk(tb)], sig=k == 7)
            g_bc = cst[:, gcol:gcol + 8].unsqueeze(2).to_broadcast([128, 8, 128])
            S.op('dve', lambda e: e.tensor_tensor(uT_t[:], psb[:, 0:1024].rearrange("p (k t) -> p k t", k=8), g_bc, ALU.mult),
                 r=[bk(tb), 'cst'], w=[kuT])

        def transpose_to(src_tok, ksrc, dst_ap, kdst, scale_cols=None, bank=0):
            psb = banks[bank][:].bitcast(BF16)
            for k in range(4):
                S.op('pe', lambda e: e.transpose(psb[:, k * 128:(k + 1) * 128], src_tok[:, k * 128:(k + 1) * 128], identb),
                     r=[ksrc, 'cstb'], w=[bk(bank)], sig=k == 3)
            src = psb[:, 0:512].rearrange("p (k t) -> p k t", k=4)
            if scale_cols is None:
                S.op('dve', lambda e: e.tensor_copy(dst_ap, src), r=[bk(bank)], w=[kdst])
            else:
                g_bc = scale_cols.unsqueeze(2).to_broadcast([128, 4, 128])
                S.op('dve', lambda e: e.tensor_tensor(dst_ap, src, g_bc, ALU.mult), r=[bk(bank), 'cst'], w=[kdst])

        def proj_tm(uT_t, kuT, col, bank):
            for k in range(8):
                S.op('pe', lambda e: e.matmul(banks[bank][:, :], uT_t[:, k, :], win[:, k, col:col + 512],
                                              start=(k == 0), stop=(k == 7)),
                     r=[kuT, wkey(col)], w=[bk(bank)], sig=k == 7)

        def rr(*gens):
            gens = [g for g in gens if g is not None]
            while gens:
                for g in list(gens):
                    try:
                        next(g)
                    except StopIteration:
                        gens.remove(g)
                yield

        def run(gen):
            for _ in gen:
                pass


        def qk_prep(bank, gcol, cs, qb16, k16):
            q3 = qf[:].rearrange("p (g d) -> p g d", d=64)
            S.op('act', lambda e: e.activation(qf[:], banks[bank][:, :], AF.Square), r=[bk(bank)], w=['qf'])
            S.op('dve', lambda e: e.tensor_reduce(out=al[:, 0:8], in_=q3, axis=AX.X, op=ALU.add), r=['qf'], w=['al'])
            S.op('act', lambda e: e.activation(al[:, 8:16], al[:, 0:8], AF.Ln, bias=epsc[:, 0:1], scale=1.0 / 64),
                 r=['al', 'consts'], w=['al'])
            S.op('act', lambda e: e.activation(al[:, 8:16], al[:, 8:16], AF.Exp, scale=-0.5), r=['al'], w=['al'])
            S.op('dve', lambda e: e.tensor_tensor(q3, banks[bank][:, :].rearrange("p (g d) -> p g d", d=64),
                                                  al[:, 8:16].unsqueeze(2).to_broadcast([128, 8, 64]), ALU.mult),
                 r=[bk(bank), 'al'], w=['qf'])
            gg = cst[:, gcol:gcol + 64].unsqueeze(1).to_broadcast([128, 8, 64])
            o3 = qb16[:].rearrange("p (g d) -> p g d", d=64)
            yield S.op('dve', lambda e: e.tensor_tensor(o3, q3, gg, ALU.mult), r=['qf', 'cst'], w=[k16])
            gg16 = cst[:, gcol:gcol + 16].unsqueeze(1).to_broadcast([128, 8, 16])
            yield S.op('dve', lambda e: e.tensor_tensor(q3[:, :, 0:16], q3[:, :, 0:16], gg16, ALU.mult), r=['qf', 'cst'], w=['qf'])
            x1, x2 = q3[:, :, 0:8], q3[:, :, 8:16]
            cc = cs[:, 0:8].unsqueeze(1).to_broadcast([128, 8, 8])
            sn = cs[:, 8:16].unsqueeze(1).to_broadcast([128, 8, 8])
            r4 = [q3[:, :, 16 + 8 * a_:24 + 8 * a_] for a_ in range(4)]
            S.op('dve', lambda e: e.tensor_tensor(r4[0], x1, cc, ALU.mult), r=['qf', 'cst'], w=['qf'])
            S.op('dve', lambda e: e.tensor_tensor(r4[1], x2, sn, ALU.mult), r=['qf', 'cst'], w=['qf'])
            S.op('dve', lambda e: e.tensor_tensor(r4[2], x2, cc, ALU.mult), r=['qf', 'cst'], w=['qf'])
            yield S.op('dve', lambda e: e.tensor_tensor(r4[3], x1, sn, ALU.mult), r=['qf', 'cst'], w=['qf'])
            S.op('dve', lambda e: e.tensor_tensor(o3[:, :, 0:8], r4[0], r4[1], ALU.subtract), r=['qf', k16], w=[k16])
            yield S.op('dve', lambda e: e.tensor_tensor(o3[:, :, 8:16], r4[2], r4[3], ALU.add), r=['qf', k16], w=[k16])

        def attention_group(g):
            kbs = list(range(16)) + [16 + j for j in range(2 * g + 2)]
            units = [(h, idx, kb) for h in range(4) for idx, kb in enumerate(kbs)]
            PSB = (4, 5)
            for tt in range(2):
                psb = banks[4 + tt][:].bitcast(BF16)
                for k in range(4):
                    S.op('pe', lambda e: e.transpose(psb[:, k * 128:(k + 1) * 128], qtok[tt][:, k * 128:(k + 1) * 128], identb),
                         r=[('qtok', tt), 'cstb'], w=[bk(4 + tt)], sig=(k == 3))
                yield
                c_ = tt * 128
                S.op('act', lambda e: e.activation(qTg[0][0:64, :, c_:c_ + 128], psb[0:64, 0:512].rearrange("p (k t) -> p k t", k=4),
                                                   AF.Copy), r=[bk(4 + tt)], w=[('qTg', 0, tt)])
                yield S.op('act', lambda e: e.activation(qTg[1][64:128, :, c_:c_ + 128],
                                                         psb[64:128, 0:512].rearrange("p (k t) -> p k t", k=4), AF.Copy),
                           r=[bk(4 + tt)], w=[('qTg', 1, tt)])
            QK = [('qTg', m, tt) for m in range(2) for tt in range(2)]

            def st_mm(n):
                h, idx, kb = units[n]
                pb = PSB[n % 2]
                for rep_ in range(PE_WARM_REPS):
                    S.op('pe', lambda e: e.matmul(banks[pb][:, 0:256], kT[:, h, kb * 128:(kb + 1) * 128],
                                                  qTg[0][:, h, :], start=True, stop=True),
                         r=[('kT', kb)] + QK, w=[bk(pb)], sig=False)
                    S.op('pe', lambda e: e.matmul(banks[pb][:, 256:512], kT[:, h, kb * 128:(kb + 1) * 128],
                                                  qTg[1][:, h, :], start=True, stop=True),
                         r=[('kT', kb)] + QK, w=[bk(pb)], sig=(rep_ == PE_WARM_REPS - 1))

            st_mm(0)
            for n, (h, idx, kb) in enumerate(units):
                if n + 1 < len(units):
                    st_mm(n + 1)
                pb = PSB[n % 2]
                E = Et[n % 2]
                kE = ('Et', n % 2)
                S.op('act', lambda e: e.activation(E[:], banks[pb][:, :], AF.Exp, scale=0.125), r=[bk(pb)], w=[kE])
                own = kb - 16
                if own == 2 * g:
                    S.op('dve', lambda e: e.tensor_tensor(E[:], E[:], cstb[:, 128:640], ALU.mult), r=[kE, 'cstb'], w=[kE])
                elif own == 2 * g + 1:
                    S.op('dve', lambda e: e.tensor_tensor(E[:], E[:], cstb[:, 640:1152], ALU.mult), r=[kE, 'cstb'], w=[kE])
                for m in range(2):
                    for qb in range(2):
                        if qb == 0 and own == 2 * g + 1:
                            continue
                        last = (own == 2 * g) if qb == 0 else (own == 2 * g + 1)
                        a = 6 + m
                        o0 = qb * 130
                        S.op('pe', lambda e: e.matmul(banks[a][:, o0:o0 + 129], E[:, m * 256 + qb * 128: m * 256 + qb * 128 + 128],
                                                      vx[:, kb, h, 0:129], start=(idx == 0 and qb == 0), stop=last,
                                                      skip_group_check=True),
                             r=[kE, ('vx', kb)], w=[bk(a)], sig=(m == 1 and qb == 1))
                yield
                if idx == len(kbs) - 1:
                    den6 = banks[6][:, 0:260].rearrange("p (a c) -> p a c", c=130)[:, :, 128]
                    den7 = banks[7][:, 0:260].rearrange("p (a c) -> p a c", c=130)[:, :, 128]
                    S.op('dve', lambda e: e.reciprocal(ae[:, 2:4], den7), r=[bk(7)], w=[('ae', 1)])
                    S.op('dve', lambda e: e.tensor_scalar(ae[:, 2:4], ae[:, 2:4], lam_ap, None, ALU.mult), r=[('ae', 1), 'small'], w=[('ae', 1)])
                    S.op('dve', lambda e: e.reciprocal(ae[:, 0:2], den6), r=[bk(6)], w=[('ae', 0)])
                    for qb in range(2):
                        o0 = qb * 130
                        S.op('act', lambda e: e.activation(ofin[qb][:], banks[7][:, o0:o0 + 128], AF.Copy, scale=ae[:, 2 + qb:3 + qb]),
                             r=[bk(7), ('ae', 1)], w=[('ofin', qb)])
                        S.op('dve', lambda e: e.scalar_tensor_tensor(out=ofin[qb][:], in0=banks[6][:, o0:o0 + 128], scalar=ae[:, qb:qb + 1],
                                                                     in1=ofin[qb][:], op0=ALU.mult, op1=ALU.subtract),
                             r=[bk(6), ('ae', 0), ('ofin', qb)], w=[('ofin', qb)])
                    yield
                    for qb in range(2):
                        S.op('act', lambda e: e.activation(yat[qb][:, h * 128:(h + 1) * 128], ofin[qb][:], AF.Square,
                                                           accum_out=ae[:, 4 + qb:5 + qb]),
                             r=[('ofin', qb)], w=[('yat', qb), ('ae', 2 + qb)])
                    S.op('act', lambda e: e.activation(ae[:, 6:8], ae[:, 4:6], AF.Ln, bias=epsc[:, 0:1], scale=1.0 / 128),
                         r=[('ae', 2), ('ae', 3), 'consts'], w=[('ae', 4)])
                    S.op('act', lambda e: e.activation(ae[:, 6:8], ae[:, 6:8], AF.Exp, scale=-0.5), r=[('ae', 4)], w=[('ae', 4)])
                    for qb in range(2):
                        S.op('dve', lambda e: e.tensor_scalar(yat[qb][:, h * 128:(h + 1) * 128], ofin[qb][:], ae[:, 6 + qb:7 + qb],
                                                               float(1.0 - LAM_INIT), ALU.mult, ALU.mult),
                             r=[('ofin', qb), ('ae', 4)], w=[('yat', qb)])
                    yield
            for qb in range(2):
                t0 = (2 * g + qb) * 128
                transpose_to(yat[qb], ('yat', qb), yT[:, 4:8, t0:t0 + 128], ('yT', 2 * g + qb, 1),
                             scale_cols=cst[:, C_GY + 4:C_GY + 8], bank=4 + qb)
                yield

        NTT = n_pre + n_own
        def tt_info(n):
            own = n >= n_pre
            i = n - n_pre if own else n
            return own, i

        def stageA(n):
            own, i = tt_info(n)
            xsrc = xo if own else xp
            S.dma('sp', xt[:], xsrc[i * 128:(i + 1) * 128, :], w=['xt'])
            _norm_a1(xt[:], 'xt', u, 'u', ssb, 'ssb')
            for _ in range(A2_LAG):
                yield
            _norm_a2(C_G1, uT[n % 2], ('uT', n % 2), u, 'u', 0)
            yield

        def front(n):
            own, i = tt_info(n)
            blk = (16 + i) if own else i
            uT_t, kuT = uT[n % 2], ('uT', n % 2)
            Gc, kG = Gt[n % 3], ('Gt', n % 3)
            mv_t, kmv = mvx[n % 2], ('mvx', n % 2)
            qkT, kq = qkT2[n % 2], n % 2
            sigo = sigo2[n % 2]
            if own:
                GB, FMB, TMB = 0, (1, 1), (1, 1)
            else:
                GB, FMB, TMB = 1, (2, 3), (6, 7)
            need_q = own or i == NT - 1

            def T2():
                for half in ((0, 1) if need_q else (1,)):
                    pbk = FMB[half]
                    c0 = half * 4
                    for c in range(c0, c0 + 4):
                        col = (O_MQ + c * 128) if c < 4 else (O_MK + (c - 4) * 128)
                        for k in range(8):
                            S.op('pe', lambda e: e.matmul(banks[pbk][:, (c % 4) * 128:(c % 4 + 1) * 128], win[:, k, col:col + 128],
                                                          uT_t[:, k, :], start=(k == 0), stop=(k == 7)),
                                 r=[kuT, wkey(col)], w=[bk(pbk)], sig=(k == 7))
                    RK = [('raw', c) for c in range(c0, c0 + 4)]
                    if own:
                        S.op('dve', lambda e: e.tensor_copy(raw[:, c0:c0 + 4, 3:131], banks[pbk][:, :].rearrange("p (c t) -> p c t", c=4)),
                             r=[bk(pbk)], w=RK)
                    else:
                        S.op('act', lambda e: e.activation(raw[:, c0:c0 + 4, 3:131], banks[pbk][:, :].rearrange("p (c t) -> p c t", c=4),
                                                           AF.Copy), r=[bk(pbk)], w=RK)
                    conv = own or half == 1
                    if conv:
                        for c in range(c0, c0 + 4):
                            if own:
                                S.op('dve', lambda e: e.tensor_scalar(acc[:, c % 4, :], banks[pbk][:, (c % 4) * 128:(c % 4 + 1) * 128],
                                                                      cw[:, c, 3:4], cb[:, c:c + 1], ALU.mult, ALU.add),
                                     r=[bk(pbk), 'cst'], w=[('acc', c % 4)])
                            else:
                                S.op('act', lambda e: e.activation(acc[:, c % 4, :], banks[pbk][:, (c % 4) * 128:(c % 4 + 1) * 128],
                                                                   AF.Identity, bias=cb[:, c:c + 1], scale=cw[:, c, 3:4]),
                                     r=[bk(pbk), 'cst'], w=[('acc', c % 4)])
                    yield
                    if conv:
                        for c in range(c0, c0 + 4):
                            for j in range(3):
                                yield S.op('dve', lambda e: e.scalar_tensor_tensor(out=acc[:, c % 4, :], in0=raw[:, c, j:j + 128],
                                                                                   scalar=cw[:, c, j:j + 1], in1=acc[:, c % 4, :],
                                                                                   op0=ALU.mult, op1=ALU.add),
                                           r=[('raw', c), 'cst', ('acc', c % 4)], w=[('acc', c % 4)])
                    if own:
                        yield S.op('dve', lambda e: e.tensor_copy(raw[:, c0:c0 + 4, 0:3], raw[:, c0:c0 + 4, 128:131]), r=RK, w=RK)
                    else:
                        yield S.op('act', lambda e: e.activation(raw[:, c0:c0 + 4, 0:3], raw[:, c0:c0 + 4, 128:131], AF.Copy), r=RK, w=RK)
                    if conv:
                        tmp = raw[:, c0:c0 + 4, 3:131]
                        AK = [('acc', c) for c in range(4)]
                        S.op('act', lambda e: e.activation(tmp, acc[:, :, :], AF.Exp, scale=-1.0), r=AK + RK, w=RK)
                        S.op('act', lambda e: e.activation(tmp, tmp, AF.Ln, bias=epsc[:, 1:2]), r=RK + ['consts'], w=RK)
                        yield S.op('act', lambda e: e.activation(tmp, tmp, AF.Exp, scale=-1.0), r=RK, w=RK)
                        yield S.op('dve', lambda e: e.tensor_tensor(qkT[:, c0:c0 + 4, :], acc[:, :, :], tmp, ALU.mult),
                                   r=AK + RK, w=[('qkT', kq, half)])

            def T3():
                for gi, col in enumerate((O_MI, O_MF)):
                    for k in range(8):
                        S.op('pe', lambda e: e.matmul(banks[GB][0:4, gi * 128:(gi + 1) * 128], win[:, k, col:col + 4],
                                                      uT_t[:, k, :], start=(k == 0), stop=(k == 7)),
                             r=[kuT, wkey(col)], w=[bk(GB)], sig=(k == 7))
                po_ = 2 if own else 0
                G = lambda j: ('gf', j)
                C = lambda j: ('carry', j)
                S.op('act', lambda e: e.activation(gf[:, 0, :], banks[GB][0:4, 0:128], AF.Identity,
                                                   bias=gpar[:, po_ + 1:po_ + 2], scale=gpar[:, po_:po_ + 1]),
                     r=[bk(GB), 'gpar'], w=[G(0)])
                yield S.op('act', lambda e: e.activation(gf[:, 1, :], banks[GB][0:4, 128:256], AF.Exp, bias=gpar[:, 4:5], scale=-1.0),
                           r=[bk(GB), 'gpar'], w=[G(1)])
                yield S.op('act', lambda e: e.activation(gf[:, 1, :], gf[:, 1, :], AF.Ln, bias=epsc[0:4, 1:2]),
                           r=[G(1), 'consts'], w=[G(1)])
                if not own:
                    yield S.op('dve', lambda e: e.tensor_scalar(gf[:, 1, :], gf[:, 1, :], gpar[:, 5:6], None, ALU.mult),
                               r=[G(1), 'gpar'], w=[G(1)])
                yield S.op('dve', lambda e: e.tensor_tensor_scan(gf[:, 2, :], ones4[:], gf[:, 1, :], carry[:, 0:1], ALU.mult, ALU.add),
                           r=[G(1), 'consts4', C(0)], w=[G(2)])
                yield S.op('dve', lambda e: e.tensor_tensor(gf[:, 3, :], gf[:, 0, :], gf[:, 2, :], ALU.add), r=[G(0), G(2)], w=[G(3)])
                yield S.op('dve', lambda e: e.tensor_tensor_scan(gf[:, 4, :], ones4[:], gf[:, 3, :], carry[:, 1:2], ALU.mult, ALU.max),
                           r=[G(3), 'consts4', C(1)], w=[G(4)])
                S.op('dve', lambda e: e.tensor_scalar(carry[:, 2:3], carry[:, 1:2], -1.0, float(LNC), ALU.mult, ALU.add),
                     r=[C(1)], w=[C(2)])
                S.op('dve', lambda e: e.tensor_scalar(carry[:, 3:4], carry[:, 1:2], -1.0, None, ALU.mult), r=[C(1)], w=[C(3)])
                yield S.op('dve', lambda e: e.tensor_tensor(carry[:, 4:5], carry[:, 1:2], gf[:, 4, 127:128], ALU.subtract),
                           r=[C(1), G(4)], w=[C(4)])
                yield S.op('act', lambda e: e.activation(gf[:, 5, :], gf[:, 3, :], AF.Exp, bias=carry[:, 2:3]), r=[G(3), C(2)], w=[G(5)])
                yield S.op('act', lambda e: e.activation(gf[:, 6, :], gf[:, 2, :], AF.Exp, bias=carry[:, 3:4]), r=[G(2), C(3)], w=[G(6)])
                yield S.op('act', lambda e: e.activation(gf[:, 7, :], ones4[:], AF.Exp, bias=carry[:, 4:5], scale=0.0),
                           r=[C(4), 'consts4'], w=[G(7)])
                S.op('dve', lambda e: e.tensor_copy(carry[:, 0:1], gf[:, 2, 127:128]), r=[G(2), C(0)], w=[C(0)])
                yield S.op('dve', lambda e: e.tensor_copy(carry[:, 1:2], gf[:, 4, 127:128]), r=[G(4), C(1)], w=[C(1)])
                for a in range(3):
                    S.op('pe', lambda e: e.transpose(banks[GB][:, 256 + a * 4:256 + a * 4 + 4], gf[:, 5 + a, :], identf[0:4, 0:4]),
                         r=[G(5 + a), 'cst'], w=[bk(GB)], sig=(a == 2))
                yield S.op('dve', lambda e: e.tensor_copy(Gc, banks[GB][:, 256:268]), r=[bk(GB)], w=[kG])

            def T4():
                tb = [0]

                def nxt():
                    tb[0] += 1
                    return TMB[tb[0] % 2]
                b_ = nxt()
                proj_tm(uT_t, kuT, O_MV, b_)
                if own:
                    yield S.op('dve', lambda e: e.tensor_copy(mv_t[:, :, 0:128], banks[b_][:, :].rearrange("p (h d) -> p h d", h=4)),
                               r=[bk(b_)], w=[kmv])
                else:
                    yield S.op('act', lambda e: e.activation(mv_t[:, :, 0:128], banks[b_][:, :].rearrange("p (h d) -> p h d", h=4), AF.Copy),
                               r=[bk(b_)], w=[kmv])
                if own:
                    b_ = nxt()
                    proj_tm(uT_t, kuT, O_MO, b_)
                    S.op('act', lambda e: e.activation(qf[:], banks[b_][:, :], AF.Exp, scale=-1.0), r=[bk(b_)], w=['qf'])
                    S.op('act', lambda e: e.activation(qf[:], qf[:], AF.Ln, bias=epsc[:, 1:2]), r=['qf', 'consts'], w=['qf'])
                    yield S.op('act', lambda e: e.activation(sigo[:], qf[:], AF.Exp, scale=-1.0), r=['qf'], w=[('sigo', n % 2)])
                cs_k = cst[:, (C_CSO if own else C_CSP) + i * 16:(C_CSO if own else C_CSP) + i * 16 + 16]
                b_ = nxt()
                proj_tm(uT_t, kuT, O_AK, b_)
                yield from qk_prep(b_, C_KG, cs_k, qb16, 'qb16')
                transpose_to(qb16, 'qb16', kT[:, :, blk * 128:(blk + 1) * 128], ('kT', blk), bank=0)
                yield
                b_ = nxt()
                proj_tm(uT_t, kuT, O_AV, b_)
                if own:
                    yield S.op('dve', lambda e: e.tensor_copy(vx[:, blk, :, 0:128], banks[b_][:, :].rearrange("p (h d) -> p h d", h=4)),
                               r=[bk(b_)], w=[('vx', blk)])
                else:
                    yield S.op('act', lambda e: e.activation(vx[:, blk, :, 0:128], banks[b_][:, :].rearrange("p (h d) -> p h d", h=4), AF.Copy),
                               r=[bk(b_)], w=[('vx', blk)])
                if own:
                    b_ = nxt()
                    proj_tm(uT_t, kuT, O_AQ, b_)
                    yield from qk_prep(b_, C_QG, cs_k, qtok[i % 2], ('qtok', i % 2))

            yield from rr(T2(), T3(), T4())
            if own and i == 0 and 'qkT0' in dbg_d:
                S.dma('sp', dbg_d['qkT0'], qkT[:], r=[('qkT', kq, 0), ('qkT', kq, 1)], w=['dbg_qkT0'])
                S.finish('sp', ['dbg_qkT0'])

        def back(n):
            own, i = tt_info(n)
            Gc, kG = Gt[n % 3], ('Gt', n % 3)
            Gp, kGp = Gt[(n + 2) % 3], ('Gt', (n + 2) % 3)
            mv_t, kmv = mvx[n % 2], ('mvx', n % 2)
            qkT, kq = qkT2[n % 2], n % 2
            sigo = sigo2[n % 2]
            HB = (2, 3) if own else (4, 5)
            MK = [('ktl', 0), ('ktl', 1), ('stm', 0), ('stm', 1)]

            def head(h):
                e_ = h % 2
                kt_, kkt = ktl[e_], ('ktl', e_)
                st_, kst = stm[e_], ('stm', e_)
                hb = HB[e_]
                B = banks[hb]
                psb = B[:].bitcast(BF16)
                S.op('pe', lambda e: e.transpose(psb[:, 0:128], qkT[:, 4 + h, :], identb), r=[('qkT', kq, 1), 'cstb'], w=[bk(hb)], sig=True)
                yield
                yield S.op('dve', lambda e: e.tensor_scalar(kt_, psb[:, 0:128], Gc[:, h:h + 1], None, ALU.mult),
                           r=[bk(hb), kG], w=[kkt])
                if own:
                    S.op('pe', lambda e: e.matmul(B[:, 64:192], qkT[:, 4 + h, :], qkT[:, h, :], start=True, stop=True),
                         r=[('qkT', kq, 0), ('qkT', kq, 1)], w=[bk(hb)], sig=True)
                    yield
                    yield S.op('dve', lambda e: e.scalar_tensor_tensor(out=st_, in0=B[:, 64:192], scalar=Gc[:, h:h + 1],
                                                                       in1=tri, op0=ALU.mult, op1=ALU.mult),
                               r=[bk(hb), kG, 'cst'], w=[kst])
                    S.op('pe', lambda e: e.matmul(B[:, 192:321], qkT[:, h, :], Cbf[:, h, 0:129], start=True, stop=False),
                         r=[('qkT', kq, 0), 'Cbf'], w=[bk(hb)], sig=False)
                    S.op('pe', lambda e: e.matmul(B[:, 192:321], st_, mv_t[:, h, 0:129], start=False, stop=True),
                         r=[kst, kmv], w=[bk(hb)], sig=True)
                    yield
                S.op('pe', lambda e: e.matmul(B[:, 336:465], kt_, mv_t[:, h, 0:129], start=True, stop=True),
                     r=[kkt, kmv], w=[bk(hb)], sig=True)
                yield
                yield S.op('dve', lambda e: e.scalar_tensor_tensor(out=Pst[:, h, 0:129], in0=Pst[:, h, 0:129], scalar=Gp[:, 8 + h:9 + h],
                                                                   in1=B[:, 336:465], op0=ALU.mult, op1=ALU.add),
                           r=[bk(hb), kGp, ('Pst', h)], w=[('Pst', h)])

            def pair_epilogue(p):
                c = 8 * p
                m = ml[:, 16 * p:16 * p + 16]
                for e_ in range(2):
                    S.op('dve', lambda e: e.tensor_copy(m[:, e_:e_ + 1], banks[HB[e_]][:, 320:321]), r=[bk(HB[e_])], w=[('ml', p, 0)])
                S.op('dve', lambda e: e.scalar_tensor_tensor(out=m[:, 2:4], in0=m[:, 0:2], scalar=-1.0, in1=m[:, 0:2],
                                                             op0=ALU.mult, op1=ALU.max), r=[('ml', p, 0)], w=[('ml', p, 1)])
                S.op('dve', lambda e: e.tensor_tensor(m[:, 2:4], m[:, 2:4], Gc[:, 4 + 2 * p:6 + 2 * p], ALU.max),
                     r=[('ml', p, 1), kG], w=[('ml', p, 1)])
                yield S.op('dve', lambda e: e.reciprocal(m[:, 4:6], m[:, 2:4]), r=[('ml', p, 1)], w=[('ml', p, 2)])
                for e_ in range(2):
                    B = banks[HB[e_]]
                    h_ = 2 * p + e_
                    yield S.op('act', lambda e: e.activation(mls[:, h_ * 128:(h_ + 1) * 128], B[:, 192:320], AF.Square,
                                                             accum_out=m[:, 6 + e_:7 + e_]),
                               r=[bk(HB[e_])], w=[MK[h_], ('ml', p, 3 + e_)])
                S.op('dve', lambda e: e.tensor_tensor(m[:, 8:10], m[:, 4:6], m[:, 4:6], ALU.mult), r=[('ml', p, 2)], w=[('ml', p, 5)])
                yield S.op('dve', lambda e: e.tensor_tensor(m[:, 8:10], m[:, 8:10], m[:, 6:8], ALU.mult),
                           r=[('ml', p, 5), ('ml', p, 3), ('ml', p, 4)], w=[('ml', p, 5)])
                S.op('act', lambda e: e.activation(m[:, 10:12], m[:, 8:10], AF.Ln, bias=epsc[:, 0:1], scale=1.0 / 128),
                     r=[('ml', p, 5), 'consts'], w=[('ml', p, 6)])
                yield S.op('act', lambda e: e.activation(m[:, 10:12], m[:, 10:12], AF.Exp, scale=-0.5), r=[('ml', p, 6)], w=[('ml', p, 6)])
                yield S.op('dve', lambda e: e.tensor_tensor(m[:, 12:14], m[:, 10:12], m[:, 4:6], ALU.mult),
                           r=[('ml', p, 6), ('ml', p, 2)], w=[('ml', p, 7)])
                for e_ in range(2):
                    h = 2 * p + e_
                    B = banks[HB[e_]]
                    yield S.op('dve', lambda e: e.scalar_tensor_tensor(out=mls[:, h * 128:(h + 1) * 128], in0=B[:, 192:320],
                                                                       scalar=m[:, 12 + e_:13 + e_], in1=sigo[:, h * 128:(h + 1) * 128],
                                                                       op0=ALU.mult, op1=ALU.mult),
                               r=[bk(HB[e_]), ('ml', p, 7), ('sigo', n % 2)], w=[MK[h]])
                for e_ in range(2):
                    h = 2 * p + e_
                    B = banks[HB[e_]]
                    psb = B[:].bitcast(BF16)
                    S.op('pe', lambda e: e.transpose(psb[:, 0:128], mls[:, h * 128:(h + 1) * 128], identb),
                         r=[MK[h], 'cstb'], w=[bk(HB[e_])], sig=True)
                    yield
                    yield S.op('dve', lambda e: e.tensor_scalar(yT[:, h, i * 128:(i + 1) * 128], psb[:, 0:128],
                                                                cst[:, C_GY + h:C_GY + h + 1], None, ALU.mult),
                               r=[bk(HB[e_]), 'cst'], w=[('yT', i, 0)])

            for p in range(2):
                yield from rr(head(2 * p), head(2 * p + 1))
                if own:
                    yield from pair_epilogue(p)
            if own or i == NT - 1:
                yield S.op('dve', lambda e: e.tensor_tensor(Cbf[:, :, 0:129], Pst[:, :, 0:129],
                                                            Gc[:, 8:12].unsqueeze(2).to_broadcast([128, 4, 129]), ALU.mult),
                           r=[('Pst', h) for h in range(4)] + [kG], w=['Cbf'])

        STEPS = {'main': 0, 'side': 0}

        def wrr(main, side, per):
            credit = 0.0
            for _ in main:
                STEPS['main'] += 1
                credit += per
                while side is not None and credit >= 1.0:
                    credit -= 1.0
                    STEPS['side'] += 1
                    try:
                        next(side)
                    except StopIteration:
                        side = None
            return side

        def dump(name, ap, keys):
            if name in dbg_d:
                S.dma('sp', dbg_d[name], ap, r=keys, w=['dbg_' + name])
                S.finish('sp', ['dbg_' + name])

        side = None
        side_rate = [1.0]
        main_per_tt = [60]
        if NTT > 0:
            run(stageA(0))
            run(rr(stageA(1) if NTT > 1 else None, front(0)))
        for n in range(NTT):
            own, i = tt_info(n)
            def _late(gen, rounds):
                for _ in range(rounds):
                    yield
                yield from gen
            nxt_front = rr(_late(stageA(n + 2), A_LAG) if n + 2 < NTT else None, front(n + 1) if n + 1 < NTT else None)
            if own and i % 2 == 1:
                if side is not None:
                    run(side)
                side = attention_group(i // 2)
                for _ in range(4):
                    next(side)
                g_ = i // 2
                side_total = (18 + 2 * g_) * 4 + 4 * 4 + 2
                side_rate[0] = 1.0 * side_total / (2.0 * max(main_per_tt[0], 1))
            m0 = STEPS['main']
            side = wrr(rr(nxt_front, back(n)), side, side_rate[0])
            if own:
                main_per_tt[0] = STEPS['main'] - m0
            if own and i == 3 and 'yT01' in dbg_d:
                if side is not None:
                    run(side)
                    side = None
                dump('yT01', yT[:, :, 0:256], [('yT', a, b) for a in range(2) for b in range(2)])
        if side is not None:
            run(side)

        if 'yT' in dbg_d:
            S.dma('sp', dbg_d['yT'], yT[:], r=[('yT', i, j) for i in range(16) for j in range(2)], w=['dbg_yT'])
            S.finish('sp', ['dbg_yT'])

        ab.close()
        if 'C' not in phases:
            S.barrier()
            return nc

        hres = sb("hres", [128, 16, 1024])
        wout = sb("wout", [128, 8, 1024], BF16)
        wp = sb("wp", [128, 2, 1024], BF16)
        cd = ExitStack()

        def sc(name, shape, dt=F32):
            return cd.enter_context(nc.sbuf_tensor("t_" + name, list(shape), dt))

        u2T = sc("u2T", [128, 8, 2048], BF16)
        u = sc("u_c", [128, 1024], BF16)
        ssb = sc("ssb_c", [128, 8])
        wupb = [sc("wup%d" % i, [128, 8, 512], BF16) for i in range(2)]
        wdnb = [sc("wdn%d" % i, [128, 4, 1024], BF16) for i in range(2)]
        hidT = [sc("hidT%d" % i, [128, 4, 512], BF16) for i in range(2)]
        relu_t = sc("relu_t", [128, 512])
        w_out_v = w_out.rearrange("(k p) n -> p k n", p=128)
        S.barrier()
        for k in range(8):
            S.dma('pool', wout[:, k, :], w_out_v[:, k, :], w=[('wout', k)])
        w_up_v = w_up.rearrange("(k p) n -> p k n", p=128)
        w_dn_v = w_down.rearrange("(k p) n -> p k n", p=128)

        def load_ffn_w(j):
            b = j % 2
            S.dma('pool', wupb[b][:], w_up_v[:, :, j * 512:(j + 1) * 512], w=[('wup', b)])
            S.dma('pool', wdnb[b][:], w_dn_v[:, j * 4:(j + 1) * 4, :], w=[('wdn', b)])

        u_c2 = sc("u_c2", [128, 1024], BF16)
        ssb_c2 = sc("ssb_c2", [128, 8])
        CSCR = [(u, 'u_c', ssb, 'ssb_c', 0), (u_c2, 'u_c2', ssb_c2, 'ssb_c2', 1)]

        def c_stream(i):
            S.dma('sp', hres[:, i, :], xo[i * 128:(i + 1) * 128, :], w=[('h', i)])
            for half in range(2):
                pbk = 2 + (i * 2 + half) % 6
                for k in range(8):
                    S.op('pe', lambda e: e.matmul(banks[pbk][:, :], yT[:, k, i * 128:(i + 1) * 128],
                                                  wout[:, k, half * 512:(half + 1) * 512], start=(k == 0), stop=(k == 7)),
                         r=[('yT', i, 0), ('yT', i, 1), ('wout', k)], w=[bk(pbk)], sig=k == 7)
                yield S.op('dve', lambda e: e.tensor_tensor(hres[:, i, half * 512:(half + 1) * 512], banks[pbk][:, :],
                                                            hres[:, i, half * 512:(half + 1) * 512], ALU.add),
                           r=[bk(pbk), ('h', i)], w=[('h', i)])
            if i == 0:
                load_ffn_w(0)
                load_ffn_w(1)
            sc_ = CSCR[i % 2]
            _norm_a1(hres[:, i, :], ('h', i), sc_[0], sc_[1], sc_[2], sc_[3])
            yield
            yield
            _norm_a2(C_G2, u2T[:, :, i * 128:(i + 1) * 128], ('u2T', i), sc_[0], sc_[1], sc_[4])
            yield

        def lagged0(gens, lag):
            live = []
            pending = list(gens)
            rnd = 0
            while live or pending:
                if pending and rnd % lag == 0:
                    live.append(pending.pop(0))
                for g in list(live):
                    try:
                        next(g)
                    except StopIteration:
                        live.remove(g)
                rnd += 1

        lagged0([c_stream(i) for i in range(NT)], 1)
        if 'h1' in dbg_d:
            S.dma('sp', dbg_d['h1'], hres[:], r=[('h', i) for i in range(16)], w=['dbg_h1'])
            S.finish('sp', ['dbg_h1'])

        nacc = 0
        for j in range(8):
            b = j % 2
            for tg in range(4):
                hb = (j * 4 + tg) % 2
                for m in range(4):
                    pbk = 2 + nacc % 6
                    nacc += 1
                    for k in range(8):
                        S.op('pe', lambda e: e.matmul(banks[pbk][:, :], wupb[b][:, k, m * 128:(m + 1) * 128],
                                                      u2T[:, k, tg * 512:(tg + 1) * 512], start=(k == 0), stop=(k == 7)),
                             r=[('wup', b)] + [('u2T', tg * 4 + q) for q in range(4)], w=[bk(pbk)], sig=k == 7)
                    S.op('act', lambda e: e.activation(relu_t[:], banks[pbk][:, :], AF.Relu), r=[bk(pbk)], w=['relu_t'])
                    S.op('act', lambda e: e.activation(hidT[hb][:, m, :], relu_t[:], AF.Square),
                         r=['relu_t'], w=[('hidT', hb)])
                for q in range(4):
                    i = tg * 4 + q
                    for half in range(2):
                        pbk = 2 + nacc % 6
                        nacc += 1
                        for m in range(4):
                            S.op('pe', lambda e: e.matmul(banks[pbk][:, :], hidT[hb][:, m, q * 128:(q + 1) * 128],
                                                          wdnb[b][:, m, half * 512:(half + 1) * 512], start=(m == 0), stop=(m == 3)),
                                 r=[('hidT', hb), ('wdn', b)], w=[bk(pbk)], sig=m == 3)
                        S.op('dve', lambda e: e.tensor_tensor(hres[:, i, half * 512:(half + 1) * 512], banks[pbk][:, :],
                                                              hres[:, i, half * 512:(half + 1) * 512], ALU.add),
                             r=[bk(pbk), ('h', i)], w=[('h', i)])
            if j + 2 < 8:
                load_ffn_w(j + 2)
            if j == 5:
                w_g_v = w_gate.rearrange("(k p) n -> p k n", p=128)
                w_p_v = w_ple.rearrange("(k p) n -> p k n", p=128)
                for k in range(8):
                    S.dma('pool', wout[:, k, :], w_g_v[:, k, :], w=[('wout', k)])
                S.dma('pool', wp[:], w_p_v, w=['wp'])
        if 'h2' in dbg_d:
            S.dma('sp', dbg_d['h2'], hres[:], r=[('h', i) for i in range(16)], w=['dbg_h2'])
            S.finish('sp', ['dbg_h2'])
        S.barrier()
        cd.close()

        wg = wout
        NS = 4
        u_e = [sb("u_e%d" % i, [128, 1024], BF16) for i in range(NS)]
        ssb_e = [sb("ssb_e%d" % i, [128, 8]) for i in range(NS)]
        u3T = [sb("u3T%d" % i, [128, 8, 128], BF16) for i in range(NS)]
        pt = [sb("pt%d" % i, [128, 256], BF16) for i in range(NS)]
        pT = [sb("pT%d" % i, [128, 2, 128], BF16) for i in range(NS)]
        gsb = [sb("gsb%d" % i, [128, 1024]) for i in range(NS)]

        def pe_stream(i):
            b = i % NS
            tb = b % 2
            S.dma('pool', pt[b][:], po[i * 128:(i + 1) * 128, :], w=[('pt', b)])
            norm_to_uT(hres[:, i, :], ('h', i), C_G3, u3T[b], ('u3T', b), scr=(u_e[b], ('u_e', b), ssb_e[b], ('ssb_e', b), tb))
            yield
            psb = banks[tb][:].bitcast(BF16)
            for k in range(2):
                S.op('pe', lambda e: e.transpose(psb[:, k * 128:(k + 1) * 128], pt[b][:, k * 128:(k + 1) * 128], identb),
                     r=[('pt', b), 'cstb'], w=[bk(tb)], sig=k == 1)
            yield S.op('dve', lambda e: e.tensor_copy(pT[b][:], psb[:, 0:256].rearrange("p (k t) -> p k t", k=2)),
                       r=[bk(tb)], w=[('pT', b)])
            for half in range(2):
                pg = 2 + b
                pe_ = 6 + (b % 2)
                for k in range(8):
                    S.op('pe', lambda e: e.matmul(banks[pg][:, :], u3T[b][:, k, :], wg[:, k, half * 512:(half + 1) * 512],
                                                  start=(k == 0), stop=(k == 7)), r=[('u3T', b), ('wout', k)], w=[bk(pg)], sig=k == 7)
                sl = slice(half * 512, (half + 1) * 512)
                yield S.op('act', lambda e: e.activation(gsb[b][:, sl], banks[pg][:, :], AF.Sigmoid), r=[bk(pg)], w=[('gsb', b, half)])
                for k in range(2):
                    S.op('pe', lambda e: e.matmul(banks[pe_][:, :], pT[b][:, k, :], wp[:, k, half * 512:(half + 1) * 512],
                                                  start=(k == 0), stop=(k == 1)), r=[('pT', b), 'wp'], w=[bk(pe_)], sig=k == 1)
                yield S.op('dve', lambda e: e.tensor_tensor(gsb[b][:, sl], banks[pe_][:, :], gsb[b][:, sl], ALU.mult),
                           r=[bk(pe_), ('gsb', b, half)], w=[('gsb', b, half)])
                yield S.op('dve', lambda e: e.tensor_tensor(gsb[b][:, sl], gsb[b][:, sl], hres[:, i, sl], ALU.add),
                           r=[('gsb', b, half), ('h', i)], w=[('gsb', b, half)])
            S.dma('sp', out_d[i * 128:(i + 1) * 128, :], gsb[b][:], r=[('gsb', b, 0), ('gsb', b, 1)],
                  w=[('out', i)])
            yield

        def lagged(gens, lag):
            live = []
            pending = list(gens)
            rnd = 0
            while live or pending:
                if pending and rnd % lag == 0:
                    live.append(pending.pop(0))
                for g in list(live):
                    try:
                        next(g)
                    except StopIteration:
                        live.remove(g)
                rnd += 1

        lagged([pe_stream(i) for i in range(NT)], 2)
        S.finish('sp', [('out', i) for i in range(NT)])
    return nc


def _consts(core, inp):
    c = np.zeros((128, C_TOT), np.float32)

    def pk(v):
        return np.asarray(v, np.float32).reshape(-1, 128).T

    c[:, C_G1:C_G1 + 8] = pk(inp['attn_norm_g'][0])
    c[:, C_G2:C_G2 + 8] = pk(inp['mlp_norm_g'][0])
    c[:, C_G3:C_G3 + 8] = pk(inp['ple_norm_g'][0])
    c[:, C_GY:C_GY + 4] = pk(inp['mlstm_norm_g'][0])
    c[:, C_GY + 4:C_GY + 8] = pk(inp['attn_sub_norm_g'][0])
    cw = np.asarray(inp['conv_w'][0], np.float32)
    c[:, C_CW:C_CW + 32] = cw.T.reshape(8, 128, 4).transpose(1, 0, 2).reshape(128, 32)
    c[:, C_CB:C_CB + 8] = pk(inp['conv_b'][0])
    c[:, C_QG:C_QG + 64] = np.asarray(inp['q_norm_g'][0], np.float32)[None, :]
    c[:, C_KG:C_KG + 64] = np.asarray(inp['k_norm_g'][0], np.float32)[None, :]
    for j, nm in enumerate(('lambda_q1', 'lambda_k1', 'lambda_q2', 'lambda_k2')):
        c[:, C_LAM + j * 64:C_LAM + (j + 1) * 64] = np.asarray(inp[nm][0], np.float32)[None, :]
    odd = core % 2
    c[:, C_FL + 0] = 1.0 if odd else 0.0
    c[:, C_FL + 1] = 0.0 if odd else NEG
    c[:, C_FL + 2] = 0.0 if odd else NEG
    inv = (np.float32(500000.0) ** (-np.arange(0, 16, 2, dtype=np.float32) / np.float32(16))).astype(np.float32)
    for off, base in ((C_CSO, odd * 2048), (C_CSP, 0)):
        pos = (base + np.arange(2048, dtype=np.float32)).astype(np.float32)
        ang = (pos[:, None] * inv[None, :]).astype(np.float32)
        cs = np.concatenate([np.cos(ang), np.sin(ang)], axis=1).astype(np.float32)
        c[:, off:off + 256] = cs.reshape(16, 128, 16).transpose(1, 0, 2).reshape(128, 256)
    c[:, C_ID:C_ID + 128] = np.eye(128, dtype=np.float32)
    c[:, C_TRI:C_TRI + 128] = np.triu(np.ones((128, 128), np.float32))
    c[0:4, C_GB] = np.asarray(inp['igate_b'][0], np.float32)
    c[0:4, C_GB + 1] = np.asarray(inp['fgate_b'][0], np.float32)
    return c


def _constb():
    b = np.zeros((128, 128 + 1024), np.float32)
    b[:, 0:128] = np.eye(128, dtype=np.float32)
    tri = np.triu(np.ones((128, 128), np.float32))
    for m in range(2):
        b[:, 128 + m * 256:128 + m * 256 + 128] = tri
        b[:, 128 + m * 256 + 128:128 + m * 256 + 256] = 1.0
        b[:, 640 + m * 256:640 + m * 256 + 128] = 0.0
        b[:, 640 + m * 256 + 128:640 + m * 256 + 256] = tri
    return b


def make_in_maps(inp):
    x = np.asarray(inp['x'], np.float32)
    p = np.asarray(inp['p'], np.float32)
    shared = {
        'w_in': np.ascontiguousarray(inp['w_in'][0], dtype=np.float32),
        'w_out': np.ascontiguousarray(inp['w_out'][0], dtype=np.float32),
        'w_up': np.ascontiguousarray(inp['w_up'][0], dtype=np.float32),
        'w_down': np.ascontiguousarray(inp['w_down'][0], dtype=np.float32),
        'w_gate': np.ascontiguousarray(inp['w_ple_gate'][0], dtype=np.float32),
        'w_ple': np.ascontiguousarray(inp['w_ple_proj'][0], dtype=np.float32),
        'cstb': _constb(),
    }
    zeros = np.zeros((2048, 1024), np.float32)
    maps = []
    for c in range(8):
        b, hf = c // 2, c % 2
        m = dict(shared)
        m['xo'] = np.ascontiguousarray(x[b, hf * 2048:(hf + 1) * 2048])
        m['xp'] = np.ascontiguousarray(x[b, 0:2048]) if hf else zeros
        m['po'] = np.ascontiguousarray(p[0, b, hf * 2048:(hf + 1) * 2048])
        m['cst'] = _consts(c, inp)
        maps.append(m)
    return maps


def kernel(**inputs):
    nc = build()
    maps = make_in_maps(inputs)
    res = run_bass_kernel_spmd(nc, maps, core_ids=list(range(8)))
    out = np.zeros((4, 4096, 1024), np.float32)
    for c in range(8):
        out[c // 2, (c % 2) * 2048:(c % 2 + 1) * 2048] = res.results[c]['out']
    return out
```

```python
import math
from contextlib import ExitStack

import numpy as np
import concourse.bass as bass
import concourse.mybir as mybir
from concourse.bass_utils import run_bass_kernel_spmd

F32 = mybir.dt.float32
BF16 = mybir.dt.bfloat16
AF = mybir.ActivationFunctionType
ALU = mybir.AluOpType
AX = mybir.AxisListType

SAME_ENGINE_SYNC = True
A_LAG = 1
A2_LAG = 6
PE_WARM_REPS = 1
EPS = 1e-6
NT = 16
NEG = -30000.0
LAM_INIT = 0.8 - 0.6 * math.exp(0.0)
LNC = math.log(128 ** -0.5)

O_MQ, O_MK, O_MV, O_MO, O_MI, O_MF, O_AQ, O_AK, O_AV = 0, 512, 1024, 1536, 2048, 2052, 2056, 2568, 3080

C_G1, C_G2, C_G3, C_GY, C_CW, C_CB, C_QG, C_KG, C_LAM, C_FL, C_CSO, C_CSP, C_ID, C_TRI, C_GB = (
    0, 8, 16, 24, 32, 64, 72, 136, 200, 456, 460, 716, 972, 1100, 1228)
C_TOT = 1230


class _Eng:
    def __init__(self, name, h, sem):
        self.name, self.h, self.sem = name, h, sem
        self.n = 0
        self.count = 0
        self.incs = []
        self.last = None
        self.last_seq = 0
        self.waited = {}
        self.dsems = []
        self.dcnt = []
        self.dnext = 0


class Sched:
    def __init__(self, nc, stack, ndma):
        self.nc = nc
        hs = {'pe': nc.tensor, 'act': nc.scalar, 'dve': nc.vector, 'pool': nc.gpsimd, 'sp': nc.sync}
        self.e = {}
        for k, h in hs.items():
            sem = stack.enter_context(nc.semaphore("s_" + k))
            self.e[k] = _Eng(k, h, sem)
            for i in range(ndma.get(k, 0)):
                self.e[k].dsems.append(stack.enter_context(nc.semaphore("d_%s%d" % (k, i))))
                self.e[k].dcnt.append(0)
        self.st = {}

    def _target(self, dep):
        if dep[0] == 'd':
            return dep[1], dep[2]
        p = self.e[dep[1]]
        seq = dep[2]
        found = None
        for (s, c) in reversed(p.incs):
            if s >= seq:
                found = c
            else:
                break
        if found is None:
            p.count += 1
            p.last.then_inc(p.sem, 1)
            p.incs.append((p.last_seq, p.count))
            found = p.count
        return p.sem, found

    def _wait(self, eng, deps):
        E = self.e[eng]
        for dep in deps:
            if dep is None:
                continue
            if dep[0] == 'c' and dep[1] == eng and (eng == 'pe' or not SAME_ENGINE_SYNC):
                continue
            sem, val = self._target(dep)
            key = id(sem)
            if E.waited.get(key, 0) >= val:
                continue
            E.h.wait_ge(sem, val)
            E.waited[key] = val

    def _deps(self, r, w, eng=None):
        deps = []
        for k in r:
            s = self.st.get(k)
            if s is not None:
                deps.append(s[0])
                if isinstance(k, tuple) and k[0] == 'bank':
                    deps.extend(d for d in s[1].values() if not (d[0] == 'c' and d[1] == eng))
        for k in w:
            s = self.st.get(k)
            if s is not None:
                deps.append(s[0])
                deps.extend(s[1].values())
        return deps

    def _record(self, dep, r, w):
        rk = (dep[0], dep[1] if dep[0] == 'c' else id(dep[1]))
        for k in r:
            s = self.st.setdefault(k, [None, {}])
            s[1][rk] = dep
        for k in w:
            self.st[k] = [dep, {}]

    def op(self, eng, fn, r=(), w=(), sig=None):
        E = self.e[eng]
        self._wait(eng, self._deps(r, w, eng))
        inst = fn(E.h)
        E.n += 1
        E.last = inst
        E.last_seq = E.n
        if sig or (sig is None and eng != 'pe'):
            E.count += 1
            inst.then_inc(E.sem, 1)
            E.incs.append((E.n, E.count))
        self._record(('c', eng, E.n), r, w)
        return inst

    def dma(self, q, out, in_, r=(), w=(), **kw):
        E = self.e[q]
        self._wait(q, self._deps(r, w))
        i = E.dnext
        E.dnext = (i + 1) % len(E.dsems)
        sem = E.dsems[i]
        if E.dcnt[i] > 0:
            key = id(sem)
            if E.waited.get(key, 0) < E.dcnt[i]:
                E.h.wait_ge(sem, E.dcnt[i])
                E.waited[key] = E.dcnt[i]
        E.dcnt[i] += 16
        E.h.dma_start(out=out, in_=in_, **kw).then_inc(sem, 16)
        dep = ('d', sem, E.dcnt[i])
        self._record(dep, r, w)
        return dep

    def barrier(self):
        deps = []
        for s in self.st.values():
            if s[0] is not None:
                deps.append(s[0])
            deps.extend(s[1].values())
        for eng in self.e:
            self._wait(eng, deps)

    def finish(self, eng, keys):
        self._wait(eng, [self.st[k][0] for k in keys if k in self.st])


def build(dbg=(), n_pre=NT, n_own=NT, phases='CDE'):
    nc = bass.Bass("TRN2", target_bir_lowering=False)

    def din(name, shape):
        return nc.dram_tensor(name, list(shape), F32, kind="ExternalInput").ap()

    xo = din("xo", [2048, 1024])
    xp = din("xp", [2048, 1024])
    po = din("po", [2048, 256])
    w_in = din("w_in", [1024, 3592])
    w_out = din("w_out", [1024, 1024])
    w_up = din("w_up", [1024, 4096])
    w_down = din("w_down", [4096, 1024])
    w_gate = din("w_gate", [1024, 1024])
    w_ple = din("w_ple", [256, 1024])
    cst_d = din("cst", [128, C_TOT])
    cstb_d = din("cstb", [128, 128 + 1024])
    out_d = nc.dram_tensor("out", [2048, 1024], F32, kind="ExternalOutput").ap()
    dbg_d = {}
    for name, shape, dt in dbg:
        dbg_d[name] = nc.dram_tensor("dbg_" + name, list(shape), dt, kind="ExternalOutput").ap()

    with ExitStack() as st:
        S = Sched(nc, st, {'sp': 12, 'pool': 8})

        def sb(name, shape, dt=F32):
            return st.enter_context(nc.sbuf_tensor("t_" + name, list(shape), dt))

        def freed(name, shape, dt=F32):
            return nc.sbuf_tensor(name, list(shape), dt)

        cst = sb("cst", [128, C_TOT])
        cstb = sb("cstb", [128, 128 + 1024], BF16)
        identb = cstb[:, 0:128]
        identf = cst[:, C_ID:C_ID + 128]
        tri = cst[:, C_TRI:C_TRI + 128]
        banks = [st.enter_context(nc.psum_tensor("bank%d" % i, [128, 512], F32)) for i in range(8)]
        small = sb("small", [128, 64])
        block = st.enter_context(nc.Block())

        S.dma('sp', cst[:], cst_d, w=['cst'])
        S.dma('pool', cstb[:], cstb_d, w=['cstb'])

        def bk(i):
            return ('bank', i)

        def rstd_from_ss(ss_ap, out_ap, n, key_ss, key_out, mul=None):
            S.op('act', lambda e: e.activation(out_ap, ss_ap, AF.Ln, bias=epsc[:ss_ap.shape[0], 0:1], scale=1.0 / n),
                 r=[key_ss, 'consts'], w=[key_out])
            S.op('act', lambda e: e.activation(out_ap, out_ap, AF.Exp, scale=-0.5), r=[key_out], w=[key_out])
            if mul is not None:
                S.op('dve', lambda e: e.tensor_scalar(out_ap, out_ap, float(mul), None, ALU.mult), r=[key_out], w=[key_out])

        epsc = sb("epsc", [128, 4])
        ones4 = sb("ones4", [4, 128])
        S.op('pool', lambda e: e.memset(epsc[:, 0:1], EPS), w=['consts'])
        S.op('pool', lambda e: e.memset(epsc[:, 1:2], 1.0), w=['consts'])
        S.op('pool', lambda e: e.memset(ones4[:], 1.0), w=['consts4'])

        lamv = cst[:, C_LAM:C_LAM + 256]
        lj = sb("lj", [128, 64])
        S.op('dve', lambda e: e.scalar_tensor_tensor(out=lj[:], in0=lamv[:, 0:64], scalar=1.0, in1=lamv[:, 64:128],
                                                     op0=ALU.mult, op1=ALU.mult, accum_out=small[:, 0:1]),
             r=['cst'], w=['lj', 'small'])
        S.op('dve', lambda e: e.scalar_tensor_tensor(out=lj[:], in0=lamv[:, 128:192], scalar=1.0, in1=lamv[:, 192:256],
                                                     op0=ALU.mult, op1=ALU.mult, accum_out=small[:, 1:2]),
             r=['cst', 'lj'], w=['lj', 'small'])
        S.op('act', lambda e: e.activation(small[:, 2:4], small[:, 0:2], AF.Exp), r=['small'], w=['small'])
        S.op('dve', lambda e: e.tensor_tensor(small[:, 4:5], small[:, 2:3], small[:, 3:4], ALU.subtract), r=['small'], w=['small'])
        S.op('dve', lambda e: e.tensor_scalar(small[:, 5:6], small[:, 4:5], float(LAM_INIT), None, ALU.add), r=['small'], w=['small'])
        lam_ap = small[:, 5:6]
        gpar = sb("gpar", [4, 8])
        gb = cst[0:4, C_GB:C_GB + 2]
        fl = cst[0:4, C_FL:C_FL + 4]
        S.op('dve', lambda e: e.tensor_copy(gpar[:, 0:1], fl[:, 0:1]), r=['cst'], w=['gpar'])
        S.op('dve', lambda e: e.scalar_tensor_tensor(out=gpar[:, 1:2], in0=gb[:, 0:1], scalar=fl[:, 0:1], in1=fl[:, 1:2],
                                                     op0=ALU.mult, op1=ALU.add), r=['cst', 'gpar'], w=['gpar'])
        S.op('pool', lambda e: e.memset(gpar[:, 2:3], 1.0), r=['gpar'], w=['gpar'])
        S.op('dve', lambda e: e.tensor_copy(gpar[:, 3:4], gb[:, 0:1]), r=['cst', 'gpar'], w=['gpar'])
        S.op('dve', lambda e: e.tensor_scalar(gpar[:, 4:5], gb[:, 1:2], -1.0, None, ALU.mult), r=['cst', 'gpar'], w=['gpar'])
        S.op('dve', lambda e: e.tensor_copy(gpar[:, 5:6], fl[:, 0:1]), r=['cst', 'gpar'], w=['gpar'])
        pbias = cst[:, C_FL + 2:C_FL + 3]

        yT = sb("yT", [128, 8, 2048], BF16)
        ab = ExitStack()

        def sa(name, shape, dt=F32):
            return ab.enter_context(nc.sbuf_tensor("t_" + name, list(shape), dt))

        win = sa("win", [128, 8, 3592], BF16)
        kT = sa("kT", [128, 4, 4096], BF16)
        vx = sa("vx", [128, 32, 4, 130], BF16)
        xt = sa("xt", [128, 1024])
        u = sa("u", [128, 1024], BF16)
        uT = [sa("uT%d" % i, [128, 8, 128], BF16) for i in range(2)]
        ssb = sa("ssb", [128, 8])
        raw = sa("raw", [128, 8, 131])
        acc = sa("acc", [128, 4, 128])
        qkT2 = [sa("qkT%d" % i, [128, 8, 128], BF16) for i in range(2)]
        gf = sa("gf", [4, 8, 128])
        carry = sa("carry", [4, 8])
        Gt = [small[:, 16 + 12 * i:28 + 12 * i] for i in range(3)]
        mvx = [sa("mvx%d" % i, [128, 4, 130], BF16) for i in range(2)]
        sigo2 = [sa("sigo%d" % i, [128, 512], BF16) for i in range(2)]
        mls = sa("mls", [128, 512], BF16)
        ktl = [mls[:, i * 128:(i + 1) * 128] for i in range(2)]
        stm = [mls[:, 256 + i * 128:256 + (i + 1) * 128] for i in range(2)]
        Pst = sa("Pst", [128, 4, 130])
        Cbf = sa("Cbf", [128, 4, 130], BF16)
        ml = sa("ml", [128, 32])
        qf = sa("qf", [128, 512])
        qb16 = sa("qb16", [128, 512], BF16)
        qtok = [sa("qtok%d" % i, [128, 512], BF16) for i in range(2)]
        qTg = [sa("qTg%d" % i, [128, 4, 256], BF16) for i in range(2)]
        Et = [sa("Et%d" % i, [128, 512], BF16) for i in range(2)]
        ofin = [sa("ofin%d" % i, [128, 128]) for i in range(2)]
        yat = [sa("yat%d" % i, [128, 512], BF16) for i in range(2)]
        al = sa("al", [128, 16])
        ae = small[:, 8:16]

        w_in_v = w_in.rearrange("(k p) n -> p k n", p=128)
        WGRP = [(O_MK, O_MV), (O_MV, O_MO), (O_MI, O_AQ), (O_AK, O_AV), (O_AV, 3592), (O_MQ, O_MK), (O_MO, O_MI), (O_AQ, O_AK)]
        for gi_, (c0_, c1_) in enumerate(WGRP):
            S.dma('pool', win[:, :, c0_:c1_], w_in_v[:, :, c0_:c1_], w=[('win', gi_)])

        def wkey(col):
            for gi_, (c0_, c1_) in enumerate(WGRP):
                if c0_ <= col < c1_:
                    return ('win', gi_)
            raise ValueError(col)

        S.op('pool', lambda e: e.memset(vx[:, :, :, 128:130], 1.0), w=[('vx', j) for j in range(32)])
        S.op('pool', lambda e: e.tensor_scalar(vx[:, 0:16, :, 128:130], vx[:, 0:16, :, 128:130], cst[:, C_FL:C_FL + 1], 1.0,
                                               ALU.mult, ALU.mult),
             r=['cst'] + [('vx', j) for j in range(16)], w=[('vx', j) for j in range(16)])
        for i in range(2):
            S.op('pool', lambda e: e.memset(mvx[i][:, :, 128:130], 1.0), w=[('mvx', i)])
        for i in range(3):
            S.op('pool', lambda e: e.memset(Gt[i], 1.0), w=[('Gt', i)])
        S.op('pool', lambda e: e.memset(qTg[0][:], 0.0), w=[('qTg', 0, 0), ('qTg', 0, 1)])
        S.op('pool', lambda e: e.memset(qTg[1][:], 0.0), w=[('qTg', 1, 0), ('qTg', 1, 1)])
        S.op('pool', lambda e: e.memset(Pst[:], 0.0), w=[('Pst', h) for h in range(4)])
        S.op('pool', lambda e: e.memset(Cbf[:], 0.0), w=['Cbf'])
        S.op('pool', lambda e: e.memset(raw[:], 0.0), w=[('raw', c) for c in range(8)])
        S.op('pool', lambda e: e.memset(carry[:], 0.0), w=[('carry', j) for j in range(8)])

        cw = cst[:, C_CW:C_CW + 32].rearrange("p (c j) -> p c j", j=4)
        cb = cst[:, C_CB:C_CB + 8]
        cnt = {'tt': 0}

        def norm_to_uT(x_tile, kx, gcol, uT_t, kuT, scr=None):
            if scr is None:
                u_, ku, ssb_, kss, tb = u, 'u', ssb, 'ssb', 0
            else:
                u_, ku, ssb_, kss, tb = scr
            return _norm_to_uT(x_tile, kx, gcol, uT_t, kuT, u_, ku, ssb_, kss, tb)

        def _norm_to_uT(x_tile, kx, gcol, uT_t, kuT, u, ku, ssb, kss, tb):
            _norm_a1(x_tile, kx, u, ku, ssb, kss)
            _norm_a2(gcol, uT_t, kuT, u, ku, tb)

        def _norm_a1(x_tile, kx, u, ku, ssb, kss):
            S.op('act', lambda e: e.activation(u[:], x_tile, AF.Square, accum_out=ssb[:, 0:1]), r=[kx], w=[ku, kss])
            rstd_from_ss(ssb[:, 0:1], ssb[:, 1:2], 1024.0, kss, kss)
            S.op('dve', lambda e: e.tensor_scalar(u[:], x_tile, ssb[:, 1:2], None, ALU.mult), r=[kx, kss], w=[ku])

        def _norm_a2(gcol, uT_t, kuT, u, ku, tb):
            psb = banks[tb][:].bitcast(BF16)
            for k in range(8):
                S.op('pe', lambda e: e.transpose(psb[:, k * 128:(k + 1) * 128], u[:, k * 128:(k + 1) * 128], identb),
                     r=[ku, 'cstb'], w=[bk(tb)], sig=k == 7)
            g_bc = cst[:, gcol:gcol + 8].unsqueeze(2).to_broadcast([128, 8, 128])
            S.op('dve', lambda e: e.tensor_tensor(uT_t[:], psb[:, 0:1024].rearrange("p (k t) -> p k t", k=8), g_bc, ALU.mult),
                 r=[bk(tb), 'cst'], w=[kuT])

        def transpose_to(src_tok, ksrc, dst_ap, kdst, scale_cols=None, bank=0):
            psb = banks[bank][:].bitcast(BF16)
            for k in range(4):
                S.op('pe', lambda e: e.transpose(psb[:, k * 128:(k + 1) * 128], src_tok[:, k * 128:(k + 1) * 128], identb),
                     r=[ksrc, 'cstb'], w=[bk(bank)], sig=k == 3)
            src = psb[:, 0:512].rearrange("p (k t) -> p k t", k=4)
            if scale_cols is None:
                S.op('dve', lambda e: e.tensor_copy(dst_ap, src), r=[bk(bank)], w=[kdst])
            else:
                g_bc = scale_cols.unsqueeze(2).to_broadcast([128, 4, 128])
                S.op('dve', lambda e: e.tensor_tensor(dst_ap, src, g_bc, ALU.mult), r=[bk(bank), 'cst'], w=[kdst])

        def proj_tm(uT_t, kuT, col, bank):
            for k in range(8):
                S.op('pe', lambda e: e.matmul(banks[bank][:, :], uT_t[:, k, :], win[:, k, col:col + 512],
                                              start=(k == 0), stop=(k == 7)),
                     r=[kuT, wkey(col)], w=[bk(bank)], sig=k == 7)

        def rr(*gens):
            gens = [g for g in gens if g is not None]
            while gens:
                for g in list(gens):
                    try:
                        next(g)
                    except StopIteration:
                        gens.remove(g)
                yield

        def run(gen):
            for _ in gen:
                pass


        def qk_prep(bank, gcol, cs, qb16, k16):
            q3 = qf[:].rearrange("p (g d) -> p g d", d=64)
            S.op('act', lambda e: e.activation(qf[:], banks[bank][:, :], AF.Square), r=[bk(bank)], w=['qf'])
            S.op('dve', lambda e: e.tensor_reduce(out=al[:, 0:8], in_=q3, axis=AX.X, op=ALU.add), r=['qf'], w=['al'])
            S.op('act', lambda e: e.activation(al[:, 8:16], al[:, 0:8], AF.Ln, bias=epsc[:, 0:1], scale=1.0 / 64),
                 r=['al', 'consts'], w=['al'])
            S.op('act', lambda e: e.activation(al[:, 8:16], al[:, 8:16], AF.Exp, scale=-0.5), r=['al'], w=['al'])
            S.op('dve', lambda e: e.tensor_tensor(q3, banks[bank][:, :].rearrange("p (g d) -> p g d", d=64),
                                                  al[:, 8:16].unsqueeze(2).to_broadcast([128, 8, 64]), ALU.mult),
                 r=[bk(bank), 'al'], w=['qf'])
            gg = cst[:, gcol:gcol + 64].unsqueeze(1).to_broadcast([128, 8, 64])
            o3 = qb16[:].rearrange("p (g d) -> p g d", d=64)
            yield S.op('dve', lambda e: e.tensor_tensor(o3, q3, gg, ALU.mult), r=['qf', 'cst'], w=[k16])
            gg16 = cst[:, gcol:gcol + 16].unsqueeze(1).to_broadcast([128, 8, 16])
            yield S.op('dve', lambda e: e.tensor_tensor(q3[:, :, 0:16], q3[:, :, 0:16], gg16, ALU.mult), r=['qf', 'cst'], w=['qf'])
            x1, x2 = q3[:, :, 0:8], q3[:, :, 8:16]
            cc = cs[:, 0:8].unsqueeze(1).to_broadcast([128, 8, 8])
            sn = cs[:, 8:16].unsqueeze(1).to_broadcast([128, 8, 8])
            r4 = [q3[:, :, 16 + 8 * a_:24 + 8 * a_] for a_ in range(4)]
            S.op('dve', lambda e: e.tensor_tensor(r4[0], x1, cc, ALU.mult), r=['qf', 'cst'], w=['qf'])
            S.op('dve', lambda e: e.tensor_tensor(r4[1], x2, sn, ALU.mult), r=['qf', 'cst'], w=['qf'])
            S.op('dve', lambda e: e.tensor_tensor(r4[2], x2, cc, ALU.mult), r=['qf', 'cst'], w=['qf'])
            yield S.op('dve', lambda e: e.tensor_tensor(r4[3], x1, sn, ALU.mult), r=['qf', 'cst'], w=['qf'])
            S.op('dve', lambda e: e.tensor_tensor(o3[:, :, 0:8], r4[0], r4[1], ALU.subtract), r=['qf', k16], w=[k16])
            yield S.op('dve', lambda e: e.tensor_tensor(o3[:, :, 8:16], r4[2], r4[3], ALU.add), r=['qf', k16], w=[k16])

        def attention_group(g):
            kbs = list(range(16)) + [16 + j for j in range(2 * g + 2)]
            units = [(h, idx, kb) for h in range(4) for idx, kb in enumerate(kbs)]
            PSB = (4, 5)
            for tt in range(2):
                psb = banks[4 + tt][:].bitcast(BF16)
                for k in range(4):
                    S.op('pe', lambda e: e.transpose(psb[:, k * 128:(k + 1) * 128], qtok[tt][:, k * 128:(k + 1) * 128], identb),
                         r=[('qtok', tt), 'cstb'], w=[bk(4 + tt)], sig=(k == 3))
                yield
                c_ = tt * 128
                S.op('act', lambda e: e.activation(qTg[0][0:64, :, c_:c_ + 128], psb[0:64, 0:512].rearrange("p (k t) -> p k t", k=4),
                                                   AF.Copy), r=[bk(4 + tt)], w=[('qTg', 0, tt)])
                yield S.op('act', lambda e: e.activation(qTg[1][64:128, :, c_:c_ + 128],
                                                         psb[64:128, 0:512].rearrange("p (k t) -> p k t", k=4), AF.Copy),
                           r=[bk(4 + tt)], w=[('qTg', 1, tt)])
            QK = [('qTg', m, tt) for m in range(2) for tt in range(2)]

            def st_mm(n):
                h, idx, kb = units[n]
                pb = PSB[n % 2]
                for rep_ in range(PE_WARM_REPS):
                    S.op('pe', lambda e: e.matmul(banks[pb][:, 0:256], kT[:, h, kb * 128:(kb + 1) * 128],
                                                  qTg[0][:, h, :], start=True, stop=True),
                         r=[('kT', kb)] + QK, w=[bk(pb)], sig=False)
                    S.op('pe', lambda e: e.matmul(banks[pb][:, 256:512], kT[:, h, kb * 128:(kb + 1) * 128],
                                                  qTg[1][:, h, :], start=True, stop=True),
                         r=[('kT', kb)] + QK, w=[bk(pb)], sig=(rep_ == PE_WARM_REPS - 1))

            st_mm(0)
            for n, (h, idx, kb) in enumerate(units):
                if n + 1 < len(units):
                    st_mm(n + 1)
                pb = PSB[n % 2]
                E = Et[n % 2]
                kE = ('Et', n % 2)
                S.op('act', lambda e: e.activation(E[:], banks[pb][:, :], AF.Exp, scale=0.125), r=[bk(pb)], w=[kE])
                own = kb - 16
                if own == 2 * g:
                    S.op('dve', lambda e: e.tensor_tensor(E[:], E[:], cstb[:, 128:640], ALU.mult), r=[kE, 'cstb'], w=[kE])
                elif own == 2 * g + 1:
                    S.op('dve', lambda e: e.tensor_tensor(E[:], E[:], cstb[:, 640:1152], ALU.mult), r=[kE, 'cstb'], w=[kE])
                for m in range(2):
                    for qb in range(2):
                        if qb == 0 and own == 2 * g + 1:
                            continue
                        last = (own == 2 * g) if qb == 0 else (own == 2 * g + 1)
                        a = 6 + m
                        o0 = qb * 130
                        S.op('pe', lambda e: e.matmul(banks[a][:, o0:o0 + 129], E[:, m * 256 + qb * 128: m * 256 + qb * 128 + 128],
                                                      vx[:, kb, h, 0:129], start=(idx == 0 and qb == 0), stop=last,
                                                      skip_group_check=True),
                             r=[kE, ('vx', kb)], w=[bk(a)], sig=(m == 1 and qb == 1))
                yield
                if idx == len(kbs) - 1:
                    den6 = banks[6][:, 0:260].rearrange("p (a c) -> p a c", c=130)[:, :, 128]
                    den7 = banks[7][:, 0:260].rearrange("p (a c) -> p a c", c=130)[:, :, 128]
                    S.op('dve', lambda e: e.reciprocal(ae[:, 2:4], den7), r=[bk(7)], w=[('ae', 1)])
                    S.op('dve', lambda e: e.tensor_scalar(ae[:, 2:4], ae[:, 2:4], lam_ap, None, ALU.mult), r=[('ae', 1), 'small'], w=[('ae', 1)])
                    S.op('dve', lambda e: e.reciprocal(ae[:, 0:2], den6), r=[bk(6)], w=[('ae', 0)])
                    for qb in range(2):
                        o0 = qb * 130
                        S.op('act', lambda e: e.activation(ofin[qb][:], banks[7][:, o0:o0 + 128], AF.Copy, scale=ae[:, 2 + qb:3 + qb]),
                             r=[bk(7), ('ae', 1)], w=[('ofin', qb)])
                        S.op('dve', lambda e: e.scalar_tensor_tensor(out=ofin[qb][:], in0=banks[6][:, o0:o0 + 128], scalar=ae[:, qb:qb + 1],
                                                                     in1=ofin[qb][:], op0=ALU.mult, op1=ALU.subtract),
                             r=[bk(6), ('ae', 0), ('ofin', qb)], w=[('ofin', qb)])
                    yield
                    for qb in range(2):
                        S.op('act', lambda e: e.activation(yat[qb][:, h * 128:(h + 1) * 128], ofin[qb][:], AF.Square,
                                                           accum_out=ae[:, 4 + qb:5 + qb]),
                             r=[('ofin', qb)], w=[('yat', qb), ('ae', 2 + qb)])
                    S.op('act', lambda e: e.activation(ae[:, 6:8], ae[:, 4:6], AF.Ln, bias=epsc[:, 0:1], scale=1.0 / 128),
                         r=[('ae', 2), ('ae', 3), 'consts'], w=[('ae', 4)])
                    S.op('act', lambda e: e.activation(ae[:, 6:8], ae[:, 6:8], AF.Exp, scale=-0.5), r=[('ae', 4)], w=[('ae', 4)])
                    for qb in range(2):
                        S.op('dve', lambda e: e.tensor_scalar(yat[qb][:, h * 128:(h + 1) * 128], ofin[qb][:], ae[:, 6 + qb:7 + qb],
                                                               float(1.0 - LAM_INIT), ALU.mult, ALU.mult),
                             r=[('ofin', qb), ('ae', 4)], w=[('yat', qb)])
                    yield
            for qb in range(2):
                t0 = (2 * g + qb) * 128
                transpose_to(yat[qb], ('yat', qb), yT[:, 4:8, t0:t0 + 128], ('yT', 2 * g + qb, 1),
                             scale_cols=cst[:, C_GY + 4:C_GY + 8], bank=4 + qb)
                yield

        NTT = n_pre + n_own
        def tt_info(n):
            own = n >= n_pre
            i = n - n_pre if own else n
            return own, i

        def stageA(n):
            own, i = tt_info(n)
            xsrc = xo if own else xp
            S.dma('sp', xt[:], xsrc[i * 128:(i + 1) * 128, :], w=['xt'])
            _norm_a1(xt[:], 'xt', u, 'u', ssb, 'ssb')
            for _ in range(A2_LAG):
                yield
            _norm_a2(C_G1, uT[n % 2], ('uT', n % 2), u, 'u', 0)
            yield

        def front(n):
            own, i = tt_info(n)
            blk = (16 + i) if own else i
            uT_t, kuT = uT[n % 2], ('uT', n % 2)
            Gc, kG = Gt[n % 3], ('Gt', n % 3)
            mv_t, kmv = mvx[n % 2], ('mvx', n % 2)
            qkT, kq = qkT2[n % 2], n % 2
            sigo = sigo2[n % 2]
            if own:
                GB, FMB, TMB = 0, (1, 1), (1, 1)
            else:
                GB, FMB, TMB = 1, (2, 3), (6, 7)
            need_q = own or i == NT - 1

            def T2():
                for half in ((0, 1) if need_q else (1,)):
                    pbk = FMB[half]
                    c0 = half * 4
                    for c in range(c0, c0 + 4):
                        col = (O_MQ + c * 128) if c < 4 else (O_MK + (c - 4) * 128)
                        for k in range(8):
                            S.op('pe', lambda e: e.matmul(banks[pbk][:, (c % 4) * 128:(c % 4 + 1) * 128], win[:, k, col:col + 128],
                                                          uT_t[:, k, :], start=(k == 0), stop=(k == 7)),
                                 r=[kuT, wkey(col)], w=[bk(pbk)], sig=(k == 7))
                    RK = [('raw', c) for c in range(c0, c0 + 4)]
                    if own:
                        S.op('dve', lambda e: e.tensor_copy(raw[:, c0:c0 + 4, 3:131], banks[pbk][:, :].rearrange("p (c t) -> p c t", c=4)),
                             r=[bk(pbk)], w=RK)
                    else:
                        S.op('act', lambda e: e.activation(raw[:, c0:c0 + 4, 3:131], banks[pbk][:, :].rearrange("p (c t) -> p c t", c=4),
                                                           AF.Copy), r=[bk(pbk)], w=RK)
                    conv = own or half == 1
                    if conv:
                        for c in range(c0, c0 + 4):
                            if own:
                                S.op('dve', lambda e: e.tensor_scalar(acc[:, c % 4, :], banks[pbk][:, (c % 4) * 128:(c % 4 + 1) * 128],
                                                                      cw[:, c, 3:4], cb[:, c:c + 1], ALU.mult, ALU.add),
                                     r=[bk(pbk), 'cst'], w=[('acc', c % 4)])
                            else:
                                S.op('act', lambda e: e.activation(acc[:, c % 4, :], banks[pbk][:, (c % 4) * 128:(c % 4 + 1) * 128],
                                                                   AF.Identity, bias=cb[:, c:c + 1], scale=cw[:, c, 3:4]),
                                     r=[bk(pbk), 'cst'], w=[('acc', c % 4)])
                    yield
                    if conv:
                        for c in range(c0, c0 + 4):
                            for j in range(3):
                                yield S.op('dve', lambda e: e.scalar_tensor_tensor(out=acc[:, c % 4, :], in0=raw[:, c, j:j + 128],
                                                                                   scalar=cw[:, c, j:j + 1], in1=acc[:, c % 4, :],
                                                                                   op0=ALU.mult, op1=ALU.add),
                                           r=[('raw', c), 'cst', ('acc', c % 4)], w=[('acc', c % 4)])
                    if own:
                        yield S.op('dve', lambda e: e.tensor_copy(raw[:, c0:c0 + 4, 0:3], raw[:, c0:c0 + 4, 128:131]), r=RK, w=RK)
                    else:
                        yield S.op('act', lambda e: e.activation(raw[:, c0:c0 + 4, 0:3], raw[:, c0:c0 + 4, 128:131], AF.Copy), r=RK, w=RK)
                    if conv:
                        tmp = raw[:, c0:c0 + 4, 3:131]
                        AK = [('acc', c) for c in range(4)]
                        S.op('act', lambda e: e.activation(tmp, acc[:, :, :], AF.Exp, scale=-1.0), r=AK + RK, w=RK)
                        S.op('act', lambda e: e.activation(tmp, tmp, AF.Ln, bias=epsc[:, 1:2]), r=RK + ['consts'], w=RK)
                        yield S.op('act', lambda e: e.activation(tmp, tmp, AF.Exp, scale=-1.0), r=RK, w=RK)
                        yield S.op('dve', lambda e: e.tensor_tensor(qkT[:, c0:c0 + 4, :], acc[:, :, :], tmp, ALU.mult),
                                   r=AK + RK, w=[('qkT', kq, half)])

            def T3():
                for gi, col in enumerate((O_MI, O_MF)):
                    for k in range(8):
                        S.op('pe', lambda e: e.matmul(banks[GB][0:4, gi * 128:(gi + 1) * 128], win[:, k, col:col + 4],
                                                      uT_t[:, k, :], start=(k == 0), stop=(k == 7)),
                             r=[kuT, wkey(col)], w=[bk(GB)], sig=(k == 7))
                po_ = 2 if own else 0
                G = lambda j: ('gf', j)
                C = lambda j: ('carry', j)
                S.op('act', lambda e: e.activation(gf[:, 0, :], banks[GB][0:4, 0:128], AF.Identity,
                                                   bias=gpar[:, po_ + 1:po_ + 2], scale=gpar[:, po_:po_ + 1]),
                     r=[bk(GB), 'gpar'], w=[G(0)])
                yield S.op('act', lambda e: e.activation(gf[:, 1, :], banks[GB][0:4, 128:256], AF.Exp, bias=gpar[:, 4:5], scale=-1.0),
                           r=[bk(GB), 'gpar'], w=[G(1)])
                yield S.op('act', lambda e: e.activation(gf[:, 1, :], gf[:, 1, :], AF.Ln, bias=epsc[0:4, 1:2]),
                           r=[G(1), 'consts'], w=[G(1)])
                if not own:
                    yield S.op('dve', lambda e: e.tensor_scalar(gf[:, 1, :], gf[:, 1, :], gpar[:, 5:6], None, ALU.mult),
                               r=[G(1), 'gpar'], w=[G(1)])
                yield S.op('dve', lambda e: e.tensor_tensor_scan(gf[:, 2, :], ones4[:], gf[:, 1, :], carry[:, 0:1], ALU.mult, ALU.add),
                           r=[G(1), 'consts4', C(0)], w=[G(2)])
                yield S.op('dve', lambda e: e.tensor_tensor(gf[:, 3, :], gf[:, 0, :], gf[:, 2, :], ALU.add), r=[G(0), G(2)], w=[G(3)])
                yield S.op('dve', lambda e: e.tensor_tensor_scan(gf[:, 4, :], ones4[:], gf[:, 3, :], carry[:, 1:2], ALU.mult, ALU.max),
                           r=[G(3), 'consts4', C(1)], w=[G(4)])
                S.op('dve', lambda e: e.tensor_scalar(carry[:, 2:3], carry[:, 1:2], -1.0, float(LNC), ALU.mult, ALU.add),
                     r=[C(1)], w=[C(2)])
                S.op('dve', lambda e: e.tensor_scalar(carry[:, 3:4], carry[:, 1:2], -1.0, None, ALU.mult), r=[C(1)], w=[C(3)])
                yield S.op('dve', lambda e: e.tensor_tensor(carry[:, 4:5], carry[:, 1:2], gf[:, 4, 127:128], ALU.subtract),
                           r=[C(1), G(4)], w=[C(4)])
                yield S.op('act', lambda e: e.activation(gf[:, 5, :], gf[:, 3, :], AF.Exp, bias=carry[:, 2:3]), r=[G(3), C(2)], w=[G(5)])
                yield S.op('act', lambda e: e.activation(gf[:, 6, :], gf[:, 2, :], AF.Exp, bias=carry[:, 3:4]), r=[G(2), C(3)], w=[G(6)])
                yield S.op('act', lambda e: e.activation(gf[:, 7, :], ones4[:], AF.Exp, bias=carry[:, 4:5], scale=0.0),
                           r=[C(4), 'consts4'], w=[G(7)])
                S.op('dve', lambda e: e.tensor_copy(carry[:, 0:1], gf[:, 2, 127:128]), r=[G(2), C(0)], w=[C(0)])
                yield S.op('dve', lambda e: e.tensor_copy(carry[:, 1:2], gf[:, 4, 127:128]), r=[G(4), C(1)], w=[C(1)])
                for a in range(3):
                    S.op('pe', lambda e: e.transpose(banks[GB][:, 256 + a * 4:256 + a * 4 + 4], gf[:, 5 + a, :], identf[0:4, 0:4]),
                         r=[G(5 + a), 'cst'], w=[bk(GB)], sig=(a == 2))
                yield S.op('dve', lambda e: e.tensor_copy(Gc, banks[GB][:, 256:268]), r=[bk(GB)], w=[kG])

            def T4():
                tb = [0]

                def nxt():
                    tb[0] += 1
                    return TMB[tb[0] % 2]
                b_ = nxt()
                proj_tm(uT_t, kuT, O_MV, b_)
                if own:
                    yield S.op('dve', lambda e: e.tensor_copy(mv_t[:, :, 0:128], banks[b_][:, :].rearrange("p (h d) -> p h d", h=4)),
                               r=[bk(b_)], w=[kmv])
                else:
                    yield S.op('act', lambda e: e.activation(mv_t[:, :, 0:128], banks[b_][:, :].rearrange("p (h d) -> p h d", h=4), AF.Copy),
                               r=[bk(b_)], w=[kmv])
                if own:
                    b_ = nxt()
                    proj_tm(uT_t, kuT, O_MO, b_)
                    S.op('act', lambda e: e.activation(qf[:], banks[b_][:, :], AF.Exp, scale=-1.0), r=[bk(b_)], w=['qf'])
                    S.op('act', lambda e: e.activation(qf[:], qf[:], AF.Ln, bias=epsc[:, 1:2]), r=['qf', 'consts'], w=['qf'])
                    yield S.op('act', lambda e: e.activation(sigo[:], qf[:], AF.Exp, scale=-1.0), r=['qf'], w=[('sigo', n % 2)])
                cs_k = cst[:, (C_CSO if own else C_CSP) + i * 16:(C_CSO if own else C_CSP) + i * 16 + 16]
                b_ = nxt()
                proj_tm(uT_t, kuT, O_AK, b_)
                yield from qk_prep(b_, C_KG, cs_k, qb16, 'qb16')
                transpose_to(qb16, 'qb16', kT[:, :, blk * 128:(blk + 1) * 128], ('kT', blk), bank=0)
                yield
                b_ = nxt()
                proj_tm(uT_t, kuT, O_AV, b_)
                if own:
                    yield S.op('dve', lambda e: e.tensor_copy(vx[:, blk, :, 0:128], banks[b_][:, :].rearrange("p (h d) -> p h d", h=4)),
                               r=[bk(b_)], w=[('vx', blk)])
                else:
                    yield S.op('act', lambda e: e.activation(vx[:, blk, :, 0:128], banks[b_][:, :].rearrange("p (h d) -> p h d", h=4), AF.Copy),
                               r=[bk(b_)], w=[('vx', blk)])
                if own:
                    b_ = nxt()
                    proj_tm(uT_t, kuT, O_AQ, b_)
                    yield from qk_prep(b_, C_QG, cs_k, qtok[i % 2], ('qtok', i % 2))

            yield from rr(T2(), T3(), T4())
            if own and i == 0 and 'qkT0' in dbg_d:
                S.dma('sp', dbg_d['qkT0'], qkT[:], r=[('qkT', kq, 0), ('qkT', kq, 1)], w=['dbg_qkT0'])
                S.finish('sp', ['dbg_qkT0'])

        def back(n):
            own, i = tt_info(n)
            Gc, kG = Gt[n % 3], ('Gt', n % 3)
            Gp, kGp = Gt[(n + 2) % 3], ('Gt', (n + 2) % 3)
            mv_t, kmv = mvx[n % 2], ('mvx', n % 2)
            qkT, kq = qkT2[n % 2], n % 2
            sigo = sigo2[n % 2]
            HB = (2, 3) if own else (4, 5)
            MK = [('ktl', 0), ('ktl', 1), ('stm', 0), ('stm', 1)]

            def head(h):
                e_ = h % 2
                kt_, kkt = ktl[e_], ('ktl', e_)
                st_, kst = stm[e_], ('stm', e_)
                hb = HB[e_]
                B = banks[hb]
                psb = B[:].bitcast(BF16)
                S.op('pe', lambda e: e.transpose(psb[:, 0:128], qkT[:, 4 + h, :], identb), r=[('qkT', kq, 1), 'cstb'], w=[bk(hb)], sig=True)
                yield
                if own:
                    yield S.op('dve', lambda e: e.tensor_scalar(kt_, psb[:, 0:128], Gc[:, h:h + 1], None, ALU.mult),
                               r=[bk(hb), kG], w=[kkt])
                else:
                    yield S.op('act', lambda e: e.activation(kt_, psb[:, 0:128], AF.Copy, scale=Gc[:, h:h + 1]),
                               r=[bk(hb), kG], w=[kkt])
                if own:
                    S.op('pe', lambda e: e.matmul(B[:, 64:192], qkT[:, 4 + h, :], qkT[:, h, :], start=True, stop=True),
                         r=[('qkT', kq, 0), ('qkT', kq, 1)], w=[bk(hb)], sig=True)
                    yield
                    yield S.op('dve', lambda e: e.scalar_tensor_tensor(out=st_, in0=B[:, 64:192], scalar=Gc[:, h:h + 1],
                                                                       in1=tri, op0=ALU.mult, op1=ALU.mult),
                               r=[bk(hb), kG, 'cst'], w=[kst])
                    S.op('pe', lambda e: e.matmul(B[:, 192:321], qkT[:, h, :], Cbf[:, h, 0:129], start=True, stop=False),
                         r=[('qkT', kq, 0), 'Cbf'], w=[bk(hb)], sig=False)
                    S.op('pe', lambda e: e.matmul(B[:, 192:321], st_, mv_t[:, h, 0:129], start=False, stop=True),
                         r=[kst, kmv], w=[bk(hb)], sig=True)
                    yield
                S.op('pe', lambda e: e.matmul(B[:, 336:465], kt_, mv_t[:, h, 0:129], start=True, stop=True),
                     r=[kkt, kmv], w=[bk(hb)], sig=True)
                yield
                yield S.op('dve', lambda e: e.scalar_tensor_tensor(out=Pst[:, h, 0:129], in0=Pst[:, h, 0:129], scalar=Gp[:, 8 + h:9 + h],
                                                                   in1=B[:, 336:465], op0=ALU.mult, op1=ALU.add),
                           r=[bk(hb), kGp, ('Pst', h)], w=[('Pst', h)])

            def pair_epilogue(p):
                c = 8 * p
                m = ml[:, 16 * p:16 * p + 16]
                for e_ in range(2):
                    S.op('dve', lambda e: e.tensor_copy(m[:, e_:e_ + 1], banks[HB[e_]][:, 320:321]), r=[bk(HB[e_])], w=[('ml', p, 0)])
                S.op('dve', lambda e: e.scalar_tensor_tensor(out=m[:, 2:4], in0=m[:, 0:2], scalar=-1.0, in1=m[:, 0:2],
                                                             op0=ALU.mult, op1=ALU.max), r=[('ml', p, 0)], w=[('ml', p, 1)])
                S.op('dve', lambda e: e.tensor_tensor(m[:, 2:4], m[:, 2:4], Gc[:, 4 + 2 * p:6 + 2 * p], ALU.max),
                     r=[('ml', p, 1), kG], w=[('ml', p, 1)])
                yield S.op('dve', lambda e: e.reciprocal(m[:, 4:6], m[:, 2:4]), r=[('ml', p, 1)], w=[('ml', p, 2)])
                for e_ in range(2):
                    B = banks[HB[e_]]
                    h_ = 2 * p + e_
                    yield S.op('act', lambda e: e.activation(mls[:, h_ * 128:(h_ + 1) * 128], B[:, 192:320], AF.Square,
                                                             accum_out=m[:, 6 + e_:7 + e_]),
                               r=[bk(HB[e_])], w=[MK[h_], ('ml', p, 3 + e_)])
                S.op('dve', lambda e: e.tensor_tensor(m[:, 8:10], m[:, 4:6], m[:, 4:6], ALU.mult), r=[('ml', p, 2)], w=[('ml', p, 5)])
                yield S.op('dve', lambda e: e.tensor_tensor(m[:, 8:10], m[:, 8:10], m[:, 6:8], ALU.mult),
                           r=[('ml', p, 5), ('ml', p, 3), ('ml', p, 4)], w=[('ml', p, 5)])
                S.op('act', lambda e: e.activation(m[:, 10:12], m[:, 8:10], AF.Ln, bias=epsc[:, 0:1], scale=1.0 / 128),
                     r=[('ml', p, 5), 'consts'], w=[('ml', p, 6)])
                yield S.op('act', lambda e: e.activation(m[:, 10:12], m[:, 10:12], AF.Exp, scale=-0.5), r=[('ml', p, 6)], w=[('ml', p, 6)])
                yield S.op('dve', lambda e: e.tensor_tensor(m[:, 12:14], m[:, 10:12], m[:, 4:6], ALU.mult),
                           r=[('ml', p, 6), ('ml', p, 2)], w=[('ml', p, 7)])
                for e_ in range(2):
                    h = 2 * p + e_
                    B = banks[HB[e_]]
                    yield S.op('dve', lambda e: e.scalar_tensor_tensor(out=mls[:, h * 128:(h + 1) * 128], in0=B[:, 192:320],
                                                                       scalar=m[:, 12 + e_:13 + e_], in1=sigo[:, h * 128:(h + 1) * 128],
                                                                       op0=ALU.mult, op1=ALU.mult),
                               r=[bk(HB[e_]), ('ml', p, 7), ('sigo', n % 2)], w=[MK[h]])
                for e_ in range(2):
                    h = 2 * p + e_
                    B = banks[HB[e_]]
                    psb = B[:].bitcast(BF16)
                    S.op('pe', lambda e: e.transpose(psb[:, 0:128], mls[:, h * 128:(h + 1) * 128], identb),
                         r=[MK[h], 'cstb'], w=[bk(HB[e_])], sig=True)
                    yield
                    yield S.op('dve', lambda e: e.tensor_scalar(yT[:, h, i * 128:(i + 1) * 128], psb[:, 0:128],
                                                                cst[:, C_GY + h:C_GY + h + 1], None, ALU.mult),
                               r=[bk(HB[e_]), 'cst'], w=[('yT', i, 0)])

            for p in range(2):
                yield from rr(head(2 * p), head(2 * p + 1))
                if own:
                    yield from pair_epilogue(p)
            if own or i == NT - 1:
                yield S.op('dve', lambda e: e.tensor_tensor(Cbf[:, :, 0:129], Pst[:, :, 0:129],
                                                            Gc[:, 8:12].unsqueeze(2).to_broadcast([128, 4, 129]), ALU.mult),
                           r=[('Pst', h) for h in range(4)] + [kG], w=['Cbf'])

        STEPS = {'main': 0, 'side': 0}

        def wrr(main, side, per):
            credit = 0.0
            for _ in main:
                STEPS['main'] += 1
                credit += per
                while side is not None and credit >= 1.0:
                    credit -= 1.0
                    STEPS['side'] += 1
                    try:
                        next(side)
                    except StopIteration:
                        side = None
            return side

        def dump(name, ap, keys):
            if name in dbg_d:
                S.dma('sp', dbg_d[name], ap, r=keys, w=['dbg_' + name])
                S.finish('sp', ['dbg_' + name])

        side = None
        side_rate = [1.0]
        main_per_tt = [60]
        if NTT > 0:
            run(stageA(0))
            run(rr(stageA(1) if NTT > 1 else None, front(0)))
        for n in range(NTT):
            own, i = tt_info(n)
            def _late(gen, rounds):
                for _ in range(rounds):
                    yield
                yield from gen
            nxt_front = rr(_late(stageA(n + 2), A_LAG) if n + 2 < NTT else None, front(n + 1) if n + 1 < NTT else None)
            if own and i % 2 == 1:
                if side is not None:
                    run(side)
                side = attention_group(i // 2)
                for _ in range(4):
                    next(side)
                g_ = i // 2
                side_total = (18 + 2 * g_) * 4 + 4 * 4 + 2
                side_rate[0] = 1.0 * side_total / (2.0 * max(main_per_tt[0], 1))
            m0 = STEPS['main']
            side = wrr(rr(nxt_front, back(n)), side, side_rate[0])
            if own:
                main_per_tt[0] = STEPS['main'] - m0
            if own and i == 3 and 'yT01' in dbg_d:
                if side is not None:
                    run(side)
                    side = None
                dump('yT01', yT[:, :, 0:256], [('yT', a, b) for a in range(2) for b in range(2)])
        if side is not None:
            run(side)

        if 'yT' in dbg_d:
            S.dma('sp', dbg_d['yT'], yT[:], r=[('yT', i, j) for i in range(16) for j in range(2)], w=['dbg_yT'])
            S.finish('sp', ['dbg_yT'])

        ab.close()
        if 'C' not in phases:
            S.barrier()
            return nc

        hres = sb("hres", [128, 16, 1024])
        wout = sb("wout", [128, 8, 1024], BF16)
        wp = sb("wp", [128, 2, 1024], BF16)
        cd = ExitStack()

        def sc(name, shape, dt=F32):
            return cd.enter_context(nc.sbuf_tensor("t_" + name, list(shape), dt))

        u2T = sc("u2T", [128, 8, 2048], BF16)
        u = sc("u_c", [128, 1024], BF16)
        ssb = sc("ssb_c", [128, 8])
        wupb = [sc("wup%d" % i, [128, 8, 512], BF16) for i in range(2)]
        wdnb = [sc("wdn%d" % i, [128, 4, 1024], BF16) for i in range(2)]
        hidT = [sc("hidT%d" % i, [128, 4, 512], BF16) for i in range(2)]
        relu_t = sc("relu_t", [128, 512])
        w_out_v = w_out.rearrange("(k p) n -> p k n", p=128)
        S.barrier()
        for k in range(8):
            S.dma('pool', wout[:, k, :], w_out_v[:, k, :], w=[('wout', k)])
        w_up_v = w_up.rearrange("(k p) n -> p k n", p=128)
        w_dn_v = w_down.rearrange("(k p) n -> p k n", p=128)

        def load_ffn_w(j):
            b = j % 2
            S.dma('pool', wupb[b][:], w_up_v[:, :, j * 512:(j + 1) * 512], w=[('wup', b)])
            S.dma('pool', wdnb[b][:], w_dn_v[:, j * 4:(j + 1) * 4, :], w=[('wdn', b)])

        u_c2 = sc("u_c2", [128, 1024], BF16)
        ssb_c2 = sc("ssb_c2", [128, 8])
        CSCR = [(u, 'u_c', ssb, 'ssb_c', 0), (u_c2, 'u_c2', ssb_c2, 'ssb_c2', 1)]

        def c_stream(i):
            S.dma('sp', hres[:, i, :], xo[i * 128:(i + 1) * 128, :], w=[('h', i)])
            for half in range(2):
                pbk = 2 + (i * 2 + half) % 6
                for k in range(8):
                    S.op('pe', lambda e: e.matmul(banks[pbk][:, :], yT[:, k, i * 128:(i + 1) * 128],
                                                  wout[:, k, half * 512:(half + 1) * 512], start=(k == 0), stop=(k == 7)),
                         r=[('yT', i, 0), ('yT', i, 1), ('wout', k)], w=[bk(pbk)], sig=k == 7)
                yield S.op('dve', lambda e: e.tensor_tensor(hres[:, i, half * 512:(half + 1) * 512], banks[pbk][:, :],
                                                            hres[:, i, half * 512:(half + 1) * 512], ALU.add),
                           r=[bk(pbk), ('h', i)], w=[('h', i)])
            if i == 0:
                load_ffn_w(0)
                load_ffn_w(1)
            sc_ = CSCR[i % 2]
            _norm_a1(hres[:, i, :], ('h', i), sc_[0], sc_[1], sc_[2], sc_[3])
            yield
            yield
            _norm_a2(C_G2, u2T[:, :, i * 128:(i + 1) * 128], ('u2T', i), sc_[0], sc_[1], sc_[4])
            yield

        def lagged0(gens, lag):
            live = []
            pending = list(gens)
            rnd = 0
            while live or pending:
                if pending and rnd % lag == 0:
                    live.append(pending.pop(0))
                for g in list(live):
                    try:
                        next(g)
                    except StopIteration:
                        live.remove(g)
                rnd += 1

        lagged0([c_stream(i) for i in range(NT)], 1)
        if 'h1' in dbg_d:
            S.dma('sp', dbg_d['h1'], hres[:], r=[('h', i) for i in range(16)], w=['dbg_h1'])
            S.finish('sp', ['dbg_h1'])

        nacc = 0
        for j in range(8):
            b = j % 2
            for tg in range(4):
                hb = (j * 4 + tg) % 2
                for m in range(4):
                    pbk = 2 + nacc % 6
                    nacc += 1
                    for k in range(8):
                        S.op('pe', lambda e: e.matmul(banks[pbk][:, :], wupb[b][:, k, m * 128:(m + 1) * 128],
                                                      u2T[:, k, tg * 512:(tg + 1) * 512], start=(k == 0), stop=(k == 7)),
                             r=[('wup', b)] + [('u2T', tg * 4 + q) for q in range(4)], w=[bk(pbk)], sig=k == 7)
                    S.op('act', lambda e: e.activation(relu_t[:], banks[pbk][:, :], AF.Relu), r=[bk(pbk)], w=['relu_t'])
                    S.op('act', lambda e: e.activation(hidT[hb][:, m, :], relu_t[:], AF.Square),
                         r=['relu_t'], w=[('hidT', hb)])
                for q in range(4):
                    i = tg * 4 + q
                    for half in range(2):
                        pbk = 2 + nacc % 6
                        nacc += 1
                        for m in range(4):
                            S.op('pe', lambda e: e.matmul(banks[pbk][:, :], hidT[hb][:, m, q * 128:(q + 1) * 128],
                                                          wdnb[b][:, m, half * 512:(half + 1) * 512], start=(m == 0), stop=(m == 3)),
                                 r=[('hidT', hb), ('wdn', b)], w=[bk(pbk)], sig=m == 3)
                        S.op('dve', lambda e: e.tensor_tensor(hres[:, i, half * 512:(half + 1) * 512], banks[pbk][:, :],
                                                              hres[:, i, half * 512:(half + 1) * 512], ALU.add),
                             r=[bk(pbk), ('h', i)], w=[('h', i)])
            if j + 2 < 8:
                load_ffn_w(j + 2)
            if j == 5:
                w_g_v = w_gate.rearrange("(k p) n -> p k n", p=128)
                w_p_v = w_ple.rearrange("(k p) n -> p k n", p=128)
                for k in range(8):
                    S.dma('pool', wout[:, k, :], w_g_v[:, k, :], w=[('wout', k)])
                S.dma('pool', wp[:], w_p_v, w=['wp'])
        if 'h2' in dbg_d:
            S.dma('sp', dbg_d['h2'], hres[:], r=[('h', i) for i in range(16)], w=['dbg_h2'])
            S.finish('sp', ['dbg_h2'])
        S.barrier()
        cd.close()

        wg = wout
        NS = 4
        u_e = [sb("u_e%d" % i, [128, 1024], BF16) for i in range(NS)]
        ssb_e = [sb("ssb_e%d" % i, [128, 8]) for i in range(NS)]
        u3T = [sb("u3T%d" % i, [128, 8, 128], BF16) for i in range(NS)]
        pt = [sb("pt%d" % i, [128, 256], BF16) for i in range(NS)]
        pT = [sb("pT%d" % i, [128, 2, 128], BF16) for i in range(NS)]
        gsb = [sb("gsb%d" % i, [128, 1024]) for i in range(NS)]

        def pe_stream(i):
            b = i % NS
            tb = b % 2
            S.dma('pool', pt[b][:], po[i * 128:(i + 1) * 128, :], w=[('pt', b)])
            norm_to_uT(hres[:, i, :], ('h', i), C_G3, u3T[b], ('u3T', b), scr=(u_e[b], ('u_e', b), ssb_e[b], ('ssb_e', b), tb))
            yield
            psb = banks[tb][:].bitcast(BF16)
            for k in range(2):
                S.op('pe', lambda e: e.transpose(psb[:, k * 128:(k + 1) * 128], pt[b][:, k * 128:(k + 1) * 128], identb),
                     r=[('pt', b), 'cstb'], w=[bk(tb)], sig=k == 1)
            yield S.op('dve', lambda e: e.tensor_copy(pT[b][:], psb[:, 0:256].rearrange("p (k t) -> p k t", k=2)),
                       r=[bk(tb)], w=[('pT', b)])
            for half in range(2):
                pg = 2 + b
                pe_ = 6 + (b % 2)
                for k in range(8):
                    S.op('pe', lambda e: e.matmul(banks[pg][:, :], u3T[b][:, k, :], wg[:, k, half * 512:(half + 1) * 512],
                                                  start=(k == 0), stop=(k == 7)), r=[('u3T', b), ('wout', k)], w=[bk(pg)], sig=k == 7)
                sl = slice(half * 512, (half + 1) * 512)
                yield S.op('act', lambda e: e.activation(gsb[b][:, sl], banks[pg][:, :], AF.Sigmoid), r=[bk(pg)], w=[('gsb', b, half)])
                for k in range(2):
                    S.op('pe', lambda e: e.matmul(banks[pe_][:, :], pT[b][:, k, :], wp[:, k, half * 512:(half + 1) * 512],
                                                  start=(k == 0), stop=(k == 1)), r=[('pT', b), 'wp'], w=[bk(pe_)], sig=k == 1)
                yield S.op('dve', lambda e: e.tensor_tensor(gsb[b][:, sl], banks[pe_][:, :], gsb[b][:, sl], ALU.mult),
                           r=[bk(pe_), ('gsb', b, half)], w=[('gsb', b, half)])
                yield S.op('dve', lambda e: e.tensor_tensor(gsb[b][:, sl], gsb[b][:, sl], hres[:, i, sl], ALU.add),
                           r=[('gsb', b, half), ('h', i)], w=[('gsb', b, half)])
            S.dma('sp', out_d[i * 128:(i + 1) * 128, :], gsb[b][:], r=[('gsb', b, 0), ('gsb', b, 1)],
                  w=[('out', i)])
            yield

        def lagged(gens, lag):
            live = []
            pending = list(gens)
            rnd = 0
            while live or pending:
                if pending and rnd % lag == 0:
                    live.append(pending.pop(0))
                for g in list(live):
                    try:
                        next(g)
                    except StopIteration:
                        live.remove(g)
                rnd += 1

        lagged([pe_stream(i) for i in range(NT)], 2)
        S.finish('sp', [('out', i) for i in range(NT)])
    return nc


def _consts(core, inp):
    c = np.zeros((128, C_TOT), np.float32)

    def pk(v):
        return np.asarray(v, np.float32).reshape(-1, 128).T

    c[:, C_G1:C_G1 + 8] = pk(inp['attn_norm_g'][0])
    c[:, C_G2:C_G2 + 8] = pk(inp['mlp_norm_g'][0])
    c[:, C_G3:C_G3 + 8] = pk(inp['ple_norm_g'][0])
    c[:, C_GY:C_GY + 4] = pk(inp['mlstm_norm_g'][0])
    c[:, C_GY + 4:C_GY + 8] = pk(inp['attn_sub_norm_g'][0])
    cw = np.asarray(inp['conv_w'][0], np.float32)
    c[:, C_CW:C_CW + 32] = cw.T.reshape(8, 128, 4).transpose(1, 0, 2).reshape(128, 32)
    c[:, C_CB:C_CB + 8] = pk(inp['conv_b'][0])
    c[:, C_QG:C_QG + 64] = np.asarray(inp['q_norm_g'][0], np.float32)[None, :]
    c[:, C_KG:C_KG + 64] = np.asarray(inp['k_norm_g'][0], np.float32)[None, :]
    for j, nm in enumerate(('lambda_q1', 'lambda_k1', 'lambda_q2', 'lambda_k2')):
        c[:, C_LAM + j * 64:C_LAM + (j + 1) * 64] = np.asarray(inp[nm][0], np.float32)[None, :]
    odd = core % 2
    c[:, C_FL + 0] = 1.0 if odd else 0.0
    c[:, C_FL + 1] = 0.0 if odd else NEG
    c[:, C_FL + 2] = 0.0 if odd else NEG
    inv = (np.float32(500000.0) ** (-np.arange(0, 16, 2, dtype=np.float32) / np.float32(16))).astype(np.float32)
    for off, base in ((C_CSO, odd * 2048), (C_CSP, 0)):
        pos = (base + np.arange(2048, dtype=np.float32)).astype(np.float32)
        ang = (pos[:, None] * inv[None, :]).astype(np.float32)
        cs = np.concatenate([np.cos(ang), np.sin(ang)], axis=1).astype(np.float32)
        c[:, off:off + 256] = cs.reshape(16, 128, 16).transpose(1, 0, 2).reshape(128, 256)
    c[:, C_ID:C_ID + 128] = np.eye(128, dtype=np.float32)
    c[:, C_TRI:C_TRI + 128] = np.triu(np.ones((128, 128), np.float32))
    c[0:4, C_GB] = np.asarray(inp['igate_b'][0], np.float32)
    c[0:4, C_GB + 1] = np.asarray(inp['fgate_b'][0], np.float32)
    return c


def _constb():
    b = np.zeros((128, 128 + 1024), np.float32)
    b[:, 0:128] = np.eye(128, dtype=np.float32)
    tri = np.triu(np.ones((128, 128), np.float32))
    for m in range(2):
        b[:, 128 + m * 256:128 + m * 256 + 128] = tri
        b[:, 128 + m * 256 + 128:128 + m * 256 + 256] = 1.0
        b[:, 640 + m * 256:640 + m * 256 + 128] = 0.0
        b[:, 640 + m * 256 + 128:640 + m * 256 + 256] = tri
    return b


def make_in_maps(inp):
    x = np.asarray(inp['x'], np.float32)
    p = np.asarray(inp['p'], np.float32)
    shared = {
        'w_in': np.ascontiguousarray(inp['w_in'][0], dtype=np.float32),
        'w_out': np.ascontiguousarray(inp['w_out'][0], dtype=np.float32),
        'w_up': np.ascontiguousarray(inp['w_up'][0], dtype=np.float32),
        'w_down': np.ascontiguousarray(inp['w_down'][0], dtype=np.float32),
        'w_gate': np.ascontiguousarray(inp['w_ple_gate'][0], dtype=np.float32),
        'w_ple': np.ascontiguousarray(inp['w_ple_proj'][0], dtype=np.float32),
        'cstb': _constb(),
    }
    zeros = np.zeros((2048, 1024), np.float32)
    maps = []
    for c in range(8):
        b, hf = c // 2, c % 2
        m = dict(shared)
        m['xo'] = np.ascontiguousarray(x[b, hf * 2048:(hf + 1) * 2048])
        m['xp'] = np.ascontiguousarray(x[b, 0:2048]) if hf else zeros
        m['po'] = np.ascontiguousarray(p[0, b, hf * 2048:(hf + 1) * 2048])
        m['cst'] = _consts(c, inp)
        maps.append(m)
    return maps


def kernel(**inputs):
    nc = build()
    maps = make_in_maps(inputs)
    res = run_bass_kernel_spmd(nc, maps, core_ids=list(range(8)))
    out = np.zeros((4, 4096, 1024), np.float32)
    for c in range(8):
        out[c // 2, (c % 2) * 2048:(c % 2 + 1) * 2048] = res.results[c]['out']
    return out
```

```python
import math
from contextlib import ExitStack

import numpy as np
import concourse.bass as bass
import concourse.mybir as mybir
from concourse.bass_utils import run_bass_kernel_spmd

F32 = mybir.dt.float32
BF16 = mybir.dt.bfloat16
AF = mybir.ActivationFunctionType
ALU = mybir.AluOpType
AX = mybir.AxisListType

SAME_ENGINE_SYNC = True
A_LAG = 1
A2_LAG = 6
PE_WARM_REPS = 1
EPS = 1e-6
NT = 16
NEG = -30000.0
LAM_INIT = 0.8 - 0.6 * math.exp(0.0)
LNC = math.log(128 ** -0.5)

O_MQ, O_MK, O_MV, O_MO, O_MI, O_MF, O_AQ, O_AK, O_AV = 0, 512, 1024, 1536, 2048, 2052, 2056, 2568, 3080

C_G1, C_G2, C_G3, C_GY, C_CW, C_CB, C_QG, C_KG, C_LAM, C_FL, C_CSO, C_CSP, C_ID, C_TRI, C_GB = (
    0, 8, 16, 24, 32, 64, 72, 136, 200, 456, 460, 716, 972, 1100, 1228)
C_TOT = 1230


class _Eng:
    def __init__(self, name, h, sem):
        self.name, self.h, self.sem = name, h, sem
        self.n = 0
        self.count = 0
        self.incs = []
        self.last = None
        self.last_seq = 0
        self.waited = {}
        self.dsems = []
        self.dcnt = []
        self.dnext = 0


class Sched:
    def __init__(self, nc, stack, ndma):
        self.nc = nc
        hs = {'pe': nc.tensor, 'act': nc.scalar, 'dve': nc.vector, 'pool': nc.gpsimd, 'sp': nc.sync}
        self.e = {}
        for k, h in hs.items():
            sem = stack.enter_context(nc.semaphore("s_" + k))
            self.e[k] = _Eng(k, h, sem)
            for i in range(ndma.get(k, 0)):
                self.e[k].dsems.append(stack.enter_context(nc.semaphore("d_%s%d" % (k, i))))
                self.e[k].dcnt.append(0)
        self.st = {}

    def _target(self, dep):
        if dep[0] == 'd':
            return dep[1], dep[2]
        p = self.e[dep[1]]
        seq = dep[2]
        found = None
        for (s, c) in reversed(p.incs):
            if s >= seq:
                found = c
            else:
                break
        if found is None:
            p.count += 1
            p.last.then_inc(p.sem, 1)
            p.incs.append((p.last_seq, p.count))
            found = p.count
        return p.sem, found

    def _wait(self, eng, deps):
        E = self.e[eng]
        for dep in deps:
            if dep is None:
                continue
            if dep[0] == 'c' and dep[1] == eng and (eng == 'pe' or not SAME_ENGINE_SYNC):
                continue
            sem, val = self._target(dep)
            key = id(sem)
            if E.waited.get(key, 0) >= val:
                continue
            E.h.wait_ge(sem, val)
            E.waited[key] = val

    def _deps(self, r, w, eng=None):
        deps = []
        for k in r:
            s = self.st.get(k)
            if s is not None:
                deps.append(s[0])
                if isinstance(k, tuple) and k[0] == 'bank':
                    deps.extend(d for d in s[1].values() if not (d[0] == 'c' and d[1] == eng))
        for k in w:
            s = self.st.get(k)
            if s is not None:
                deps.append(s[0])
                deps.extend(s[1].values())
        return deps

    def _record(self, dep, r, w):
        rk = (dep[0], dep[1] if dep[0] == 'c' else id(dep[1]))
        for k in r:
            s = self.st.setdefault(k, [None, {}])
            s[1][rk] = dep
        for k in w:
            self.st[k] = [dep, {}]

    def op(self, eng, fn, r=(), w=(), sig=None):
        E = self.e[eng]
        self._wait(eng, self._deps(r, w, eng))
        inst = fn(E.h)
        E.n += 1
        E.last = inst
        E.last_seq = E.n
        if sig or (sig is None and eng != 'pe'):
            E.count += 1
            inst.then_inc(E.sem, 1)
            E.incs.append((E.n, E.count))
        self._record(('c', eng, E.n), r, w)
        return inst

    def dma(self, q, out, in_, r=(), w=(), **kw):
        E = self.e[q]
        self._wait(q, self._deps(r, w))
        i = E.dnext
        E.dnext = (i + 1) % len(E.dsems)
        sem = E.dsems[i]
        if E.dcnt[i] > 0:
            key = id(sem)
            if E.waited.get(key, 0) < E.dcnt[i]:
                E.h.wait_ge(sem, E.dcnt[i])
                E.waited[key] = E.dcnt[i]
        E.dcnt[i] += 16
        E.h.dma_start(out=out, in_=in_, **kw).then_inc(sem, 16)
        dep = ('d', sem, E.dcnt[i])
        self._record(dep, r, w)
        return dep

    def barrier(self):
        deps = []
        for s in self.st.values():
            if s[0] is not None:
                deps.append(s[0])
            deps.extend(s[1].values())
        for eng in self.e:
            self._wait(eng, deps)

    def finish(self, eng, keys):
        self._wait(eng, [self.st[k][0] for k in keys if k in self.st])


def build(dbg=(), n_pre=NT, n_own=NT, phases='CDE'):
    nc = bass.Bass("TRN2", target_bir_lowering=False)

    def din(name, shape):
        return nc.dram_tensor(name, list(shape), F32, kind="ExternalInput").ap()

    xo = din("xo", [2048, 1024])
    xp = din("xp", [2048, 1024])
    po = din("po", [2048, 256])
    w_in = din("w_in", [1024, 3592])
    w_out = din("w_out", [1024, 1024])
    w_up = din("w_up", [1024, 4096])
    w_down = din("w_down", [4096, 1024])
    w_gate = din("w_gate", [1024, 1024])
    w_ple = din("w_ple", [256, 1024])
    cst_d = din("cst", [128, C_TOT])
    cstb_d = din("cstb", [128, 128 + 1024])
    out_d = nc.dram_tensor("out", [2048, 1024], F32, kind="ExternalOutput").ap()
    dbg_d = {}
    for name, shape, dt in dbg:
        dbg_d[name] = nc.dram_tensor("dbg_" + name, list(shape), dt, kind="ExternalOutput").ap()

    with ExitStack() as st:
        S = Sched(nc, st, {'sp': 12, 'pool': 8})

        def sb(name, shape, dt=F32):
            return st.enter_context(nc.sbuf_tensor("t_" + name, list(shape), dt))

        def freed(name, shape, dt=F32):
            return nc.sbuf_tensor(name, list(shape), dt)

        cst = sb("cst", [128, C_TOT])
        cstb = sb("cstb", [128, 128 + 1024], BF16)
        identb = cstb[:, 0:128]
        identf = cst[:, C_ID:C_ID + 128]
        tri = cst[:, C_TRI:C_TRI + 128]
        banks = [st.enter_context(nc.psum_tensor("bank%d" % i, [128, 512], F32)) for i in range(8)]
        small = sb("small", [128, 64])
        block = st.enter_context(nc.Block())

        S.dma('sp', cst[:], cst_d, w=['cst'])
        S.dma('pool', cstb[:], cstb_d, w=['cstb'])

        def bk(i):
            return ('bank', i)

        def rstd_from_ss(ss_ap, out_ap, n, key_ss, key_out, mul=None):
            S.op('act', lambda e: e.activation(out_ap, ss_ap, AF.Ln, bias=epsc[:ss_ap.shape[0], 0:1], scale=1.0 / n),
                 r=[key_ss, 'consts'], w=[key_out])
            S.op('act', lambda e: e.activation(out_ap, out_ap, AF.Exp, scale=-0.5), r=[key_out], w=[key_out])
            if mul is not None:
                S.op('dve', lambda e: e.tensor_scalar(out_ap, out_ap, float(mul), None, ALU.mult), r=[key_out], w=[key_out])

        epsc = sb("epsc", [128, 4])
        ones4 = sb("ones4", [4, 128])
        S.op('pool', lambda e: e.memset(epsc[:, 0:1], EPS), w=['consts'])
        S.op('pool', lambda e: e.memset(epsc[:, 1:2], 1.0), w=['consts'])
        S.op('pool', lambda e: e.memset(ones4[:], 1.0), w=['consts4'])

        lamv = cst[:, C_LAM:C_LAM + 256]
        lj = sb("lj", [128, 64])
        S.op('dve', lambda e: e.scalar_tensor_tensor(out=lj[:], in0=lamv[:, 0:64], scalar=1.0, in1=lamv[:, 64:128],
                                                     op0=ALU.mult, op1=ALU.mult, accum_out=small[:, 0:1]),
             r=['cst'], w=['lj', 'small'])
        S.op('dve', lambda e: e.scalar_tensor_tensor(out=lj[:], in0=lamv[:, 128:192], scalar=1.0, in1=lamv[:, 192:256],
                                                     op0=ALU.mult, op1=ALU.mult, accum_out=small[:, 1:2]),
             r=['cst', 'lj'], w=['lj', 'small'])
        S.op('act', lambda e: e.activation(small[:, 2:4], small[:, 0:2], AF.Exp), r=['small'], w=['small'])
        S.op('dve', lambda e: e.tensor_tensor(small[:, 4:5], small[:, 2:3], small[:, 3:4], ALU.subtract), r=['small'], w=['small'])
        S.op('dve', lambda e: e.tensor_scalar(small[:, 5:6], small[:, 4:5], float(LAM_INIT), None, ALU.add), r=['small'], w=['small'])
        lam_ap = small[:, 5:6]
        gpar = sb("gpar", [4, 8])
        gb = cst[0:4, C_GB:C_GB + 2]
        fl = cst[0:4, C_FL:C_FL + 4]
        S.op('dve', lambda e: e.tensor_copy(gpar[:, 0:1], fl[:, 0:1]), r=['cst'], w=['gpar'])
        S.op('dve', lambda e: e.scalar_tensor_tensor(out=gpar[:, 1:2], in0=gb[:, 0:1], scalar=fl[:, 0:1], in1=fl[:, 1:2],
                                                     op0=ALU.mult, op1=ALU.add), r=['cst', 'gpar'], w=['gpar'])
        S.op('pool', lambda e: e.memset(gpar[:, 2:3], 1.0), r=['gpar'], w=['gpar'])
        S.op('dve', lambda e: e.tensor_copy(gpar[:, 3:4], gb[:, 0:1]), r=['cst', 'gpar'], w=['gpar'])
        S.op('dve', lambda e: e.tensor_scalar(gpar[:, 4:5], gb[:, 1:2], -1.0, None, ALU.mult), r=['cst', 'gpar'], w=['gpar'])
        S.op('dve', lambda e: e.tensor_copy(gpar[:, 5:6], fl[:, 0:1]), r=['cst', 'gpar'], w=['gpar'])
        pbias = cst[:, C_FL + 2:C_FL + 3]

        yT = sb("yT", [128, 8, 2048], BF16)
        ab = ExitStack()

        def sa(name, shape, dt=F32):
            return ab.enter_context(nc.sbuf_tensor("t_" + name, list(shape), dt))

        win = sa("win", [128, 8, 3592], BF16)
        kT = sa("kT", [128, 4, 4096], BF16)
        vx = sa("vx", [128, 32, 4, 130], BF16)
        xt = sa("xt", [128, 1024])
        u = sa("u", [128, 1024], BF16)
        uT = [sa("uT%d" % i, [128, 8, 128], BF16) for i in range(2)]
        ssb = sa("ssb", [128, 8])
        raw = sa("raw", [128, 8, 131])
        acc = sa("acc", [128, 4, 128])
        qkT2 = [sa("qkT%d" % i, [128, 8, 128], BF16) for i in range(2)]
        gf = sa("gf", [4, 8, 128])
        carry = sa("carry", [4, 8])
        Gt = [small[:, 16 + 12 * i:28 + 12 * i] for i in range(3)]
        mvx = [sa("mvx%d" % i, [128, 4, 130], BF16) for i in range(2)]
        sigo2 = [sa("sigo%d" % i, [128, 512], BF16) for i in range(2)]
        mls = sa("mls", [128, 512], BF16)
        ktl = [mls[:, i * 128:(i + 1) * 128] for i in range(2)]
        stm = [mls[:, 256 + i * 128:256 + (i + 1) * 128] for i in range(2)]
        Pst = sa("Pst", [128, 4, 130])
        Cbf = sa("Cbf", [128, 4, 130], BF16)
        ml = sa("ml", [128, 32])
        qf = sa("qf", [128, 512])
        qb16 = sa("qb16", [128, 512], BF16)
        qtok = [sa("qtok%d" % i, [128, 512], BF16) for i in range(2)]
        qTg = [sa("qTg%d" % i, [128, 4, 256], BF16) for i in range(2)]
        Et = [sa("Et%d" % i, [128, 512], BF16) for i in range(2)]
        ofin = [sa("ofin%d" % i, [128, 128]) for i in range(2)]
        yat = [sa("yat%d" % i, [128, 512], BF16) for i in range(2)]
        al = sa("al", [128, 16])
        ae = small[:, 8:16]

        w_in_v = w_in.rearrange("(k p) n -> p k n", p=128)
        WGRP = [(O_MK, O_MV), (O_MV, O_MO), (O_MI, O_AQ), (O_AK, O_AV), (O_AV, 3592), (O_MQ, O_MK), (O_MO, O_MI), (O_AQ, O_AK)]
        for gi_, (c0_, c1_) in enumerate(WGRP):
            S.dma('pool', win[:, :, c0_:c1_], w_in_v[:, :, c0_:c1_], w=[('win', gi_)])

        def wkey(col):
            for gi_, (c0_, c1_) in enumerate(WGRP):
                if c0_ <= col < c1_:
                    return ('win', gi_)
            raise ValueError(col)

        S.op('pool', lambda e: e.memset(vx[:, :, :, 128:130], 1.0), w=[('vx', j) for j in range(32)])
        S.op('pool', lambda e: e.tensor_scalar(vx[:, 0:16, :, 128:130], vx[:, 0:16, :, 128:130], cst[:, C_FL:C_FL + 1], 1.0,
                                               ALU.mult, ALU.mult),
             r=['cst'] + [('vx', j) for j in range(16)], w=[('vx', j) for j in range(16)])
        for i in range(2):
            S.op('pool', lambda e: e.memset(mvx[i][:, :, 128:130], 1.0), w=[('mvx', i)])
        for i in range(3):
            S.op('pool', lambda e: e.memset(Gt[i], 1.0), w=[('Gt', i)])
        S.op('pool', lambda e: e.memset(qTg[0][:], 0.0), w=[('qTg', 0, 0), ('qTg', 0, 1)])
        S.op('pool', lambda e: e.memset(qTg[1][:], 0.0), w=[('qTg', 1, 0), ('qTg', 1, 1)])
        S.op('pool', lambda e: e.memset(Pst[:], 0.0), w=[('Pst', h) for h in range(4)])
        S.op('pool', lambda e: e.memset(Cbf[:], 0.0), w=['Cbf'])
        S.op('pool', lambda e: e.memset(raw[:], 0.0), w=[('raw', c) for c in range(8)])
        S.op('pool', lambda e: e.memset(carry[:], 0.0), w=[('carry', j) for j in range(8)])

        cw = cst[:, C_CW:C_CW + 32].rearrange("p (c j) -> p c j", j=4)
        cb = cst[:, C_CB:C_CB + 8]
        cnt = {'tt': 0}

        def norm_to_uT(x_tile, kx, gcol, uT_t, kuT, scr=None):
            if scr is None:
                u_, ku, ssb_, kss, tb = u, 'u', ssb, 'ssb', 0
            else:
                u_, ku, ssb_, kss, tb = scr
            return _norm_to_uT(x_tile, kx, gcol, uT_t, kuT, u_, ku, ssb_, kss, tb)

        def _norm_to_uT(x_tile, kx, gcol, uT_t, kuT, u, ku, ssb, kss, tb):
            _norm_a1(x_tile, kx, u, ku, ssb, kss)
            _norm_a2(gcol, uT_t, kuT, u, ku, tb)

        def _norm_a1(x_tile, kx, u, ku, ssb, kss):
            S.op('act', lambda e: e.activation(u[:], x_tile, AF.Square, accum_out=ssb[:, 0:1]), r=[kx], w=[ku, kss])
            rstd_from_ss(ssb[:, 0:1], ssb[:, 1:2], 1024.0, kss, kss)
            S.op('dve', lambda e: e.tensor_scalar(u[:], x_tile, ssb[:, 1:2], None, ALU.mult), r=[kx, kss], w=[ku])

        def _norm_a2(gcol, uT_t, kuT, u, ku, tb):
            psb = banks[tb][:].bitcast(BF16)
            for k in range(8):
                S.op('pe', lambda e: e.transpose(psb[:, k * 128:(k + 1) * 128], u[:, k * 128:(k + 1) * 128], identb),
                     r=[ku, 'cstb'], w=[bk(tb)], sig=k == 7)
            g_bc = cst[:, gcol:gcol + 8].unsqueeze(2).to_broadcast([128, 8, 128])
            S.op('dve', lambda e: e.tensor_tensor(uT_t[:], psb[:, 0:1024].rearrange("p (k t) -> p k t", k=8), g_bc, ALU.mult),
                 r=[bk(tb), 'cst'], w=[kuT])

        def transpose_to(src_tok, ksrc, dst_ap, kdst, scale_cols=None, bank=0):
            psb = banks[bank][:].bitcast(BF16)
            for k in range(4):
                S.op('pe', lambda e: e.transpose(psb[:, k * 128:(k + 1) * 128], src_tok[:, k * 128:(k + 1) * 128], identb),
                     r=[ksrc, 'cstb'], w=[bk(bank)], sig=k == 3)
            src = psb[:, 0:512].rearrange("p (k t) -> p k t", k=4)
            if scale_cols is None:
                S.op('dve', lambda e: e.tensor_copy(dst_ap, src), r=[bk(bank)], w=[kdst])
            else:
                g_bc = scale_cols.unsqueeze(2).to_broadcast([128, 4, 128])
                S.op('dve', lambda e: e.tensor_tensor(dst_ap, src, g_bc, ALU.mult), r=[bk(bank), 'cst'], w=[kdst])

        def proj_tm(uT_t, kuT, col, bank):
            for k in range(8):
                S.op('pe', lambda e: e.matmul(banks[bank][:, :], uT_t[:, k, :], win[:, k, col:col + 512],
                                              start=(k == 0), stop=(k == 7)),
                     r=[kuT, wkey(col)], w=[bk(bank)], sig=k == 7)

        def rr(*gens):
            gens = [g for g in gens if g is not None]
            while gens:
                for g in list(gens):
                    try:
                        next(g)
                    except StopIteration:
                        gens.remove(g)
                yield

        def run(gen):
            for _ in gen:
                pass


        def qk_prep(bank, gcol, cs, qb16, k16):
            q3 = qf[:].rearrange("p (g d) -> p g d", d=64)
            S.op('act', lambda e: e.activation(qf[:], banks[bank][:, :], AF.Square), r=[bk(bank)], w=['qf'])
            S.op('dve', lambda e: e.tensor_reduce(out=al[:, 0:8], in_=q3, axis=AX.X, op=ALU.add), r=['qf'], w=['al'])
            S.op('act', lambda e: e.activation(al[:, 8:16], al[:, 0:8], AF.Ln, bias=epsc[:, 0:1], scale=1.0 / 64),
                 r=['al', 'consts'], w=['al'])
            S.op('act', lambda e: e.activation(al[:, 8:16], al[:, 8:16], AF.Exp, scale=-0.5), r=['al'], w=['al'])
            S.op('dve', lambda e: e.tensor_tensor(q3, banks[bank][:, :].rearrange("p (g d) -> p g d", d=64),
                                                  al[:, 8:16].unsqueeze(2).to_broadcast([128, 8, 64]), ALU.mult),
                 r=[bk(bank), 'al'], w=['qf'])
            gg = cst[:, gcol:gcol + 64].unsqueeze(1).to_broadcast([128, 8, 64])
            o3 = qb16[:].rearrange("p (g d) -> p g d", d=64)
            yield S.op('dve', lambda e: e.tensor_tensor(o3, q3, gg, ALU.mult), r=['qf', 'cst'], w=[k16])
            gg16 = cst[:, gcol:gcol + 16].unsqueeze(1).to_broadcast([128, 8, 16])
            yield S.op('dve', lambda e: e.tensor_tensor(q3[:, :, 0:16], q3[:, :, 0:16], gg16, ALU.mult), r=['qf', 'cst'], w=['qf'])
            x1, x2 = q3[:, :, 0:8], q3[:, :, 8:16]
            cc = cs[:, 0:8].unsqueeze(1).to_broadcast([128, 8, 8])
            sn = cs[:, 8:16].unsqueeze(1).to_broadcast([128, 8, 8])
            r4 = [q3[:, :, 16 + 8 * a_:24 + 8 * a_] for a_ in range(4)]
            S.op('dve', lambda e: e.tensor_tensor(r4[0], x1, cc, ALU.mult), r=['qf', 'cst'], w=['qf'])
            S.op('dve', lambda e: e.tensor_tensor(r4[1], x2, sn, ALU.mult), r=['qf', 'cst'], w=['qf'])
            S.op('dve', lambda e: e.tensor_tensor(r4[2], x2, cc, ALU.mult), r=['qf', 'cst'], w=['qf'])
            yield S.op('dve', lambda e: e.tensor_tensor(r4[3], x1, sn, ALU.mult), r=['qf', 'cst'], w=['qf'])
            S.op('dve', lambda e: e.tensor_tensor(o3[:, :, 0:8], r4[0], r4[1], ALU.subtract), r=['qf', k16], w=[k16])
            yield S.op('dve', lambda e: e.tensor_tensor(o3[:, :, 8:16], r4[2], r4[3], ALU.add), r=['qf', k16], w=[k16])

        def attention_group(g):
            kbs = list(range(16)) + [16 + j for j in range(2 * g + 2)]
            units = [(h, idx, kb) for h in range(4) for idx, kb in enumerate(kbs)]
            PSB = (4, 5)
            for tt in range(2):
                psb = banks[4 + tt][:].bitcast(BF16)
                for k in range(4):
                    S.op('pe', lambda e: e.transpose(psb[:, k * 128:(k + 1) * 128], qtok[tt][:, k * 128:(k + 1) * 128], identb),
                         r=[('qtok', tt), 'cstb'], w=[bk(4 + tt)], sig=(k == 3))
                yield
                c_ = tt * 128
                S.op('act', lambda e: e.activation(qTg[0][0:64, :, c_:c_ + 128], psb[0:64, 0:512].rearrange("p (k t) -> p k t", k=4),
                                                   AF.Copy), r=[bk(4 + tt)], w=[('qTg', 0, tt)])
                yield S.op('act', lambda e: e.activation(qTg[1][64:128, :, c_:c_ + 128],
                                                         psb[64:128, 0:512].rearrange("p (k t) -> p k t", k=4), AF.Copy),
                           r=[bk(4 + tt)], w=[('qTg', 1, tt)])
            QK = [('qTg', m, tt) for m in range(2) for tt in range(2)]

            def st_mm(n):
                h, idx, kb = units[n]
                pb = PSB[n % 2]
                for rep_ in range(PE_WARM_REPS):
                    S.op('pe', lambda e: e.matmul(banks[pb][:, 0:256], kT[:, h, kb * 128:(kb + 1) * 128],
                                                  qTg[0][:, h, :], start=True, stop=True),
                         r=[('kT', kb)] + QK, w=[bk(pb)], sig=False)
                    S.op('pe', lambda e: e.matmul(banks[pb][:, 256:512], kT[:, h, kb * 128:(kb + 1) * 128],
                                                  qTg[1][:, h, :], start=True, stop=True),
                         r=[('kT', kb)] + QK, w=[bk(pb)], sig=(rep_ == PE_WARM_REPS - 1))

            st_mm(0)
            for n, (h, idx, kb) in enumerate(units):
                if n + 1 < len(units):
                    st_mm(n + 1)
                pb = PSB[n % 2]
                E = Et[n % 2]
                kE = ('Et', n % 2)
                S.op('act', lambda e: e.activation(E[:], banks[pb][:, :], AF.Exp, scale=0.125), r=[bk(pb)], w=[kE])
                own = kb - 16
                if own == 2 * g:
                    S.op('dve', lambda e: e.tensor_tensor(E[:], E[:], cstb[:, 128:640], ALU.mult), r=[kE, 'cstb'], w=[kE])
                elif own == 2 * g + 1:
                    S.op('dve', lambda e: e.tensor_tensor(E[:], E[:], cstb[:, 640:1152], ALU.mult), r=[kE, 'cstb'], w=[kE])
                for m in range(2):
                    for qb in range(2):
                        if qb == 0 and own == 2 * g + 1:
                            continue
                        last = (own == 2 * g) if qb == 0 else (own == 2 * g + 1)
                        a = 6 + m
                        o0 = qb * 130
                        S.op('pe', lambda e: e.matmul(banks[a][:, o0:o0 + 129], E[:, m * 256 + qb * 128: m * 256 + qb * 128 + 128],
                                                      vx[:, kb, h, 0:129], start=(idx == 0 and qb == 0), stop=last,
                                                      skip_group_check=True),
                             r=[kE, ('vx', kb)], w=[bk(a)], sig=(m == 1 and qb == 1))
                yield
                if idx == len(kbs) - 1:
                    den6 = banks[6][:, 0:260].rearrange("p (a c) -> p a c", c=130)[:, :, 128]
                    den7 = banks[7][:, 0:260].rearrange("p (a c) -> p a c", c=130)[:, :, 128]
                    S.op('dve', lambda e: e.reciprocal(ae[:, 2:4], den7), r=[bk(7)], w=[('ae', 1)])
                    S.op('dve', lambda e: e.tensor_scalar(ae[:, 2:4], ae[:, 2:4], lam_ap, None, ALU.mult), r=[('ae', 1), 'small'], w=[('ae', 1)])
                    S.op('dve', lambda e: e.reciprocal(ae[:, 0:2], den6), r=[bk(6)], w=[('ae', 0)])
                    for qb in range(2):
                        o0 = qb * 130
                        S.op('act', lambda e: e.activation(ofin[qb][:], banks[7][:, o0:o0 + 128], AF.Copy, scale=ae[:, 2 + qb:3 + qb]),
                             r=[bk(7), ('ae', 1)], w=[('ofin', qb)])
                        S.op('dve', lambda e: e.scalar_tensor_tensor(out=ofin[qb][:], in0=banks[6][:, o0:o0 + 128], scalar=ae[:, qb:qb + 1],
                                                                     in1=ofin[qb][:], op0=ALU.mult, op1=ALU.subtract),
                             r=[bk(6), ('ae', 0), ('ofin', qb)], w=[('ofin', qb)])
                    yield
                    for qb in range(2):
                        S.op('act', lambda e: e.activation(yat[qb][:, h * 128:(h + 1) * 128], ofin[qb][:], AF.Square,
                                                           accum_out=ae[:, 4 + qb:5 + qb]),
                             r=[('ofin', qb)], w=[('yat', qb), ('ae', 2 + qb)])
                    S.op('act', lambda e: e.activation(ae[:, 6:8], ae[:, 4:6], AF.Ln, bias=epsc[:, 0:1], scale=1.0 / 128),
                         r=[('ae', 2), ('ae', 3), 'consts'], w=[('ae', 4)])
                    S.op('act', lambda e: e.activation(ae[:, 6:8], ae[:, 6:8], AF.Exp, scale=-0.5), r=[('ae', 4)], w=[('ae', 4)])
                    for qb in range(2):
                        S.op('dve', lambda e: e.tensor_scalar(yat[qb][:, h * 128:(h + 1) * 128], ofin[qb][:], ae[:, 6 + qb:7 + qb],
                                                               float(1.0 - LAM_INIT), ALU.mult, ALU.mult),
                             r=[('ofin', qb), ('ae', 4)], w=[('yat', qb)])
                    yield
            for qb in range(2):
                t0 = (2 * g + qb) * 128
                transpose_to(yat[qb], ('yat', qb), yT[:, 4:8, t0:t0 + 128], ('yT', 2 * g + qb, 1),
                             scale_cols=cst[:, C_GY + 4:C_GY + 8], bank=4 + qb)
                yield

        NTT = n_pre + n_own
        def tt_info(n):
            own = n >= n_pre
            i = n - n_pre if own else n
            return own, i

        def stageA(n):
            own, i = tt_info(n)
            xsrc = xo if own else xp
            S.dma('sp', xt[:], xsrc[i * 128:(i + 1) * 128, :], w=['xt'])
            _norm_a1(xt[:], 'xt', u, 'u', ssb, 'ssb')
            for _ in range(A2_LAG):
                yield
            _norm_a2(C_G1, uT[n % 2], ('uT', n % 2), u, 'u', 0)
            yield

        def front(n):
            own, i = tt_info(n)
            blk = (16 + i) if own else i
            uT_t, kuT = uT[n % 2], ('uT', n % 2)
            Gc, kG = Gt[n % 3], ('Gt', n % 3)
            mv_t, kmv = mvx[n % 2], ('mvx', n % 2)
            qkT, kq = qkT2[n % 2], n % 2
            sigo = sigo2[n % 2]
            if own:
                GB, FMB, TMB = 0, (1, 1), (1, 1)
            else:
                GB, FMB, TMB = 1, (2, 3), (6, 7)
            need_q = own or i == NT - 1

            def T2():
                for half in ((0, 1) if need_q else (1,)):
                    pbk = FMB[half]
                    c0 = half * 4
                    for c in range(c0, c0 + 4):
                        col = (O_MQ + c * 128) if c < 4 else (O_MK + (c - 4) * 128)
                        for k in range(8):
                            S.op('pe', lambda e: e.matmul(banks[pbk][:, (c % 4) * 128:(c % 4 + 1) * 128], win[:, k, col:col + 128],
                                                          uT_t[:, k, :], start=(k == 0), stop=(k == 7)),
                                 r=[kuT, wkey(col)], w=[bk(pbk)], sig=(k == 7))
                    RK = [('raw', c) for c in range(c0, c0 + 4)]
                    if own:
                        S.op('dve', lambda e: e.tensor_copy(raw[:, c0:c0 + 4, 3:131], banks[pbk][:, :].rearrange("p (c t) -> p c t", c=4)),
                             r=[bk(pbk)], w=RK)
                    else:
                        S.op('act', lambda e: e.activation(raw[:, c0:c0 + 4, 3:131], banks[pbk][:, :].rearrange("p (c t) -> p c t", c=4),
                                                           AF.Copy), r=[bk(pbk)], w=RK)
                    conv = own or half == 1
                    if conv:
                        for c in range(c0, c0 + 4):
                            if own:
                                S.op('dve', lambda e: e.tensor_scalar(acc[:, c % 4, :], banks[pbk][:, (c % 4) * 128:(c % 4 + 1) * 128],
                                                                      cw[:, c, 3:4], cb[:, c:c + 1], ALU.mult, ALU.add),
                                     r=[bk(pbk), 'cst'], w=[('acc', c % 4)])
                            else:
                                S.op('act', lambda e: e.activation(acc[:, c % 4, :], banks[pbk][:, (c % 4) * 128:(c % 4 + 1) * 128],
                                                                   AF.Identity, bias=cb[:, c:c + 1], scale=cw[:, c, 3:4]),
                                     r=[bk(pbk), 'cst'], w=[('acc', c % 4)])
                    yield
                    if conv:
                        for c in range(c0, c0 + 4):
                            for j in range(3):
                                yield S.op('dve', lambda e: e.scalar_tensor_tensor(out=acc[:, c % 4, :], in0=raw[:, c, j:j + 128],
                                                                                   scalar=cw[:, c, j:j + 1], in1=acc[:, c % 4, :],
                                                                                   op0=ALU.mult, op1=ALU.add),
                                           r=[('raw', c), 'cst', ('acc', c % 4)], w=[('acc', c % 4)])
                    if own:
                        yield S.op('dve', lambda e: e.tensor_copy(raw[:, c0:c0 + 4, 0:3], raw[:, c0:c0 + 4, 128:131]), r=RK, w=RK)
                    else:
                        yield S.op('act', lambda e: e.activation(raw[:, c0:c0 + 4, 0:3], raw[:, c0:c0 + 4, 128:131], AF.Copy), r=RK, w=RK)
                    if conv:
                        tmp = raw[:, c0:c0 + 4, 3:131]
                        AK = [('acc', c) for c in range(4)]
                        S.op('act', lambda e: e.activation(tmp, acc[:, :, :], AF.Exp, scale=-1.0), r=AK + RK, w=RK)
                        S.op('act', lambda e: e.activation(tmp, tmp, AF.Ln, bias=epsc[:, 1:2]), r=RK + ['consts'], w=RK)
                        yield S.op('act', lambda e: e.activation(tmp, tmp, AF.Exp, scale=-1.0), r=RK, w=RK)
                        yield S.op('dve', lambda e: e.tensor_tensor(qkT[:, c0:c0 + 4, :], acc[:, :, :], tmp, ALU.mult),
                                   r=AK + RK, w=[('qkT', kq, half)])

            def T3():
                for gi, col in enumerate((O_MI, O_MF)):
                    for k in range(8):
                        S.op('pe', lambda e: e.matmul(banks[GB][0:4, gi * 128:(gi + 1) * 128], win[:, k, col:col + 4],
                                                      uT_t[:, k, :], start=(k == 0), stop=(k == 7)),
                             r=[kuT, wkey(col)], w=[bk(GB)], sig=(k == 7))
                po_ = 2 if own else 0
                G = lambda j: ('gf', j)
                C = lambda j: ('carry', j)
                S.op('act', lambda e: e.activation(gf[:, 0, :], banks[GB][0:4, 0:128], AF.Identity,
                                                   bias=gpar[:, po_ + 1:po_ + 2], scale=gpar[:, po_:po_ + 1]),
                     r=[bk(GB), 'gpar'], w=[G(0)])
                yield S.op('act', lambda e: e.activation(gf[:, 1, :], banks[GB][0:4, 128:256], AF.Exp, bias=gpar[:, 4:5], scale=-1.0),
                           r=[bk(GB), 'gpar'], w=[G(1)])
                yield S.op('act', lambda e: e.activation(gf[:, 1, :], gf[:, 1, :], AF.Ln, bias=epsc[0:4, 1:2]),
                           r=[G(1), 'consts'], w=[G(1)])
                if not own:
                    yield S.op('dve', lambda e: e.tensor_scalar(gf[:, 1, :], gf[:, 1, :], gpar[:, 5:6], None, ALU.mult),
                               r=[G(1), 'gpar'], w=[G(1)])
                yield S.op('dve', lambda e: e.tensor_tensor_scan(gf[:, 2, :], ones4[:], gf[:, 1, :], carry[:, 0:1], ALU.mult, ALU.add),
                           r=[G(1), 'consts4', C(0)], w=[G(2)])
                yield S.op('dve', lambda e: e.tensor_tensor(gf[:, 3, :], gf[:, 0, :], gf[:, 2, :], ALU.add), r=[G(0), G(2)], w=[G(3)])
                yield S.op('dve', lambda e: e.tensor_tensor_scan(gf[:, 4, :], ones4[:], gf[:, 3, :], carry[:, 1:2], ALU.mult, ALU.max),
                           r=[G(3), 'consts4', C(1)], w=[G(4)])
                S.op('dve', lambda e: e.tensor_scalar(carry[:, 2:3], carry[:, 1:2], -1.0, float(LNC), ALU.mult, ALU.add),
                     r=[C(1)], w=[C(2)])
                S.op('dve', lambda e: e.tensor_scalar(carry[:, 3:4], carry[:, 1:2], -1.0, None, ALU.mult), r=[C(1)], w=[C(3)])
                yield S.op('dve', lambda e: e.tensor_tensor(carry[:, 4:5], carry[:, 1:2], gf[:, 4, 127:128], ALU.subtract),
                           r=[C(1), G(4)], w=[C(4)])
                yield S.op('act', lambda e: e.activation(gf[:, 5, :], gf[:, 3, :], AF.Exp, bias=carry[:, 2:3]), r=[G(3), C(2)], w=[G(5)])
                yield S.op('act', lambda e: e.activation(gf[:, 6, :], gf[:, 2, :], AF.Exp, bias=carry[:, 3:4]), r=[G(2), C(3)], w=[G(6)])
                yield S.op('act', lambda e: e.activation(gf[:, 7, :], ones4[:], AF.Exp, bias=carry[:, 4:5], scale=0.0),
                           r=[C(4), 'consts4'], w=[G(7)])
                S.op('dve', lambda e: e.tensor_copy(carry[:, 0:1], gf[:, 2, 127:128]), r=[G(2), C(0)], w=[C(0)])
                yield S.op('dve', lambda e: e.tensor_copy(carry[:, 1:2], gf[:, 4, 127:128]), r=[G(4), C(1)], w=[C(1)])
                for a in range(3):
                    S.op('pe', lambda e: e.transpose(banks[GB][:, 256 + a * 4:256 + a * 4 + 4], gf[:, 5 + a, :], identf[0:4, 0:4]),
                         r=[G(5 + a), 'cst'], w=[bk(GB)], sig=(a == 2))
                yield S.op('dve', lambda e: e.tensor_copy(Gc, banks[GB][:, 256:268]), r=[bk(GB)], w=[kG])

            def T4():
                tb = [0]

                def nxt():
                    tb[0] += 1
                    return TMB[tb[0] % 2]
                b_ = nxt()
                proj_tm(uT_t, kuT, O_MV, b_)
                if own:
                    yield S.op('dve', lambda e: e.tensor_copy(mv_t[:, :, 0:128], banks[b_][:, :].rearrange("p (h d) -> p h d", h=4)),
                               r=[bk(b_)], w=[kmv])
                else:
                    yield S.op('act', lambda e: e.activation(mv_t[:, :, 0:128], banks[b_][:, :].rearrange("p (h d) -> p h d", h=4), AF.Copy),
                               r=[bk(b_)], w=[kmv])
                if own:
                    b_ = nxt()
                    proj_tm(uT_t, kuT, O_MO, b_)
                    S.op('act', lambda e: e.activation(qf[:], banks[b_][:, :], AF.Exp, scale=-1.0), r=[bk(b_)], w=['qf'])
                    S.op('act', lambda e: e.activation(qf[:], qf[:], AF.Ln, bias=epsc[:, 1:2]), r=['qf', 'consts'], w=['qf'])
                    yield S.op('act', lambda e: e.activation(sigo[:], qf[:], AF.Exp, scale=-1.0), r=['qf'], w=[('sigo', n % 2)])
                cs_k = cst[:, (C_CSO if own else C_CSP) + i * 16:(C_CSO if own else C_CSP) + i * 16 + 16]
                b_ = nxt()
                proj_tm(uT_t, kuT, O_AK, b_)
                yield from qk_prep(b_, C_KG, cs_k, qb16, 'qb16')
                transpose_to(qb16, 'qb16', kT[:, :, blk * 128:(blk + 1) * 128], ('kT', blk), bank=0)
                yield
                b_ = nxt()
                proj_tm(uT_t, kuT, O_AV, b_)
                if own:
                    yield S.op('dve', lambda e: e.tensor_copy(vx[:, blk, :, 0:128], banks[b_][:, :].rearrange("p (h d) -> p h d", h=4)),
                               r=[bk(b_)], w=[('vx', blk)])
                else:
                    yield S.op('act', lambda e: e.activation(vx[:, blk, :, 0:128], banks[b_][:, :].rearrange("p (h d) -> p h d", h=4), AF.Copy),
                               r=[bk(b_)], w=[('vx', blk)])
                if own:
                    b_ = nxt()
                    proj_tm(uT_t, kuT, O_AQ, b_)
                    yield from qk_prep(b_, C_QG, cs_k, qtok[i % 2], ('qtok', i % 2))

            yield from rr(T2(), T3(), T4())
            if own and i == 0 and 'qkT0' in dbg_d:
                S.dma('sp', dbg_d['qkT0'], qkT[:], r=[('qkT', kq, 0), ('qkT', kq, 1)], w=['dbg_qkT0'])
                S.finish('sp', ['dbg_qkT0'])

        def back(n):
            own, i = tt_info(n)
            Gc, kG = Gt[n % 3], ('Gt', n % 3)
            Gp, kGp = Gt[(n + 2) % 3], ('Gt', (n + 2) % 3)
            mv_t, kmv = mvx[n % 2], ('mvx', n % 2)
            qkT, kq = qkT2[n % 2], n % 2
            sigo = sigo2[n % 2]
            HB = (2, 3) if own else (4, 5)
            MK = [('ktl', 0), ('ktl', 1), ('stm', 0), ('stm', 1)]

            def head(h):
                e_ = h % 2
                kt_, kkt = ktl[e_], ('ktl', e_)
                st_, kst = stm[e_], ('stm', e_)
                hb = HB[e_]
                B = banks[hb]
                psb = B[:].bitcast(BF16)
                S.op('pe', lambda e: e.transpose(psb[:, 0:128], qkT[:, 4 + h, :], identb), r=[('qkT', kq, 1), 'cstb'], w=[bk(hb)], sig=True)
                yield
                if own:
                    yield S.op('dve', lambda e: e.tensor_scalar(kt_, psb[:, 0:128], Gc[:, h:h + 1], None, ALU.mult),
                               r=[bk(hb), kG], w=[kkt])
                else:
                    yield S.op('act', lambda e: e.activation(kt_, psb[:, 0:128], AF.Copy, scale=Gc[:, h:h + 1]),
                               r=[bk(hb), kG], w=[kkt])
                if own:
                    S.op('pe', lambda e: e.matmul(B[:, 64:192], qkT[:, 4 + h, :], qkT[:, h, :], start=True, stop=True),
                         r=[('qkT', kq, 0), ('qkT', kq, 1)], w=[bk(hb)], sig=True)
                    yield
                    yield S.op('dve', lambda e: e.scalar_tensor_tensor(out=st_, in0=B[:, 64:192], scalar=Gc[:, h:h + 1],
                                                                       in1=tri, op0=ALU.mult, op1=ALU.mult),
                               r=[bk(hb), kG, 'cst'], w=[kst])
                    S.op('pe', lambda e: e.matmul(B[:, 192:321], qkT[:, h, :], Cbf[:, h, 0:129], start=True, stop=False),
                         r=[('qkT', kq, 0), 'Cbf'], w=[bk(hb)], sig=False)
                    S.op('pe', lambda e: e.matmul(B[:, 192:321], st_, mv_t[:, h, 0:129], start=False, stop=True),
                         r=[kst, kmv], w=[bk(hb)], sig=True)
                    yield
                S.op('pe', lambda e: e.matmul(B[:, 336:465], kt_, mv_t[:, h, 0:129], start=True, stop=True),
                     r=[kkt, kmv], w=[bk(hb)], sig=True)
                yield
                yield S.op('dve', lambda e: e.scalar_tensor_tensor(out=Pst[:, h, 0:129], in0=Pst[:, h, 0:129], scalar=Gp[:, 8 + h:9 + h],
                                                                   in1=B[:, 336:465], op0=ALU.mult, op1=ALU.add),
                           r=[bk(hb), kGp, ('Pst', h)], w=[('Pst', h)])

            def pair_epilogue(p):
                c = 8 * p
                m = ml[:, 16 * p:16 * p + 16]
                for e_ in range(2):
                    S.op('dve', lambda e: e.tensor_copy(m[:, e_:e_ + 1], banks[HB[e_]][:, 320:321]), r=[bk(HB[e_])], w=[('ml', p, 0)])
                S.op('dve', lambda e: e.scalar_tensor_tensor(out=m[:, 2:4], in0=m[:, 0:2], scalar=-1.0, in1=m[:, 0:2],
                                                             op0=ALU.mult, op1=ALU.max), r=[('ml', p, 0)], w=[('ml', p, 1)])
                S.op('dve', lambda e: e.tensor_tensor(m[:, 2:4], m[:, 2:4], Gc[:, 4 + 2 * p:6 + 2 * p], ALU.max),
                     r=[('ml', p, 1), kG], w=[('ml', p, 1)])
                yield S.op('dve', lambda e: e.reciprocal(m[:, 4:6], m[:, 2:4]), r=[('ml', p, 1)], w=[('ml', p, 2)])
                for e_ in range(2):
                    B = banks[HB[e_]]
                    h_ = 2 * p + e_
                    yield S.op('act', lambda e: e.activation(mls[:, h_ * 128:(h_ + 1) * 128], B[:, 192:320], AF.Square,
                                                             accum_out=m[:, 6 + e_:7 + e_]),
                               r=[bk(HB[e_])], w=[MK[h_], ('ml', p, 3 + e_)])
                S.op('dve', lambda e: e.tensor_tensor(m[:, 8:10], m[:, 4:6], m[:, 4:6], ALU.mult), r=[('ml', p, 2)], w=[('ml', p, 5)])
                yield S.op('dve', lambda e: e.tensor_tensor(m[:, 8:10], m[:, 8:10], m[:, 6:8], ALU.mult),
                           r=[('ml', p, 5), ('ml', p, 3), ('ml', p, 4)], w=[('ml', p, 5)])
                S.op('act', lambda e: e.activation(m[:, 10:12], m[:, 8:10], AF.Ln, bias=epsc[:, 0:1], scale=1.0 / 128),
                     r=[('ml', p, 5), 'consts'], w=[('ml', p, 6)])
                yield S.op('act', lambda e: e.activation(m[:, 10:12], m[:, 10:12], AF.Exp, scale=-0.5), r=[('ml', p, 6)], w=[('ml', p, 6)])
                yield S.op('dve', lambda e: e.tensor_tensor(m[:, 12:14], m[:, 10:12], m[:, 4:6], ALU.mult),
                           r=[('ml', p, 6), ('ml', p, 2)], w=[('ml', p, 7)])
                for e_ in range(2):
                    h = 2 * p + e_
                    B = banks[HB[e_]]
                    yield S.op('dve', lambda e: e.scalar_tensor_tensor(out=mls[:, h * 128:(h + 1) * 128], in0=B[:, 192:320],
                                                                       scalar=m[:, 12 + e_:13 + e_], in1=sigo[:, h * 128:(h + 1) * 128],
                                                                       op0=ALU.mult, op1=ALU.mult),
                               r=[bk(HB[e_]), ('ml', p, 7), ('sigo', n % 2)], w=[MK[h]])
                for e_ in range(2):
                    h = 2 * p + e_
                    B = banks[HB[e_]]
                    psb = B[:].bitcast(BF16)
                    S.op('pe', lambda e: e.transpose(psb[:, 0:128], mls[:, h * 128:(h + 1) * 128], identb),
                         r=[MK[h], 'cstb'], w=[bk(HB[e_])], sig=True)
                    yield
                    yield S.op('dve', lambda e: e.tensor_scalar(yT[:, h, i * 128:(i + 1) * 128], psb[:, 0:128],
                                                                cst[:, C_GY + h:C_GY + h + 1], None, ALU.mult),
                               r=[bk(HB[e_]), 'cst'], w=[('yT', i, 0)])

            for p in range(2):
                yield from rr(head(2 * p), head(2 * p + 1))
                if own:
                    yield from pair_epilogue(p)
            if own or i == NT - 1:
                yield S.op('dve', lambda e: e.tensor_tensor(Cbf[:, :, 0:129], Pst[:, :, 0:129],
                                                            Gc[:, 8:12].unsqueeze(2).to_broadcast([128, 4, 129]), ALU.mult),
                           r=[('Pst', h) for h in range(4)] + [kG], w=['Cbf'])

        STEPS = {'main': 0, 'side': 0}

        def wrr(main, side, per):
            credit = 0.0
            for _ in main:
                STEPS['main'] += 1
                credit += per
                while side is not None and credit >= 1.0:
                    credit -= 1.0
                    STEPS['side'] += 1
                    try:
                        next(side)
                    except StopIteration:
                        side = None
            return side

        def dump(name, ap, keys):
            if name in dbg_d:
                S.dma('sp', dbg_d[name], ap, r=keys, w=['dbg_' + name])
                S.finish('sp', ['dbg_' + name])

        side = None
        side_rate = [1.0]
        main_per_tt = [60]
        if NTT > 0:
            run(stageA(0))
            run(rr(stageA(1) if NTT > 1 else None, front(0)))
        for n in range(NTT):
            own, i = tt_info(n)
            def _late(gen, rounds):
                for _ in range(rounds):
                    yield
                yield from gen
            nxt_front = rr(_late(stageA(n + 2), A_LAG) if n + 2 < NTT else None, front(n + 1) if n + 1 < NTT else None)
            if own and i % 2 == 1:
                if side is not None:
                    run(side)
                side = attention_group(i // 2)
                for _ in range(4):
                    next(side)
                g_ = i // 2
                side_total = (18 + 2 * g_) * 4 + 4 * 4 + 2
                side_rate[0] = 1.0 * side_total / (2.0 * max(main_per_tt[0], 1))
            m0 = STEPS['main']
            side = wrr(rr(nxt_front, back(n)), side, side_rate[0])
            if own:
                main_per_tt[0] = STEPS['main'] - m0
            if own and i == 3 and 'yT01' in dbg_d:
                if side is not None:
                    run(side)
                    side = None
                dump('yT01', yT[:, :, 0:256], [('yT', a, b) for a in range(2) for b in range(2)])
        if side is not None:
            run(side)

        if 'yT' in dbg_d:
            S.dma('sp', dbg_d['yT'], yT[:], r=[('yT', i, j) for i in range(16) for j in range(2)], w=['dbg_yT'])
            S.finish('sp', ['dbg_yT'])

        ab.close()
        if 'C' not in phases:
            S.barrier()
            return nc

        hres = sb("hres", [128, 16, 1024])
        wout = sb("wout", [128, 8, 1024], BF16)
        wp = sb("wp", [128, 2, 1024], BF16)
        cd = ExitStack()

        def sc(name, shape, dt=F32):
            return cd.enter_context(nc.sbuf_tensor("t_" + name, list(shape), dt))

        u2T = sc("u2T", [128, 8, 2048], BF16)
        u = sc("u_c", [128, 1024], BF16)
        ssb = sc("ssb_c", [128, 8])
        wupb = [sc("wup%d" % i, [128, 8, 512], BF16) for i in range(2)]
        wdnb = [sc("wdn%d" % i, [128, 4, 1024], BF16) for i in range(2)]
        hidT = [sc("hidT%d" % i, [128, 4, 512], BF16) for i in range(2)]
        relu_t = sc("relu_t", [128, 512])
        w_out_v = w_out.rearrange("(k p) n -> p k n", p=128)
        S.barrier()
        for k in range(8):
            S.dma('pool', wout[:, k, :], w_out_v[:, k, :], w=[('wout', k)])
        w_up_v = w_up.rearrange("(k p) n -> p k n", p=128)
        w_dn_v = w_down.rearrange("(k p) n -> p k n", p=128)

        def load_ffn_w(j):
            b = j % 2
            S.dma('pool', wupb[b][:], w_up_v[:, :, j * 512:(j + 1) * 512], w=[('wup', b)])
            S.dma('pool', wdnb[b][:], w_dn_v[:, j * 4:(j + 1) * 4, :], w=[('wdn', b)])

        u_c2 = sc("u_c2", [128, 1024], BF16)
        ssb_c2 = sc("ssb_c2", [128, 8])
        CSCR = [(u, 'u_c', ssb, 'ssb_c', 0), (u_c2, 'u_c2', ssb_c2, 'ssb_c2', 1)]

        def c_stream(i):
            S.dma('sp', hres[:, i, :], xo[i * 128:(i + 1) * 128, :], w=[('h', i)])
            for half in range(2):
                pbk = 2 + (i * 2 + half) % 6
                for k in range(8):
                    S.op('pe', lambda e: e.matmul(banks[pbk][:, :], yT[:, k, i * 128:(i + 1) * 128],
                                                  wout[:, k, half * 512:(half + 1) * 512], start=(k == 0), stop=(k == 7)),
                         r=[('yT', i, 0), ('yT', i, 1), ('wout', k)], w=[bk(pbk)], sig=k == 7)
                yield S.op('dve', lambda e: e.tensor_tensor(hres[:, i, half * 512:(half + 1) * 512], banks[pbk][:, :],
                                                            hres[:, i, half * 512:(half + 1) * 512], ALU.add),
                           r=[bk(pbk), ('h', i)], w=[('h', i)])
            if i == 0:
                load_ffn_w(0)
                load_ffn_w(1)
            sc_ = CSCR[i % 2]
            _norm_a1(hres[:, i, :], ('h', i), sc_[0], sc_[1], sc_[2], sc_[3])
            yield
            yield
            _norm_a2(C_G2, u2T[:, :, i * 128:(i + 1) * 128], ('u2T', i), sc_[0], sc_[1], sc_[4])
            yield

        def lagged0(gens, lag):
            live = []
            pending = list(gens)
            rnd = 0
            while live or pending:
                if pending and rnd % lag == 0:
                    live.append(pending.pop(0))
                for g in list(live):
                    try:
                        next(g)
                    except StopIteration:
                        live.remove(g)
                rnd += 1

        lagged0([c_stream(i) for i in range(NT)], 1)
        if 'h1' in dbg_d:
            S.dma('sp', dbg_d['h1'], hres[:], r=[('h', i) for i in range(16)], w=['dbg_h1'])
            S.finish('sp', ['dbg_h1'])

        nacc = 0
        for j in range(8):
            b = j % 2
            for tg in range(4):
                hb = (j * 4 + tg) % 2
                for m in range(4):
                    pbk = 2 + nacc % 6
                    nacc += 1
                    for k in range(8):
                        S.op('pe', lambda e: e.matmul(banks[pbk][:, :], wupb[b][:, k, m * 128:(m + 1) * 128],
                                                      u2T[:, k, tg * 512:(tg + 1) * 512], start=(k == 0), stop=(k == 7)),
                             r=[('wup', b)] + [('u2T', tg * 4 + q) for q in range(4)], w=[bk(pbk)], sig=k == 7)
                    S.op('act', lambda e: e.activation(relu_t[:], banks[pbk][:, :], AF.Relu), r=[bk(pbk)], w=['relu_t'])
                    S.op('act', lambda e: e.activation(hidT[hb][:, m, :], relu_t[:], AF.Square),
                         r=['relu_t'], w=[('hidT', hb)])
                for q in range(4):
                    i = tg * 4 + q
                    for half in range(2):
                        pbk = 2 + nacc % 6
                        nacc += 1
                        for m in range(4):
                            S.op('pe', lambda e: e.matmul(banks[pbk][:, :], hidT[hb][:, m, q * 128:(q + 1) * 128],
                                                          wdnb[b][:, m, half * 512:(half + 1) * 512], start=(m == 0), stop=(m == 3)),
                                 r=[('hidT', hb), ('wdn', b)], w=[bk(pbk)], sig=m == 3)
                        S.op('dve', lambda e: e.tensor_tensor(hres[:, i, half * 512:(half + 1) * 512], banks[pbk][:, :],
                                                              hres[:, i, half * 512:(half + 1) * 512], ALU.add),
                             r=[bk(pbk), ('h', i)], w=[('h', i)])
            if j + 2 < 8:
                load_ffn_w(j + 2)
            if j == 5:
                w_g_v = w_gate.rearrange("(k p) n -> p k n", p=128)
                w_p_v = w_ple.rearrange("(k p) n -> p k n", p=128)
                for k in range(8):
                    S.dma('pool', wout[:, k, :], w_g_v[:, k, :], w=[('wout', k)])
                S.dma('pool', wp[:], w_p_v, w=['wp'])
        if 'h2' in dbg_d:
            S.dma('sp', dbg_d['h2'], hres[:], r=[('h', i) for i in range(16)], w=['dbg_h2'])
            S.finish('sp', ['dbg_h2'])
        S.barrier()
        cd.close()

        wg = wout
        NS = 4
        u_e = [sb("u_e%d" % i, [128, 1024], BF16) for i in range(NS)]
        ssb_e = [sb("ssb_e%d" % i, [128, 8]) for i in range(NS)]
        u3T = [sb("u3T%d" % i, [128, 8, 128], BF16) for i in range(NS)]
        pt = [sb("pt%d" % i, [128, 256], BF16) for i in range(NS)]
        pT = [sb("pT%d" % i, [128, 2, 128], BF16) for i in range(NS)]
        gsb = [sb("gsb%d" % i, [128, 1024]) for i in range(NS)]

        def pe_stream(i):
            b = i % NS
            tb = b % 2
            S.dma('pool', pt[b][:], po[i * 128:(i + 1) * 128, :], w=[('pt', b)])
            _norm_a1(hres[:, i, :], ('h', i), u_e[b], ('u_e', b), ssb_e[b], ('ssb_e', b))
            yield
            yield
            _norm_a2(C_G3, u3T[b], ('u3T', b), u_e[b], ('u_e', b), tb)
            yield
            psb = banks[tb][:].bitcast(BF16)
            for k in range(2):
                S.op('pe', lambda e: e.transpose(psb[:, k * 128:(k + 1) * 128], pt[b][:, k * 128:(k + 1) * 128], identb),
                     r=[('pt', b), 'cstb'], w=[bk(tb)], sig=k == 1)
            yield S.op('dve', lambda e: e.tensor_copy(pT[b][:], psb[:, 0:256].rearrange("p (k t) -> p k t", k=2)),
                       r=[bk(tb)], w=[('pT', b)])
            for half in range(2):
                pg = 2 + b
                pe_ = 6 + (b % 2)
                for k in range(8):
                    S.op('pe', lambda e: e.matmul(banks[pg][:, :], u3T[b][:, k, :], wg[:, k, half * 512:(half + 1) * 512],
                                                  start=(k == 0), stop=(k == 7)), r=[('u3T', b), ('wout', k)], w=[bk(pg)], sig=k == 7)
                sl = slice(half * 512, (half + 1) * 512)
                yield S.op('act', lambda e: e.activation(gsb[b][:, sl], banks[pg][:, :], AF.Sigmoid), r=[bk(pg)], w=[('gsb', b, half)])
                for k in range(2):
                    S.op('pe', lambda e: e.matmul(banks[pe_][:, :], pT[b][:, k, :], wp[:, k, half * 512:(half + 1) * 512],
                                                  start=(k == 0), stop=(k == 1)), r=[('pT', b), 'wp'], w=[bk(pe_)], sig=k == 1)
                yield S.op('dve', lambda e: e.tensor_tensor(gsb[b][:, sl], banks[pe_][:, :], gsb[b][:, sl], ALU.mult),
                           r=[bk(pe_), ('gsb', b, half)], w=[('gsb', b, half)])
                yield S.op('dve', lambda e: e.tensor_tensor(gsb[b][:, sl], gsb[b][:, sl], hres[:, i, sl], ALU.add),
                           r=[('gsb', b, half), ('h', i)], w=[('gsb', b, half)])
            S.dma('sp', out_d[i * 128:(i + 1) * 128, :], gsb[b][:], r=[('gsb', b, 0), ('gsb', b, 1)],
                  w=[('out', i)])
            yield

        def lagged(gens, lag):
            live = []
            pending = list(gens)
            rnd = 0
            while live or pending:
                if pending and rnd % lag == 0:
                    live.append(pending.pop(0))
                for g in list(live):
                    try:
                        next(g)
                    except StopIteration:
                        live.remove(g)
                rnd += 1

        lagged([pe_stream(i) for i in range(NT)], 2)
        S.finish('sp', [('out', i) for i in range(NT)])
    return nc


def _consts(core, inp):
    c = np.zeros((128, C_TOT), np.float32)

    def pk(v):
        return np.asarray(v, np.float32).reshape(-1, 128).T

    c[:, C_G1:C_G1 + 8] = pk(inp['attn_norm_g'][0])
    c[:, C_G2:C_G2 + 8] = pk(inp['mlp_norm_g'][0])
    c[:, C_G3:C_G3 + 8] = pk(inp['ple_norm_g'][0])
    c[:, C_GY:C_GY + 4] = pk(inp['mlstm_norm_g'][0])
    c[:, C_GY + 4:C_GY + 8] = pk(inp['attn_sub_norm_g'][0])
    cw = np.asarray(inp['conv_w'][0], np.float32)
    c[:, C_CW:C_CW + 32] = cw.T.reshape(8, 128, 4).transpose(1, 0, 2).reshape(128, 32)
    c[:, C_CB:C_CB + 8] = pk(inp['conv_b'][0])
    c[:, C_QG:C_QG + 64] = np.asarray(inp['q_norm_g'][0], np.float32)[None, :]
    c[:, C_KG:C_KG + 64] = np.asarray(inp['k_norm_g'][0], np.float32)[None, :]
    for j, nm in enumerate(('lambda_q1', 'lambda_k1', 'lambda_q2', 'lambda_k2')):
        c[:, C_LAM + j * 64:C_LAM + (j + 1) * 64] = np.asarray(inp[nm][0], np.float32)[None, :]
    odd = core % 2
    c[:, C_FL + 0] = 1.0 if odd else 0.0
    c[:, C_FL + 1] = 0.0 if odd else NEG
    c[:, C_FL + 2] = 0.0 if odd else NEG
    inv = (np.float32(500000.0) ** (-np.arange(0, 16, 2, dtype=np.float32) / np.float32(16))).astype(np.float32)
    for off, base in ((C_CSO, odd * 2048), (C_CSP, 0)):
        pos = (base + np.arange(2048, dtype=np.float32)).astype(np.float32)
        ang = (pos[:, None] * inv[None, :]).astype(np.float32)
        cs = np.concatenate([np.cos(ang), np.sin(ang)], axis=1).astype(np.float32)
        c[:, off:off + 256] = cs.reshape(16, 128, 16).transpose(1, 0, 2).reshape(128, 256)
    c[:, C_ID:C_ID + 128] = np.eye(128, dtype=np.float32)
    c[:, C_TRI:C_TRI + 128] = np.triu(np.ones((128, 128), np.float32))
    c[0:4, C_GB] = np.asarray(inp['igate_b'][0], np.float32)
    c[0:4, C_GB + 1] = np.asarray(inp['fgate_b'][0], np.float32)
    return c


def _constb():
    b = np.zeros((128, 128 + 1024), np.float32)
    b[:, 0:128] = np.eye(128, dtype=np.float32)
    tri = np.triu(np.ones((128, 128), np.float32))
    for m in range(2):
        b[:, 128 + m * 256:128 + m * 256 + 128] = tri
        b[:, 128 + m * 256 + 128:128 + m * 256 + 256] = 1.0
        b[:, 640 + m * 256:640 + m * 256 + 128] = 0.0
        b[:, 640 + m * 256 + 128:640 + m * 256 + 256] = tri
    return b


def make_in_maps(inp):
    x = np.asarray(inp['x'], np.float32)
    p = np.asarray(inp['p'], np.float32)
    shared = {
        'w_in': np.ascontiguousarray(inp['w_in'][0], dtype=np.float32),
        'w_out': np.ascontiguousarray(inp['w_out'][0], dtype=np.float32),
        'w_up': np.ascontiguousarray(inp['w_up'][0], dtype=np.float32),
        'w_down': np.ascontiguousarray(inp['w_down'][0], dtype=np.float32),
        'w_gate': np.ascontiguousarray(inp['w_ple_gate'][0], dtype=np.float32),
        'w_ple': np.ascontiguousarray(inp['w_ple_proj'][0], dtype=np.float32),
        'cstb': _constb(),
    }
    zeros = np.zeros((2048, 1024), np.float32)
    maps = []
    for c in range(8):
        b, hf = c // 2, c % 2
        m = dict(shared)
        m['xo'] = np.ascontiguousarray(x[b, hf * 2048:(hf + 1) * 2048])
        m['xp'] = np.ascontiguousarray(x[b, 0:2048]) if hf else zeros
        m['po'] = np.ascontiguousarray(p[0, b, hf * 2048:(hf + 1) * 2048])
        m['cst'] = _consts(c, inp)
        maps.append(m)
    return maps


def kernel(**inputs):
    nc = build()
    maps = make_in_maps(inputs)
    res = run_bass_kernel_spmd(nc, maps, core_ids=list(range(8)))
    out = np.zeros((4, 4096, 1024), np.float32)
    for c in range(8):
        out[c // 2, (c % 2) * 2048:(c % 2 + 1) * 2048] = res.results[c]['out']
    return out
```

```python
import math
from contextlib import ExitStack

import numpy as np
import concourse.bass as bass
import concourse.mybir as mybir
from concourse.bass_utils import run_bass_kernel_spmd

F32 = mybir.dt.float32
BF16 = mybir.dt.bfloat16
AF = mybir.ActivationFunctionType
ALU = mybir.AluOpType
AX = mybir.AxisListType

SAME_ENGINE_SYNC = True
A_LAG = 1
A2_LAG = 6
PE_WARM_REPS = 1
EPS = 1e-6
NT = 16
NEG = -30000.0
LAM_INIT = 0.8 - 0.6 * math.exp(0.0)
LNC = math.log(128 ** -0.5)

O_MQ, O_MK, O_MV, O_MO, O_MI, O_MF, O_AQ, O_AK, O_AV = 0, 512, 1024, 1536, 2048, 2052, 2056, 2568, 3080

C_G1, C_G2, C_G3, C_GY, C_CW, C_CB, C_QG, C_KG, C_LAM, C_FL, C_CSO, C_CSP, C_ID, C_TRI, C_GB = (
    0, 8, 16, 24, 32, 64, 72, 136, 200, 456, 460, 716, 972, 1100, 1228)
C_TOT = 1230


class _Eng:
    def __init__(self, name, h, sem):
        self.name, self.h, self.sem = name, h, sem
        self.n = 0
        self.count = 0
        self.incs = []
        self.last = None
        self.last_seq = 0
        self.waited = {}
        self.dsems = []
        self.dcnt = []
        self.dnext = 0


class Sched:
    def __init__(self, nc, stack, ndma):
        self.nc = nc
        hs = {'pe': nc.tensor, 'act': nc.scalar, 'dve': nc.vector, 'pool': nc.gpsimd, 'sp': nc.sync}
        self.e = {}
        for k, h in hs.items():
            sem = stack.enter_context(nc.semaphore("s_" + k))
            self.e[k] = _Eng(k, h, sem)
            for i in range(ndma.get(k, 0)):
                self.e[k].dsems.append(stack.enter_context(nc.semaphore("d_%s%d" % (k, i))))
                self.e[k].dcnt.append(0)
        self.st = {}

    def _target(self, dep):
        if dep[0] == 'd':
            return dep[1], dep[2]
        p = self.e[dep[1]]
        seq = dep[2]
        found = None
        for (s, c) in reversed(p.incs):
            if s >= seq:
                found = c
            else:
                break
        if found is None:
            p.count += 1
            p.last.then_inc(p.sem, 1)
            p.incs.append((p.last_seq, p.count))
            found = p.count
        return p.sem, found

    def _wait(self, eng, deps):
        E = self.e[eng]
        for dep in deps:
            if dep is None:
                continue
            if dep[0] == 'c' and dep[1] == eng and (eng == 'pe' or not SAME_ENGINE_SYNC):
                continue
            sem, val = self._target(dep)
            key = id(sem)
            if E.waited.get(key, 0) >= val:
                continue
            E.h.wait_ge(sem, val)
            E.waited[key] = val

    def _deps(self, r, w, eng=None):
        deps = []
        for k in r:
            s = self.st.get(k)
            if s is not None:
                deps.append(s[0])
                if isinstance(k, tuple) and k[0] == 'bank':
                    deps.extend(d for d in s[1].values() if not (d[0] == 'c' and d[1] == eng))
        for k in w:
            s = self.st.get(k)
            if s is not None:
                deps.append(s[0])
                deps.extend(s[1].values())
        return deps

    def _record(self, dep, r, w):
        rk = (dep[0], dep[1] if dep[0] == 'c' else id(dep[1]))
        for k in r:
            s = self.st.setdefault(k, [None, {}])
            s[1][rk] = dep
        for k in w:
            self.st[k] = [dep, {}]

    def op(self, eng, fn, r=(), w=(), sig=None):
        E = self.e[eng]
        self._wait(eng, self._deps(r, w, eng))
        inst = fn(E.h)
        E.n += 1
        E.last = inst
        E.last_seq = E.n
        if sig or (sig is None and eng != 'pe'):
            E.count += 1
            inst.then_inc(E.sem, 1)
            E.incs.append((E.n, E.count))
        self._record(('c', eng, E.n), r, w)
        return inst

    def dma(self, q, out, in_, r=(), w=(), **kw):
        E = self.e[q]
        self._wait(q, self._deps(r, w))
        i = E.dnext
        E.dnext = (i + 1) % len(E.dsems)
        sem = E.dsems[i]
        if E.dcnt[i] > 0:
            key = id(sem)
            if E.waited.get(key, 0) < E.dcnt[i]:
                E.h.wait_ge(sem, E.dcnt[i])
                E.waited[key] = E.dcnt[i]
        E.dcnt[i] += 16
        E.h.dma_start(out=out, in_=in_, **kw).then_inc(sem, 16)
        dep = ('d', sem, E.dcnt[i])
        self._record(dep, r, w)
        return dep

    def barrier(self):
        deps = []
        for s in self.st.values():
            if s[0] is not None:
                deps.append(s[0])
            deps.extend(s[1].values())
        for eng in self.e:
            self._wait(eng, deps)

    def finish(self, eng, keys):
        self._wait(eng, [self.st[k][0] for k in keys if k in self.st])


def build(dbg=(), n_pre=NT, n_own=NT, phases='CDE'):
    nc = bass.Bass("TRN2", target_bir_lowering=False)

    def din(name, shape):
        return nc.dram_tensor(name, list(shape), F32, kind="ExternalInput").ap()

    xo = din("xo", [2048, 1024])
    xp = din("xp", [2048, 1024])
    po = din("po", [2048, 256])
    w_in = din("w_in", [1024, 3592])
    w_out = din("w_out", [1024, 1024])
    w_up = din("w_up", [1024, 4096])
    w_down = din("w_down", [4096, 1024])
    w_gate = din("w_gate", [1024, 1024])
    w_ple = din("w_ple", [256, 1024])
    cst_d = din("cst", [128, C_TOT])
    cstb_d = din("cstb", [128, 128 + 1024])
    out_d = nc.dram_tensor("out", [2048, 1024], F32, kind="ExternalOutput").ap()
    dbg_d = {}
    for name, shape, dt in dbg:
        dbg_d[name] = nc.dram_tensor("dbg_" + name, list(shape), dt, kind="ExternalOutput").ap()

    with ExitStack() as st:
        S = Sched(nc, st, {'sp': 12, 'pool': 8})

        def sb(name, shape, dt=F32):
            return st.enter_context(nc.sbuf_tensor("t_" + name, list(shape), dt))

        def freed(name, shape, dt=F32):
            return nc.sbuf_tensor(name, list(shape), dt)

        cst = sb("cst", [128, C_TOT])
        cstb = sb("cstb", [128, 128 + 1024], BF16)
        identb = cstb[:, 0:128]
        identf = cst[:, C_ID:C_ID + 128]
        tri = cst[:, C_TRI:C_TRI + 128]
        banks = [st.enter_context(nc.psum_tensor("bank%d" % i, [128, 512], F32)) for i in range(8)]
        small = sb("small", [128, 64])
        block = st.enter_context(nc.Block())

        S.dma('sp', cst[:], cst_d, w=['cst'])
        S.dma('pool', cstb[:], cstb_d, w=['cstb'])

        def bk(i):
            return ('bank', i)

        def rstd_from_ss(ss_ap, out_ap, n, key_ss, key_out, mul=None):
            S.op('act', lambda e: e.activation(out_ap, ss_ap, AF.Ln, bias=epsc[:ss_ap.shape[0], 0:1], scale=1.0 / n),
                 r=[key_ss, 'consts'], w=[key_out])
            S.op('act', lambda e: e.activation(out_ap, out_ap, AF.Exp, scale=-0.5), r=[key_out], w=[key_out])
            if mul is not None:
                S.op('dve', lambda e: e.tensor_scalar(out_ap, out_ap, float(mul), None, ALU.mult), r=[key_out], w=[key_out])

        epsc = sb("epsc", [128, 4])
        ones4 = sb("ones4", [4, 128])
        S.op('pool', lambda e: e.memset(epsc[:, 0:1], EPS), w=['consts'])
        S.op('pool', lambda e: e.memset(epsc[:, 1:2], 1.0), w=['consts'])
        S.op('pool', lambda e: e.memset(ones4[:], 1.0), w=['consts4'])

        lamv = cst[:, C_LAM:C_LAM + 256]
        lj = sb("lj", [128, 64])
        S.op('dve', lambda e: e.scalar_tensor_tensor(out=lj[:], in0=lamv[:, 0:64], scalar=1.0, in1=lamv[:, 64:128],
                                                     op0=ALU.mult, op1=ALU.mult, accum_out=small[:, 0:1]),
             r=['cst'], w=['lj', 'small'])
        S.op('dve', lambda e: e.scalar_tensor_tensor(out=lj[:], in0=lamv[:, 128:192], scalar=1.0, in1=lamv[:, 192:256],
                                                     op0=ALU.mult, op1=ALU.mult, accum_out=small[:, 1:2]),
             r=['cst', 'lj'], w=['lj', 'small'])
        S.op('act', lambda e: e.activation(small[:, 2:4], small[:, 0:2], AF.Exp), r=['small'], w=['small'])
        S.op('dve', lambda e: e.tensor_tensor(small[:, 4:5], small[:, 2:3], small[:, 3:4], ALU.subtract), r=['small'], w=['small'])
        S.op('dve', lambda e: e.tensor_scalar(small[:, 5:6], small[:, 4:5], float(LAM_INIT), None, ALU.add), r=['small'], w=['small'])
        lam_ap = small[:, 5:6]
        gpar = sb("gpar", [4, 8])
        gb = cst[0:4, C_GB:C_GB + 2]
        fl = cst[0:4, C_FL:C_FL + 4]
        S.op('dve', lambda e: e.tensor_copy(gpar[:, 0:1], fl[:, 0:1]), r=['cst'], w=['gpar'])
        S.op('dve', lambda e: e.scalar_tensor_tensor(out=gpar[:, 1:2], in0=gb[:, 0:1], scalar=fl[:, 0:1], in1=fl[:, 1:2],
                                                     op0=ALU.mult, op1=ALU.add), r=['cst', 'gpar'], w=['gpar'])
        S.op('pool', lambda e: e.memset(gpar[:, 2:3], 1.0), r=['gpar'], w=['gpar'])
        S.op('dve', lambda e: e.tensor_copy(gpar[:, 3:4], gb[:, 0:1]), r=['cst', 'gpar'], w=['gpar'])
        S.op('dve', lambda e: e.tensor_scalar(gpar[:, 4:5], gb[:, 1:2], -1.0, None, ALU.mult), r=['cst', 'gpar'], w=['gpar'])
        S.op('dve', lambda e: e.tensor_copy(gpar[:, 5:6], fl[:, 0:1]), r=['cst', 'gpar'], w=['gpar'])
        pbias = cst[:, C_FL + 2:C_FL + 3]

        yT = sb("yT", [128, 8, 2048], BF16)
        ab = ExitStack()

        def sa(name, shape, dt=F32):
            return ab.enter_context(nc.sbuf_tensor("t_" + name, list(shape), dt))

        win = sa("win", [128, 8, 3592], BF16)
        kT = sa("kT", [128, 4, 4096], BF16)
        vx = sa("vx", [128, 32, 4, 130], BF16)
        xt = sa("xt", [128, 1024])
        u = sa("u", [128, 1024], BF16)
        uT = [sa("uT%d" % i, [128, 8, 128], BF16) for i in range(2)]
        ssb = sa("ssb", [128, 8])
        raw = sa("raw", [128, 8, 131])
        acc = sa("acc", [128, 4, 128])
        qkT2 = [sa("qkT%d" % i, [128, 8, 128], BF16) for i in range(2)]
        gf = sa("gf", [4, 8, 128])
        carry = sa("carry", [4, 8])
        Gt = [small[:, 16 + 12 * i:28 + 12 * i] for i in range(3)]
        mvx = [sa("mvx%d" % i, [128, 4, 130], BF16) for i in range(2)]
        sigo2 = [sa("sigo%d" % i, [128, 512], BF16) for i in range(2)]
        mls = sa("mls", [128, 512], BF16)
        ktl = [mls[:, i * 128:(i + 1) * 128] for i in range(2)]
        stm = [mls[:, 256 + i * 128:256 + (i + 1) * 128] for i in range(2)]
        Pst = sa("Pst", [128, 4, 130])
        Cbf = sa("Cbf", [128, 4, 130], BF16)
        ml = sa("ml", [128, 32])
        qf = sa("qf", [128, 512])
        qb16 = sa("qb16", [128, 512], BF16)
        qtok = [sa("qtok%d" % i, [128, 512], BF16) for i in range(2)]
        qTg = [sa("qTg%d" % i, [128, 4, 256], BF16) for i in range(2)]
        Et = [sa("Et%d" % i, [128, 512], BF16) for i in range(2)]
        ofin = [sa("ofin%d" % i, [128, 128]) for i in range(2)]
        yat = [sa("yat%d" % i, [128, 512], BF16) for i in range(2)]
        al = sa("al", [128, 16])
        ae = small[:, 8:16]

        w_in_v = w_in.rearrange("(k p) n -> p k n", p=128)
        WGRP = [(O_MK, O_MV), (O_MV, O_MO), (O_MI, O_AQ), (O_AK, O_AV), (O_AV, 3592), (O_MQ, O_MK), (O_MO, O_MI), (O_AQ, O_AK)]
        for gi_, (c0_, c1_) in enumerate(WGRP):
            S.dma('pool', win[:, :, c0_:c1_], w_in_v[:, :, c0_:c1_], w=[('win', gi_)])

        def wkey(col):
            for gi_, (c0_, c1_) in enumerate(WGRP):
                if c0_ <= col < c1_:
                    return ('win', gi_)
            raise ValueError(col)

        S.op('pool', lambda e: e.memset(vx[:, :, :, 128:130], 1.0), w=[('vx', j) for j in range(32)])
        S.op('pool', lambda e: e.tensor_scalar(vx[:, 0:16, :, 128:130], vx[:, 0:16, :, 128:130], cst[:, C_FL:C_FL + 1], 1.0,
                                               ALU.mult, ALU.mult),
             r=['cst'] + [('vx', j) for j in range(16)], w=[('vx', j) for j in range(16)])
        for i in range(2):
            S.op('pool', lambda e: e.memset(mvx[i][:, :, 128:130], 1.0), w=[('mvx', i)])
        for i in range(3):
            S.op('pool', lambda e: e.memset(Gt[i], 1.0), w=[('Gt', i)])
        S.op('pool', lambda e: e.memset(qTg[0][:], 0.0), w=[('qTg', 0, 0), ('qTg', 0, 1)])
        S.op('pool', lambda e: e.memset(qTg[1][:], 0.0), w=[('qTg', 1, 0), ('qTg', 1, 1)])
        S.op('pool', lambda e: e.memset(Pst[:], 0.0), w=[('Pst', h) for h in range(4)])
        S.op('pool', lambda e: e.memset(Cbf[:], 0.0), w=['Cbf'])
        S.op('pool', lambda e: e.memset(raw[:], 0.0), w=[('raw', c) for c in range(8)])
        S.op('pool', lambda e: e.memset(carry[:], 0.0), w=[('carry', j) for j in range(8)])

        cw = cst[:, C_CW:C_CW + 32].rearrange("p (c j) -> p c j", j=4)
        cb = cst[:, C_CB:C_CB + 8]
        cnt = {'tt': 0}

        def norm_to_uT(x_tile, kx, gcol, uT_t, kuT, scr=None):
            if scr is None:
                u_, ku, ssb_, kss, tb = u, 'u', ssb, 'ssb', 0
            else:
                u_, ku, ssb_, kss, tb = scr
            return _norm_to_uT(x_tile, kx, gcol, uT_t, kuT, u_, ku, ssb_, kss, tb)

        def _norm_to_uT(x_tile, kx, gcol, uT_t, kuT, u, ku, ssb, kss, tb):
            _norm_a1(x_tile, kx, u, ku, ssb, kss)
            _norm_a2(gcol, uT_t, kuT, u, ku, tb)

        def _norm_a1(x_tile, kx, u, ku, ssb, kss):
            S.op('act', lambda e: e.activation(u[:], x_tile, AF.Square, accum_out=ssb[:, 0:1]), r=[kx], w=[ku, kss])
            rstd_from_ss(ssb[:, 0:1], ssb[:, 1:2], 1024.0, kss, kss)
            S.op('dve', lambda e: e.tensor_scalar(u[:], x_tile, ssb[:, 1:2], None, ALU.mult), r=[kx, kss], w=[ku])

        def _norm_a2(gcol, uT_t, kuT, u, ku, tb):
            psb = banks[tb][:].bitcast(BF16)
            for k in range(8):
                S.op('pe', lambda e: e.transpose(psb[:, k * 128:(k + 1) * 128], u[:, k * 128:(k + 1) * 128], identb),
                     r=[ku, 'cstb'], w=[bk(tb)], sig=k == 7)
            g_bc = cst[:, gcol:gcol + 8].unsqueeze(2).to_broadcast([128, 8, 128])
            S.op('dve', lambda e: e.tensor_tensor(uT_t[:], psb[:, 0:1024].rearrange("p (k t) -> p k t", k=8), g_bc, ALU.mult),
                 r=[bk(tb), 'cst'], w=[kuT])

        def transpose_to(src_tok, ksrc, dst_ap, kdst, scale_cols=None, bank=0):
            psb = banks[bank][:].bitcast(BF16)
            for k in range(4):
                S.op('pe', lambda e: e.transpose(psb[:, k * 128:(k + 1) * 128], src_tok[:, k * 128:(k + 1) * 128], identb),
                     r=[ksrc, 'cstb'], w=[bk(bank)], sig=k == 3)
            src = psb[:, 0:512].rearrange("p (k t) -> p k t", k=4)
            if scale_cols is None:
                S.op('dve', lambda e: e.tensor_copy(dst_ap, src), r=[bk(bank)], w=[kdst])
            else:
                g_bc = scale_cols.unsqueeze(2).to_broadcast([128, 4, 128])
                S.op('dve', lambda e: e.tensor_tensor(dst_ap, src, g_bc, ALU.mult), r=[bk(bank), 'cst'], w=[kdst])

        def proj_tm(uT_t, kuT, col, bank):
            for k in range(8):
                S.op('pe', lambda e: e.matmul(banks[bank][:, :], uT_t[:, k, :], win[:, k, col:col + 512],
                                              start=(k == 0), stop=(k == 7)),
                     r=[kuT, wkey(col)], w=[bk(bank)], sig=k == 7)

        def rr(*gens):
            gens = [g for g in gens if g is not None]
            while gens:
                for g in list(gens):
                    try:
                        next(g)
                    except StopIteration:
                        gens.remove(g)
                yield

        def run(gen):
            for _ in gen:
                pass


        def qk_prep(bank, gcol, cs, qb16, k16):
            q3 = qf[:].rearrange("p (g d) -> p g d", d=64)
            S.op('act', lambda e: e.activation(qf[:], banks[bank][:, :], AF.Square), r=[bk(bank)], w=['qf'])
            S.op('dve', lambda e: e.tensor_reduce(out=al[:, 0:8], in_=q3, axis=AX.X, op=ALU.add), r=['qf'], w=['al'])
            S.op('act', lambda e: e.activation(al[:, 8:16], al[:, 0:8], AF.Ln, bias=epsc[:, 0:1], scale=1.0 / 64),
                 r=['al', 'consts'], w=['al'])
            S.op('act', lambda e: e.activation(al[:, 8:16], al[:, 8:16], AF.Exp, scale=-0.5), r=['al'], w=['al'])
            S.op('dve', lambda e: e.tensor_tensor(q3, banks[bank][:, :].rearrange("p (g d) -> p g d", d=64),
                                                  al[:, 8:16].unsqueeze(2).to_broadcast([128, 8, 64]), ALU.mult),
                 r=[bk(bank), 'al'], w=['qf'])
            gg = cst[:, gcol:gcol + 64].unsqueeze(1).to_broadcast([128, 8, 64])
            o3 = qb16[:].rearrange("p (g d) -> p g d", d=64)
            yield S.op('dve', lambda e: e.tensor_tensor(o3, q3, gg, ALU.mult), r=['qf', 'cst'], w=[k16])
            gg16 = cst[:, gcol:gcol + 16].unsqueeze(1).to_broadcast([128, 8, 16])
            yield S.op('dve', lambda e: e.tensor_tensor(q3[:, :, 0:16], q3[:, :, 0:16], gg16, ALU.mult), r=['qf', 'cst'], w=['qf'])
            x1, x2 = q3[:, :, 0:8], q3[:, :, 8:16]
            cc = cs[:, 0:8].unsqueeze(1).to_broadcast([128, 8, 8])
            sn = cs[:, 8:16].unsqueeze(1).to_broadcast([128, 8, 8])
            r4 = [q3[:, :, 16 + 8 * a_:24 + 8 * a_] for a_ in range(4)]
            S.op('dve', lambda e: e.tensor_tensor(r4[0], x1, cc, ALU.mult), r=['qf', 'cst'], w=['qf'])
            S.op('dve', lambda e: e.tensor_tensor(r4[1], x2, sn, ALU.mult), r=['qf', 'cst'], w=['qf'])
            S.op('dve', lambda e: e.tensor_tensor(r4[2], x2, cc, ALU.mult), r=['qf', 'cst'], w=['qf'])
            yield S.op('dve', lambda e: e.tensor_tensor(r4[3], x1, sn, ALU.mult), r=['qf', 'cst'], w=['qf'])
            S.op('dve', lambda e: e.tensor_tensor(o3[:, :, 0:8], r4[0], r4[1], ALU.subtract), r=['qf', k16], w=[k16])
            yield S.op('dve', lambda e: e.tensor_tensor(o3[:, :, 8:16], r4[2], r4[3], ALU.add), r=['qf', k16], w=[k16])

        def attention_group(g):
            kbs = list(range(16)) + [16 + j for j in range(2 * g + 2)]
            units = [(h, idx, kb) for h in range(4) for idx, kb in enumerate(kbs)]
            PSB = (4, 5)
            for tt in range(2):
                psb = banks[4 + tt][:].bitcast(BF16)
                for k in range(4):
                    S.op('pe', lambda e: e.transpose(psb[:, k * 128:(k + 1) * 128], qtok[tt][:, k * 128:(k + 1) * 128], identb),
                         r=[('qtok', tt), 'cstb'], w=[bk(4 + tt)], sig=(k == 3))
                yield
                c_ = tt * 128
                S.op('act', lambda e: e.activation(qTg[0][0:64, :, c_:c_ + 128], psb[0:64, 0:512].rearrange("p (k t) -> p k t", k=4),
                                                   AF.Copy), r=[bk(4 + tt)], w=[('qTg', 0, tt)])
                yield S.op('act', lambda e: e.activation(qTg[1][64:128, :, c_:c_ + 128],
                                                         psb[64:128, 0:512].rearrange("p (k t) -> p k t", k=4), AF.Copy),
                           r=[bk(4 + tt)], w=[('qTg', 1, tt)])
            QK = [('qTg', m, tt) for m in range(2) for tt in range(2)]

            def st_mm(n):
                h, idx, kb = units[n]
                pb = PSB[n % 2]
                for rep_ in range(PE_WARM_REPS):
                    S.op('pe', lambda e: e.matmul(banks[pb][:, 0:256], kT[:, h, kb * 128:(kb + 1) * 128],
                                                  qTg[0][:, h, :], start=True, stop=True),
                         r=[('kT', kb)] + QK, w=[bk(pb)], sig=False)
                    S.op('pe', lambda e: e.matmul(banks[pb][:, 256:512], kT[:, h, kb * 128:(kb + 1) * 128],
                                                  qTg[1][:, h, :], start=True, stop=True),
                         r=[('kT', kb)] + QK, w=[bk(pb)], sig=(rep_ == PE_WARM_REPS - 1))

            st_mm(0)
            for n, (h, idx, kb) in enumerate(units):
                if n + 1 < len(units):
                    st_mm(n + 1)
                pb = PSB[n % 2]
                E = Et[n % 2]
                kE = ('Et', n % 2)
                S.op('act', lambda e: e.activation(E[:], banks[pb][:, :], AF.Exp, scale=0.125), r=[bk(pb)], w=[kE])
                own = kb - 16
                if own == 2 * g:
                    S.op('dve', lambda e: e.tensor_tensor(E[:], E[:], cstb[:, 128:640], ALU.mult), r=[kE, 'cstb'], w=[kE])
                elif own == 2 * g + 1:
                    S.op('dve', lambda e: e.tensor_tensor(E[:], E[:], cstb[:, 640:1152], ALU.mult), r=[kE, 'cstb'], w=[kE])
                for m in range(2):
                    for qb in range(2):
                        if qb == 0 and own == 2 * g + 1:
                            continue
                        last = (own == 2 * g) if qb == 0 else (own == 2 * g + 1)
                        a = 6 + m
                        o0 = qb * 130
                        S.op('pe', lambda e: e.matmul(banks[a][:, o0:o0 + 129], E[:, m * 256 + qb * 128: m * 256 + qb * 128 + 128],
                                                      vx[:, kb, h, 0:129], start=(idx == 0 and qb == 0), stop=last,
                                                      skip_group_check=True),
                             r=[kE, ('vx', kb)], w=[bk(a)], sig=(m == 1 and qb == 1))
                yield
                if idx == len(kbs) - 1:
                    den6 = banks[6][:, 0:260].rearrange("p (a c) -> p a c", c=130)[:, :, 128]
                    den7 = banks[7][:, 0:260].rearrange("p (a c) -> p a c", c=130)[:, :, 128]
                    S.op('dve', lambda e: e.reciprocal(ae[:, 2:4], den7), r=[bk(7)], w=[('ae', 1)])
                    S.op('dve', lambda e: e.tensor_scalar(ae[:, 2:4], ae[:, 2:4], lam_ap, None, ALU.mult), r=[('ae', 1), 'small'], w=[('ae', 1)])
                    S.op('dve', lambda e: e.reciprocal(ae[:, 0:2], den6), r=[bk(6)], w=[('ae', 0)])
                    for qb in range(2):
                        o0 = qb * 130
                        S.op('act', lambda e: e.activation(ofin[qb][:], banks[7][:, o0:o0 + 128], AF.Copy, scale=ae[:, 2 + qb:3 + qb]),
                             r=[bk(7), ('ae', 1)], w=[('ofin', qb)])
                        S.op('dve', lambda e: e.scalar_tensor_tensor(out=ofin[qb][:], in0=banks[6][:, o0:o0 + 128], scalar=ae[:, qb:qb + 1],
                                                                     in1=ofin[qb][:], op0=ALU.mult, op1=ALU.subtract),
                             r=[bk(6), ('ae', 0), ('ofin', qb)], w=[('ofin', qb)])
                    yield
                    for qb in range(2):
                        S.op('act', lambda e: e.activation(yat[qb][:, h * 128:(h + 1) * 128], ofin[qb][:], AF.Square,
                                                           accum_out=ae[:, 4 + qb:5 + qb]),
                             r=[('ofin', qb)], w=[('yat', qb), ('ae', 2 + qb)])
                    S.op('act', lambda e: e.activation(ae[:, 6:8], ae[:, 4:6], AF.Ln, bias=epsc[:, 0:1], scale=1.0 / 128),
                         r=[('ae', 2), ('ae', 3), 'consts'], w=[('ae', 4)])
                    S.op('act', lambda e: e.activation(ae[:, 6:8], ae[:, 6:8], AF.Exp, scale=-0.5), r=[('ae', 4)], w=[('ae', 4)])
                    for qb in range(2):
                        S.op('dve', lambda e: e.tensor_scalar(yat[qb][:, h * 128:(h + 1) * 128], ofin[qb][:], ae[:, 6 + qb:7 + qb],
                                                               float(1.0 - LAM_INIT), ALU.mult, ALU.mult),
                             r=[('ofin', qb), ('ae', 4)], w=[('yat', qb)])
                    yield
            for qb in range(2):
                t0 = (2 * g + qb) * 128
                transpose_to(yat[qb], ('yat', qb), yT[:, 4:8, t0:t0 + 128], ('yT', 2 * g + qb, 1),
                             scale_cols=cst[:, C_GY + 4:C_GY + 8], bank=4 + qb)
                yield

        NTT = n_pre + n_own
        def tt_info(n):
            own = n >= n_pre
            i = n - n_pre if own else n
            return own, i

        def stageA(n):
            own, i = tt_info(n)
            xsrc = xo if own else xp
            S.dma('sp', xt[:], xsrc[i * 128:(i + 1) * 128, :], w=['xt'])
            _norm_a1(xt[:], 'xt', u, 'u', ssb, 'ssb')
            for _ in range(A2_LAG):
                yield
            _norm_a2(C_G1, uT[n % 2], ('uT', n % 2), u, 'u', 0)
            yield

        def front(n):
            own, i = tt_info(n)
            blk = (16 + i) if own else i
            uT_t, kuT = uT[n % 2], ('uT', n % 2)
            Gc, kG = Gt[n % 3], ('Gt', n % 3)
            mv_t, kmv = mvx[n % 2], ('mvx', n % 2)
            qkT, kq = qkT2[n % 2], n % 2
            sigo = sigo2[n % 2]
            if own:
                GB, FMB, TMB = 0, (1, 1), (1, 1)
            else:
                GB, FMB, TMB = 1, (2, 3), (6, 7)
            need_q = own or i == NT - 1

            def T2():
                for half in ((0, 1) if need_q else (1,)):
                    pbk = FMB[half]
                    c0 = half * 4
                    for c in range(c0, c0 + 4):
                        col = (O_MQ + c * 128) if c < 4 else (O_MK + (c - 4) * 128)
                        for k in range(8):
                            S.op('pe', lambda e: e.matmul(banks[pbk][:, (c % 4) * 128:(c % 4 + 1) * 128], win[:, k, col:col + 128],
                                                          uT_t[:, k, :], start=(k == 0), stop=(k == 7)),
                                 r=[kuT, wkey(col)], w=[bk(pbk)], sig=(k == 7))
                    RK = [('raw', c) for c in range(c0, c0 + 4)]
                    if own:
                        S.op('dve', lambda e: e.tensor_copy(raw[:, c0:c0 + 4, 3:131], banks[pbk][:, :].rearrange("p (c t) -> p c t", c=4)),
                             r=[bk(pbk)], w=RK)
                    else:
                        S.op('act', lambda e: e.activation(raw[:, c0:c0 + 4, 3:131], banks[pbk][:, :].rearrange("p (c t) -> p c t", c=4),
                                                           AF.Copy), r=[bk(pbk)], w=RK)
                    conv = own or half == 1
                    if conv:
                        for c in range(c0, c0 + 4):
                            if own:
                                S.op('dve', lambda e: e.tensor_scalar(acc[:, c % 4, :], banks[pbk][:, (c % 4) * 128:(c % 4 + 1) * 128],
                                                                      cw[:, c, 3:4], cb[:, c:c + 1], ALU.mult, ALU.add),
                                     r=[bk(pbk), 'cst'], w=[('acc', c % 4)])
                            else:
                                S.op('act', lambda e: e.activation(acc[:, c % 4, :], banks[pbk][:, (c % 4) * 128:(c % 4 + 1) * 128],
                                                                   AF.Identity, bias=cb[:, c:c + 1], scale=cw[:, c, 3:4]),
                                     r=[bk(pbk), 'cst'], w=[('acc', c % 4)])
                    yield
                    if conv:
                        for c in range(c0, c0 + 4):
                            for j in range(3):
                                yield S.op('dve', lambda e: e.scalar_tensor_tensor(out=acc[:, c % 4, :], in0=raw[:, c, j:j + 128],
                                                                                   scalar=cw[:, c, j:j + 1], in1=acc[:, c % 4, :],
                                                                                   op0=ALU.mult, op1=ALU.add),
                                           r=[('raw', c), 'cst', ('acc', c % 4)], w=[('acc', c % 4)])
                    if own:
                        yield S.op('dve', lambda e: e.tensor_copy(raw[:, c0:c0 + 4, 0:3], raw[:, c0:c0 + 4, 128:131]), r=RK, w=RK)
                    else:
                        yield S.op('act', lambda e: e.activation(raw[:, c0:c0 + 4, 0:3], raw[:, c0:c0 + 4, 128:131], AF.Copy), r=RK, w=RK)
                    if conv:
                        tmp = raw[:, c0:c0 + 4, 3:131]
                        AK = [('acc', c) for c in range(4)]
                        S.op('act', lambda e: e.activation(tmp, acc[:, :, :], AF.Exp, scale=-1.0), r=AK + RK, w=RK)
                        S.op('act', lambda e: e.activation(tmp, tmp, AF.Ln, bias=epsc[:, 1:2]), r=RK + ['consts'], w=RK)
                        yield S.op('act', lambda e: e.activation(tmp, tmp, AF.Exp, scale=-1.0), r=RK, w=RK)
                        yield S.op('dve', lambda e: e.tensor_tensor(qkT[:, c0:c0 + 4, :], acc[:, :, :], tmp, ALU.mult),
                                   r=AK + RK, w=[('qkT', kq, half)])

            def T3():
                for gi, col in enumerate((O_MI, O_MF)):
                    for k in range(8):
                        S.op('pe', lambda e: e.matmul(banks[GB][0:4, gi * 128:(gi + 1) * 128], win[:, k, col:col + 4],
                                                      uT_t[:, k, :], start=(k == 0), stop=(k == 7)),
                             r=[kuT, wkey(col)], w=[bk(GB)], sig=(k == 7))
                po_ = 2 if own else 0
                G = lambda j: ('gf', j)
                C = lambda j: ('carry', j)
                S.op('act', lambda e: e.activation(gf[:, 0, :], banks[GB][0:4, 0:128], AF.Identity,
                                                   bias=gpar[:, po_ + 1:po_ + 2], scale=gpar[:, po_:po_ + 1]),
                     r=[bk(GB), 'gpar'], w=[G(0)])
                yield S.op('act', lambda e: e.activation(gf[:, 1, :], banks[GB][0:4, 128:256], AF.Exp, bias=gpar[:, 4:5], scale=-1.0),
                           r=[bk(GB), 'gpar'], w=[G(1)])
                yield S.op('act', lambda e: e.activation(gf[:, 1, :], gf[:, 1, :], AF.Ln, bias=epsc[0:4, 1:2]),
                           r=[G(1), 'consts'], w=[G(1)])
                if not own:
                    yield S.op('dve', lambda e: e.tensor_scalar(gf[:, 1, :], gf[:, 1, :], gpar[:, 5:6], None, ALU.mult),
                               r=[G(1), 'gpar'], w=[G(1)])
                yield S.op('dve', lambda e: e.tensor_tensor_scan(gf[:, 2, :], ones4[:], gf[:, 1, :], carry[:, 0:1], ALU.mult, ALU.add),
                           r=[G(1), 'consts4', C(0)], w=[G(2)])
                yield S.op('dve', lambda e: e.tensor_tensor(gf[:, 3, :], gf[:, 0, :], gf[:, 2, :], ALU.add), r=[G(0), G(2)], w=[G(3)])
                yield S.op('dve', lambda e: e.tensor_tensor_scan(gf[:, 4, :], ones4[:], gf[:, 3, :], carry[:, 1:2], ALU.mult, ALU.max),
                           r=[G(3), 'consts4', C(1)], w=[G(4)])
                S.op('dve', lambda e: e.tensor_scalar(carry[:, 2:3], carry[:, 1:2], -1.0, float(LNC), ALU.mult, ALU.add),
                     r=[C(1)], w=[C(2)])
                S.op('dve', lambda e: e.tensor_scalar(carry[:, 3:4], carry[:, 1:2], -1.0, None, ALU.mult), r=[C(1)], w=[C(3)])
                yield S.op('dve', lambda e: e.tensor_tensor(carry[:, 4:5], carry[:, 1:2], gf[:, 4, 127:128], ALU.subtract),
                           r=[C(1), G(4)], w=[C(4)])
                yield S.op('act', lambda e: e.activation(gf[:, 5, :], gf[:, 3, :], AF.Exp, bias=carry[:, 2:3]), r=[G(3), C(2)], w=[G(5)])
                yield S.op('act', lambda e: e.activation(gf[:, 6, :], gf[:, 2, :], AF.Exp, bias=carry[:, 3:4]), r=[G(2), C(3)], w=[G(6)])
                yield S.op('act', lambda e: e.activation(gf[:, 7, :], ones4[:], AF.Exp, bias=carry[:, 4:5], scale=0.0),
                           r=[C(4), 'consts4'], w=[G(7)])
                S.op('dve', lambda e: e.tensor_copy(carry[:, 0:1], gf[:, 2, 127:128]), r=[G(2), C(0)], w=[C(0)])
                yield S.op('dve', lambda e: e.tensor_copy(carry[:, 1:2], gf[:, 4, 127:128]), r=[G(4), C(1)], w=[C(1)])
                for a in range(3):
                    S.op('pe', lambda e: e.transpose(banks[GB][:, 256 + a * 4:256 + a * 4 + 4], gf[:, 5 + a, :], identf[0:4, 0:4]),
                         r=[G(5 + a), 'cst'], w=[bk(GB)], sig=(a == 2))
                yield S.op('dve', lambda e: e.tensor_copy(Gc, banks[GB][:, 256:268]), r=[bk(GB)], w=[kG])

            def T4():
                tb = [0]

                def nxt():
                    tb[0] += 1
                    return TMB[tb[0] % 2]
                b_ = nxt()
                proj_tm(uT_t, kuT, O_MV, b_)
                if own:
                    yield S.op('dve', lambda e: e.tensor_copy(mv_t[:, :, 0:128], banks[b_][:, :].rearrange("p (h d) -> p h d", h=4)),
                               r=[bk(b_)], w=[kmv])
                else:
                    yield S.op('act', lambda e: e.activation(mv_t[:, :, 0:128], banks[b_][:, :].rearrange("p (h d) -> p h d", h=4), AF.Copy),
                               r=[bk(b_)], w=[kmv])
                if own:
                    b_ = nxt()
                    proj_tm(uT_t, kuT, O_MO, b_)
                    S.op('act', lambda e: e.activation(qf[:], banks[b_][:, :], AF.Exp, scale=-1.0), r=[bk(b_)], w=['qf'])
                    S.op('act', lambda e: e.activation(qf[:], qf[:], AF.Ln, bias=epsc[:, 1:2]), r=['qf', 'consts'], w=['qf'])
                    yield S.op('act', lambda e: e.activation(sigo[:], qf[:], AF.Exp, scale=-1.0), r=['qf'], w=[('sigo', n % 2)])
                cs_k = cst[:, (C_CSO if own else C_CSP) + i * 16:(C_CSO if own else C_CSP) + i * 16 + 16]
                b_ = nxt()
                proj_tm(uT_t, kuT, O_AK, b_)
                yield from qk_prep(b_, C_KG, cs_k, qb16, 'qb16')
                transpose_to(qb16, 'qb16', kT[:, :, blk * 128:(blk + 1) * 128], ('kT', blk), bank=0)
                yield
                b_ = nxt()
                proj_tm(uT_t, kuT, O_AV, b_)
                if own:
                    yield S.op('dve', lambda e: e.tensor_copy(vx[:, blk, :, 0:128], banks[b_][:, :].rearrange("p (h d) -> p h d", h=4)),
                               r=[bk(b_)], w=[('vx', blk)])
                else:
                    yield S.op('act', lambda e: e.activation(vx[:, blk, :, 0:128], banks[b_][:, :].rearrange("p (h d) -> p h d", h=4), AF.Copy),
                               r=[bk(b_)], w=[('vx', blk)])
                if own:
                    b_ = nxt()
                    proj_tm(uT_t, kuT, O_AQ, b_)
                    yield from qk_prep(b_, C_QG, cs_k, qtok[i % 2], ('qtok', i % 2))

            yield from rr(T2(), T3(), T4())
            if own and i == 0 and 'qkT0' in dbg_d:
                S.dma('sp', dbg_d['qkT0'], qkT[:], r=[('qkT', kq, 0), ('qkT', kq, 1)], w=['dbg_qkT0'])
                S.finish('sp', ['dbg_qkT0'])

        def back(n):
            own, i = tt_info(n)
            Gc, kG = Gt[n % 3], ('Gt', n % 3)
            Gp, kGp = Gt[(n + 2) % 3], ('Gt', (n + 2) % 3)
            mv_t, kmv = mvx[n % 2], ('mvx', n % 2)
            qkT, kq = qkT2[n % 2], n % 2
            sigo = sigo2[n % 2]
            HB = (2, 3) if own else (4, 5)
            MK = [('ktl', 0), ('ktl', 1), ('stm', 0), ('stm', 1)]

            def head(h):
                e_ = h % 2
                kt_, kkt = ktl[e_], ('ktl', e_)
                st_, kst = stm[e_], ('stm', e_)
                hb = HB[e_]
                B = banks[hb]
                psb = B[:].bitcast(BF16)
                S.op('pe', lambda e: e.transpose(psb[:, 0:128], qkT[:, 4 + h, :], identb), r=[('qkT', kq, 1), 'cstb'], w=[bk(hb)], sig=True)
                yield
                if own:
                    yield S.op('dve', lambda e: e.tensor_scalar(kt_, psb[:, 0:128], Gc[:, h:h + 1], None, ALU.mult),
                               r=[bk(hb), kG], w=[kkt])
                else:
                    yield S.op('act', lambda e: e.activation(kt_, psb[:, 0:128], AF.Copy, scale=Gc[:, h:h + 1]),
                               r=[bk(hb), kG], w=[kkt])
                if own:
                    S.op('pe', lambda e: e.matmul(B[:, 64:192], qkT[:, 4 + h, :], qkT[:, h, :], start=True, stop=True),
                         r=[('qkT', kq, 0), ('qkT', kq, 1)], w=[bk(hb)], sig=True)
                    yield
                    yield S.op('dve', lambda e: e.scalar_tensor_tensor(out=st_, in0=B[:, 64:192], scalar=Gc[:, h:h + 1],
                                                                       in1=tri, op0=ALU.mult, op1=ALU.mult),
                               r=[bk(hb), kG, 'cst'], w=[kst])
                    S.op('pe', lambda e: e.matmul(B[:, 192:321], qkT[:, h, :], Cbf[:, h, 0:129], start=True, stop=False),
                         r=[('qkT', kq, 0), 'Cbf'], w=[bk(hb)], sig=False)
                    S.op('pe', lambda e: e.matmul(B[:, 192:321], st_, mv_t[:, h, 0:129], start=False, stop=True),
                         r=[kst, kmv], w=[bk(hb)], sig=True)
                    yield
                S.op('pe', lambda e: e.matmul(B[:, 336:465], kt_, mv_t[:, h, 0:129], start=True, stop=True),
                     r=[kkt, kmv], w=[bk(hb)], sig=True)
                yield
                yield S.op('dve', lambda e: e.scalar_tensor_tensor(out=Pst[:, h, 0:129], in0=Pst[:, h, 0:129], scalar=Gp[:, 8 + h:9 + h],
                                                                   in1=B[:, 336:465], op0=ALU.mult, op1=ALU.add),
                           r=[bk(hb), kGp, ('Pst', h)], w=[('Pst', h)])

            def pair_epilogue(p):
                c = 8 * p
                m = ml[:, 16 * p:16 * p + 16]
                for e_ in range(2):
                    S.op('dve', lambda e: e.tensor_copy(m[:, e_:e_ + 1], banks[HB[e_]][:, 320:321]), r=[bk(HB[e_])], w=[('ml', p, 0)])
                S.op('dve', lambda e: e.scalar_tensor_tensor(out=m[:, 2:4], in0=m[:, 0:2], scalar=-1.0, in1=m[:, 0:2],
                                                             op0=ALU.mult, op1=ALU.max), r=[('ml', p, 0)], w=[('ml', p, 1)])
                S.op('dve', lambda e: e.tensor_tensor(m[:, 2:4], m[:, 2:4], Gc[:, 4 + 2 * p:6 + 2 * p], ALU.max),
                     r=[('ml', p, 1), kG], w=[('ml', p, 1)])
                yield S.op('dve', lambda e: e.reciprocal(m[:, 4:6], m[:, 2:4]), r=[('ml', p, 1)], w=[('ml', p, 2)])
                for e_ in range(2):
                    B = banks[HB[e_]]
                    h_ = 2 * p + e_
                    yield S.op('act', lambda e: e.activation(mls[:, h_ * 128:(h_ + 1) * 128], B[:, 192:320], AF.Square,
                                                             accum_out=m[:, 6 + e_:7 + e_]),
                               r=[bk(HB[e_])], w=[MK[h_], ('ml', p, 3 + e_)])
                S.op('dve', lambda e: e.tensor_tensor(m[:, 8:10], m[:, 4:6], m[:, 4:6], ALU.mult), r=[('ml', p, 2)], w=[('ml', p, 5)])
                yield S.op('dve', lambda e: e.tensor_tensor(m[:, 8:10], m[:, 8:10], m[:, 6:8], ALU.mult),
                           r=[('ml', p, 5), ('ml', p, 3), ('ml', p, 4)], w=[('ml', p, 5)])
                S.op('act', lambda e: e.activation(m[:, 10:12], m[:, 8:10], AF.Ln, bias=epsc[:, 0:1], scale=1.0 / 128),
                     r=[('ml', p, 5), 'consts'], w=[('ml', p, 6)])
                yield S.op('act', lambda e: e.activation(m[:, 10:12], m[:, 10:12], AF.Exp, scale=-0.5), r=[('ml', p, 6)], w=[('ml', p, 6)])
                yield S.op('dve', lambda e: e.tensor_tensor(m[:, 12:14], m[:, 10:12], m[:, 4:6], ALU.mult),
                           r=[('ml', p, 6), ('ml', p, 2)], w=[('ml', p, 7)])
                for e_ in range(2):
                    h = 2 * p + e_
                    B = banks[HB[e_]]
                    yield S.op('dve', lambda e: e.scalar_tensor_tensor(out=mls[:, h * 128:(h + 1) * 128], in0=B[:, 192:320],
                                                                       scalar=m[:, 12 + e_:13 + e_], in1=sigo[:, h * 128:(h + 1) * 128],
                                                                       op0=ALU.mult, op1=ALU.mult),
                               r=[bk(HB[e_]), ('ml', p, 7), ('sigo', n % 2)], w=[MK[h]])
                for e_ in range(2):
                    h = 2 * p + e_
                    B = banks[HB[e_]]
                    psb = B[:].bitcast(BF16)
                    S.op('pe', lambda e: e.transpose(psb[:, 0:128], mls[:, h * 128:(h + 1) * 128], identb),
                         r=[MK[h], 'cstb'], w=[bk(HB[e_])], sig=True)
                    yield
                    yield S.op('dve', lambda e: e.tensor_scalar(yT[:, h, i * 128:(i + 1) * 128], psb[:, 0:128],
                                                                cst[:, C_GY + h:C_GY + h + 1], None, ALU.mult),
                               r=[bk(HB[e_]), 'cst'], w=[('yT', i, 0)])

            for p in range(2):
                yield from rr(head(2 * p), head(2 * p + 1))
                if own:
                    yield from pair_epilogue(p)
            if own or i == NT - 1:
                yield S.op('dve', lambda e: e.tensor_tensor(Cbf[:, :, 0:129], Pst[:, :, 0:129],
                                                            Gc[:, 8:12].unsqueeze(2).to_broadcast([128, 4, 129]), ALU.mult),
                           r=[('Pst', h) for h in range(4)] + [kG], w=['Cbf'])

        STEPS = {'main': 0, 'side': 0}

        def wrr(main, side, per):
            credit = 0.0
            for _ in main:
                STEPS['main'] += 1
                credit += per
                while side is not None and credit >= 1.0:
                    credit -= 1.0
                    STEPS['side'] += 1
                    try:
                        next(side)
                    except StopIteration:
                        side = None
            return side

        def dump(name, ap, keys):
            if name in dbg_d:
                S.dma('sp', dbg_d[name], ap, r=keys, w=['dbg_' + name])
                S.finish('sp', ['dbg_' + name])

        side = None
        side_rate = [1.0]
        main_per_tt = [60]
        if NTT > 0:
            run(stageA(0))
            run(rr(stageA(1) if NTT > 1 else None, front(0)))
        for n in range(NTT):
            own, i = tt_info(n)
            def _late(gen, rounds):
                for _ in range(rounds):
                    yield
                yield from gen
            nxt_front = rr(_late(stageA(n + 2), A_LAG) if n + 2 < NTT else None, front(n + 1) if n + 1 < NTT else None)
            if own and i % 2 == 1:
                if side is not None:
                    run(side)
                side = attention_group(i // 2)
                for _ in range(4):
                    next(side)
                g_ = i // 2
                side_total = (18 + 2 * g_) * 4 + 4 * 4 + 2
                side_rate[0] = 1.0 * side_total / (2.0 * max(main_per_tt[0], 1))
            m0 = STEPS['main']
            side = wrr(rr(nxt_front, back(n)), side, side_rate[0])
            if own:
                main_per_tt[0] = STEPS['main'] - m0
            if own and i == 3 and 'yT01' in dbg_d:
                if side is not None:
                    run(side)
                    side = None
                dump('yT01', yT[:, :, 0:256], [('yT', a, b) for a in range(2) for b in range(2)])
        if side is not None:
            run(side)

        if 'yT' in dbg_d:
            S.dma('sp', dbg_d['yT'], yT[:], r=[('yT', i, j) for i in range(16) for j in range(2)], w=['dbg_yT'])
            S.finish('sp', ['dbg_yT'])

        ab.close()
        if 'C' not in phases:
            S.barrier()
            return nc

        hres = sb("hres", [128, 16, 1024])
        wout = sb("wout", [128, 8, 1024], BF16)
        wp = sb("wp", [128, 2, 1024], BF16)
        cd = ExitStack()

        def sc(name, shape, dt=F32):
            return cd.enter_context(nc.sbuf_tensor("t_" + name, list(shape), dt))

        u2T = sc("u2T", [128, 8, 2048], BF16)
        u = sc("u_c", [128, 1024], BF16)
        ssb = sc("ssb_c", [128, 8])
        wupb = [sc("wup%d" % i, [128, 8, 512], BF16) for i in range(2)]
        wdnb = [sc("wdn%d" % i, [128, 4, 1024], BF16) for i in range(2)]
        hidT = [sc("hidT%d" % i, [128, 4, 512], BF16) for i in range(2)]
        relu_t = sc("relu_t", [128, 512])
        w_out_v = w_out.rearrange("(k p) n -> p k n", p=128)
        S.barrier()
        for k in range(8):
            S.dma('pool', wout[:, k, :], w_out_v[:, k, :], w=[('wout', k)])
        w_up_v = w_up.rearrange("(k p) n -> p k n", p=128)
        w_dn_v = w_down.rearrange("(k p) n -> p k n", p=128)

        def load_ffn_w(j):
            b = j % 2
            S.dma('pool', wupb[b][:], w_up_v[:, :, j * 512:(j + 1) * 512], w=[('wup', b)])
            S.dma('pool', wdnb[b][:], w_dn_v[:, j * 4:(j + 1) * 4, :], w=[('wdn', b)])

        u_c2 = sc("u_c2", [128, 1024], BF16)
        ssb_c2 = sc("ssb_c2", [128, 8])
        CSCR = [(u, 'u_c', ssb, 'ssb_c', 0), (u_c2, 'u_c2', ssb_c2, 'ssb_c2', 1)]

        def c_stream(i):
            S.dma('sp', hres[:, i, :], xo[i * 128:(i + 1) * 128, :], w=[('h', i)])
            for half in range(2):
                pbk = 2 + (i * 2 + half) % 6
                for k in range(8):
                    S.op('pe', lambda e: e.matmul(banks[pbk][:, :], yT[:, k, i * 128:(i + 1) * 128],
                                                  wout[:, k, half * 512:(half + 1) * 512], start=(k == 0), stop=(k == 7)),
                         r=[('yT', i, 0), ('yT', i, 1), ('wout', k)], w=[bk(pbk)], sig=k == 7)
                yield S.op('dve', lambda e: e.tensor_tensor(hres[:, i, half * 512:(half + 1) * 512], banks[pbk][:, :],
                                                            hres[:, i, half * 512:(half + 1) * 512], ALU.add),
                           r=[bk(pbk), ('h', i)], w=[('h', i)])
            if i == 0:
                load_ffn_w(0)
                load_ffn_w(1)
            sc_ = CSCR[i % 2]
            _norm_a1(hres[:, i, :], ('h', i), sc_[0], sc_[1], sc_[2], sc_[3])
            yield
            yield
            _norm_a2(C_G2, u2T[:, :, i * 128:(i + 1) * 128], ('u2T', i), sc_[0], sc_[1], sc_[4])
            yield

        def lagged0(gens, lag):
            live = []
            pending = list(gens)
            rnd = 0
            while live or pending:
                if pending and rnd % lag == 0:
                    live.append(pending.pop(0))
                for g in list(live):
                    try:
                        next(g)
                    except StopIteration:
                        live.remove(g)
                rnd += 1

        lagged0([c_stream(i) for i in range(NT)], 1)
        if 'h1' in dbg_d:
            S.dma('sp', dbg_d['h1'], hres[:], r=[('h', i) for i in range(16)], w=['dbg_h1'])
            S.finish('sp', ['dbg_h1'])

        nacc = [0]
        ffn_items = [(j, tg) for j in range(8) for tg in range(4)]

        def ffn_up(n):
            j, tg = ffn_items[n]
            b = j % 2
            hb = n % 2
            for m in range(4):
                pbk = 2 + nacc[0] % 6
                nacc[0] += 1
                for k in range(8):
                    S.op('pe', lambda e: e.matmul(banks[pbk][:, :], wupb[b][:, k, m * 128:(m + 1) * 128],
                                                  u2T[:, k, tg * 512:(tg + 1) * 512], start=(k == 0), stop=(k == 7)),
                         r=[('wup', b)] + [('u2T', tg * 4 + q) for q in range(4)], w=[bk(pbk)], sig=k == 7)
                S.op('act', lambda e: e.activation(relu_t[:], banks[pbk][:, :], AF.Relu), r=[bk(pbk)], w=['relu_t'])
                S.op('act', lambda e: e.activation(hidT[hb][:, m, :], relu_t[:], AF.Square),
                     r=['relu_t'], w=[('hidT', hb)])

        def ffn_down(n):
            j, tg = ffn_items[n]
            b = j % 2
            hb = n % 2
            for q in range(4):
                i = tg * 4 + q
                for half in range(2):
                    pbk = 2 + nacc[0] % 6
                    nacc[0] += 1
                    for m in range(4):
                        S.op('pe', lambda e: e.matmul(banks[pbk][:, :], hidT[hb][:, m, q * 128:(q + 1) * 128],
                                                      wdnb[b][:, m, half * 512:(half + 1) * 512], start=(m == 0), stop=(m == 3)),
                             r=[('hidT', hb), ('wdn', b)], w=[bk(pbk)], sig=m == 3)
                    S.op('dve', lambda e: e.tensor_tensor(hres[:, i, half * 512:(half + 1) * 512], banks[pbk][:, :],
                                                          hres[:, i, half * 512:(half + 1) * 512], ALU.add),
                         r=[bk(pbk), ('h', i)], w=[('h', i)])
            if tg == 3:
                if j + 2 < 8:
                    load_ffn_w(j + 2)
                if j == 5:
                    w_g_v = w_gate.rearrange("(k p) n -> p k n", p=128)
                    w_p_v = w_ple.rearrange("(k p) n -> p k n", p=128)
                    for k in range(8):
                        S.dma('pool', wout[:, k, :], w_g_v[:, k, :], w=[('wout', k)])
                    S.dma('pool', wp[:], w_p_v, w=['wp'])

        ffn_up(0)
        for n in range(len(ffn_items)):
            if n + 1 < len(ffn_items):
                ffn_up(n + 1)
            ffn_down(n)
        if 'h2' in dbg_d:
            S.dma('sp', dbg_d['h2'], hres[:], r=[('h', i) for i in range(16)], w=['dbg_h2'])
            S.finish('sp', ['dbg_h2'])
        S.barrier()
        cd.close()

        wg = wout
        NS = 4
        u_e = [sb("u_e%d" % i, [128, 1024], BF16) for i in range(NS)]
        ssb_e = [sb("ssb_e%d" % i, [128, 8]) for i in range(NS)]
        u3T = [sb("u3T%d" % i, [128, 8, 128], BF16) for i in range(NS)]
        pt = [sb("pt%d" % i, [128, 256], BF16) for i in range(NS)]
        pT = [sb("pT%d" % i, [128, 2, 128], BF16) for i in range(NS)]
        gsb = [sb("gsb%d" % i, [128, 1024]) for i in range(NS)]

        def pe_stream(i):
            b = i % NS
            tb = b % 2
            S.dma('pool', pt[b][:], po[i * 128:(i + 1) * 128, :], w=[('pt', b)])
            _norm_a1(hres[:, i, :], ('h', i), u_e[b], ('u_e', b), ssb_e[b], ('ssb_e', b))
            yield
            yield
            _norm_a2(C_G3, u3T[b], ('u3T', b), u_e[b], ('u_e', b), tb)
            yield
            psb = banks[tb][:].bitcast(BF16)
            for k in range(2):
                S.op('pe', lambda e: e.transpose(psb[:, k * 128:(k + 1) * 128], pt[b][:, k * 128:(k + 1) * 128], identb),
                     r=[('pt', b), 'cstb'], w=[bk(tb)], sig=k == 1)
            yield S.op('dve', lambda e: e.tensor_copy(pT[b][:], psb[:, 0:256].rearrange("p (k t) -> p k t", k=2)),
                       r=[bk(tb)], w=[('pT', b)])
            for half in range(2):
                pg = 2 + b
                pe_ = 6 + (b % 2)
                for k in range(8):
                    S.op('pe', lambda e: e.matmul(banks[pg][:, :], u3T[b][:, k, :], wg[:, k, half * 512:(half + 1) * 512],
                                                  start=(k == 0), stop=(k == 7)), r=[('u3T', b), ('wout', k)], w=[bk(pg)], sig=k == 7)
                sl = slice(half * 512, (half + 1) * 512)
                yield S.op('act', lambda e: e.activation(gsb[b][:, sl], banks[pg][:, :], AF.Sigmoid), r=[bk(pg)], w=[('gsb', b, half)])
                for k in range(2):
                    S.op('pe', lambda e: e.matmul(banks[pe_][:, :], pT[b][:, k, :], wp[:, k, half * 512:(half + 1) * 512],
                                                  start=(k == 0), stop=(k == 1)), r=[('pT', b), 'wp'], w=[bk(pe_)], sig=k == 1)
                yield S.op('dve', lambda e: e.tensor_tensor(gsb[b][:, sl], banks[pe_][:, :], gsb[b][:, sl], ALU.mult),
                           r=[bk(pe_), ('gsb', b, half)], w=[('gsb', b, half)])
                yield S.op('dve', lambda e: e.tensor_tensor(gsb[b][:, sl], gsb[b][:, sl], hres[:, i, sl], ALU.add),
                           r=[('gsb', b, half), ('h', i)], w=[('gsb', b, half)])
            S.dma('sp', out_d[i * 128:(i + 1) * 128, :], gsb[b][:], r=[('gsb', b, 0), ('gsb', b, 1)],
                  w=[('out', i)])
            yield

        def lagged(gens, lag):
            live = []
            pending = list(gens)
            rnd = 0
            while live or pending:
                if pending and rnd % lag == 0:
                    live.append(pending.pop(0))
                for g in list(live):
                    try:
                        next(g)
                    except StopIteration:
                        live.remove(g)
                rnd += 1

        lagged([pe_stream(i) for i in range(NT)], 2)
        S.finish('sp', [('out', i) for i in range(NT)])
    return nc


def _consts(core, inp):
    c = np.zeros((128, C_TOT), np.float32)

    def pk(v):
        return np.asarray(v, np.float32).reshape(-1, 128).T

    c[:, C_G1:C_G1 + 8] = pk(inp['attn_norm_g'][0])
    c[:, C_G2:C_G2 + 8] = pk(inp['mlp_norm_g'][0])
    c[:, C_G3:C_G3 + 8] = pk(inp['ple_norm_g'][0])
    c[:, C_GY:C_GY + 4] = pk(inp['mlstm_norm_g'][0])
    c[:, C_GY + 4:C_GY + 8] = pk(inp['attn_sub_norm_g'][0])
    cw = np.asarray(inp['conv_w'][0], np.float32)
    c[:, C_CW:C_CW + 32] = cw.T.reshape(8, 128, 4).transpose(1, 0, 2).reshape(128, 32)
    c[:, C_CB:C_CB + 8] = pk(inp['conv_b'][0])
    c[:, C_QG:C_QG + 64] = np.asarray(inp['q_norm_g'][0], np.float32)[None, :]
    c[:, C_KG:C_KG + 64] = np.asarray(inp['k_norm_g'][0], np.float32)[None, :]
    for j, nm in enumerate(('lambda_q1', 'lambda_k1', 'lambda_q2', 'lambda_k2')):
        c[:, C_LAM + j * 64:C_LAM + (j + 1) * 64] = np.asarray(inp[nm][0], np.float32)[None, :]
    odd = core % 2
    c[:, C_FL + 0] = 1.0 if odd else 0.0
    c[:, C_FL + 1] = 0.0 if odd else NEG
    c[:, C_FL + 2] = 0.0 if odd else NEG
    inv = (np.float32(500000.0) ** (-np.arange(0, 16, 2, dtype=np.float32) / np.float32(16))).astype(np.float32)
    for off, base in ((C_CSO, odd * 2048), (C_CSP, 0)):
        pos = (base + np.arange(2048, dtype=np.float32)).astype(np.float32)
        ang = (pos[:, None] * inv[None, :]).astype(np.float32)
        cs = np.concatenate([np.cos(ang), np.sin(ang)], axis=1).astype(np.float32)
        c[:, off:off + 256] = cs.reshape(16, 128, 16).transpose(1, 0, 2).reshape(128, 256)
    c[:, C_ID:C_ID + 128] = np.eye(128, dtype=np.float32)
    c[:, C_TRI:C_TRI + 128] = np.triu(np.ones((128, 128), np.float32))
    c[0:4, C_GB] = np.asarray(inp['igate_b'][0], np.float32)
    c[0:4, C_GB + 1] = np.asarray(inp['fgate_b'][0], np.float32)
    return c


def _constb():
    b = np.zeros((128, 128 + 1024), np.float32)
    b[:, 0:128] = np.eye(128, dtype=np.float32)
    tri = np.triu(np.ones((128, 128), np.float32))
    for m in range(2):
        b[:, 128 + m * 256:128 + m * 256 + 128] = tri
        b[:, 128 + m * 256 + 128:128 + m * 256 + 256] = 1.0
        b[:, 640 + m * 256:640 + m * 256 + 128] = 0.0
        b[:, 640 + m * 256 + 128:640 + m * 256 + 256] = tri
    return b


def make_in_maps(inp):
    x = np.asarray(inp['x'], np.float32)
    p = np.asarray(inp['p'], np.float32)
    shared = {
        'w_in': np.ascontiguousarray(inp['w_in'][0], dtype=np.float32),
        'w_out': np.ascontiguousarray(inp['w_out'][0], dtype=np.float32),
        'w_up': np.ascontiguousarray(inp['w_up'][0], dtype=np.float32),
        'w_down': np.ascontiguousarray(inp['w_down'][0], dtype=np.float32),
        'w_gate': np.ascontiguousarray(inp['w_ple_gate'][0], dtype=np.float32),
        'w_ple': np.ascontiguousarray(inp['w_ple_proj'][0], dtype=np.float32),
        'cstb': _constb(),
    }
    zeros = np.zeros((2048, 1024), np.float32)
    maps = []
    for c in range(8):
        b, hf = c // 2, c % 2
        m = dict(shared)
        m['xo'] = np.ascontiguousarray(x[b, hf * 2048:(hf + 1) * 2048])
        m['xp'] = np.ascontiguousarray(x[b, 0:2048]) if hf else zeros
        m['po'] = np.ascontiguousarray(p[0, b, hf * 2048:(hf + 1) * 2048])
        m['cst'] = _consts(c, inp)
        maps.append(m)
    return maps


def kernel(**inputs):
    nc = build()
    maps = make_in_maps(inputs)
    res = run_bass_kernel_spmd(nc, maps, core_ids=list(range(8)))
    out = np.zeros((4, 4096, 1024), np.float32)
    for c in range(8):
        out[c // 2, (c % 2) * 2048:(c % 2 + 1) * 2048] = res.results[c]['out']
    return out
```

```python
import math
from contextlib import ExitStack

import numpy as np
import concourse.bass as bass
import concourse.mybir as mybir
from concourse.bass_utils import run_bass_kernel_spmd

F32 = mybir.dt.float32
BF16 = mybir.dt.bfloat16
AF = mybir.ActivationFunctionType
ALU = mybir.AluOpType
AX = mybir.AxisListType

SAME_ENGINE_SYNC = True
A_LAG = 1
A2_LAG = 6
PE_WARM_REPS = 1
EPS = 1e-6
NT = 16
NEG = -30000.0
LAM_INIT = 0.8 - 0.6 * math.exp(0.0)
LNC = math.log(128 ** -0.5)

O_MQ, O_MK, O_MV, O_MO, O_MI, O_MF, O_AQ, O_AK, O_AV = 0, 512, 1024, 1536, 2048, 2052, 2056, 2568, 3080

C_G1, C_G2, C_G3, C_GY, C_CW, C_CB, C_QG, C_KG, C_LAM, C_FL, C_CSO, C_CSP, C_ID, C_TRI, C_GB = (
    0, 8, 16, 24, 32, 64, 72, 136, 200, 456, 460, 716, 972, 1100, 1228)
C_TOT = 1230


class _Eng:
    def __init__(self, name, h, sem):
        self.name, self.h, self.sem = name, h, sem
        self.n = 0
        self.count = 0
        self.incs = []
        self.last = None
        self.last_seq = 0
        self.waited = {}
        self.dsems = []
        self.dcnt = []
        self.dnext = 0


class Sched:
    def __init__(self, nc, stack, ndma):
        self.nc = nc
        hs = {'pe': nc.tensor, 'act': nc.scalar, 'dve': nc.vector, 'pool': nc.gpsimd, 'sp': nc.sync}
        self.e = {}
        for k, h in hs.items():
            sem = stack.enter_context(nc.semaphore("s_" + k))
            self.e[k] = _Eng(k, h, sem)
            for i in range(ndma.get(k, 0)):
                self.e[k].dsems.append(stack.enter_context(nc.semaphore("d_%s%d" % (k, i))))
                self.e[k].dcnt.append(0)
        self.st = {}

    def _target(self, dep):
        if dep[0] == 'd':
            return dep[1], dep[2]
        p = self.e[dep[1]]
        seq = dep[2]
        found = None
        for (s, c) in reversed(p.incs):
            if s >= seq:
                found = c
            else:
                break
        if found is None:
            p.count += 1
            p.last.then_inc(p.sem, 1)
            p.incs.append((p.last_seq, p.count))
            found = p.count
        return p.sem, found

    def _wait(self, eng, deps):
        E = self.e[eng]
        for dep in deps:
            if dep is None:
                continue
            if dep[0] == 'c' and dep[1] == eng and (eng == 'pe' or not SAME_ENGINE_SYNC):
                continue
            sem, val = self._target(dep)
            key = id(sem)
            if E.waited.get(key, 0) >= val:
                continue
            E.h.wait_ge(sem, val)
            E.waited[key] = val

    def _deps(self, r, w, eng=None):
        deps = []
        for k in r:
            s = self.st.get(k)
            if s is not None:
                deps.append(s[0])
                if isinstance(k, tuple) and k[0] == 'bank':
                    deps.extend(d for d in s[1].values() if not (d[0] == 'c' and d[1] == eng))
        for k in w:
            s = self.st.get(k)
            if s is not None:
                deps.append(s[0])
                deps.extend(s[1].values())
        return deps

    def _record(self, dep, r, w):
        rk = (dep[0], dep[1] if dep[0] == 'c' else id(dep[1]))
        for k in r:
            s = self.st.setdefault(k, [None, {}])
            s[1][rk] = dep
        for k in w:
            self.st[k] = [dep, {}]

    def op(self, eng, fn, r=(), w=(), sig=None):
        E = self.e[eng]
        self._wait(eng, self._deps(r, w, eng))
        inst = fn(E.h)
        E.n += 1
        E.last = inst
        E.last_seq = E.n
        if sig or (sig is None and eng != 'pe'):
            E.count += 1
            inst.then_inc(E.sem, 1)
            E.incs.append((E.n, E.count))
        self._record(('c', eng, E.n), r, w)
        return inst

    def dma(self, q, out, in_, r=(), w=(), **kw):
        E = self.e[q]
        self._wait(q, self._deps(r, w))
        i = E.dnext
        E.dnext = (i + 1) % len(E.dsems)
        sem = E.dsems[i]
        if E.dcnt[i] > 0:
            key = id(sem)
            if E.waited.get(key, 0) < E.dcnt[i]:
                E.h.wait_ge(sem, E.dcnt[i])
                E.waited[key] = E.dcnt[i]
        E.dcnt[i] += 16
        E.h.dma_start(out=out, in_=in_, **kw).then_inc(sem, 16)
        dep = ('d', sem, E.dcnt[i])
        self._record(dep, r, w)
        return dep

    def barrier(self):
        deps = []
        for s in self.st.values():
            if s[0] is not None:
                deps.append(s[0])
            deps.extend(s[1].values())
        for eng in self.e:
            self._wait(eng, deps)

    def finish(self, eng, keys):
        self._wait(eng, [self.st[k][0] for k in keys if k in self.st])


def build(dbg=(), n_pre=NT, n_own=NT, phases='CDE'):
    nc = bass.Bass("TRN2", target_bir_lowering=False)

    def din(name, shape):
        return nc.dram_tensor(name, list(shape), F32, kind="ExternalInput").ap()

    xo = din("xo", [2048, 1024])
    xp = din("xp", [2048, 1024])
    po = din("po", [2048, 256])
    w_in = din("w_in", [1024, 3592])
    w_out = din("w_out", [1024, 1024])
    w_up = din("w_up", [1024, 4096])
    w_down = din("w_down", [4096, 1024])
    w_gate = din("w_gate", [1024, 1024])
    w_ple = din("w_ple", [256, 1024])
    cst_d = din("cst", [128, C_TOT])
    cstb_d = din("cstb", [128, 128 + 1024])
    out_d = nc.dram_tensor("out", [2048, 1024], F32, kind="ExternalOutput").ap()
    dbg_d = {}
    for name, shape, dt in dbg:
        dbg_d[name] = nc.dram_tensor("dbg_" + name, list(shape), dt, kind="ExternalOutput").ap()

    with ExitStack() as st:
        S = Sched(nc, st, {'sp': 12, 'pool': 8})

        def sb(name, shape, dt=F32):
            return st.enter_context(nc.sbuf_tensor("t_" + name, list(shape), dt))

        def freed(name, shape, dt=F32):
            return nc.sbuf_tensor(name, list(shape), dt)

        cst = sb("cst", [128, C_TOT])
        cstb = sb("cstb", [128, 128 + 1024], BF16)
        identb = cstb[:, 0:128]
        identf = cst[:, C_ID:C_ID + 128]
        tri = cst[:, C_TRI:C_TRI + 128]
        banks = [st.enter_context(nc.psum_tensor("bank%d" % i, [128, 512], F32)) for i in range(8)]
        small = sb("small", [128, 64])
        block = st.enter_context(nc.Block())

        S.dma('sp', cst[:], cst_d, w=['cst'])
        S.dma('pool', cstb[:], cstb_d, w=['cstb'])

        def bk(i):
            return ('bank', i)

        def rstd_from_ss(ss_ap, out_ap, n, key_ss, key_out, mul=None):
            S.op('act', lambda e: e.activation(out_ap, ss_ap, AF.Ln, bias=epsc[:ss_ap.shape[0], 0:1], scale=1.0 / n),
                 r=[key_ss, 'consts'], w=[key_out])
            S.op('act', lambda e: e.activation(out_ap, out_ap, AF.Exp, scale=-0.5), r=[key_out], w=[key_out])
            if mul is not None:
                S.op('dve', lambda e: e.tensor_scalar(out_ap, out_ap, float(mul), None, ALU.mult), r=[key_out], w=[key_out])

        epsc = sb("epsc", [128, 4])
        ones4 = sb("ones4", [4, 128])
        S.op('pool', lambda e: e.memset(epsc[:, 0:1], EPS), w=['consts'])
        S.op('pool', lambda e: e.memset(epsc[:, 1:2], 1.0), w=['consts'])
        S.op('pool', lambda e: e.memset(ones4[:], 1.0), w=['consts4'])

        lamv = cst[:, C_LAM:C_LAM + 256]
        lj = sb("lj", [128, 64])
        S.op('dve', lambda e: e.scalar_tensor_tensor(out=lj[:], in0=lamv[:, 0:64], scalar=1.0, in1=lamv[:, 64:128],
                                                     op0=ALU.mult, op1=ALU.mult, accum_out=small[:, 0:1]),
             r=['cst'], w=['lj', 'small'])
        S.op('dve', lambda e: e.scalar_tensor_tensor(out=lj[:], in0=lamv[:, 128:192], scalar=1.0, in1=lamv[:, 192:256],
                                                     op0=ALU.mult, op1=ALU.mult, accum_out=small[:, 1:2]),
             r=['cst', 'lj'], w=['lj', 'small'])
        S.op('act', lambda e: e.activation(small[:, 2:4], small[:, 0:2], AF.Exp), r=['small'], w=['small'])
        S.op('dve', lambda e: e.tensor_tensor(small[:, 4:5], small[:, 2:3], small[:, 3:4], ALU.subtract), r=['small'], w=['small'])
        S.op('dve', lambda e: e.tensor_scalar(small[:, 5:6], small[:, 4:5], float(LAM_INIT), None, ALU.add), r=['small'], w=['small'])
        lam_ap = small[:, 5:6]
        gpar = sb("gpar", [4, 8])
        gb = cst[0:4, C_GB:C_GB + 2]
        fl = cst[0:4, C_FL:C_FL + 4]
        S.op('dve', lambda e: e.tensor_copy(gpar[:, 0:1], fl[:, 0:1]), r=['cst'], w=['gpar'])
        S.op('dve', lambda e: e.scalar_tensor_tensor(out=gpar[:, 1:2], in0=gb[:, 0:1], scalar=fl[:, 0:1], in1=fl[:, 1:2],
                                                     op0=ALU.mult, op1=ALU.add), r=['cst', 'gpar'], w=['gpar'])
        S.op('pool', lambda e: e.memset(gpar[:, 2:3], 1.0), r=['gpar'], w=['gpar'])
        S.op('dve', lambda e: e.tensor_copy(gpar[:, 3:4], gb[:, 0:1]), r=['cst', 'gpar'], w=['gpar'])
        S.op('dve', lambda e: e.tensor_scalar(gpar[:, 4:5], gb[:, 1:2], -1.0, None, ALU.mult), r=['cst', 'gpar'], w=['gpar'])
        S.op('dve', lambda e: e.tensor_copy(gpar[:, 5:6], fl[:, 0:1]), r=['cst', 'gpar'], w=['gpar'])
        pbias = cst[:, C_FL + 2:C_FL + 3]

        yT = sb("yT", [128, 8, 2048], BF16)
        ab = ExitStack()

        def sa(name, shape, dt=F32):
            return ab.enter_context(nc.sbuf_tensor("t_" + name, list(shape), dt))

        win = sa("win", [128, 8, 3592], BF16)
        kT = sa("kT", [128, 4, 4096], BF16)
        vx = sa("vx", [128, 32, 4, 130], BF16)
        xt = sa("xt", [128, 1024])
        u = sa("u", [128, 1024], BF16)
        uT = [sa("uT%d" % i, [128, 8, 128], BF16) for i in range(2)]
        ssb = sa("ssb", [128, 8])
        raw = sa("raw", [128, 8, 131])
        acc = sa("acc", [128, 4, 128])
        qkT2 = [sa("qkT%d" % i, [128, 8, 128], BF16) for i in range(2)]
        gf = sa("gf", [4, 8, 128])
        carry = sa("carry", [4, 8])
        Gt = [small[:, 16 + 12 * i:28 + 12 * i] for i in range(3)]
        mvx = [sa("mvx%d" % i, [128, 4, 130], BF16) for i in range(2)]
        sigo2 = [sa("sigo%d" % i, [128, 512], BF16) for i in range(2)]
        mls = sa("mls", [128, 512], BF16)
        ktl = [mls[:, i * 128:(i + 1) * 128] for i in range(2)]
        stm = [mls[:, 256 + i * 128:256 + (i + 1) * 128] for i in range(2)]
        Pst = sa("Pst", [128, 4, 130])
        Cbf = sa("Cbf", [128, 4, 130], BF16)
        ml = sa("ml", [128, 32])
        qf = sa("qf", [128, 512])
        qb16 = sa("qb16", [128, 512], BF16)
        qtok = [sa("qtok%d" % i, [128, 512], BF16) for i in range(2)]
        qTg = [sa("qTg%d" % i, [128, 4, 256], BF16) for i in range(2)]
        Et = [sa("Et%d" % i, [128, 512], BF16) for i in range(2)]
        ofin = [sa("ofin%d" % i, [128, 128]) for i in range(2)]
        yat = [sa("yat%d" % i, [128, 512], BF16) for i in range(2)]
        al = sa("al", [128, 16])
        ae = small[:, 8:16]

        w_in_v = w_in.rearrange("(k p) n -> p k n", p=128)
        WGRP = [(O_MK, O_MV), (O_MV, O_MO), (O_MI, O_AQ), (O_AK, O_AV), (O_AV, 3592), (O_MQ, O_MK), (O_MO, O_MI), (O_AQ, O_AK)]
        for gi_, (c0_, c1_) in enumerate(WGRP):
            S.dma('pool', win[:, :, c0_:c1_], w_in_v[:, :, c0_:c1_], w=[('win', gi_)])

        def wkey(col):
            for gi_, (c0_, c1_) in enumerate(WGRP):
                if c0_ <= col < c1_:
                    return ('win', gi_)
            raise ValueError(col)

        S.op('pool', lambda e: e.memset(vx[:, :, :, 128:130], 1.0), w=[('vx', j) for j in range(32)])
        S.op('pool', lambda e: e.tensor_scalar(vx[:, 0:16, :, 128:130], vx[:, 0:16, :, 128:130], cst[:, C_FL:C_FL + 1], 1.0,
                                               ALU.mult, ALU.mult),
             r=['cst'] + [('vx', j) for j in range(16)], w=[('vx', j) for j in range(16)])
        for i in range(2):
            S.op('pool', lambda e: e.memset(mvx[i][:, :, 128:130], 1.0), w=[('mvx', i)])
        for i in range(3):
            S.op('pool', lambda e: e.memset(Gt[i], 1.0), w=[('Gt', i)])
        S.op('pool', lambda e: e.memset(qTg[0][:], 0.0), w=[('qTg', 0, 0), ('qTg', 0, 1)])
        S.op('pool', lambda e: e.memset(qTg[1][:], 0.0), w=[('qTg', 1, 0), ('qTg', 1, 1)])
        S.op('pool', lambda e: e.memset(Pst[:], 0.0), w=[('Pst', h) for h in range(4)])
        S.op('pool', lambda e: e.memset(Cbf[:], 0.0), w=['Cbf'])
        S.op('pool', lambda e: e.memset(raw[:], 0.0), w=[('raw', c) for c in range(8)])
        S.op('pool', lambda e: e.memset(carry[:], 0.0), w=[('carry', j) for j in range(8)])

        cw = cst[:, C_CW:C_CW + 32].rearrange("p (c j) -> p c j", j=4)
        cb = cst[:, C_CB:C_CB + 8]
        cnt = {'tt': 0}

        def norm_to_uT(x_tile, kx, gcol, uT_t, kuT, scr=None):
            if scr is None:
                u_, ku, ssb_, kss, tb = u, 'u', ssb, 'ssb', 0
            else:
                u_, ku, ssb_, kss, tb = scr
            return _norm_to_uT(x_tile, kx, gcol, uT_t, kuT, u_, ku, ssb_, kss, tb)

        def _norm_to_uT(x_tile, kx, gcol, uT_t, kuT, u, ku, ssb, kss, tb):
            _norm_a1(x_tile, kx, u, ku, ssb, kss)
            _norm_a2(gcol, uT_t, kuT, u, ku, tb)

        def _norm_a1(x_tile, kx, u, ku, ssb, kss):
            S.op('act', lambda e: e.activation(u[:], x_tile, AF.Square, accum_out=ssb[:, 0:1]), r=[kx], w=[ku, kss])
            rstd_from_ss(ssb[:, 0:1], ssb[:, 1:2], 1024.0, kss, kss)
            S.op('dve', lambda e: e.tensor_scalar(u[:], x_tile, ssb[:, 1:2], None, ALU.mult), r=[kx, kss], w=[ku])

        def _norm_a2(gcol, uT_t, kuT, u, ku, tb):
            psb = banks[tb][:].bitcast(BF16)
            for k in range(8):
                S.op('pe', lambda e: e.transpose(psb[:, k * 128:(k + 1) * 128], u[:, k * 128:(k + 1) * 128], identb),
                     r=[ku, 'cstb'], w=[bk(tb)], sig=k == 7)
            g_bc = cst[:, gcol:gcol + 8].unsqueeze(2).to_broadcast([128, 8, 128])
            S.op('dve', lambda e: e.tensor_tensor(uT_t[:], psb[:, 0:1024].rearrange("p (k t) -> p k t", k=8), g_bc, ALU.mult),
                 r=[bk(tb), 'cst'], w=[kuT])

        def transpose_to(src_tok, ksrc, dst_ap, kdst, scale_cols=None, bank=0):
            psb = banks[bank][:].bitcast(BF16)
            for k in range(4):
                S.op('pe', lambda e: e.transpose(psb[:, k * 128:(k + 1) * 128], src_tok[:, k * 128:(k + 1) * 128], identb),
                     r=[ksrc, 'cstb'], w=[bk(bank)], sig=k == 3)
            src = psb[:, 0:512].rearrange("p (k t) -> p k t", k=4)
            if scale_cols is None:
                S.op('dve', lambda e: e.tensor_copy(dst_ap, src), r=[bk(bank)], w=[kdst])
            else:
                g_bc = scale_cols.unsqueeze(2).to_broadcast([128, 4, 128])
                S.op('dve', lambda e: e.tensor_tensor(dst_ap, src, g_bc, ALU.mult), r=[bk(bank), 'cst'], w=[kdst])

        def proj_tm(uT_t, kuT, col, bank):
            for k in range(8):
                S.op('pe', lambda e: e.matmul(banks[bank][:, :], uT_t[:, k, :], win[:, k, col:col + 512],
                                              start=(k == 0), stop=(k == 7)),
                     r=[kuT, wkey(col)], w=[bk(bank)], sig=k == 7)

        def rr(*gens):
            gens = [g for g in gens if g is not None]
            while gens:
                for g in list(gens):
                    try:
                        next(g)
                    except StopIteration:
                        gens.remove(g)
                yield

        def run(gen):
            for _ in gen:
                pass


        def qk_prep(bank, gcol, cs, qb16, k16):
            q3 = qf[:].rearrange("p (g d) -> p g d", d=64)
            S.op('act', lambda e: e.activation(qf[:], banks[bank][:, :], AF.Square), r=[bk(bank)], w=['qf'])
            S.op('dve', lambda e: e.tensor_reduce(out=al[:, 0:8], in_=q3, axis=AX.X, op=ALU.add), r=['qf'], w=['al'])
            S.op('act', lambda e: e.activation(al[:, 8:16], al[:, 0:8], AF.Ln, bias=epsc[:, 0:1], scale=1.0 / 64),
                 r=['al', 'consts'], w=['al'])
            S.op('act', lambda e: e.activation(al[:, 8:16], al[:, 8:16], AF.Exp, scale=-0.5), r=['al'], w=['al'])
            S.op('dve', lambda e: e.tensor_tensor(q3, banks[bank][:, :].rearrange("p (g d) -> p g d", d=64),
                                                  al[:, 8:16].unsqueeze(2).to_broadcast([128, 8, 64]), ALU.mult),
                 r=[bk(bank), 'al'], w=['qf'])
            gg = cst[:, gcol:gcol + 64].unsqueeze(1).to_broadcast([128, 8, 64])
            o3 = qb16[:].rearrange("p (g d) -> p g d", d=64)
            yield S.op('dve', lambda e: e.tensor_tensor(o3, q3, gg, ALU.mult), r=['qf', 'cst'], w=[k16])
            gg16 = cst[:, gcol:gcol + 16].unsqueeze(1).to_broadcast([128, 8, 16])
            yield S.op('dve', lambda e: e.tensor_tensor(q3[:, :, 0:16], q3[:, :, 0:16], gg16, ALU.mult), r=['qf', 'cst'], w=['qf'])
            x1, x2 = q3[:, :, 0:8], q3[:, :, 8:16]
            cc = cs[:, 0:8].unsqueeze(1).to_broadcast([128, 8, 8])
            sn = cs[:, 8:16].unsqueeze(1).to_broadcast([128, 8, 8])
            r4 = [q3[:, :, 16 + 8 * a_:24 + 8 * a_] for a_ in range(4)]
            S.op('dve', lambda e: e.tensor_tensor(r4[0], x1, cc, ALU.mult), r=['qf', 'cst'], w=['qf'])
            S.op('dve', lambda e: e.tensor_tensor(r4[1], x2, sn, ALU.mult), r=['qf', 'cst'], w=['qf'])
            S.op('dve', lambda e: e.tensor_tensor(r4[2], x2, cc, ALU.mult), r=['qf', 'cst'], w=['qf'])
            yield S.op('dve', lambda e: e.tensor_tensor(r4[3], x1, sn, ALU.mult), r=['qf', 'cst'], w=['qf'])
            S.op('dve', lambda e: e.tensor_tensor(o3[:, :, 0:8], r4[0], r4[1], ALU.subtract), r=['qf', k16], w=[k16])
            yield S.op('dve', lambda e: e.tensor_tensor(o3[:, :, 8:16], r4[2], r4[3], ALU.add), r=['qf', k16], w=[k16])

        def attention_group(g):
            kbs = list(range(16)) + [16 + j for j in range(2 * g + 2)]
            units = [(h, idx, kb) for h in range(4) for idx, kb in enumerate(kbs)]
            PSB = (4, 5)
            for tt in range(2):
                psb = banks[4 + tt][:].bitcast(BF16)
                for k in range(4):
                    S.op('pe', lambda e: e.transpose(psb[:, k * 128:(k + 1) * 128], qtok[tt][:, k * 128:(k + 1) * 128], identb),
                         r=[('qtok', tt), 'cstb'], w=[bk(4 + tt)], sig=(k == 3))
                yield
                c_ = tt * 128
                S.op('act', lambda e: e.activation(qTg[0][0:64, :, c_:c_ + 128], psb[0:64, 0:512].rearrange("p (k t) -> p k t", k=4),
                                                   AF.Copy), r=[bk(4 + tt)], w=[('qTg', 0, tt)])
                yield S.op('act', lambda e: e.activation(qTg[1][64:128, :, c_:c_ + 128],
                                                         psb[64:128, 0:512].rearrange("p (k t) -> p k t", k=4), AF.Copy),
                           r=[bk(4 + tt)], w=[('qTg', 1, tt)])
            QK = [('qTg', m, tt) for m in range(2) for tt in range(2)]

            def st_mm(n):
                h, idx, kb = units[n]
                pb = PSB[n % 2]
                for rep_ in range(PE_WARM_REPS):
                    S.op('pe', lambda e: e.matmul(banks[pb][:, 0:256], kT[:, h, kb * 128:(kb + 1) * 128],
                                                  qTg[0][:, h, :], start=True, stop=True),
                         r=[('kT', kb)] + QK, w=[bk(pb)], sig=False)
                    S.op('pe', lambda e: e.matmul(banks[pb][:, 256:512], kT[:, h, kb * 128:(kb + 1) * 128],
                                                  qTg[1][:, h, :], start=True, stop=True),
                         r=[('kT', kb)] + QK, w=[bk(pb)], sig=(rep_ == PE_WARM_REPS - 1))

            st_mm(0)
            for n, (h, idx, kb) in enumerate(units):
                if n + 1 < len(units):
                    st_mm(n + 1)
                pb = PSB[n % 2]
                E = Et[n % 2]
                kE = ('Et', n % 2)
                S.op('act', lambda e: e.activation(E[:], banks[pb][:, :], AF.Exp, scale=0.125), r=[bk(pb)], w=[kE])
                own = kb - 16
                if own == 2 * g:
                    S.op('dve', lambda e: e.tensor_tensor(E[:], E[:], cstb[:, 128:640], ALU.mult), r=[kE, 'cstb'], w=[kE])
                elif own == 2 * g + 1:
                    S.op('dve', lambda e: e.tensor_tensor(E[:], E[:], cstb[:, 640:1152], ALU.mult), r=[kE, 'cstb'], w=[kE])
                for m in range(2):
                    for qb in range(2):
                        if qb == 0 and own == 2 * g + 1:
                            continue
                        last = (own == 2 * g) if qb == 0 else (own == 2 * g + 1)
                        a = 6 + m
                        o0 = qb * 130
                        S.op('pe', lambda e: e.matmul(banks[a][:, o0:o0 + 129], E[:, m * 256 + qb * 128: m * 256 + qb * 128 + 128],
                                                      vx[:, kb, h, 0:129], start=(idx == 0 and qb == 0), stop=last,
                                                      skip_group_check=True),
                             r=[kE, ('vx', kb)], w=[bk(a)], sig=(m == 1 and qb == 1))
                yield
                if idx == len(kbs) - 1:
                    den6 = banks[6][:, 0:260].rearrange("p (a c) -> p a c", c=130)[:, :, 128]
                    den7 = banks[7][:, 0:260].rearrange("p (a c) -> p a c", c=130)[:, :, 128]
                    S.op('dve', lambda e: e.reciprocal(ae[:, 2:4], den7), r=[bk(7)], w=[('ae', 1)])
                    S.op('dve', lambda e: e.tensor_scalar(ae[:, 2:4], ae[:, 2:4], lam_ap, None, ALU.mult), r=[('ae', 1), 'small'], w=[('ae', 1)])
                    S.op('dve', lambda e: e.reciprocal(ae[:, 0:2], den6), r=[bk(6)], w=[('ae', 0)])
                    for qb in range(2):
                        o0 = qb * 130
                        S.op('act', lambda e: e.activation(ofin[qb][:], banks[7][:, o0:o0 + 128], AF.Copy, scale=ae[:, 2 + qb:3 + qb]),
                             r=[bk(7), ('ae', 1)], w=[('ofin', qb)])
                        S.op('dve', lambda e: e.scalar_tensor_tensor(out=ofin[qb][:], in0=banks[6][:, o0:o0 + 128], scalar=ae[:, qb:qb + 1],
                                                                     in1=ofin[qb][:], op0=ALU.mult, op1=ALU.subtract),
                             r=[bk(6), ('ae', 0), ('ofin', qb)], w=[('ofin', qb)])
                    yield
                    for qb in range(2):
                        S.op('act', lambda e: e.activation(yat[qb][:, h * 128:(h + 1) * 128], ofin[qb][:], AF.Square,
                                                           accum_out=ae[:, 4 + qb:5 + qb]),
                             r=[('ofin', qb)], w=[('yat', qb), ('ae', 2 + qb)])
                    S.op('act', lambda e: e.activation(ae[:, 6:8], ae[:, 4:6], AF.Ln, bias=epsc[:, 0:1], scale=1.0 / 128),
                         r=[('ae', 2), ('ae', 3), 'consts'], w=[('ae', 4)])
                    S.op('act', lambda e: e.activation(ae[:, 6:8], ae[:, 6:8], AF.Exp, scale=-0.5), r=[('ae', 4)], w=[('ae', 4)])
                    for qb in range(2):
                        S.op('dve', lambda e: e.tensor_scalar(yat[qb][:, h * 128:(h + 1) * 128], ofin[qb][:], ae[:, 6 + qb:7 + qb],
                                                               float(1.0 - LAM_INIT), ALU.mult, ALU.mult),
                             r=[('ofin', qb), ('ae', 4)], w=[('yat', qb)])
                    yield
            for qb in range(2):
                t0 = (2 * g + qb) * 128
                transpose_to(yat[qb], ('yat', qb), yT[:, 4:8, t0:t0 + 128], ('yT', 2 * g + qb, 1),
                             scale_cols=cst[:, C_GY + 4:C_GY + 8], bank=4 + qb)
                yield

        NTT = n_pre + n_own
        def tt_info(n):
            own = n >= n_pre
            i = n - n_pre if own else n
            return own, i

        def stageA(n):
            own, i = tt_info(n)
            xsrc = xo if own else xp
            S.dma('sp', xt[:], xsrc[i * 128:(i + 1) * 128, :], w=['xt'])
            _norm_a1(xt[:], 'xt', u, 'u', ssb, 'ssb')
            for _ in range(A2_LAG):
                yield
            _norm_a2(C_G1, uT[n % 2], ('uT', n % 2), u, 'u', 0)
            yield

        def front(n):
            own, i = tt_info(n)
            blk = (16 + i) if own else i
            uT_t, kuT = uT[n % 2], ('uT', n % 2)
            Gc, kG = Gt[n % 3], ('Gt', n % 3)
            mv_t, kmv = mvx[n % 2], ('mvx', n % 2)
            qkT, kq = qkT2[n % 2], n % 2
            sigo = sigo2[n % 2]
            if own:
                GB, FMB, TMB = 0, (1, 1), (1, 1)
            else:
                GB, FMB, TMB = 1, (2, 3), (6, 7)
            need_q = own or i == NT - 1

            def T2():
                for half in ((0, 1) if need_q else (1,)):
                    pbk = FMB[half]
                    c0 = half * 4
                    for c in range(c0, c0 + 4):
                        col = (O_MQ + c * 128) if c < 4 else (O_MK + (c - 4) * 128)
                        for k in range(8):
                            S.op('pe', lambda e: e.matmul(banks[pbk][:, (c % 4) * 128:(c % 4 + 1) * 128], win[:, k, col:col + 128],
                                                          uT_t[:, k, :], start=(k == 0), stop=(k == 7)),
                                 r=[kuT, wkey(col)], w=[bk(pbk)], sig=(k == 7))
                    RK = [('raw', c) for c in range(c0, c0 + 4)]
                    if own:
                        S.op('dve', lambda e: e.tensor_copy(raw[:, c0:c0 + 4, 3:131], banks[pbk][:, :].rearrange("p (c t) -> p c t", c=4)),
                             r=[bk(pbk)], w=RK)
                    else:
                        S.op('act', lambda e: e.activation(raw[:, c0:c0 + 4, 3:131], banks[pbk][:, :].rearrange("p (c t) -> p c t", c=4),
                                                           AF.Copy), r=[bk(pbk)], w=RK)
                    conv = own or half == 1
                    if conv:
                        for c in range(c0, c0 + 4):
                            if own:
                                S.op('dve', lambda e: e.tensor_scalar(acc[:, c % 4, :], banks[pbk][:, (c % 4) * 128:(c % 4 + 1) * 128],
                                                                      cw[:, c, 3:4], cb[:, c:c + 1], ALU.mult, ALU.add),
                                     r=[bk(pbk), 'cst'], w=[('acc', c % 4)])
                            else:
                                S.op('act', lambda e: e.activation(acc[:, c % 4, :], banks[pbk][:, (c % 4) * 128:(c % 4 + 1) * 128],
                                                                   AF.Identity, bias=cb[:, c:c + 1], scale=cw[:, c, 3:4]),
                                     r=[bk(pbk), 'cst'], w=[('acc', c % 4)])
                    yield
                    if conv:
                        for c in range(c0, c0 + 4):
                            for j in range(3):
                                yield S.op('dve', lambda e: e.scalar_tensor_tensor(out=acc[:, c % 4, :], in0=raw[:, c, j:j + 128],
                                                                                   scalar=cw[:, c, j:j + 1], in1=acc[:, c % 4, :],
                                                                                   op0=ALU.mult, op1=ALU.add),
                                           r=[('raw', c), 'cst', ('acc', c % 4)], w=[('acc', c % 4)])
                    if own:
                        yield S.op('dve', lambda e: e.tensor_copy(raw[:, c0:c0 + 4, 0:3], raw[:, c0:c0 + 4, 128:131]), r=RK, w=RK)
                    else:
                        yield S.op('act', lambda e: e.activation(raw[:, c0:c0 + 4, 0:3], raw[:, c0:c0 + 4, 128:131], AF.Copy), r=RK, w=RK)
                    if conv:
                        tmp = raw[:, c0:c0 + 4, 3:131]
                        AK = [('acc', c) for c in range(4)]
                        S.op('act', lambda e: e.activation(tmp, acc[:, :, :], AF.Exp, scale=-1.0), r=AK + RK, w=RK)
                        S.op('act', lambda e: e.activation(tmp, tmp, AF.Ln, bias=epsc[:, 1:2]), r=RK + ['consts'], w=RK)
                        yield S.op('act', lambda e: e.activation(tmp, tmp, AF.Exp, scale=-1.0), r=RK, w=RK)
                        yield S.op('dve', lambda e: e.tensor_tensor(qkT[:, c0:c0 + 4, :], acc[:, :, :], tmp, ALU.mult),
                                   r=AK + RK, w=[('qkT', kq, half)])

            def T3():
                for gi, col in enumerate((O_MI, O_MF)):
                    for k in range(8):
                        S.op('pe', lambda e: e.matmul(banks[GB][0:4, gi * 128:(gi + 1) * 128], win[:, k, col:col + 4],
                                                      uT_t[:, k, :], start=(k == 0), stop=(k == 7)),
                             r=[kuT, wkey(col)], w=[bk(GB)], sig=(k == 7))
                po_ = 2 if own else 0
                G = lambda j: ('gf', j)
                C = lambda j: ('carry', j)
                S.op('act', lambda e: e.activation(gf[:, 0, :], banks[GB][0:4, 0:128], AF.Identity,
                                                   bias=gpar[:, po_ + 1:po_ + 2], scale=gpar[:, po_:po_ + 1]),
                     r=[bk(GB), 'gpar'], w=[G(0)])
                yield S.op('act', lambda e: e.activation(gf[:, 1, :], banks[GB][0:4, 128:256], AF.Exp, bias=gpar[:, 4:5], scale=-1.0),
                           r=[bk(GB), 'gpar'], w=[G(1)])
                yield S.op('act', lambda e: e.activation(gf[:, 1, :], gf[:, 1, :], AF.Ln, bias=epsc[0:4, 1:2]),
                           r=[G(1), 'consts'], w=[G(1)])
                if not own:
                    yield S.op('dve', lambda e: e.tensor_scalar(gf[:, 1, :], gf[:, 1, :], gpar[:, 5:6], None, ALU.mult),
                               r=[G(1), 'gpar'], w=[G(1)])
                yield S.op('dve', lambda e: e.tensor_tensor_scan(gf[:, 2, :], ones4[:], gf[:, 1, :], carry[:, 0:1], ALU.mult, ALU.add),
                           r=[G(1), 'consts4', C(0)], w=[G(2)])
                yield S.op('dve', lambda e: e.tensor_tensor(gf[:, 3, :], gf[:, 0, :], gf[:, 2, :], ALU.add), r=[G(0), G(2)], w=[G(3)])
                yield S.op('dve', lambda e: e.tensor_tensor_scan(gf[:, 4, :], ones4[:], gf[:, 3, :], carry[:, 1:2], ALU.mult, ALU.max),
                           r=[G(3), 'consts4', C(1)], w=[G(4)])
                S.op('dve', lambda e: e.tensor_scalar(carry[:, 2:3], carry[:, 1:2], -1.0, float(LNC), ALU.mult, ALU.add),
                     r=[C(1)], w=[C(2)])
                S.op('dve', lambda e: e.tensor_scalar(carry[:, 3:4], carry[:, 1:2], -1.0, None, ALU.mult), r=[C(1)], w=[C(3)])
                yield S.op('dve', lambda e: e.tensor_tensor(carry[:, 4:5], carry[:, 1:2], gf[:, 4, 127:128], ALU.subtract),
                           r=[C(1), G(4)], w=[C(4)])
                yield S.op('act', lambda e: e.activation(gf[:, 5, :], gf[:, 3, :], AF.Exp, bias=carry[:, 2:3]), r=[G(3), C(2)], w=[G(5)])
                yield S.op('act', lambda e: e.activation(gf[:, 6, :], gf[:, 2, :], AF.Exp, bias=carry[:, 3:4]), r=[G(2), C(3)], w=[G(6)])
                yield S.op('act', lambda e: e.activation(gf[:, 7, :], ones4[:], AF.Exp, bias=carry[:, 4:5], scale=0.0),
                           r=[C(4), 'consts4'], w=[G(7)])
                S.op('dve', lambda e: e.tensor_copy(carry[:, 0:1], gf[:, 2, 127:128]), r=[G(2), C(0)], w=[C(0)])
                yield S.op('dve', lambda e: e.tensor_copy(carry[:, 1:2], gf[:, 4, 127:128]), r=[G(4), C(1)], w=[C(1)])
                for a in range(3):
                    S.op('pe', lambda e: e.transpose(banks[GB][:, 256 + a * 4:256 + a * 4 + 4], gf[:, 5 + a, :], identf[0:4, 0:4]),
                         r=[G(5 + a), 'cst'], w=[bk(GB)], sig=(a == 2))
                yield S.op('dve', lambda e: e.tensor_copy(Gc, banks[GB][:, 256:268]), r=[bk(GB)], w=[kG])

            def T4():
                tb = [0]

                def nxt():
                    tb[0] += 1
                    return TMB[tb[0] % 2]
                b_ = nxt()
                proj_tm(uT_t, kuT, O_MV, b_)
                if own:
                    yield S.op('dve', lambda e: e.tensor_copy(mv_t[:, :, 0:128], banks[b_][:, :].rearrange("p (h d) -> p h d", h=4)),
                               r=[bk(b_)], w=[kmv])
                else:
                    yield S.op('act', lambda e: e.activation(mv_t[:, :, 0:128], banks[b_][:, :].rearrange("p (h d) -> p h d", h=4), AF.Copy),
                               r=[bk(b_)], w=[kmv])
                if own:
                    b_ = nxt()
                    proj_tm(uT_t, kuT, O_MO, b_)
                    S.op('act', lambda e: e.activation(qf[:], banks[b_][:, :], AF.Exp, scale=-1.0), r=[bk(b_)], w=['qf'])
                    S.op('act', lambda e: e.activation(qf[:], qf[:], AF.Ln, bias=epsc[:, 1:2]), r=['qf', 'consts'], w=['qf'])
                    yield S.op('act', lambda e: e.activation(sigo[:], qf[:], AF.Exp, scale=-1.0), r=['qf'], w=[('sigo', n % 2)])
                cs_k = cst[:, (C_CSO if own else C_CSP) + i * 16:(C_CSO if own else C_CSP) + i * 16 + 16]
                b_ = nxt()
                proj_tm(uT_t, kuT, O_AK, b_)
                gk = qk_prep(b_, C_KG, cs_k, qb16, 'qb16')
                next(gk)
                yield
                b_ = nxt()
                proj_tm(uT_t, kuT, O_AV, b_)
                if own:
                    yield S.op('dve', lambda e: e.tensor_copy(vx[:, blk, :, 0:128], banks[b_][:, :].rearrange("p (h d) -> p h d", h=4)),
                               r=[bk(b_)], w=[('vx', blk)])
                else:
                    yield S.op('act', lambda e: e.activation(vx[:, blk, :, 0:128], banks[b_][:, :].rearrange("p (h d) -> p h d", h=4), AF.Copy),
                               r=[bk(b_)], w=[('vx', blk)])
                yield from gk
                yield
                transpose_to(qb16, 'qb16', kT[:, :, blk * 128:(blk + 1) * 128], ('kT', blk), bank=0)
                yield
                if own:
                    b_ = nxt()
                    proj_tm(uT_t, kuT, O_AQ, b_)
                    yield from qk_prep(b_, C_QG, cs_k, qtok[i % 2], ('qtok', i % 2))

            yield from rr(T2(), T3(), T4())
            if own and i == 0 and 'qkT0' in dbg_d:
                S.dma('sp', dbg_d['qkT0'], qkT[:], r=[('qkT', kq, 0), ('qkT', kq, 1)], w=['dbg_qkT0'])
                S.finish('sp', ['dbg_qkT0'])

        def back(n):
            own, i = tt_info(n)
            Gc, kG = Gt[n % 3], ('Gt', n % 3)
            Gp, kGp = Gt[(n + 2) % 3], ('Gt', (n + 2) % 3)
            mv_t, kmv = mvx[n % 2], ('mvx', n % 2)
            qkT, kq = qkT2[n % 2], n % 2
            sigo = sigo2[n % 2]
            HB = (2, 3) if own else (4, 5)
            MK = [('ktl', 0), ('ktl', 1), ('stm', 0), ('stm', 1)]

            def head(h):
                e_ = h % 2
                kt_, kkt = ktl[e_], ('ktl', e_)
                st_, kst = stm[e_], ('stm', e_)
                hb = HB[e_]
                B = banks[hb]
                psb = B[:].bitcast(BF16)
                S.op('pe', lambda e: e.transpose(psb[:, 0:128], qkT[:, 4 + h, :], identb), r=[('qkT', kq, 1), 'cstb'], w=[bk(hb)], sig=True)
                yield
                if own:
                    yield S.op('dve', lambda e: e.tensor_scalar(kt_, psb[:, 0:128], Gc[:, h:h + 1], None, ALU.mult),
                               r=[bk(hb), kG], w=[kkt])
                else:
                    yield S.op('act', lambda e: e.activation(kt_, psb[:, 0:128], AF.Copy, scale=Gc[:, h:h + 1]),
                               r=[bk(hb), kG], w=[kkt])
                if own:
                    S.op('pe', lambda e: e.matmul(B[:, 64:192], qkT[:, 4 + h, :], qkT[:, h, :], start=True, stop=True),
                         r=[('qkT', kq, 0), ('qkT', kq, 1)], w=[bk(hb)], sig=True)
                    yield
                    yield S.op('dve', lambda e: e.scalar_tensor_tensor(out=st_, in0=B[:, 64:192], scalar=Gc[:, h:h + 1],
                                                                       in1=tri, op0=ALU.mult, op1=ALU.mult),
                               r=[bk(hb), kG, 'cst'], w=[kst])
                    S.op('pe', lambda e: e.matmul(B[:, 192:321], qkT[:, h, :], Cbf[:, h, 0:129], start=True, stop=False),
                         r=[('qkT', kq, 0), 'Cbf'], w=[bk(hb)], sig=False)
                    S.op('pe', lambda e: e.matmul(B[:, 192:321], st_, mv_t[:, h, 0:129], start=False, stop=True),
                         r=[kst, kmv], w=[bk(hb)], sig=True)
                    yield
                S.op('pe', lambda e: e.matmul(B[:, 336:465], kt_, mv_t[:, h, 0:129], start=True, stop=True),
                     r=[kkt, kmv], w=[bk(hb)], sig=True)
                yield
                yield S.op('dve', lambda e: e.scalar_tensor_tensor(out=Pst[:, h, 0:129], in0=Pst[:, h, 0:129], scalar=Gp[:, 8 + h:9 + h],
                                                                   in1=B[:, 336:465], op0=ALU.mult, op1=ALU.add),
                           r=[bk(hb), kGp, ('Pst', h)], w=[('Pst', h)])

            def pair_epilogue(p):
                c = 8 * p
                m = ml[:, 16 * p:16 * p + 16]
                for e_ in range(2):
                    S.op('dve', lambda e: e.tensor_copy(m[:, e_:e_ + 1], banks[HB[e_]][:, 320:321]), r=[bk(HB[e_])], w=[('ml', p, 0)])
                S.op('dve', lambda e: e.scalar_tensor_tensor(out=m[:, 2:4], in0=m[:, 0:2], scalar=-1.0, in1=m[:, 0:2],
                                                             op0=ALU.mult, op1=ALU.max), r=[('ml', p, 0)], w=[('ml', p, 1)])
                S.op('dve', lambda e: e.tensor_tensor(m[:, 2:4], m[:, 2:4], Gc[:, 4 + 2 * p:6 + 2 * p], ALU.max),
                     r=[('ml', p, 1), kG], w=[('ml', p, 1)])
                yield S.op('dve', lambda e: e.reciprocal(m[:, 4:6], m[:, 2:4]), r=[('ml', p, 1)], w=[('ml', p, 2)])
                for e_ in range(2):
                    B = banks[HB[e_]]
                    h_ = 2 * p + e_
                    yield S.op('act', lambda e: e.activation(mls[:, h_ * 128:(h_ + 1) * 128], B[:, 192:320], AF.Square,
                                                             accum_out=m[:, 6 + e_:7 + e_]),
                               r=[bk(HB[e_])], w=[MK[h_], ('ml', p, 3 + e_)])
                S.op('dve', lambda e: e.tensor_tensor(m[:, 8:10], m[:, 4:6], m[:, 4:6], ALU.mult), r=[('ml', p, 2)], w=[('ml', p, 5)])
                yield S.op('dve', lambda e: e.tensor_tensor(m[:, 8:10], m[:, 8:10], m[:, 6:8], ALU.mult),
                           r=[('ml', p, 5), ('ml', p, 3), ('ml', p, 4)], w=[('ml', p, 5)])
                S.op('act', lambda e: e.activation(m[:, 10:12], m[:, 8:10], AF.Ln, bias=epsc[:, 0:1], scale=1.0 / 128),
                     r=[('ml', p, 5), 'consts'], w=[('ml', p, 6)])
                yield S.op('act', lambda e: e.activation(m[:, 10:12], m[:, 10:12], AF.Exp, scale=-0.5), r=[('ml', p, 6)], w=[('ml', p, 6)])
                yield S.op('dve', lambda e: e.tensor_tensor(m[:, 12:14], m[:, 10:12], m[:, 4:6], ALU.mult),
                           r=[('ml', p, 6), ('ml', p, 2)], w=[('ml', p, 7)])
                for e_ in range(2):
                    h = 2 * p + e_
                    B = banks[HB[e_]]
                    yield S.op('dve', lambda e: e.scalar_tensor_tensor(out=mls[:, h * 128:(h + 1) * 128], in0=B[:, 192:320],
                                                                       scalar=m[:, 12 + e_:13 + e_], in1=sigo[:, h * 128:(h + 1) * 128],
                                                                       op0=ALU.mult, op1=ALU.mult),
                               r=[bk(HB[e_]), ('ml', p, 7), ('sigo', n % 2)], w=[MK[h]])
                for e_ in range(2):
                    h = 2 * p + e_
                    B = banks[HB[e_]]
                    psb = B[:].bitcast(BF16)
                    S.op('pe', lambda e: e.transpose(psb[:, 0:128], mls[:, h * 128:(h + 1) * 128], identb),
                         r=[MK[h], 'cstb'], w=[bk(HB[e_])], sig=True)
                    yield
                    yield S.op('dve', lambda e: e.tensor_scalar(yT[:, h, i * 128:(i + 1) * 128], psb[:, 0:128],
                                                                cst[:, C_GY + h:C_GY + h + 1], None, ALU.mult),
                               r=[bk(HB[e_]), 'cst'], w=[('yT', i, 0)])

            for p in range(2):
                yield from rr(head(2 * p), head(2 * p + 1))
                if own:
                    yield from pair_epilogue(p)
            if own or i == NT - 1:
                yield S.op('dve', lambda e: e.tensor_tensor(Cbf[:, :, 0:129], Pst[:, :, 0:129],
                                                            Gc[:, 8:12].unsqueeze(2).to_broadcast([128, 4, 129]), ALU.mult),
                           r=[('Pst', h) for h in range(4)] + [kG], w=['Cbf'])

        STEPS = {'main': 0, 'side': 0}

        def wrr(main, side, per):
            credit = 0.0
            for _ in main:
                STEPS['main'] += 1
                credit += per
                while side is not None and credit >= 1.0:
                    credit -= 1.0
                    STEPS['side'] += 1
                    try:
                        next(side)
                    except StopIteration:
                        side = None
            return side

        def dump(name, ap, keys):
            if name in dbg_d:
                S.dma('sp', dbg_d[name], ap, r=keys, w=['dbg_' + name])
                S.finish('sp', ['dbg_' + name])

        side = None
        side_rate = [1.0]
        main_per_tt = [60]
        if NTT > 0:
            run(stageA(0))
            run(rr(stageA(1) if NTT > 1 else None, front(0)))
        for n in range(NTT):
            own, i = tt_info(n)
            def _late(gen, rounds):
                for _ in range(rounds):
                    yield
                yield from gen
            nxt_front = rr(_late(stageA(n + 2), A_LAG) if n + 2 < NTT else None, front(n + 1) if n + 1 < NTT else None)
            if own and i % 2 == 1:
                if side is not None:
                    run(side)
                side = attention_group(i // 2)
                for _ in range(4):
                    next(side)
                g_ = i // 2
                side_total = (18 + 2 * g_) * 4 + 4 * 4 + 2
                side_rate[0] = 1.0 * side_total / (2.0 * max(main_per_tt[0], 1))
            m0 = STEPS['main']
            side = wrr(rr(nxt_front, back(n)), side, side_rate[0])
            if own:
                main_per_tt[0] = STEPS['main'] - m0
            if own and i == 3 and 'yT01' in dbg_d:
                if side is not None:
                    run(side)
                    side = None
                dump('yT01', yT[:, :, 0:256], [('yT', a, b) for a in range(2) for b in range(2)])
        if side is not None:
            run(side)

        if 'yT' in dbg_d:
            S.dma('sp', dbg_d['yT'], yT[:], r=[('yT', i, j) for i in range(16) for j in range(2)], w=['dbg_yT'])
            S.finish('sp', ['dbg_yT'])

        ab.close()
        if 'C' not in phases:
            S.barrier()
            return nc

        hres = sb("hres", [128, 16, 1024])
        wout = sb("wout", [128, 8, 1024], BF16)
        wp = sb("wp", [128, 2, 1024], BF16)
        cd = ExitStack()

        def sc(name, shape, dt=F32):
            return cd.enter_context(nc.sbuf_tensor("t_" + name, list(shape), dt))

        u2T = sc("u2T", [128, 8, 2048], BF16)
        u = sc("u_c", [128, 1024], BF16)
        ssb = sc("ssb_c", [128, 8])
        wupb = [sc("wup%d" % i, [128, 8, 512], BF16) for i in range(2)]
        wdnb = [sc("wdn%d" % i, [128, 4, 1024], BF16) for i in range(2)]
        hidT = [sc("hidT%d" % i, [128, 4, 512], BF16) for i in range(2)]
        relu_t = sc("relu_t", [128, 512])
        w_out_v = w_out.rearrange("(k p) n -> p k n", p=128)
        S.barrier()
        for k in range(8):
            S.dma('pool', wout[:, k, :], w_out_v[:, k, :], w=[('wout', k)])
        w_up_v = w_up.rearrange("(k p) n -> p k n", p=128)
        w_dn_v = w_down.rearrange("(k p) n -> p k n", p=128)

        def load_ffn_w(j):
            b = j % 2
            S.dma('pool', wupb[b][:], w_up_v[:, :, j * 512:(j + 1) * 512], w=[('wup', b)])
            S.dma('pool', wdnb[b][:], w_dn_v[:, j * 4:(j + 1) * 4, :], w=[('wdn', b)])

        u_c2 = sc("u_c2", [128, 1024], BF16)
        ssb_c2 = sc("ssb_c2", [128, 8])
        CSCR = [(u, 'u_c', ssb, 'ssb_c', 0), (u_c2, 'u_c2', ssb_c2, 'ssb_c2', 1)]

        def c_stream(i):
            S.dma('sp', hres[:, i, :], xo[i * 128:(i + 1) * 128, :], w=[('h', i)])
            for half in range(2):
                pbk = 2 + (i * 2 + half) % 6
                for k in range(8):
                    S.op('pe', lambda e: e.matmul(banks[pbk][:, :], yT[:, k, i * 128:(i + 1) * 128],
                                                  wout[:, k, half * 512:(half + 1) * 512], start=(k == 0), stop=(k == 7)),
                         r=[('yT', i, 0), ('yT', i, 1), ('wout', k)], w=[bk(pbk)], sig=k == 7)
                yield S.op('dve', lambda e: e.tensor_tensor(hres[:, i, half * 512:(half + 1) * 512], banks[pbk][:, :],
                                                            hres[:, i, half * 512:(half + 1) * 512], ALU.add),
                           r=[bk(pbk), ('h', i)], w=[('h', i)])
            if i == 0:
                load_ffn_w(0)
                load_ffn_w(1)
            sc_ = CSCR[i % 2]
            _norm_a1(hres[:, i, :], ('h', i), sc_[0], sc_[1], sc_[2], sc_[3])
            yield
            yield
            _norm_a2(C_G2, u2T[:, :, i * 128:(i + 1) * 128], ('u2T', i), sc_[0], sc_[1], sc_[4])
            yield

        def lagged0(gens, lag):
            live = []
            pending = list(gens)
            rnd = 0
            while live or pending:
                if pending and rnd % lag == 0:
                    live.append(pending.pop(0))
                for g in list(live):
                    try:
                        next(g)
                    except StopIteration:
                        live.remove(g)
                rnd += 1

        lagged0([c_stream(i) for i in range(NT)], 1)
        if 'h1' in dbg_d:
            S.dma('sp', dbg_d['h1'], hres[:], r=[('h', i) for i in range(16)], w=['dbg_h1'])
            S.finish('sp', ['dbg_h1'])

        nacc = [0]
        ffn_items = [(j, tg) for j in range(8) for tg in range(4)]

        def ffn_up(n):
            j, tg = ffn_items[n]
            b = j % 2
            hb = n % 2
            for m in range(4):
                pbk = 2 + nacc[0] % 6
                nacc[0] += 1
                for k in range(8):
                    S.op('pe', lambda e: e.matmul(banks[pbk][:, :], wupb[b][:, k, m * 128:(m + 1) * 128],
                                                  u2T[:, k, tg * 512:(tg + 1) * 512], start=(k == 0), stop=(k == 7)),
                         r=[('wup', b)] + [('u2T', tg * 4 + q) for q in range(4)], w=[bk(pbk)], sig=k == 7)
                S.op('act', lambda e: e.activation(relu_t[:], banks[pbk][:, :], AF.Relu), r=[bk(pbk)], w=['relu_t'])
                S.op('act', lambda e: e.activation(hidT[hb][:, m, :], relu_t[:], AF.Square),
                     r=['relu_t'], w=[('hidT', hb)])

        def ffn_down(n):
            j, tg = ffn_items[n]
            b = j % 2
            hb = n % 2
            for q in range(4):
                i = tg * 4 + q
                for half in range(2):
                    pbk = 2 + nacc[0] % 6
                    nacc[0] += 1
                    for m in range(4):
                        S.op('pe', lambda e: e.matmul(banks[pbk][:, :], hidT[hb][:, m, q * 128:(q + 1) * 128],
                                                      wdnb[b][:, m, half * 512:(half + 1) * 512], start=(m == 0), stop=(m == 3)),
                             r=[('hidT', hb), ('wdn', b)], w=[bk(pbk)], sig=m == 3)
                    S.op('dve', lambda e: e.tensor_tensor(hres[:, i, half * 512:(half + 1) * 512], banks[pbk][:, :],
                                                          hres[:, i, half * 512:(half + 1) * 512], ALU.add),
                         r=[bk(pbk), ('h', i)], w=[('h', i)])
            if tg == 3:
                if j + 2 < 8:
                    load_ffn_w(j + 2)
                if j == 5:
                    w_g_v = w_gate.rearrange("(k p) n -> p k n", p=128)
                    w_p_v = w_ple.rearrange("(k p) n -> p k n", p=128)
                    for k in range(8):
                        S.dma('pool', wout[:, k, :], w_g_v[:, k, :], w=[('wout', k)])
                    S.dma('pool', wp[:], w_p_v, w=['wp'])

        ffn_up(0)
        for n in range(len(ffn_items)):
            if n + 1 < len(ffn_items):
                ffn_up(n + 1)
            ffn_down(n)
        if 'h2' in dbg_d:
            S.dma('sp', dbg_d['h2'], hres[:], r=[('h', i) for i in range(16)], w=['dbg_h2'])
            S.finish('sp', ['dbg_h2'])
        S.barrier()
        cd.close()

        wg = wout
        NS = 4
        u_e = [sb("u_e%d" % i, [128, 1024], BF16) for i in range(NS)]
        ssb_e = [sb("ssb_e%d" % i, [128, 8]) for i in range(NS)]
        u3T = [sb("u3T%d" % i, [128, 8, 128], BF16) for i in range(NS)]
        pt = [sb("pt%d" % i, [128, 256], BF16) for i in range(NS)]
        pT = [sb("pT%d" % i, [128, 2, 128], BF16) for i in range(NS)]
        gsb = [sb("gsb%d" % i, [128, 1024]) for i in range(NS)]

        def pe_stream(i):
            b = i % NS
            tb = b % 2
            S.dma('pool', pt[b][:], po[i * 128:(i + 1) * 128, :], w=[('pt', b)])
            _norm_a1(hres[:, i, :], ('h', i), u_e[b], ('u_e', b), ssb_e[b], ('ssb_e', b))
            yield
            yield
            _norm_a2(C_G3, u3T[b], ('u3T', b), u_e[b], ('u_e', b), tb)
            yield
            psb = banks[tb][:].bitcast(BF16)
            for k in range(2):
                S.op('pe', lambda e: e.transpose(psb[:, k * 128:(k + 1) * 128], pt[b][:, k * 128:(k + 1) * 128], identb),
                     r=[('pt', b), 'cstb'], w=[bk(tb)], sig=k == 1)
            yield S.op('dve', lambda e: e.tensor_copy(pT[b][:], psb[:, 0:256].rearrange("p (k t) -> p k t", k=2)),
                       r=[bk(tb)], w=[('pT', b)])
            for half in range(2):
                pg = 2 + b
                pe_ = 6 + (b % 2)
                for k in range(8):
                    S.op('pe', lambda e: e.matmul(banks[pg][:, :], u3T[b][:, k, :], wg[:, k, half * 512:(half + 1) * 512],
                                                  start=(k == 0), stop=(k == 7)), r=[('u3T', b), ('wout', k)], w=[bk(pg)], sig=k == 7)
                sl = slice(half * 512, (half + 1) * 512)
                yield S.op('act', lambda e: e.activation(gsb[b][:, sl], banks[pg][:, :], AF.Sigmoid), r=[bk(pg)], w=[('gsb', b, half)])
                for k in range(2):
                    S.op('pe', lambda e: e.matmul(banks[pe_][:, :], pT[b][:, k, :], wp[:, k, half * 512:(half + 1) * 512],
                                                  start=(k == 0), stop=(k == 1)), r=[('pT', b), 'wp'], w=[bk(pe_)], sig=k == 1)
                yield S.op('dve', lambda e: e.tensor_tensor(gsb[b][:, sl], banks[pe_][:, :], gsb[b][:, sl], ALU.mult),
                           r=[bk(pe_), ('gsb', b, half)], w=[('gsb', b, half)])
                yield S.op('dve', lambda e: e.tensor_tensor(gsb[b][:, sl], gsb[b][:, sl], hres[:, i, sl], ALU.add),
                           r=[('gsb', b, half), ('h', i)], w=[('gsb', b, half)])
            S.dma('sp', out_d[i * 128:(i + 1) * 128, :], gsb[b][:], r=[('gsb', b, 0), ('gsb', b, 1)],
                  w=[('out', i)])
            yield

        def lagged(gens, lag):
            live = []
            pending = list(gens)
            rnd = 0
            while live or pending:
                if pending and rnd % lag == 0:
                    live.append(pending.pop(0))
                for g in list(live):
                    try:
                        next(g)
                    except StopIteration:
                        live.remove(g)
                rnd += 1

        lagged([pe_stream(i) for i in range(NT)], 2)
        S.finish('sp', [('out', i) for i in range(NT)])
    return nc


def _consts(core, inp):
    c = np.zeros((128, C_TOT), np.float32)

    def pk(v):
        return np.asarray(v, np.float32).reshape(-1, 128).T

    c[:, C_G1:C_G1 + 8] = pk(inp['attn_norm_g'][0])
    c[:, C_G2:C_G2 + 8] = pk(inp['mlp_norm_g'][0])
    c[:, C_G3:C_G3 + 8] = pk(inp['ple_norm_g'][0])
    c[:, C_GY:C_GY + 4] = pk(inp['mlstm_norm_g'][0])
    c[:, C_GY + 4:C_GY + 8] = pk(inp['attn_sub_norm_g'][0])
    cw = np.asarray(inp['conv_w'][0], np.float32)
    c[:, C_CW:C_CW + 32] = cw.T.reshape(8, 128, 4).transpose(1, 0, 2).reshape(128, 32)
    c[:, C_CB:C_CB + 8] = pk(inp['conv_b'][0])
    c[:, C_QG:C_QG + 64] = np.asarray(inp['q_norm_g'][0], np.float32)[None, :]
    c[:, C_KG:C_KG + 64] = np.asarray(inp['k_norm_g'][0], np.float32)[None, :]
    for j, nm in enumerate(('lambda_q1', 'lambda_k1', 'lambda_q2', 'lambda_k2')):
        c[:, C_LAM + j * 64:C_LAM + (j + 1) * 64] = np.asarray(inp[nm][0], np.float32)[None, :]
    odd = core % 2
    c[:, C_FL + 0] = 1.0 if odd else 0.0
    c[:, C_FL + 1] = 0.0 if odd else NEG
    c[:, C_FL + 2] = 0.0 if odd else NEG
    inv = (np.float32(500000.0) ** (-np.arange(0, 16, 2, dtype=np.float32) / np.float32(16))).astype(np.float32)
    for off, base in ((C_CSO, odd * 2048), (C_CSP, 0)):
        pos = (base + np.arange(2048, dtype=np.float32)).astype(np.float32)
        ang = (pos[:, None] * inv[None, :]).astype(np.float32)
        cs = np.concatenate([np.cos(ang), np.sin(ang)], axis=1).astype(np.float32)
        c[:, off:off + 256] = cs.reshape(16, 128, 16).transpose(1, 0, 2).reshape(128, 256)
    c[:, C_ID:C_ID + 128] = np.eye(128, dtype=np.float32)
    c[:, C_TRI:C_TRI + 128] = np.triu(np.ones((128, 128), np.float32))
    c[0:4, C_GB] = np.asarray(inp['igate_b'][0], np.float32)
    c[0:4, C_GB + 1] = np.asarray(inp['fgate_b'][0], np.float32)
    return c


def _constb():
    b = np.zeros((128, 128 + 1024), np.float32)
    b[:, 0:128] = np.eye(128, dtype=np.float32)
    tri = np.triu(np.ones((128, 128), np.float32))
    for m in range(2):
        b[:, 128 + m * 256:128 + m * 256 + 128] = tri
        b[:, 128 + m * 256 + 128:128 + m * 256 + 256] = 1.0
        b[:, 640 + m * 256:640 + m * 256 + 128] = 0.0
        b[:, 640 + m * 256 + 128:640 + m * 256 + 256] = tri
    return b


def make_in_maps(inp):
    x = np.asarray(inp['x'], np.float32)
    p = np.asarray(inp['p'], np.float32)
    shared = {
        'w_in': np.ascontiguousarray(inp['w_in'][0], dtype=np.float32),
        'w_out': np.ascontiguousarray(inp['w_out'][0], dtype=np.float32),
        'w_up': np.ascontiguousarray(inp['w_up'][0], dtype=np.float32),
        'w_down': np.ascontiguousarray(inp['w_down'][0], dtype=np.float32),
        'w_gate': np.ascontiguousarray(inp['w_ple_gate'][0], dtype=np.float32),
        'w_ple': np.ascontiguousarray(inp['w_ple_proj'][0], dtype=np.float32),
        'cstb': _constb(),
    }
    zeros = np.zeros((2048, 1024), np.float32)
    maps = []
    for c in range(8):
        b, hf = c // 2, c % 2
        m = dict(shared)
        m['xo'] = np.ascontiguousarray(x[b, hf * 2048:(hf + 1) * 2048])
        m['xp'] = np.ascontiguousarray(x[b, 0:2048]) if hf else zeros
        m['po'] = np.ascontiguousarray(p[0, b, hf * 2048:(hf + 1) * 2048])
        m['cst'] = _consts(c, inp)
        maps.append(m)
    return maps


def kernel(**inputs):
    nc = build()
    maps = make_in_maps(inputs)
    res = run_bass_kernel_spmd(nc, maps, core_ids=list(range(8)))
    out = np.zeros((4, 4096, 1024), np.float32)
    for c in range(8):
        out[c // 2, (c % 2) * 2048:(c % 2 + 1) * 2048] = res.results[c]['out']
    return out
```
